# Optimizing a Trainium2 kernel written in Bass

```python
import math
import jax, jax.numpy as jnp
from jax import lax
import numpy as np

D_MODEL = 1024
BATCH = 8
SEQ = 4096
DEPTH = 1

N_MEM = 256
EPS = 1e-6
NEG = -1e30
D_FF = 2816
DA_HEADS = 4
DA_QK_DIM = 64
DA_V_DIM = 2 * DA_QK_DIM
NSA_HEADS = 8
NSA_KV_GROUPS = 2
NSA_REP = NSA_HEADS // NSA_KV_GROUPS
NSA_HEAD_DIM = 64
CMP_LEN = 32
CMP_STRIDE = 16
CMP_HIDDEN = 128
SEL_LEN = 64
SEL_TOPK = 16
WIN = 512
FORCE_SCORE = 1e4
XA_HEADS = 4
XA_HEAD_DIM = D_MODEL // XA_HEADS
Q_BLOCK = 128
SEL_Q_BLOCK = 64

DA_Q = DA_HEADS * 2 * DA_QK_DIM
DA_V = DA_HEADS * DA_V_DIM
NSA_Q = NSA_HEADS * NSA_HEAD_DIM
NSA_KV = NSA_KV_GROUPS * NSA_HEAD_DIM
NSA_GATE = NSA_HEADS * 3
MIX_SIZES = (DA_Q, DA_Q, DA_V, NSA_Q, NSA_KV, NSA_KV, NSA_KV, NSA_KV, NSA_KV, NSA_KV, NSA_GATE)
MIX_IN = sum(MIX_SIZES)
MIX_OUT = DA_V + NSA_Q

kernel_name = 'hymba_diff_nsa_macaron_alibi_mem'


def rms_norm(x, g):
    xf = x.astype(jnp.float32)
    y = xf * lax.rsqrt(jnp.mean(xf * xf, axis=-1, keepdims=True) + EPS)
    return (y * g.astype(jnp.float32)).astype(x.dtype)


def masked_softmax(s, mask):
    p = jax.nn.softmax(jnp.where(mask, s, NEG), axis=-1)
    return jnp.where(mask, p, 0.0)


def alibi_slopes(n):
    return jnp.asarray(2.0 ** (-8.0 * np.arange(1, n + 1) / n), dtype=jnp.float32)


def swiglu(x, w_gate, w_up, w_down):
    return (jax.nn.silu(x @ w_gate) * (x @ w_up)) @ w_down


def diff_attention(q, k, v, lam, lam_init, subln_g, slopes):
    B, H, _, T, dq = q.shape
    dv = v.shape[-1]
    nq = T // Q_BLOCK
    scale = dq ** -0.5
    qb = q.reshape(B, H, 2, nq, Q_BLOCK, dq).transpose(3, 0, 1, 2, 4, 5)
    kpos = jnp.arange(T)

    def block(args):
        qi, n = args
        qpos = n * Q_BLOCK + jnp.arange(Q_BLOCK)
        dist = qpos[:, None] - kpos[None, :]
        mask = dist >= 0
        bias = -slopes[:, None, None] * dist.astype(jnp.float32)
        s = jnp.einsum('bhmqd,bhmkd->bhmqk', qi, k,
                       preferred_element_type=jnp.float32) * scale + bias[None, :, None]
        p = masked_softmax(s, mask)
        a = p[:, :, 0] - lam * p[:, :, 1]
        return jnp.einsum('bhqk,bhkd->bhqd', a.astype(v.dtype), v)

    o = lax.map(block, (qb, jnp.arange(nq)))
    o = o.transpose(1, 2, 0, 3, 4).reshape(B, H, T, dv)
    return rms_norm(o, subln_g) * (1.0 - lam_init)


def compress(kv, pe, w1, w2):
    B, G, T, d = kv.shape
    r = CMP_LEN // CMP_STRIDE
    nch = T // CMP_STRIDE
    nc = nch - r + 1
    chunks = kv.reshape(B, G, nch, CMP_STRIDE, d)
    blocks = jnp.concatenate([chunks[:, :, i:i + nc] for i in range(r)], axis=3)
    blocks = (blocks + pe).reshape(B, G, nc, CMP_LEN * d)
    return jax.nn.silu(blocks @ w1) @ w2


def nsa_attention(q, k_c, v_c, k_s, v_s, k_w, v_w, gates,
                  cmp_k_pe, cmp_k_w1, cmp_k_w2, cmp_v_pe, cmp_v_w1, cmp_v_w2, slopes):
    B, G, R, T, d = q.shape
    scale = d ** -0.5
    f32 = jnp.float32
    tpos = jnp.arange(T)
    sl = slopes[None, :, :, None, None]

    kc = compress(k_c, cmp_k_pe, cmp_k_w1, cmp_k_w2)
    vc = compress(v_c, cmp_v_pe, cmp_v_w1, cmp_v_w2)
    nc = kc.shape[2]
    c_start = jnp.arange(nc) * CMP_STRIDE
    c_end = c_start + CMP_LEN - 1
    dist_c = tpos[:, None] - c_end[None, :]
    s_c = jnp.einsum('bgrtd,bgcd->bgrtc', q, kc, preferred_element_type=f32) * scale \
        - sl * dist_c.astype(f32)
    p_c = masked_softmax(s_c, dist_c >= 0)
    o_cmp = jnp.einsum('bgrtc,bgcd->bgrtd', p_c.astype(vc.dtype), vc)

    nsel = T // SEL_LEN
    s_start = jnp.arange(nsel) * SEL_LEN
    overlap = jnp.clip(jnp.minimum(c_start[:, None] + CMP_LEN, s_start[None, :] + SEL_LEN)
                       - jnp.maximum(c_start[:, None], s_start[None, :]), 0, None)
    m_cs = overlap.astype(f32) / CMP_LEN
    imp = jnp.einsum('bgrtc,cj->bgtj', p_c, m_cs)
    blk = jnp.arange(nsel)[None, :]
    cur = (tpos // SEL_LEN)[:, None]
    forced = (blk == 0) | (blk == cur) | (blk == cur - 1)
    score = jnp.where(forced, FORCE_SCORE, jnp.where(blk <= cur, imp, -1.0))
    topk = min(SEL_TOPK, nsel)
    _, idx = lax.top_k(score, topk)

    ks_blocks = k_s.reshape(B, G, nsel, SEL_LEN, d)
    vs_blocks = v_s.reshape(B, G, nsel, SEL_LEN, d)
    nqb = T // SEL_Q_BLOCK
    q_sel = q.reshape(B, G, R, nqb, SEL_Q_BLOCK, d).transpose(3, 0, 1, 2, 4, 5)
    i_sel = idx.reshape(B, G, nqb, SEL_Q_BLOCK, topk).transpose(2, 0, 1, 3, 4)
    bi = jnp.arange(B)[:, None, None, None]
    gi = jnp.arange(G)[None, :, None, None]

    def sel_block(args):
        qi, ii, n = args
        kg = ks_blocks[bi, gi, ii]
        vg = vs_blocks[bi, gi, ii]
        qpos = n * SEL_Q_BLOCK + jnp.arange(SEL_Q_BLOCK)
        kpos = ii[..., None] * SEL_LEN + jnp.arange(SEL_LEN)
        dist = qpos[:, None, None] - kpos
        s = jnp.einsum('bgrqd,bgqkld->bgrqkl', qi, kg, preferred_element_type=f32) * scale \
            - slopes[None, :, :, None, None, None] * dist[:, :, None].astype(f32)
        s = s.reshape(B, G, R, SEL_Q_BLOCK, topk * SEL_LEN)
        mask = (dist >= 0).reshape(B, G, 1, SEL_Q_BLOCK, topk * SEL_LEN)
        p = masked_softmax(s, mask).reshape(B, G, R, SEL_Q_BLOCK, topk, SEL_LEN)
        return jnp.einsum('bgrqkl,bgqkld->bgrqd', p.astype(vg.dtype), vg)

    o_sel = lax.map(sel_block, (q_sel, i_sel, jnp.arange(nqb)))
    o_sel = o_sel.transpose(1, 2, 3, 0, 4, 5).reshape(B, G, R, T, d)

    nwb = T // Q_BLOCK
    nback = WIN // Q_BLOCK
    wlen = (nback + 1) * Q_BLOCK
    def band(t):
        tp = jnp.pad(t, ((0, 0), (0, 0), (WIN, 0), (0, 0))).reshape(B, G, nwb + nback, Q_BLOCK, d)
        tw = jnp.concatenate([tp[:, :, i:i + nwb] for i in range(nback + 1)], axis=3)
        return tw.transpose(2, 0, 1, 3, 4)
    kwin = band(k_w)
    vwin = band(v_w)
    q_win = q.reshape(B, G, R, nwb, Q_BLOCK, d).transpose(3, 0, 1, 2, 4, 5)

    def win_block(args):
        qi, ki, vi, n = args
        qpos = n * Q_BLOCK + jnp.arange(Q_BLOCK)
        kpos = n * Q_BLOCK - WIN + jnp.arange(wlen)
        dist = qpos[:, None] - kpos[None, :]
        mask = (dist >= 0) & (dist < WIN) & (kpos[None, :] >= 0)
        s = jnp.einsum('bgrqd,bgkd->bgrqk', qi, ki, preferred_element_type=f32) * scale \
            - sl * dist.astype(f32)
        p = masked_softmax(s, mask)
        return jnp.einsum('bgrqk,bgkd->bgrqd', p.astype(vi.dtype), vi)

    o_win = lax.map(win_block, (q_win, kwin, vwin, jnp.arange(nwb)))
    o_win = o_win.transpose(1, 2, 3, 0, 4, 5).reshape(B, G, R, T, d)

    return (gates[..., 0:1] * o_cmp + gates[..., 1:2] * o_sel + gates[..., 2:3] * o_win)


def mem_cross_attention(h, m, w_q, w_k, w_v, w_o):
    B, T, _ = h.shape
    M = m.shape[1]
    q = (h @ w_q).reshape(B, T, XA_HEADS, XA_HEAD_DIM)
    k = (m @ w_k).reshape(B, M, XA_HEADS, XA_HEAD_DIM)
    v = (m @ w_v).reshape(B, M, XA_HEADS, XA_HEAD_DIM)
    s = jnp.einsum('bthd,bmhd->bhtm', q, k, preferred_element_type=jnp.float32) * XA_HEAD_DIM ** -0.5
    p = jax.nn.softmax(s, axis=-1)
    o = jnp.einsum('bhtm,bmhd->bthd', p.astype(v.dtype), v)
    return o.reshape(B, T, XA_HEADS * XA_HEAD_DIM) @ w_o


def setup_inputs(seed: int = 0) -> dict:
    key = jax.random.key(seed)
    keys = jax.random.split(key, 40)
    kit = (keys[i] for i in range(40))
    L = DEPTH

    def nrm(shape, scale):
        return jax.random.normal(next(kit), shape, jnp.float32) * scale

    def gain(n):
        return 1.0 + nrm((L, n), 0.05)

    d = D_MODEL
    inp = {}
    inp['x'] = nrm((BATCH, SEQ, d), 1.0)
    inp['mem'] = nrm((BATCH, N_MEM, d), 1.0)
    inp['ffn1_pre_g'] = gain(d)
    inp['ffn1_post_g'] = gain(d)
    inp['ffn1_w_gate'] = nrm((L, d, D_FF), d ** -0.5)
    inp['ffn1_w_up'] = nrm((L, d, D_FF), d ** -0.5)
    inp['ffn1_w_down'] = nrm((L, D_FF, d), D_FF ** -0.5)
    inp['mix_pre_g'] = gain(d)
    inp['mix_post_g'] = gain(d)
    inp['w_mix_in'] = nrm((L, d, MIX_IN), d ** -0.5)
    inp['da_lambda_q1'] = nrm((L, DA_QK_DIM), 0.1)
    inp['da_lambda_k1'] = nrm((L, DA_QK_DIM), 0.1)
    inp['da_lambda_q2'] = nrm((L, DA_QK_DIM), 0.1)
    inp['da_lambda_k2'] = nrm((L, DA_QK_DIM), 0.1)
    inp['da_subln_g'] = gain(DA_V_DIM)
    inp['cmp_k_pe'] = nrm((L, CMP_LEN, NSA_HEAD_DIM), 0.02)
    inp['cmp_k_w1'] = nrm((L, CMP_LEN * NSA_HEAD_DIM, CMP_HIDDEN), (CMP_LEN * NSA_HEAD_DIM) ** -0.5)
    inp['cmp_k_w2'] = nrm((L, CMP_HIDDEN, NSA_HEAD_DIM), CMP_HIDDEN ** -0.5)
    inp['cmp_v_pe'] = nrm((L, CMP_LEN, NSA_HEAD_DIM), 0.02)
    inp['cmp_v_w1'] = nrm((L, CMP_LEN * NSA_HEAD_DIM, CMP_HIDDEN), (CMP_LEN * NSA_HEAD_DIM) ** -0.5)
    inp['cmp_v_w2'] = nrm((L, CMP_HIDDEN, NSA_HEAD_DIM), CMP_HIDDEN ** -0.5)
    inp['w_mix_out'] = nrm((L, MIX_OUT, d), MIX_OUT ** -0.5)
    inp['xa_pre_g'] = gain(d)
    inp['xa_post_g'] = gain(d)
    inp['mem_norm_g'] = gain(d)
    inp['xa_w_q'] = nrm((L, d, XA_HEADS * XA_HEAD_DIM), d ** -0.5)
    inp['xa_w_k'] = nrm((L, d, XA_HEADS * XA_HEAD_DIM), d ** -0.5)
    inp['xa_w_v'] = nrm((L, d, XA_HEADS * XA_HEAD_DIM), d ** -0.5)
    inp['xa_w_o'] = nrm((L, XA_HEADS * XA_HEAD_DIM, d), (XA_HEADS * XA_HEAD_DIM) ** -0.5)
    inp['ffn2_pre_g'] = gain(d)
    inp['ffn2_post_g'] = gain(d)
    inp['ffn2_w_gate'] = nrm((L, d, D_FF), d ** -0.5)
    inp['ffn2_w_up'] = nrm((L, d, D_FF), d ** -0.5)
    inp['ffn2_w_down'] = nrm((L, D_FF, d), D_FF ** -0.5)
    return inp


def reference(x, mem, ffn1_pre_g, ffn1_post_g, ffn1_w_gate, ffn1_w_up, ffn1_w_down,
              mix_pre_g, mix_post_g, w_mix_in, da_lambda_q1, da_lambda_k1, da_lambda_q2,
              da_lambda_k2, da_subln_g, cmp_k_pe, cmp_k_w1, cmp_k_w2, cmp_v_pe, cmp_v_w1,
              cmp_v_w2, w_mix_out, xa_pre_g, xa_post_g, mem_norm_g, xa_w_q, xa_w_k, xa_w_v,
              xa_w_o, ffn2_pre_g, ffn2_post_g, ffn2_w_gate, ffn2_w_up, ffn2_w_down):
    B, T, _ = x.shape
    split_idx = [int(v) for v in np.cumsum(MIX_SIZES)[:-1]]
    da_slopes = alibi_slopes(DA_HEADS)
    nsa_slopes = alibi_slopes(NSA_HEADS).reshape(NSA_KV_GROUPS, NSA_REP)
    for l in range(DEPTH):
        h = rms_norm(x, ffn1_pre_g[l])
        x = x + 0.5 * rms_norm(swiglu(h, ffn1_w_gate[l], ffn1_w_up[l], ffn1_w_down[l]), ffn1_post_g[l])

        h = rms_norm(x, mix_pre_g[l])
        z = h @ w_mix_in[l]
        da_q, da_k, da_v, n_q, n_kc, n_vc, n_ks, n_vs, n_kw, n_vw, n_g = jnp.split(z, split_idx, axis=-1)

        lam_init = 0.8 - 0.6 * math.exp(-0.3 * l)
        lam = (jnp.exp(jnp.sum(da_lambda_q1[l] * da_lambda_k1[l]))
               - jnp.exp(jnp.sum(da_lambda_q2[l] * da_lambda_k2[l])) + lam_init)
        q_a = da_q.reshape(B, T, DA_HEADS, 2, DA_QK_DIM).transpose(0, 2, 3, 1, 4)
        k_a = da_k.reshape(B, T, DA_HEADS, 2, DA_QK_DIM).transpose(0, 2, 3, 1, 4)
        v_a = da_v.reshape(B, T, DA_HEADS, DA_V_DIM).transpose(0, 2, 1, 3)
        o_a = diff_attention(q_a, k_a, v_a, lam, lam_init, da_subln_g[l], da_slopes)
        o_a = o_a.transpose(0, 2, 1, 3).reshape(B, T, DA_V)

        def kvh(t):
            return t.reshape(B, T, NSA_KV_GROUPS, NSA_HEAD_DIM).transpose(0, 2, 1, 3)
        q_b = n_q.reshape(B, T, NSA_KV_GROUPS, NSA_REP, NSA_HEAD_DIM).transpose(0, 2, 3, 1, 4)
        gates = jax.nn.sigmoid(n_g.reshape(B, T, NSA_KV_GROUPS, NSA_REP, 3)).transpose(0, 2, 3, 1, 4)
        o_b = nsa_attention(q_b, kvh(n_kc), kvh(n_vc), kvh(n_ks), kvh(n_vs), kvh(n_kw), kvh(n_vw),
                            gates, cmp_k_pe[l], cmp_k_w1[l], cmp_k_w2[l],
                            cmp_v_pe[l], cmp_v_w1[l], cmp_v_w2[l], nsa_slopes)
        o_b = o_b.transpose(0, 3, 1, 2, 4).reshape(B, T, NSA_Q)

        o = jnp.concatenate([o_a, o_b], axis=-1) @ w_mix_out[l]
        x = x + rms_norm(o, mix_post_g[l])

        h = rms_norm(x, xa_pre_g[l])
        m = rms_norm(mem, mem_norm_g[l])
        x = x + rms_norm(mem_cross_attention(h, m, xa_w_q[l], xa_w_k[l], xa_w_v[l], xa_w_o[l]), xa_post_g[l])

        h = rms_norm(x, ffn2_pre_g[l])
        x = x + 0.5 * rms_norm(swiglu(h, ffn2_w_gate[l], ffn2_w_up[l], ffn2_w_down[l]), ffn2_post_g[l])
    return x
```

```python
import math
from contextlib import ExitStack

import numpy as np
import ml_dtypes

import concourse.bass as bass
import concourse.mybir as mybir
from concourse.bass_utils import run_bass_kernel_spmd

F32 = mybir.dt.float32
BF16 = mybir.dt.bfloat16
AF = mybir.ActivationFunctionType
ALU = mybir.AluOpType
AX = mybir.AxisListType

SEM_LIMIT = 30000
DT_SIZE = {F32: 4, BF16: 2}

T = 4096
D = 1024
DFF = 2816
NT = T // 128
NCH = T // 512
MIXIN = 2840
NEGBIG = -30000.0
EPS = 1e-6


class Buf:
    __slots__ = ("name", "w", "r")

    def __init__(self, name=""):
        self.name = name
        self.w = None
        self.r = {}


class Tile:
    __slots__ = ("ap", "b")

    def __init__(self, ap, name=""):
        self.ap = ap
        self.b = Buf(name)

    def __getitem__(self, k):
        return self.ap[k]


class Sched:
    ENG = ("pe", "act", "dve", "pool", "sp")

    def __init__(self, nc, stack):
        self.nc = nc
        self.stack = stack
        self.q = {e: [] for e in self.ENG}
        self.cnt = {}
        self.sems = {}
        self.seen = {e: {} for e in self.ENG}
        self.nsem = 0
        self.nops = 0

    def _sem(self, key):
        if key not in self.sems:
            self.sems[key] = self.stack.enter_context(self.nc.semaphore("s%d" % self.nsem))
            self.nsem += 1
        return self.sems[key]

    def _bump(self, base, inc):
        ep, v = self.cnt.get(base, (0, 0))
        if v + inc > SEM_LIMIT:
            ep, v = ep + 1, 0
        v += inc
        self.cnt[base] = (ep, v)
        key = (base, ep)
        self._sem(key)
        return key, v

    def op(self, eng, fn, reads=(), writes=(), dma=None, skip_same=False):
        reads = [t.b if isinstance(t, Tile) else t for t in reads]
        writes = [t.b if isinstance(t, Tile) else t for t in writes]
        deps = []
        for b in reads:
            if b.w is not None:
                deps.append(b.w)
        for b in writes:
            if b.w is not None:
                deps.append(b.w)
            deps.extend(b.r.items())
        if dma is not None:
            dma = (dma, eng)
            ep, v = self.cnt.get(dma, (0, 0))
            if v > 0:
                deps.append(((dma, ep), v))
        waits = {}
        seen = self.seen[eng]
        for key, v in deps:
            if skip_same and key[0] == eng:
                continue
            if seen.get(key, 0) >= v:
                continue
            if waits.get(key, 0) < v:
                waits[key] = v
        for key, v in waits.items():
            seen[key] = v
        if dma is None:
            key, v = self._bump(eng, 1)
            inc = 1
        else:
            key, v = self._bump(dma, 16)
            inc = 16
        self.q[eng].append((list(waits.items()), fn, key, inc))
        self.nops += 1
        ev = (key, v)
        for b in reads:
            if b.r.get(key, 0) < v:
                b.r[key] = v
        for b in writes:
            b.w = ev
            b.r = {}
        return ev

    def barrier(self):
        allv = [((base, ep), v) for base, (ep, v) in self.cnt.items() if v > 0]
        for e in self.ENG:
            waits = []
            for key, v in allv:
                if self.seen[e].get(key, 0) < v:
                    waits.append((key, v))
                    self.seen[e][key] = v
            if waits:
                self.q[e].append((waits, None, None, 0))

    def emit(self):
        nc = self.nc
        sems = self.sems
        q = self.q

        def replay(name, e):
            for waits, fn, key, inc in q[name]:
                for k, v in waits:
                    e.wait_ge(sems[k], v)
                if fn is not None:
                    fn(e).then_inc(sems[key], inc)

        with nc.Block() as block:
            @block.tensor
            def _(e):
                replay("pe", e)

            @block.scalar
            def _(e):
                replay("act", e)

            @block.vector
            def _(e):
                replay("dve", e)

            @block.gpsimd
            def _(e):
                replay("pool", e)

            @block.sync
            def _(e):
                replay("sp", e)


class Arena:
    def __init__(self, nc, stack, nbytes):
        self.t = stack.enter_context(nc.sbuf_tensor("arena", [128, nbytes // 4], F32))
        self.top = 0
        self.cap = nbytes
        self.peak = 0

    def alloc(self, shape, dtype, name=""):
        n = int(np.prod(shape)) * DT_SIZE[dtype]
        n4 = (n + 3) // 4
        off = self.top // 4
        self.top += n4 * 4
        self.peak = max(self.peak, self.top)
        assert self.top <= self.cap, ("SBUF arena overflow", name, self.top, self.cap)
        ap = self.t[:, off:off + n4]
        if dtype != F32:
            ap = ap.bitcast(dtype)
            ap = ap[:, 0:int(np.prod(shape))]
        if len(shape) == 2:
            ap = ap.rearrange("p (a b) -> p a b", a=shape[0])
        elif len(shape) == 3:
            ap = ap.rearrange("p (a b c) -> p a b c", a=shape[0], b=shape[1])
        return Tile(ap, name)


class KB:
    def __init__(self, nc, st, debug):
        self.nc = nc
        self.S = Sched(nc, st)
        self.A = Arena(nc, st, 212000)
        self.psum = st.enter_context(nc.psum_tensor("psum", [128, 4096], F32))
        self.pb = [Buf("bank%d" % i) for i in range(8)]
        self.debug = debug

    def bank(self, i, n=1):
        return self.psum[:, i * 512:(i + n) * 512]

    def bank_bf(self, i):
        return self.psum[:, i * 512:(i + 1) * 512].bitcast(BF16)

    def dma(self, q, out, in_, R=(), W=(), sem=None):
        return self.S.op(q, lambda e: e.dma_start(out=out, in_=in_), reads=R, writes=W, dma=sem)

    def mm(self, out, lhsT, rhs, start, stop, R=(), W=()):
        return self.S.op("pe", lambda e: e.matmul(out, lhsT=lhsT, rhs=rhs, start=start, stop=stop,
                                                  skip_group_check=True),
                         reads=R, writes=W, skip_same=True)

    def tr(self, out, in_, R=(), W=()):
        ident = self.ident
        return self.S.op("pe", lambda e: e.transpose(out=out, in_=in_, identity=ident.ap),
                         reads=list(R) + [ident], writes=W, skip_same=True)

    def act(self, out, in_, func, R=(), W=(), **kw):
        return self.S.op("act", lambda e: e.activation(out=out, in_=in_, func=func, **kw), reads=R, writes=W)

    def v(self, fn, R=(), W=()):
        return self.S.op("dve", fn, reads=R, writes=W)

    def g(self, fn, R=(), W=()):
        return self.S.op("pool", fn, reads=R, writes=W)

    rstd_mode = "sqrt"

    def rstd_chain(self, ms_ap, t):
        self.v(lambda e: e.tensor_scalar(out=ms_ap, in0=ms_ap, scalar1=EPS, scalar2=None, op0=ALU.add), R=[t], W=[t])
        if self.rstd_mode == "ln":
            self.act(ms_ap, ms_ap, AF.Ln, R=[t], W=[t])
            self.act(ms_ap, ms_ap, AF.Exp, R=[t], W=[t], scale=-0.5)
        else:
            self.act(ms_ap, ms_ap, AF.Sqrt, R=[t], W=[t])
            self.v(lambda e: e.reciprocal(out=ms_ap, in_=ms_ap), R=[t], W=[t])

    def load_bcast(self, dram_row, n, name, sem):
        t = self.A.alloc([n], F32, name)
        self.dma("sp", t.ap, dram_row.partition_broadcast(128), W=[t], sem=sem)
        return t

    def load_w_groups(self, dram_w, rows, cols, name, sem, gcols):
        nch = rows // 128
        t = self.A.alloc([nch, cols], BF16, name)
        src = dram_w.rearrange("(c p) n -> p c n", p=128)
        bufs = []
        for g0 in range(0, cols, gcols):
            g1 = min(cols, g0 + gcols)
            b = Buf("%s_g%d" % (name, g0))
            bufs.append(b)
            self.dma("pool", t.ap[:, :, g0:g1], src[:, :, g0:g1], W=[b], sem=sem)
        return t, (lambda col: bufs[col // gcols])

    def load_w(self, dram_w, rows, cols, name, sem, defer=False):
        nch = rows // 128
        t = self.A.alloc([nch, cols], BF16, name)
        if defer:
            return t, (lambda: self._issue_w(t, dram_w, nch, sem))
        self._issue_w(t, dram_w, nch, sem)
        return t

    def _issue_w(self, t, dram_w, nch, sem):
        src = dram_w.rearrange("(c p) n -> p c n", p=128)
        step = max(1, nch // 4)
        for c0 in range(0, nch, step):
            c1 = min(nch, c0 + step)
            self.dma("pool", t.ap[:, c0:c1, :], src[:, c0:c1, :], W=[t], sem=sem)


def ffn_phase(k, src, dst, wg_d, wu_d, wd_d, gpre_d, gpost_d, tag):
    A, S = k.A, k.S
    mark = A.top
    GC = 640
    nchw = D // 128
    wg = A.alloc([nchw, DFF], BF16, "wg")
    wu = A.alloc([nchw, DFF], BF16, "wu")
    wgb, wub = [], []
    for g0 in range(0, DFF, GC):
        for (t_, d_, bl, sm) in ((wg, wg_d, wgb, "w0"), (wu, wu_d, wub, "w1")):
            b = Buf("wgrp")
            bl.append(b)
            g1 = min(DFF, g0 + GC)
            k.dma("pool", t_.ap[:, :, g0:g1], d_.rearrange("(c p) n -> p c n", p=128)[:, :, g0:g1],
                  W=[b], sem=sm)
    wd = k.load_w(wd_d, DFF, D, "wd", "w2")
    gpre = k.load_bcast(gpre_d, D, "gpre", "c0")
    gpost = k.load_bcast(gpost_d, D, "gpost", "c1")
    k.v(lambda e: e.tensor_scalar(out=gpost.ap, in0=gpost.ap, scalar1=0.5, scalar2=None, op0=ALU.mult),
        R=[gpost], W=[gpost])
    xs = [A.alloc([D], F32, "xs%d" % i) for i in range(2)]
    xr = [A.alloc([D], F32, "xr%d" % i) for i in range(2)]
    hn = [A.alloc([D], BF16, "hn%d" % i) for i in range(2)]
    hT = [A.alloc([8, 512], BF16, "hT%d" % i) for i in range(2)]
    AT = A.alloc([22, 512], BF16, "AT")
    sg = [A.alloc([512], BF16, "sg%d" % i) for i in range(2)]
    ytmp = A.alloc([D], F32, "ytmp")
    junk = A.alloc([D], BF16, "junk")
    ms = [A.alloc([4], F32, "ms%d" % i) for i in range(2)]
    ms2 = [A.alloc([1], F32, "ms2%d" % i) for i in range(2)]
    srct = src.rearrange("(n p) d -> n p d", p=128)
    dstt = dst.rearrange("(n p) d -> n p d", p=128)
    PB = k.pb
    cnt = {"x": 0, "r": 0}

    def pn_chain(c, tt):
        m = ms[c % 2]
        gt = 4 * c + tt
        i = tt % 2
        x = xs[i]
        k.dma("sp", x.ap, srct[gt], W=[x], sem="xs%d" % i)
        hh = hn[i]
        k.act(hh.ap, x.ap, AF.Square, R=[x], W=[hh, m], scale=1.0 / 32.0, accum_out=m.ap[:, tt:tt + 1])
        k.rstd_chain(m.ap[:, tt:tt + 1], m)
        k.v(lambda e: e.scalar_tensor_tensor(out=hh.ap, in0=x.ap, scalar=m.ap[:, tt:tt + 1], in1=gpre.ap,
                                             op0=ALU.mult, op1=ALU.mult), R=[x, m, gpre], W=[hh])

    def pn_tr(c, tt):
        h = hT[c % 2]
        hh = hn[tt % 2]
        pT = k.bank_bf(6)
        for dc in range(8):
            k.tr(pT[:, dc * 128:(dc + 1) * 128], hh.ap[:, dc * 128:(dc + 1) * 128], R=[hh], W=[PB[6]])
        k.act(h.ap[:, :, tt * 128:(tt + 1) * 128], pT.rearrange("p (a b) -> p a b", a=8), AF.Copy,
              R=[PB[6]], W=[h])

    def prenorm(c):
        for tt in range(4):
            pn_chain(c, tt)
            pn_tr(c, tt)

    def gateup(c):
        h = hT[c % 2]
        for f in range(22):
            gb, ub = f % 2, 2 + f % 2
            for dc in range(8):
                k.mm(k.bank(gb), wg.ap[:, dc, f * 128:(f + 1) * 128], h.ap[:, dc, :], dc == 0, dc == 7,
                     R=[wgb[f * 128 // GC], h], W=[PB[gb]])
            for dc in range(8):
                k.mm(k.bank(ub), wu.ap[:, dc, f * 128:(f + 1) * 128], h.ap[:, dc, :], dc == 0, dc == 7,
                     R=[wub[f * 128 // GC], h], W=[PB[ub]])
            s = sg[f % 2]
            k.act(s.ap, k.bank(gb), AF.Silu, R=[PB[gb]], W=[s])
            k.v(lambda e, s=s, ub=ub, f=f: e.tensor_tensor(out=AT.ap[:, f, :], in0=s.ap, in1=k.bank(ub), op=ALU.mult),
                R=[s, PB[ub]], W=[AT])
            if c + 1 < NCH:
                if f in (1, 6, 11, 16):
                    pn_chain(c + 1, (f - 1) // 5)
                if f in (4, 9, 14, 19):
                    pn_tr(c + 1, (f - 4) // 5)

    def down(c):
        for tt in range(4):
            gt = 4 * c + tt
            i = cnt["r"] % 2
            cnt["r"] += 1
            r = xr[i]
            k.dma("sp", r.ap, srct[gt], W=[r], sem="xr%d" % i)
            for half in range(2):
                for f in range(22):
                    k.mm(k.bank(4 + half), AT.ap[:, f, tt * 128:(tt + 1) * 128],
                         wd.ap[:, f, half * 512:(half + 1) * 512], f == 0, f == 21,
                         R=[AT, wd], W=[PB[4 + half]])
            m2 = ms2[i]
            k.v(lambda e: e.tensor_copy(out=ytmp.ap, in_=k.bank(4, 2)), R=[PB[4], PB[5]], W=[ytmp])
            k.act(junk.ap, ytmp.ap, AF.Square, R=[ytmp], W=[junk, m2], scale=1.0 / 32.0, accum_out=m2.ap)
            k.rstd_chain(m2.ap, m2)
            k.v(lambda e, m2=m2: e.scalar_tensor_tensor(out=ytmp.ap, in0=ytmp.ap, scalar=m2.ap, in1=gpost.ap,
                                                        op0=ALU.mult, op1=ALU.mult),
                R=[m2, gpost, ytmp], W=[ytmp])
            k.g(lambda e, r=r: e.tensor_tensor(out=r.ap, in0=r.ap, in1=ytmp.ap, op=ALU.add), R=[r, ytmp], W=[r])
            k.dma("pool", dstt[gt], r.ap, R=[r], sem="xo%d" % i)

    prenorm(0)
    for c in range(NCH):
        gateup(c)
        down(c)
    S.barrier()
    A.top = mark


def make_prenorm(k, gpre, nx=2):
    A = k.A
    xs = [A.alloc([D], F32, "pxs%d" % i) for i in range(nx)]
    hn = [A.alloc([D], BF16, "phn%d" % i) for i in range(2)]
    hT = [A.alloc([8, 512], BF16, "phT%d" % i) for i in range(2)]
    ms = [A.alloc([4], F32, "pms%d" % i) for i in range(2)]
    cnt = {"x": 0}
    PB = k.pb

    def norm_tile(x, m, tt, h, g, bank=6):
        i = cnt["x"] % 2
        cnt["x"] += 1
        hh = hn[i]
        k.act(hh.ap, x.ap if isinstance(x, Tile) else x[0], AF.Square, R=[x if isinstance(x, Tile) else x[1]],
              W=[hh, m], scale=1.0 / 32.0, accum_out=m.ap[:, tt:tt + 1])
        k.rstd_chain(m.ap[:, tt:tt + 1], m)
        xa = x.ap if isinstance(x, Tile) else x[0]
        xt = x if isinstance(x, Tile) else x[1]
        k.v(lambda e: e.scalar_tensor_tensor(out=hh.ap, in0=xa, scalar=m.ap[:, tt:tt + 1], in1=g.ap,
                                             op0=ALU.mult, op1=ALU.mult), R=[xt, m, g], W=[hh])
        pT = k.bank_bf(bank)
        for dc in range(8):
            k.tr(pT[:, dc * 128:(dc + 1) * 128], hh.ap[:, dc * 128:(dc + 1) * 128], R=[hh], W=[PB[bank]])
        k.act(h.ap[:, :, tt * 128:(tt + 1) * 128], pT.rearrange("p (a b) -> p a b", a=8), AF.Copy,
              R=[PB[bank]], W=[h])

    def prenorm(c, srct):
        h = hT[c % 2]
        m = ms[c % 2]
        for tt in range(4):
            gt = 4 * c + tt
            i = cnt["x"] % nx
            x = xs[i]
            k.dma("sp", x.ap, srct[gt], W=[x], sem="pxs%d" % i)
            norm_tile(x, m, tt, h, gpre)
        return h

    def chain(c, tt, srct):
        m = ms[c % 2]
        gt = 4 * c + tt
        x = xs[tt % nx]
        k.dma("sp", x.ap, srct[gt], W=[x], sem="pxs%d" % (tt % nx))
        hh = hn[tt % 2]
        k.act(hh.ap, x.ap, AF.Square, R=[x], W=[hh, m], scale=1.0 / 32.0, accum_out=m.ap[:, tt:tt + 1])
        k.rstd_chain(m.ap[:, tt:tt + 1], m)
        k.v(lambda e: e.scalar_tensor_tensor(out=hh.ap, in0=x.ap, scalar=m.ap[:, tt:tt + 1], in1=gpre.ap,
                                             op0=ALU.mult, op1=ALU.mult), R=[x, m, gpre], W=[hh])

    def tr(c, tt, bank=6):
        h = hT[c % 2]
        hh = hn[tt % 2]
        pT = k.bank_bf(bank)
        for dc in range(8):
            k.tr(pT[:, dc * 128:(dc + 1) * 128], hh.ap[:, dc * 128:(dc + 1) * 128], R=[hh], W=[PB[bank]])
        k.act(h.ap[:, :, tt * 128:(tt + 1) * 128], pT.rearrange("p (a b) -> p a b", a=8), AF.Copy,
              R=[PB[bank]], W=[h])
        return h

    prenorm.chain = chain
    prenorm.tr = tr
    prenorm.norm_tile = norm_tile
    prenorm.hT = hT
    prenorm.ms = ms
    return prenorm


FM = [(0, 4, 0.125), (512, 4, 1.0), (1536, 4, 0.125), (2048, 1, 1.0), (2176, 1, 1.0), (2304, 1, 1.0), (2560, 1, 1.0)]


def inproj_phase(k, x1, I, zT, vtok, gts):
    A, S, PB = k.A, k.S, k.pb
    mark = A.top
    gpre = k.load_bcast(I["mix_pre_g"], D, "gpre", "c0")
    win, wgrp = k.load_w_groups(I["w_mix_in"], D, MIXIN, "win", "w0", 256)
    prenorm = make_prenorm(k, gpre, nx=4)
    stg = [A.alloc([512], BF16, "stg%d" % i) for i in range(3)]
    vst = [A.alloc([768], BF16, "vst%d" % i) for i in range(2)]
    gst = [A.alloc([24], F32, "gst%d" % i) for i in range(2)]
    srct = x1.rearrange("(n p) d -> n p d", p=128)
    vtokt = vtok.rearrange("(n p) d -> n p d", p=128)
    gtst = gts.rearrange("(n p) d -> n p d", p=128)
    hnext = prenorm(0, srct)
    for c in range(NCH):
        h = hnext
        i = 0
        for (z0, ntile, sc) in FM:
            for ft in range(ntile):
                bk = i % 2
                col = z0 + ft * 128
                for dc in range(8):
                    k.mm(k.bank(bk), win.ap[:, dc, col:col + 128], h.ap[:, dc, :], dc == 0, dc == 7,
                         R=[wgrp(col), h], W=[PB[bk]])
                s = stg[i % 3]
                if sc != 1.0:
                    k.v(lambda e, s=s, bk=bk, sc=sc: e.tensor_scalar(out=s.ap, in0=k.bank(bk), scalar1=sc, scalar2=None,
                                                                     op0=ALU.mult), R=[PB[bk]], W=[s])
                else:
                    k.act(s.ap, k.bank(bk), AF.Copy, R=[PB[bk]], W=[s])
                k.dma("pool" if sc != 1.0 else "act", zT[col:col + 128, c * 512:(c + 1) * 512], s.ap, R=[s],
                      sem="stg%d" % (i % 3))
                if c + 1 < NCH:
                    if i in (1, 5, 9, 13):
                        prenorm.chain(c + 1, (i - 1) // 4, srct)
                    if i in (3, 7, 11, 15):
                        hnext = prenorm.tr(c + 1, (i - 3) // 4)
                i += 1
        for tt in range(4):
            gt = 4 * c + tt
            hs = h.ap[:, :, tt * 128:(tt + 1) * 128]
            for dc in range(8):
                k.mm(k.bank(2), hs[:, dc, :], win.ap[:, dc, 1024:1536], dc == 0, dc == 7,
                     R=[wgrp(1024), wgrp(1280), h], W=[PB[2]])
            for (o0, z0, n) in ((0, 2432, 128), (128, 2688, 128), (256, 2816, 24)):
                for dc in range(8):
                    k.mm(k.bank(3)[:, o0:o0 + n], hs[:, dc, :], win.ap[:, dc, z0:z0 + n], dc == 0, dc == 7,
                         R=[wgrp(z0), h], W=[PB[3]])
            vs = vst[gt % 2]
            k.act(vs.ap[:, 0:512], k.bank(2), AF.Copy, R=[PB[2]], W=[vs])
            k.v(lambda e, vs=vs: e.tensor_copy(out=vs.ap[:, 512:768], in_=k.bank(3)[:, 0:256]), R=[PB[3]], W=[vs])
            k.dma("pool", vtokt[gt], vs.ap, R=[vs], sem="vst%d" % (gt % 2))
            gs = gst[gt % 2]
            k.v(lambda e, gs=gs: e.tensor_copy(out=gs.ap, in_=k.bank(3)[:, 256:280]), R=[PB[3]], W=[gs])
            k.dma("pool", gtst[gt], gs.ap, R=[gs], sem="gst%d" % (gt % 2))
    S.barrier()
    A.top = mark


class AttnPipe:
    def __init__(self, k, PT):
        self.k = k
        self.PT = PT
        self.n = 0
        self.pending = None
        self.obank_first = {}

    def _qk(self, st):
        k = self.k
        sb = st["sb"]
        lo, hi = st["lo"], st["hi"]
        out = k.bank(sb)[:, lo:hi]
        ex = st["extra"]
        k.mm(out, st["kT"], st["qT"], True, len(ex) == 0, R=st["Rqk"], W=[k.pb[sb]])
        for n, (lhsT, rhs, a, b, R) in enumerate(ex):
            k.mm(k.bank(sb)[:, a:b], lhsT, rhs, False, n == len(ex) - 1, R=R, W=[k.pb[sb]])

    def _rest(self, st):
        k = self.k
        sb = st["sb"]
        if "restfn" in st:
            st["restfn"](sb)
            return
        lo, hi = st["lo"], st["hi"]
        pt = st["pt"]
        k.act(pt.ap[0:st["kp"], lo:hi], k.bank(sb)[0:st["kp"], lo:hi], AF.Exp, R=[k.pb[sb]], W=[pt], scale=st["scale"])
        if st.get("mask") is not None:
            mt, eng = st["mask"]
            k.S.op(eng, lambda e: e.tensor_tensor(out=pt.ap[:, lo:hi], in0=pt.ap[:, lo:hi], in1=mt.ap[:, lo:hi],
                                                  op=ALU.mult), reads=[pt, mt], writes=[pt])
        for (oreg, ob, lhs_lo, lhs_hi, rhs, start, stop, R) in st["pv"]:
            k.mm(oreg, pt.ap[0:st["kp"], lhs_lo:lhs_hi], rhs, start, stop, R=[pt] + R, W=[k.pb[ob]])
        if st.get("evac") is not None:
            st["evac"]()

    def push(self, st):
        st["sb"] = self.n % 4
        st["pt"] = self.PT[self.n % len(self.PT)]
        self.n += 1
        if "qkfn" in st:
            st["qkfn"](st["sb"])
        else:
            self._qk(st)
        if self.pending is None:
            self.pending = []
        self.pending.append(st)
        if len(self.pending) > 3:
            self._rest(self.pending.pop(0))

    def flush(self):
        while self.pending:
            self._rest(self.pending.pop(0))


def attn_phase(k, I, zT, vtok, gts, osc):
    A, S, PB = k.A, k.S, k.pb
    mark = A.top
    def cload(name, shape, dtype, src, q="sp"):
        t = A.alloc(shape, dtype, name)
        k.dma(q, t.ap, src, W=[t], sem="c0")
        return t
    tric = cload("tric", [128], BF16, I["c_tric"])
    tril = cload("tril", [128], BF16, I["c_tril"])
    gates = cload("gates", [NT, 24], F32, gts.rearrange("(n p) j -> p n j", p=128))
    k.act(gates.ap, gates.ap, AF.Exp, R=[gates], W=[gates], scale=-1.0)
    k.v(lambda e: e.tensor_scalar(out=gates.ap, in0=gates.ap, scalar1=1.0, scalar2=None, op0=ALU.add), R=[gates], W=[gates])
    k.v(lambda e: e.reciprocal(out=gates.ap, in_=gates.ap), R=[gates], W=[gates])
    lq1 = k.load_bcast(I["da_lambda_q1"], 64, "lq1", "c1")
    lk1 = k.load_bcast(I["da_lambda_k1"], 64, "lk1", "c1")
    lq2 = k.load_bcast(I["da_lambda_q2"], 64, "lq2", "c1")
    lk2 = k.load_bcast(I["da_lambda_k2"], 64, "lk2", "c1")
    lsum = A.alloc([2], F32, "lsum")
    neglam = A.alloc([1], F32, "neglam")
    k.v(lambda e: e.tensor_tensor(out=lq1.ap, in0=lq1.ap, in1=lk1.ap, op=ALU.mult), R=[lq1, lk1], W=[lq1])
    k.v(lambda e: e.tensor_tensor(out=lq2.ap, in0=lq2.ap, in1=lk2.ap, op=ALU.mult), R=[lq2, lk2], W=[lq2])
    k.v(lambda e: e.reduce_sum(out=lsum.ap[:, 0:1], in_=lq1.ap, axis=AX.X), R=[lq1], W=[lsum])
    k.v(lambda e: e.reduce_sum(out=lsum.ap[:, 1:2], in_=lq2.ap, axis=AX.X), R=[lq2], W=[lsum])
    k.act(lsum.ap, lsum.ap, AF.Exp, R=[lsum], W=[lsum])
    lam_init = 0.8 - 0.6 * math.exp(-0.3 * 0)
    k.v(lambda e: e.tensor_tensor(out=neglam.ap, in0=lsum.ap[:, 1:2], in1=lsum.ap[:, 0:1], op=ALU.subtract),
        R=[lsum], W=[neglam])
    k.v(lambda e: e.tensor_scalar(out=neglam.ap, in0=neglam.ap, scalar1=-lam_init, scalar2=None, op0=ALU.add),
        R=[neglam], W=[neglam])
    gsub = k.load_bcast(I["da_subln_g"], 128, "gsub", "c1")
    k.v(lambda e: e.tensor_scalar(out=gsub.ap, in0=gsub.ap, scalar1=1.0 - lam_init, scalar2=None, op0=ALU.mult),
        R=[gsub], W=[gsub])

    QB = [A.alloc([T], BF16, "QB%d" % i) for i in range(2)]
    KBf = [A.alloc([T], BF16, "KB%d" % i) for i in range(2)]
    PT = [A.alloc([512], BF16, "PT%d" % i) for i in range(4)]
    pipe = AttnPipe(k, PT)
    rl = [A.alloc([4], F32, "rl%d" % i) for i in range(2)]
    rg = [A.alloc([4], F32, "rg%d" % i) for i in range(2)]
    t2 = [A.alloc([4, 128], F32, "t2%d" % i) for i in range(2)]
    ob = [A.alloc([4, 128], BF16, "ob%d" % i) for i in range(2)]
    msd = [A.alloc([4], F32, "msd%d" % i) for i in range(2)]
    osct = osc.rearrange("(n p) d -> p n d", p=128)
    state = {"cj": 0, "ev": 0}

    def oview(pair):
        return k.bank(4 + 2 * pair, 2).rearrange("p (j w) -> p j w", j=4)

    def load_q(buf, zrow, slot):
        k.dma("sp", buf.ap[0:64, :], zT[zrow:zrow + 64, :], W=[buf], sem="q%s" % buf.b.name)
        k.dma("sp", buf.ap[64:68, :], I["c_qaug"][slot], W=[buf], sem="q%s" % buf.b.name)

    def load_k(buf, zrow):
        k.dma("sp", buf.ap[0:64, :], zT[zrow:zrow + 64, :], W=[buf], sem="k%s" % buf.b.name)
        k.dma("sp", buf.ap[64:68, :], I["c_kaug"], W=[buf], sem="k%s" % buf.b.name)

    def load_v(buf, col, dv):
        src = vtok.rearrange("(n p) d -> p n d", p=128)
        for n0 in range(0, NT, 8):
            k.dma("sp", buf.ap[:, n0:n0 + 8, 0:dv], src[:, n0:n0 + 8, col:col + dv], W=[buf], sem="v%s" % buf.b.name)

    def run_job(qb, kT_of, v_of, dv1, tiles_of, extras_of, evac_of, scale=1.0, kp_of=None, hook=None):
        for c in range(NCH):
            pair = state["cj"] % 2
            state["cj"] += 1
            ov = oview(pair)
            tl = tiles_of(c)
            last_for = {}
            for n, (kt, j0, j1) in enumerate(tl):
                for j in range(j0, j1):
                    last_for[j] = n
            started = set()
            for n, (kt, j0, j1) in enumerate(tl):
                lo, hi = j0 * 128, j1 * 128
                kTap, kR = kT_of(kt)
                vap, vR = v_of(kt)
                kp = 128 if kp_of is None else kp_of(kt)
                pv = []
                for j in range(j0, j1):
                    obk = 4 + 2 * pair + j // 2
                    start = obk not in started
                    started.add(obk)
                    pv.append((ov[:, j, 0:dv1], obk, j * 128, (j + 1) * 128, vap, start, last_for[j] == n, vR))
                st = dict(kT=kTap, qT=qb.ap[0:68, c * 512 + lo:c * 512 + hi], lo=lo, hi=hi, Rqk=[qb] + kR,
                          extra=extras_of(c, kt, j0, j1), pv=pv, scale=scale, kp=kp,
                          evac=(evac_of(c, pair) if n == len(tl) - 1 else None))
                pipe.push(st)
            if hook is not None:
                hook(c)

    def causal_tiles(c):
        tl = [(kt, 0, 4) for kt in range(4 * c)]
        tl += [(4 * c + i, i, 4) for i in range(4)]
        return tl

    def causal_extras(c, kt, j0, j1):
        if kt >= 4 * c:
            i = kt - 4 * c
            return [(k.ident.ap, tric.ap, i * 128, (i + 1) * 128, [k.ident, tric])]
        return []

    def rl_of(ov, col, i, clamp=False):
        r = rl[i]
        if clamp:
            k.v(lambda e: e.tensor_scalar(out=r.ap.rearrange("p (a b) -> p a b", b=1), in0=ov[:, :, col:col + 1],
                                          scalar1=1e-30, scalar2=None, op0=ALU.max), R=[], W=[r])
            k.v(lambda e: e.reciprocal(out=r.ap, in_=r.ap), R=[r], W=[r])
        else:
            k.v(lambda e: e.reciprocal(out=r.ap.rearrange("p (a b) -> p a b", b=1), in_=ov[:, :, col:col + 1]),
                R=[], W=[r])
        return r

    cmask = cload("cmask", [2560], BF16, I["c_cmask"], q="act")
    expand = cload("expand", [4096], BF16, I["c_expand"], q="act")
    selA = cload("selA", [NT, 64], F32, I["c_selA"].rearrange("(n p) j -> p n j", p=128), q="act")
    selB = cload("selB", [NT, 64], F32, I["c_selB"].rearrange("(n p) j -> p n j", p=128), q="act")
    tricn = cload("tricn", [128], BF16, I["c_tricn"], q="act")
    w1 = [A.alloc([32, 128], BF16, "w1%d" % i) for i in range(2)]
    w2 = [A.alloc([64], BF16, "w2%d" % i) for i in range(2)]
    peT = [A.alloc([32], F32, "peT%d" % i) for i in range(2)]
    peTb = [A.alloc([32], BF16, "peTb%d" % i) for i in range(2)]
    cb = [A.alloc([1], F32, "cb%d" % i) for i in range(2)]
    for i, nm in enumerate(("k", "v")):
        k.dma("pool", w1[i].ap[0:64], I["cmp_%s_w1" % nm].rearrange("(pos d) h -> d pos h", d=64), W=[w1[i]], sem="w0")
        k.dma("pool", w2[i].ap, I["cmp_%s_w2" % nm], W=[w2[i]], sem="w0")
        pe_src = I["cmp_%s_pe" % nm].rearrange("pos d -> d pos")
        S.op("act", lambda e, i=i, pe_src=pe_src: e.dma_start(out=peT[i].ap[0:64], in_=pe_src,
                                                           allow_slow_non_contiguous=True),
             writes=[peT[i].b], dma="c1")
    def da_evac(h, m):
        def evac_of(c, pair):
            def ev():
                i = state["ev"] % 2
                state["ev"] += 1
                ov = oview(pair)
                OB = [PB[4 + 2 * pair], PB[5 + 2 * pair]]
                r = rl[i]
                k.v(lambda e: e.reciprocal(out=r.ap.rearrange("p (a b) -> p a b", b=1), in_=ov[:, :, 128:129]),
                    R=OB, W=[r])
                rb = r.ap.rearrange("p (a b) -> p a b", b=1).to_broadcast([128, 4, 128])
                dch = datmp.ap[:, 4 * c:4 * c + 4, :]
                if m == 0:
                    k.v(lambda e: e.tensor_tensor(out=dch, in0=ov[:, :, 0:128], in1=rb, op=ALU.mult),
                        R=OB + [r], W=[datmp])
                    return
                tt_ = t2[i]
                k.v(lambda e: e.tensor_tensor(out=tt_.ap, in0=ov[:, :, 0:128], in1=rb, op=ALU.mult),
                    R=OB + [r], W=[tt_])
                k.v(lambda e: e.scalar_tensor_tensor(out=dch, in0=tt_.ap, scalar=neglam.ap, in1=dch,
                                                     op0=ALU.mult, op1=ALU.add), R=[tt_, neglam, datmp], W=[datmp])
                k.v(lambda e: e.tensor_tensor(out=tt_.ap, in0=dch, in1=dch, op=ALU.mult), R=[datmp], W=[tt_])
                k.v(lambda e: e.reduce_sum(out=msall.ap[:, 4 * c:4 * c + 4], in_=tt_.ap, axis=AX.X), R=[tt_], W=[msall])
                if c == NCH - 1:
                    k.v(lambda e: e.tensor_scalar(out=msall.ap, in0=msall.ap, scalar1=1.0 / 128.0, scalar2=EPS,
                                                  op0=ALU.mult, op1=ALU.add), R=[msall], W=[msall])
                    k.act(msall.ap, msall.ap, AF.Ln, R=[msall], W=[msall])
                    k.act(msall.ap, msall.ap, AF.Exp, R=[msall], W=[msall], scale=-0.5)
                    for q4 in range(4):
                        sl = slice(8 * q4, 8 * q4 + 8)
                        mb = msall.ap[:, sl].rearrange("p (a b) -> p a b", b=1).to_broadcast([128, 8, 128])
                        gb = gsub.ap.rearrange("p (a b) -> p a b", a=1).broadcast_to([128, 8, 128])
                        k.v(lambda e, sl=sl, mb=mb: e.tensor_tensor(out=datmp.ap[:, sl, :], in0=datmp.ap[:, sl, :], in1=mb,
                                                                    op=ALU.mult), R=[datmp, msall], W=[datmp])
                        k.v(lambda e, sl=sl, gb=gb: e.tensor_tensor(out=oball.ap[:, sl, :], in0=datmp.ap[:, sl, :], in1=gb,
                                                                    op=ALU.mult), R=[datmp, gsub], W=[oball])
                    k.dma("pool", osct[:, :, h * 128:(h + 1) * 128], oball.ap, R=[oball], sem="oball")
            return ev
        return evac_of

    da_jobs = [(h, m) for h in range(4) for m in range(2)]

    def da_load(n):
        h, m = da_jobs[n]
        load_q(QB[n % 2], h * 128 + m * 64, h)
        load_k(KBf[n % 2], 512 + h * 128 + m * 64)
        if m == 0:
            load_v(VA[h % 2], h * 128, 128)

    if k.stage_attn & 1:
        mda = A.top
        VA = [A.alloc([NT, 129], BF16, "VA%d" % i) for i in range(2)]
        for t in VA:
            k.g(lambda e, t=t: e.memset(t.ap[:, :, 128:129], 1.0), W=[t])
        datmp = A.alloc([NT, 128], F32, "datmp")
        msall = A.alloc([NT], F32, "msall")
        oball = A.alloc([NT, 128], BF16, "oball")
        da_load(0)
        for n, (h, m) in enumerate(da_jobs):
            if n + 1 < len(da_jobs):
                da_load(n + 1)
            qb, kb, vb = QB[n % 2], KBf[n % 2], VA[h % 2]
            run_job(qb,
                    lambda kt, kb=kb: (kb.ap[0:68, kt * 128:(kt + 1) * 128], [kb]),
                    lambda kt, vb=vb: (vb.ap[:, kt, :], [vb]),
                    129, causal_tiles, causal_extras, da_evac(h, m))
        pipe.flush()
        S.barrier()
        A.top = mda

    if k.stage_attn & 2:
        VN = [A.alloc([NT, 65], BF16, "VN%d" % i) for i in range(2)]
        for t in VN:
            k.g(lambda e, t=t: e.memset(t.ap[:, :, 64:65], 1.0), W=[t])
        onsa = [A.alloc([NT, 64], F32, "onsa%d" % i) for i in range(4)]
        imp = A.alloc([NT, 64], F32, "imp")
        QS = [A.alloc([T], BF16, "QS%d" % i) for i in range(4)]
        Mtiles = [A.alloc([512], BF16, "Mt%d" % i) for i in range(3)]
        for i, nm in enumerate(("k", "v")):
            k.v(lambda e, i=i: e.tensor_copy(out=peTb[i].ap[0:64], in_=peT[i].ap[0:64]), R=[peT[i]], W=[peTb[i]])
            for pos in range(32):
                k.mm(k.bank(3)[:, i:i + 1], w1[i].ap[0:64, pos, :], peTb[i].ap[0:64, pos:pos + 1], pos == 0, pos == 31,
                     R=[w1[i], peTb[i]], W=[PB[3]])
            k.v(lambda e, i=i: e.tensor_copy(out=cb[i].ap, in_=k.bank(3)[:, i:i + 1]), R=[PB[3]], W=[cb[i]])
        cin = [QB[0], QB[1]]
        AcT = [A.alloc([256], BF16, "AcT%d" % i) for i in range(2)]
        KcT = A.alloc([256], BF16, "KcT")
        Vc = A.alloc([2, 129], BF16, "Vc")
        negT = A.alloc([T], BF16, "negT")
        score = A.alloc([NT, 64], F32, "score")
        sc2 = A.alloc([64], F32, "sc2")
        m8 = A.alloc([8], F32, "m8")
        thr = A.alloc([NT], F32, "thr")
        negm = A.alloc([NT, 64], BF16, "negm")
        k.g(lambda e: e.memset(Vc.ap, 0.0), W=[Vc])
        k.g(lambda e: e.memset(Vc.ap[:, :, 128:129], 1.0), W=[Vc])
        k.dma("sp", Vc.ap[:, :, 64:128], I["c_mcs"].rearrange("(n p) j -> p n j", p=128), W=[Vc], sem="c1")
        print("NSA arena top", A.top)
        k.g(lambda e: e.memset(KcT.ap, 0.0), W=[KcT])
        k.dma("sp", KcT.ap[64:68, :], I["c_kaugc"], W=[KcT], sem="c1")

        for g in range(2):
            for i, zr in enumerate((2048, 2176)):
                k.dma("sp", cin[i].ap[0:64, :], zT[zr + g * 64:zr + g * 64 + 64, :], W=[cin[i]], sem="cin%d" % i)
                for pos in range(32):
                    k.mm(k.bank(3)[:, 8:8 + 255], w1[i].ap[0:64, pos, :], cin[i].ap[0:64, pos:pos + 16 * 254 + 1:16],
                         pos == 0, pos == 31, R=[w1[i], cin[i]], W=[PB[3]])
                k.g(lambda e, i=i: e.memset(AcT[i].ap, 0.0), W=[AcT[i]])
                k.act(AcT[i].ap[:, 0:255], k.bank(3)[:, 8:8 + 255], AF.Silu, R=[PB[3], cb[i]], W=[AcT[i]], bias=cb[i].ap)
            k.mm(k.bank(3)[0:64, 0:255], w2[0].ap, AcT[0].ap[:, 0:255], True, True, R=[w2[0], AcT[0]], W=[PB[3]])
            k.v(lambda e: e.tensor_copy(out=KcT.ap[0:64, 0:255], in_=k.bank(3)[0:64, 0:255]), R=[PB[3]], W=[KcT])
            for ct in range(2):
                k.mm(k.bank(3)[:, 256 + ct * 64:256 + (ct + 1) * 64], AcT[1].ap[:, ct * 128:(ct + 1) * 128], w2[1].ap,
                     True, True, R=[w2[1], AcT[1]], W=[PB[3]])
            k.v(lambda e: e.tensor_copy(out=Vc.ap[:, :, 0:64],
                                        in_=k.bank(3)[:, 256:384].rearrange("p (a b) -> p a b", a=2)),
                R=[PB[3]], W=[Vc])

            def cmp_tiles(c):
                tl = [(0, 0, 4)]
                if c >= 4:
                    tl.append((1, 0, 4))
                return tl

            def cmp_extras(c, kt, j0, j1):
                if kt == 0 and c >= 5:
                    return []
                off = c * 512 if kt == 0 else (c - 4) * 512
                return [(k.ident.ap, cmask.ap[:, off:off + 512], 0, 512, [k.ident, cmask])]

            def cmp_evac(r, head):
                def evac_of(c, pair):
                    def ev():
                        i = state["ev"] % 2
                        state["ev"] += 1
                        ov = oview(pair)
                        OB = [PB[4 + 2 * pair], PB[5 + 2 * pair]]
                        r_ = rl[i]
                        r3 = r_.ap.rearrange("p (a b) -> p a b", b=1)
                        k.v(lambda e: e.tensor_scalar(out=r3, in0=ov[:, :, 128:129], scalar1=1e-30, scalar2=None,
                                                      op0=ALU.max), R=OB, W=[r_])
                        k.v(lambda e: e.reciprocal(out=r_.ap, in_=r_.ap), R=[r_], W=[r_])
                        g_ = rg[i]
                        k.v(lambda e: e.tensor_tensor(out=g_.ap.rearrange("p (a b) -> p a b", b=1), in0=r3,
                                                      in1=gates.ap[:, 4 * c:4 * c + 4, head * 3:head * 3 + 1], op=ALU.mult),
                            R=[r_, gates], W=[g_])
                        gb = g_.ap.rearrange("p (a b) -> p a b", b=1).to_broadcast([128, 4, 64])
                        k.v(lambda e: e.tensor_tensor(out=onsa[r].ap[:, 4 * c:4 * c + 4, :], in0=ov[:, :, 0:64], in1=gb,
                                                      op=ALU.mult), R=OB + [g_], W=[onsa[r]])
                        rb = r3.to_broadcast([128, 4, 64])
                        ich = imp.ap[:, 4 * c:4 * c + 4, :]
                        if r == 0:
                            k.v(lambda e: e.tensor_tensor(out=ich, in0=ov[:, :, 64:128], in1=rb, op=ALU.mult),
                                R=OB + [r_], W=[imp])
                        else:
                            tt_ = t2[i]
                            k.v(lambda e: e.tensor_tensor(out=tt_.ap[:, :, 0:64], in0=ov[:, :, 64:128], in1=rb,
                                                          op=ALU.mult), R=OB + [r_], W=[tt_])
                            k.g(lambda e: e.tensor_tensor(out=ich, in0=ich, in1=tt_.ap[:, :, 0:64], op=ALU.add),
                                R=[tt_, imp], W=[imp])
                    return ev
                return evac_of

            def acc_evac(r, head, branch, final):
                def evac_of(c, pair, ov=None, OB=None):
                    def ev(ov=ov, OB=OB):
                        i = state["ev"] % 2
                        state["ev"] += 1
                        if ov is None:
                            ov = oview(pair)
                            OB = [PB[4 + 2 * pair], PB[5 + 2 * pair]]
                        r_ = rl[i]
                        r3 = r_.ap.rearrange("p (a b) -> p a b", b=1)
                        k.v(lambda e: e.reciprocal(out=r3, in_=ov[:, :, 64:65]), R=OB, W=[r_])
                        g_ = rg[i]
                        k.v(lambda e: e.tensor_tensor(out=g_.ap.rearrange("p (a b) -> p a b", b=1), in0=r3,
                                                      in1=gates.ap[:, 4 * c:4 * c + 4, head * 3 + branch:head * 3 + branch + 1],
                                                      op=ALU.mult), R=[r_, gates], W=[g_])
                        gb = g_.ap.rearrange("p (a b) -> p a b", b=1).to_broadcast([128, 4, 64])
                        tt_ = t2[i]
                        k.v(lambda e: e.tensor_tensor(out=tt_.ap[:, :, 0:64], in0=ov[:, :, 0:64], in1=gb, op=ALU.mult),
                            R=OB + [g_], W=[tt_])
                        och = onsa[r].ap[:, 4 * c:4 * c + 4, :]
                        if not final:
                            k.g(lambda e: e.tensor_tensor(out=och, in0=och, in1=tt_.ap[:, :, 0:64], op=ALU.add),
                                R=[tt_, onsa[r]], W=[onsa[r]])
                        else:
                            o_ = ob[i]
                            k.v(lambda e: e.tensor_tensor(out=o_.ap[:, :, 0:64], in0=och, in1=tt_.ap[:, :, 0:64], op=ALU.add),
                                R=[tt_, onsa[r]], W=[o_])
                            k.dma("pool", osct[:, 4 * c:4 * c + 4, 512 + head * 64:512 + (head + 1) * 64], o_.ap[:, :, 0:64],
                                  R=[o_], sem="ob%d" % i)
                    return ev
                return evac_of

            for r in range(4):
                load_q(QS[r], 1536 + (g * 4 + r) * 64, 4 + g * 4 + r)
            kvs = KBf[0], VN[0]
            kvw = KBf[1], VN[1]
            load_k(kvs[0], 2304 + g * 64)
            load_v(kvs[1], 512 + g * 64, 64)
            load_k(kvw[0], 2560 + g * 64)
            load_v(kvw[1], 640 + g * 64, 64)

            for r in range(4):
                run_job(QS[r],
                        lambda kt: (KcT.ap[0:68, kt * 128:(kt + 1) * 128], [KcT]),
                        lambda kt: (Vc.ap[:, kt, :], [Vc]),
                        129, cmp_tiles, cmp_extras, cmp_evac(r, g * 4 + r))
            pipe.flush()

            k.v(lambda e: e.tensor_tensor(out=score.ap, in0=imp.ap, in1=selA.ap, op=ALU.mult), R=[imp, selA], W=[score])
            k.v(lambda e: e.tensor_tensor(out=score.ap, in0=score.ap, in1=selB.ap, op=ALU.add), R=[score, selB], W=[score])

            def sel_piece(n):
                k.v(lambda e: e.max(out=m8.ap, in_=score.ap[:, n, :]), R=[score], W=[m8])
                k.v(lambda e: e.match_replace(out=sc2.ap, in_to_replace=m8.ap, in_values=score.ap[:, n, :],
                                              imm_value=-3.0), R=[score, m8], W=[sc2])
                k.v(lambda e: e.max(out=m8.ap, in_=sc2.ap), R=[sc2], W=[m8])
                k.v(lambda e: e.tensor_copy(out=thr.ap[:, n:n + 1], in_=m8.ap[:, 7:8]), R=[m8], W=[thr])

            def sel_extras(c, kt, j0, j1):
                ex = [(expand.ap[0:64, kt * 128:(kt + 1) * 128], negT.ap[0:64, c * 512 + j0 * 128:c * 512 + j1 * 128],
                       j0 * 128, j1 * 128, [expand, negT])]
                return ex + causal_extras(c, kt, j0, j1)

            def win_tiles(c):
                tl = []
                for kt in range(max(0, 4 * c - 4), 4 * c + 4):
                    j0 = max(0, kt - 4 * c)
                    j1 = min(3, kt - 4 * c + 4) + 1
                    tl.append((kt, j0, j1))
                return tl

            def win_extras(c, kt, j0, j1):
                ex = []
                if kt >= 4 * c:
                    i = kt - 4 * c
                    ex.append((k.ident.ap, tric.ap, i * 128, (i + 1) * 128, [k.ident, tric]))
                if kt < 4 * c:
                    i = kt - 4 * c + 4
                    ex.append((k.ident.ap, tril.ap, i * 128, (i + 1) * 128, [k.ident, tril]))
                return ex

            seln = {"n": 0}

            def win_hook(c):
                sel_piece(seln["n"])
                seln["n"] += 1

            for r in range(4):
                kb, vb = kvw
                run_job(QS[r],
                        lambda kt, kb=kb: (kb.ap[0:68, kt * 128:(kt + 1) * 128], [kb]),
                        lambda kt, vb=vb: (vb.ap[:, kt, :], [vb]),
                        65, win_tiles, win_extras, acc_evac(r, g * 4 + r, 2, False), hook=win_hook)
            pipe.flush()
            tb = thr.ap.rearrange("p (a b) -> p a b", b=1).to_broadcast([128, NT, 64])
            k.v(lambda e: e.tensor_tensor(out=score.ap, in0=score.ap, in1=tb, op=ALU.is_ge), R=[score, thr], W=[score])
            k.v(lambda e: e.tensor_tensor(out=score.ap, in0=score.ap, in1=selA.ap, op=ALU.mult), R=[score, selA], W=[score])
            k.v(lambda e: e.tensor_copy(out=negm.ap, in_=score.ap), R=[score], W=[negm])
            for n0 in range(0, NT, 8):
                pT = k.bank_bf(3)
                for n in range(n0, n0 + 8):
                    k.tr(pT[0:64, (n - n0) * 128:(n - n0 + 1) * 128], negm.ap[:, n, :], R=[negm], W=[PB[3]])
                k.v(lambda e, n0=n0, pT=pT: e.tensor_copy(out=negT.ap[0:64, n0 * 128:(n0 + 8) * 128], in_=pT[0:64, :]),
                    R=[PB[3]], W=[negT])
            kb, vb = kvs
            mi = 0
            for c in range(NCH):
                tl = causal_tiles(c)
                started = set()
                for n, (kt, j0, j1) in enumerate(tl):
                    lo, hi = j0 * 128, j1 * 128
                    diag = kt >= 4 * c
                    Mt = Mtiles[mi % 3]
                    mi += 1
                    def mqk(sb, lo=lo, hi=hi, kt=kt, c=c, diag=diag):
                        k.mm(k.bank(sb)[:, lo:hi], expand.ap[0:64, kt * 128:(kt + 1) * 128],
                             negT.ap[0:64, c * 512 + lo:c * 512 + hi], True, not diag, R=[expand, negT], W=[PB[sb]])
                        if diag:
                            i = kt - 4 * c
                            k.mm(k.bank(sb)[:, i * 128:(i + 1) * 128], k.ident.ap, tricn.ap, False, True,
                                 R=[k.ident, tricn], W=[PB[sb]])

                    def mrest(sb, lo=lo, hi=hi, Mt=Mt):
                        k.v(lambda e: e.tensor_scalar(out=Mt.ap[:, lo:hi], in0=k.bank(sb)[:, lo:hi], scalar1=0.0,
                                                      scalar2=None, op0=ALU.max), R=[PB[sb]], W=[Mt])

                    pipe.push(dict(qkfn=mqk, restfn=mrest))
                    for r in range(4):
                        ovr = k.bank(4 + r)[:, 0:260].rearrange("p (j w) -> p j w", j=4)
                        pv = []
                        for j in range(j0, j1):
                            start = (4 + r) not in started
                            started.add(4 + r)
                            pv.append((ovr[:, j, 0:65], 4 + r, j * 128, (j + 1) * 128, vb.ap[:, kt, :], start,
                                       kt == 4 * c + j, [vb]))
                        last = n == len(tl) - 1
                        st = dict(kT=kb.ap[0:68, kt * 128:(kt + 1) * 128],
                                  qT=QS[r].ap[0:68, c * 512 + lo:c * 512 + hi], lo=lo, hi=hi, Rqk=[QS[r], kb],
                                  extra=[], pv=pv, scale=1.0, kp=128, mask=(Mt, "dve"),
                                  evac=(acc_evac(r, g * 4 + r, 1, True)(c, None, ovr, [PB[4 + r]]) if last else None))
                        pipe.push(st)
            pipe.flush()
    S.barrier()
    A.top = mark


def outxa_phase(k, I, x1, osc, x3):
    A, S, PB = k.A, k.S, k.pb
    mark = A.top
    wout, go_wout = k.load_w(I["w_mix_out"], D, D, "wout", "w0", defer=True)
    wq, go_wq = k.load_w(I["xa_w_q"], D, D, "wq", "w1", defer=True)
    wo, go_wo = k.load_w(I["xa_w_o"], D, D, "wo", "w2", defer=True)
    gmixpost = k.load_bcast(I["mix_post_g"], D, "gmp", "c0")
    gxapre = k.load_bcast(I["xa_pre_g"], D, "gxp", "c0")
    gxapost = k.load_bcast(I["xa_post_g"], D, "gxo", "c0")
    KxT = A.alloc([8, 256], BF16, "KxT")
    Vx = A.alloc([2, D], BF16, "Vx")
    ones = A.alloc([128], BF16, "ones")
    k.g(lambda e: e.memset(ones.ap, 1.0), W=[ones])
    hn = [A.alloc([D], BF16, "hn%d" % i) for i in range(2)]
    junk = A.alloc([D], BF16, "junk")
    TB = 2

    tfc = {"n": 0}

    def to_fm(src_ap, src_R, dst, col0, nhn, Wb=None):
        Wb = [dst] if Wb is None else Wb
        tb = (TB, 7)[tfc["n"] % 2]
        tfc["n"] += 1
        pT = k.bank_bf(tb)
        for dc in range(8):
            k.tr(pT[:, dc * 128:(dc + 1) * 128], src_ap[:, dc * 128:(dc + 1) * 128], R=src_R, W=[PB[tb]])
        k.act(dst.ap[:, :, col0:col0 + 128], pT.rearrange("p (a b) -> p a b", a=8), AF.Copy, R=[PB[tb]], W=Wb)

    m2 = A.top
    gmem = k.load_bcast(I["mem_norm_g"], D, "gmem", "c0")
    wk = k.load_w(I["xa_w_k"], D, D, "wk", "w3")
    wv = k.load_w(I["xa_w_v"], D, D, "wv", "w4")
    go_wout()
    go_wq()
    go_wo()
    mT = A.alloc([8, 256], BF16, "mT")
    msm = A.alloc([2], F32, "msm")
    memt = I["mem"].rearrange("(n p) d -> n p d", p=128)
    xm = [A.alloc([D], F32, "xm%d" % i) for i in range(2)]
    for mt in range(2):
        k.dma("sp", xm[mt].ap, memt[mt], W=[xm[mt]], sem="c1")
        k.act(junk.ap, xm[mt].ap, AF.Square, R=[xm[mt]], W=[junk, msm], scale=1.0 / 32.0, accum_out=msm.ap[:, mt:mt + 1])
    k.rstd_chain(msm.ap, msm)
    for mt in range(2):
        k.v(lambda e, mt=mt: e.scalar_tensor_tensor(out=hn[mt].ap, in0=xm[mt].ap, scalar=msm.ap[:, mt:mt + 1],
                                                    in1=gmem.ap, op0=ALU.mult, op1=ALU.mult),
            R=[xm[mt], msm, gmem], W=[hn[mt]])
        to_fm(hn[mt].ap, [hn[mt]], mT, mt * 128, mt)
    for ft in range(8):
        bk = ft % 2
        for dc in range(8):
            k.mm(k.bank(bk)[:, 0:256], wk.ap[:, dc, ft * 128:(ft + 1) * 128], mT.ap[:, dc, :], dc == 0, dc == 7,
                 R=[wk, mT], W=[PB[bk]])
        k.act(KxT.ap[:, ft, :], k.bank(bk)[:, 0:256], AF.Copy, R=[PB[bk]], W=[KxT])
    for mt in range(2):
        for half in range(2):
            bk = half
            for dc in range(8):
                k.mm(k.bank(bk), mT.ap[:, dc, mt * 128:(mt + 1) * 128], wv.ap[:, dc, half * 512:(half + 1) * 512],
                     dc == 0, dc == 7, R=[wv, mT], W=[PB[bk]])
            k.v(lambda e, mt=mt, half=half, bk=bk: e.tensor_copy(
                out=Vx.ap[:, mt, half * 512:(half + 1) * 512], in_=k.bank(bk)), R=[PB[bk]], W=[Vx])
    S.barrier()
    A.top = m2
    ot = [A.alloc([D], BF16, "ot%d" % i) for i in range(4)]
    oT = A.alloc([8, 512], BF16, "oT")
    x2c = [[A.alloc([D], F32, "x2c%d_%d" % (i, j)) for j in range(4)] for i in range(3)]
    yb = [A.alloc([D], F32, "yb%d" % i) for i in range(4)]
    yb2 = [A.alloc([D], F32, "yb2%d" % i) for i in range(4)]
    msA = [A.alloc([4], F32, "msA%d" % i) for i in range(2)]
    msB = [A.alloc([4], F32, "msB%d" % i) for i in range(2)]
    msC = [A.alloc([4], F32, "msC%d" % i) for i in range(2)]
    h3T = A.alloc([8, 512], BF16, "h3T")
    qxT = [A.alloc([8, 512], BF16, "qxT%d" % i) for i in range(2)]
    PT = [A.alloc([512], BF16, "PTx%d" % i) for i in range(4)]
    oxT = oT
    rLb = [A.alloc([512], F32, "rLb%d" % i) for i in range(2)]
    osct = osc.rearrange("(n p) d -> n p d", p=128)
    x1tt = x1.rearrange("(n p) d -> n p d", p=128)
    x3t = x3.rearrange("(n p) d -> n p d", p=128)
    SB = [3, 4, 5]
    print("outxa arena top", A.top)
    cnt = {"s": 0, "e": 0}

    oTb = [Buf("oTb%d" % i) for i in range(4)]

    def proj_tile(srcT, tt, w, dst):
        for half in range(2):
            for fc in range(8):
                k.mm(k.bank(half), srcT.ap[:, fc, tt * 128:(tt + 1) * 128], w.ap[:, fc, half * 512:(half + 1) * 512],
                     fc == 0, fc == 7, R=[oTb[tt], w], W=[PB[half]])
            k.v(lambda e, half=half: e.tensor_copy(out=dst.ap[:, half * 512:(half + 1) * 512], in_=k.bank(half)),
                R=[PB[half]], W=[dst])

    def front1(c):
        X = x2c[c % 3]
        for tt in range(4):
            k.dma("sp", ot[tt].ap, osct[4 * c + tt], W=[ot[tt]], sem="ot%d" % tt)
        for tt in range(4):
            k.dma("sp", X[tt].ap, x1tt[4 * c + tt], W=[X[tt]], sem="x2c%d_%d" % (c % 3, tt))
        tf = lambda tt: to_fm(ot[tt].ap, [ot[tt]], oT, tt * 128, tt % 2, Wb=[oTb[tt]])
        pj = lambda tt: proj_tile(oT, tt, wout, yb[tt])
        tf(0); tf(1); pj(0); tf(2); pj(1); tf(3); pj(2); pj(3)

    def front2a(c):
        X = x2c[c % 3]
        mA, mB = msA[c % 2], msB[c % 2]
        for tt in range(4):
            k.act(junk.ap, yb[tt].ap, AF.Square, R=[yb[tt]], W=[junk, mA], scale=1.0 / 32.0, accum_out=mA.ap[:, tt:tt + 1])
        k.rstd_chain(mA.ap, mA)
        for tt in range(4):
            k.v(lambda e, tt=tt: e.scalar_tensor_tensor(out=yb[tt].ap, in0=yb[tt].ap, scalar=mA.ap[:, tt:tt + 1],
                                                        in1=gmixpost.ap, op0=ALU.mult, op1=ALU.mult),
                R=[yb[tt], mA, gmixpost], W=[yb[tt]])
            k.g(lambda e, tt=tt: e.tensor_tensor(out=X[tt].ap, in0=X[tt].ap, in1=yb[tt].ap, op=ALU.add),
                R=[yb[tt], X[tt]], W=[X[tt]])

    def front2b(c):
        X = x2c[c % 3]
        mA, mB = msA[c % 2], msB[c % 2]
        for tt in range(4):
            k.act(junk.ap, X[tt].ap, AF.Square, R=[X[tt]], W=[junk, mB], scale=1.0 / 32.0, accum_out=mB.ap[:, tt:tt + 1])
        k.rstd_chain(mB.ap, mB)

    def mixed(c, cb):
        X = x2c[c % 3]
        mB = msB[c % 2]

        def stt(tt):
            hh = hn[tt % 2]
            k.v(lambda e: e.scalar_tensor_tensor(out=hh.ap, in0=X[tt].ap, scalar=mB.ap[:, tt:tt + 1],
                                                 in1=gxapre.ap, op0=ALU.mult, op1=ALU.mult),
                R=[X[tt], mB, gxapre], W=[hh])

        tf = lambda tt: to_fm(hn[tt % 2].ap, [hn[tt % 2]], h3T, tt * 128, tt % 2)
        pj = lambda tt: proj_tile(oxT, tt, wo, yb2[tt])
        stt(0); stt(1); tf(0); pj(0); stt(2); tf(1); pj(1); stt(3); tf(2); pj(2); tf(3); pj(3)

    def front2c(c):
        q_ = qxT[c % 2]
        for ft in range(8):
            bk = SB[ft % 3]
            for dc in range(8):
                k.mm(k.bank(bk), wq.ap[:, dc, ft * 128:(ft + 1) * 128], h3T.ap[:, dc, :], dc == 0, dc == 7,
                     R=[wq, h3T], W=[PB[bk]])
            if ft % 2 == 0:
                k.act(q_.ap[:, ft, :], k.bank(bk), AF.Copy, R=[PB[bk]], W=[q_])
            else:
                k.v(lambda e, ft=ft, bk=bk: e.tensor_copy(out=q_.ap[:, ft, :], in_=k.bank(bk)), R=[PB[bk]], W=[q_])

    def back1(c, heads):
        q_ = qxT[c % 2]

        def qk(hh):
            for mt in range(2):
                bk = (3, 4)[mt]
                for j in range(2):
                    k.mm(k.bank(bk), KxT.ap[:, hh * 2 + j, mt * 128:(mt + 1) * 128], q_.ap[:, hh * 2 + j, :],
                         j == 0, j == 1, R=[KxT, q_], W=[PB[bk]])
                pt = PT[(2 * hh + mt) % 4]
                k.act(pt.ap, k.bank(bk), AF.Exp, R=[PB[bk]], W=[pt], scale=1.0 / 16.0)

        def lo(hh):
            pts = [PT[(2 * hh + mt) % 4] for mt in range(2)]
            for mt in range(2):
                k.mm(k.bank(7), ones.ap, pts[mt].ap, mt == 0, mt == 1, R=[ones, pts[mt]], W=[PB[7]])
            for dvc in range(2):
                for mt in range(2):
                    k.mm(k.bank(5 + dvc), Vx.ap[:, mt, hh * 256 + dvc * 128:hh * 256 + (dvc + 1) * 128], pts[mt].ap,
                         mt == 0, mt == 1, R=[Vx, pts[mt]], W=[PB[5 + dvc]])
            rL = rLb[hh % 2]
            k.act(rL.ap, k.bank(7), AF.Ln, R=[PB[7]], W=[rL])
            k.act(rL.ap, rL.ap, AF.Exp, R=[rL], W=[rL], scale=-1.0)
            for dvc in range(2):
                k.v(lambda e, dvc=dvc: e.tensor_tensor(out=oxT.ap[:, hh * 2 + dvc, :], in0=k.bank(5 + dvc), in1=rL.ap,
                                                       op=ALU.mult), R=[PB[5 + dvc], rL], W=oTb)

        qk(heads[0])
        for n, hh in enumerate(heads):
            if n + 1 < len(heads):
                qk(heads[n + 1])
            lo(hh)

    def back2(c):
        X = x2c[c % 3]
        mC = msC[c % 2]
        for tt in range(4):
            proj_tile(oxT, tt, wo, yb2[tt])

    def back2b(c):
        X = x2c[c % 3]
        mC = msC[c % 2]
        for tt in range(4):
            k.act(junk.ap, yb2[tt].ap, AF.Square, R=[yb2[tt]], W=[junk, mC], scale=1.0 / 32.0, accum_out=mC.ap[:, tt:tt + 1])
        k.rstd_chain(mC.ap, mC)
        for tt in range(4):
            gt = 4 * c + tt
            k.v(lambda e, tt=tt: e.scalar_tensor_tensor(out=yb2[tt].ap, in0=yb2[tt].ap, scalar=mC.ap[:, tt:tt + 1],
                                                        in1=gxapost.ap, op0=ALU.mult, op1=ALU.mult),
                R=[yb2[tt], mC, gxapost], W=[yb2[tt]])
            k.g(lambda e, tt=tt: e.tensor_tensor(out=yb2[tt].ap, in0=yb2[tt].ap, in1=X[tt].ap, op=ALU.add),
                R=[yb2[tt], X[tt]], W=[yb2[tt]])
            k.dma("pool", x3t[gt], yb2[tt].ap, R=[yb2[tt]], sem="x3o%d" % tt)

    front1(0)
    front2a(0)
    front2b(0)
    for tt in range(4):
        k.v(lambda e, tt=tt: e.scalar_tensor_tensor(out=hn[tt % 2].ap, in0=x2c[0][tt].ap, scalar=msB[0].ap[:, tt:tt + 1],
                                                    in1=gxapre.ap, op0=ALU.mult, op1=ALU.mult),
            R=[x2c[0][tt], msB[0], gxapre], W=[hn[tt % 2]])
        to_fm(hn[tt % 2].ap, [hn[tt % 2]], h3T, tt * 128, tt % 2)
    front2c(0)
    for c in range(NCH):
        nxt = c + 1 < NCH
        if nxt:
            front1(c + 1)
            front2a(c + 1)
        back1(c, (0, 1, 2, 3))
        if nxt:
            front2b(c + 1)
            mixed(c + 1, c)
            front2c(c + 1)
        else:
            back2(c)
        back2b(c)
    S.barrier()
    A.top = mark


def build(stage=99, debug=False, stage_attn=3):
    nc = bass.Bass("TRN2", target_bir_lowering=False)
    dt = lambda name, shape, dtype, kind: nc.dram_tensor(name, list(shape), dtype, kind=kind).ap()
    I = {}
    for name, shape in IN_SHAPES.items():
        I[name] = dt(name, shape, F32, "ExternalInput")
    for name, (shape, dtype) in CONST_SHAPES.items():
        I[name] = dt(name, shape, dtype, "ExternalInput")
    skind = "ExternalOutput" if debug else "Internal"
    x1 = dt("x1", [T, D], F32, skind)
    zT = dt("zT", [MIXIN, T], BF16, skind)
    vtok = dt("vtok", [T, 768], BF16, skind)
    gts = dt("gts", [T, 24], F32, skind)
    osc = dt("osc", [T, D], BF16, skind)
    x3 = dt("x3", [T, D], F32, skind)
    out = dt("out", [T, D], F32, "ExternalOutput")
    with ExitStack() as st:
        k = KB(nc, st, debug)
        k.stage_attn = stage_attn
        k.ident = k.A.alloc([128], BF16, "ident")
        k.dma("sp", k.ident.ap, I["c_ident"], W=[k.ident], sem="c9")
        ffn_phase(k, I["x"], x1, I["ffn1_w_gate"], I["ffn1_w_up"], I["ffn1_w_down"],
                  I["ffn1_pre_g"], I["ffn1_post_g"], "f1")
        if stage >= 2:
            inproj_phase(k, x1, I, zT, vtok, gts)
        k.rstd_mode = "ln"
        if stage >= 3:
            attn_phase(k, I, zT, vtok, gts, osc)
        if stage >= 4:
            outxa_phase(k, I, x1, osc, x3)
        k.rstd_mode = "sqrt"
        if stage >= 5:
            ffn_phase(k, x3, out, I["ffn2_w_gate"], I["ffn2_w_up"], I["ffn2_w_down"],
                      I["ffn2_pre_g"], I["ffn2_post_g"], "f2")
        k.S.barrier()
        k.S.emit()
        print("ops", k.S.nops, "sems", k.S.nsem, "arena peak", k.A.peak)
    return nc


IN_SHAPES = {
    "x": (T, D), "mem": (256, D),
    "ffn1_pre_g": (1, D), "ffn1_post_g": (1, D),
    "ffn1_w_gate": (D, DFF), "ffn1_w_up": (D, DFF), "ffn1_w_down": (DFF, D),
    "mix_pre_g": (1, D), "mix_post_g": (1, D), "w_mix_in": (D, MIXIN),
    "da_lambda_q1": (1, 64), "da_lambda_k1": (1, 64), "da_lambda_q2": (1, 64), "da_lambda_k2": (1, 64),
    "da_subln_g": (1, 128),
    "cmp_k_pe": (32, 64), "cmp_k_w1": (2048, 128), "cmp_k_w2": (128, 64),
    "cmp_v_pe": (32, 64), "cmp_v_w1": (2048, 128), "cmp_v_w2": (128, 64),
    "w_mix_out": (D, D),
    "xa_pre_g": (1, D), "xa_post_g": (1, D), "mem_norm_g": (1, D),
    "xa_w_q": (D, D), "xa_w_k": (D, D), "xa_w_v": (D, D), "xa_w_o": (D, D),
    "ffn2_pre_g": (1, D), "ffn2_post_g": (1, D),
    "ffn2_w_gate": (D, DFF), "ffn2_w_up": (D, DFF), "ffn2_w_down": (DFF, D),
}
PER_CORE = ("x", "mem")

CONST_SHAPES = {
    "c_ident": ((128, 128), BF16),
    "c_tric": ((128, 128), BF16),
    "c_tril": ((128, 128), BF16),
    "c_tricn": ((128, 128), BF16),
    "c_cmask": ((128, 2560), BF16),
    "c_expand": ((128, 4096), BF16),
    "c_selA": ((T, 64), F32),
    "c_selB": ((T, 64), F32),
    "c_mcs": ((256, 64), BF16),
    "c_qaug": ((12, 4, T), BF16),
    "c_kaug": ((4, T), BF16),
    "c_kaugc": ((4, 256), BF16),
}


def make_consts():
    bf = ml_dtypes.bfloat16
    c = {}
    c["c_ident"] = np.eye(128, dtype=np.float32).astype(bf)
    kk = np.arange(128)[:, None]
    qq = np.arange(128)[None, :]
    c["c_tric"] = np.where(kk <= qq, 0.0, NEGBIG).astype(np.float32).astype(bf)
    c["c_tril"] = np.where(kk > qq, 0.0, NEGBIG).astype(np.float32).astype(bf)
    c["c_tricn"] = np.where(kk <= qq, 0.0, -1.0).astype(np.float32).astype(bf)
    tt = np.arange(2560)[None, :]
    c["c_cmask"] = np.where(tt - 16 * kk >= 31, 0.0, NEGBIG).astype(np.float32).astype(bf)
    ex = np.zeros((128, T), np.float32)
    ex[:64] = (np.arange(T)[None, :] // 64 == np.arange(64)[:, None])
    c["c_expand"] = ex.astype(bf)
    t = np.arange(T)
    cur = (t // 64)[:, None]
    blk = np.arange(64)[None, :]
    Am = (blk <= cur).astype(np.float32)
    forced = ((blk == 0) | (blk == cur) | (blk == cur - 1)).astype(np.float32)
    c["c_selA"] = Am
    c["c_selB"] = (1e4 * forced - (1.0 - Am)).astype(np.float32)
    cs = np.arange(255) * 16
    ss = np.arange(64) * 64
    ov = np.clip(np.minimum(cs[:, None] + 32, ss[None, :] + 64) - np.maximum(cs[:, None], ss[None, :]), 0, None)
    mcs = np.zeros((256, 64), np.float32)
    mcs[:255] = ov / 32.0
    c["c_mcs"] = mcs.astype(bf)
    a = (t // 64).astype(np.float32)
    b = (t % 64).astype(np.float32)
    slopes = list(2.0 ** (-8.0 * np.arange(1, 5) / 4)) + list(2.0 ** (-8.0 * np.arange(1, 9) / 8))
    qa = np.zeros((12, 4, T), np.float32)
    for s, sl in enumerate(slopes):
        qa[s, 0] = -sl * 64 * a
        qa[s, 1] = -sl * b
        qa[s, 2] = sl
        qa[s, 3] = sl
    c["c_qaug"] = qa.astype(bf)
    ka = np.stack([np.ones(T), np.ones(T), 64 * a, b]).astype(np.float32)
    c["c_kaug"] = ka.astype(bf)
    pc = np.arange(256) * 16 + 31
    kc = np.stack([np.ones(256), np.ones(256), 64.0 * (pc // 64), 1.0 * (pc % 64)]).astype(np.float32)
    kc[:, 255] = 0
    c["c_kaugc"] = kc.astype(bf)
    return c


_CACHE = {}


def kernel(**inputs):
    n = 8
    if "nc" not in _CACHE:
        _CACHE["nc"] = build()
    nc = _CACHE["nc"]
    consts = make_consts()
    in_maps = []
    for i in range(n):
        m = {}
        for name, shape in IN_SHAPES.items():
            a = np.asarray(inputs[name], dtype=np.float32)
            a = a[i] if name in PER_CORE else a[0]
            m[name] = np.ascontiguousarray(a.reshape(shape))
        m.update(consts)
        in_maps.append(m)
    res = run_bass_kernel_spmd(nc, in_maps, core_ids=list(range(n)))
    return np.stack([r["out"] for r in res.results], axis=0).astype(np.float32)
```

```python
import math
from contextlib import ExitStack

import numpy as np
import ml_dtypes

import concourse.bass as bass
import concourse.mybir as mybir
from concourse.bass_utils import run_bass_kernel_spmd

F32 = mybir.dt.float32
BF16 = mybir.dt.bfloat16
AF = mybir.ActivationFunctionType
ALU = mybir.AluOpType
AX = mybir.AxisListType

SEM_LIMIT = 30000
DT_SIZE = {F32: 4, BF16: 2}

T = 4096
D = 1024
DFF = 2816
NT = T // 128
NCH = T // 512
MIXIN = 2840
NEGBIG = -30000.0
EPS = 1e-6


class Buf:
    __slots__ = ("name", "w", "r")

    def __init__(self, name=""):
        self.name = name
        self.w = None
        self.r = {}


class Tile:
    __slots__ = ("ap", "b")

    def __init__(self, ap, name=""):
        self.ap = ap
        self.b = Buf(name)

    def __getitem__(self, k):
        return self.ap[k]


class Sched:
    ENG = ("pe", "act", "dve", "pool", "sp")

    def __init__(self, nc, stack):
        self.nc = nc
        self.stack = stack
        self.q = {e: [] for e in self.ENG}
        self.cnt = {}
        self.sems = {}
        self.seen = {e: {} for e in self.ENG}
        self.nsem = 0
        self.nops = 0

    def _sem(self, key):
        if key not in self.sems:
            self.sems[key] = self.stack.enter_context(self.nc.semaphore("s%d" % self.nsem))
            self.nsem += 1
        return self.sems[key]

    def _bump(self, base, inc):
        ep, v = self.cnt.get(base, (0, 0))
        if v + inc > SEM_LIMIT:
            ep, v = ep + 1, 0
        v += inc
        self.cnt[base] = (ep, v)
        key = (base, ep)
        self._sem(key)
        return key, v

    def op(self, eng, fn, reads=(), writes=(), dma=None, skip_same=False):
        reads = [t.b if isinstance(t, Tile) else t for t in reads]
        writes = [t.b if isinstance(t, Tile) else t for t in writes]
        deps = []
        for b in reads:
            if b.w is not None:
                deps.append(b.w)
        for b in writes:
            if b.w is not None:
                deps.append(b.w)
            deps.extend(b.r.items())
        if dma is not None:
            dma = (dma, eng)
            ep, v = self.cnt.get(dma, (0, 0))
            if v > 0:
                deps.append(((dma, ep), v))
        waits = {}
        seen = self.seen[eng]
        for key, v in deps:
            if skip_same and key[0] == eng:
                continue
            if seen.get(key, 0) >= v:
                continue
            if waits.get(key, 0) < v:
                waits[key] = v
        for key, v in waits.items():
            seen[key] = v
        if dma is None:
            key, v = self._bump(eng, 1)
            inc = 1
        else:
            key, v = self._bump(dma, 16)
            inc = 16
        self.q[eng].append((list(waits.items()), fn, key, inc))
        self.nops += 1
        ev = (key, v)
        for b in reads:
            if b.r.get(key, 0) < v:
                b.r[key] = v
        for b in writes:
            b.w = ev
            b.r = {}
        return ev

    def barrier(self):
        allv = [((base, ep), v) for base, (ep, v) in self.cnt.items() if v > 0]
        for e in self.ENG:
            waits = []
            for key, v in allv:
                if self.seen[e].get(key, 0) < v:
                    waits.append((key, v))
                    self.seen[e][key] = v
            if waits:
                self.q[e].append((waits, None, None, 0))

    def emit(self):
        nc = self.nc
        sems = self.sems
        q = self.q

        def replay(name, e):
            for waits, fn, key, inc in q[name]:
                for k, v in waits:
                    e.wait_ge(sems[k], v)
                if fn is not None:
                    fn(e).then_inc(sems[key], inc)

        with nc.Block() as block:
            @block.tensor
            def _(e):
                replay("pe", e)

            @block.scalar
            def _(e):
                replay("act", e)

            @block.vector
            def _(e):
                replay("dve", e)

            @block.gpsimd
            def _(e):
                replay("pool", e)

            @block.sync
            def _(e):
                replay("sp", e)


class Arena:
    def __init__(self, nc, stack, nbytes):
        self.t = stack.enter_context(nc.sbuf_tensor("arena", [128, nbytes // 4], F32))
        self.top = 0
        self.cap = nbytes
        self.peak = 0

    def alloc(self, shape, dtype, name=""):
        n = int(np.prod(shape)) * DT_SIZE[dtype]
        n4 = (n + 3) // 4
        off = self.top // 4
        self.top += n4 * 4
        self.peak = max(self.peak, self.top)
        assert self.top <= self.cap, ("SBUF arena overflow", name, self.top, self.cap)
        ap = self.t[:, off:off + n4]
        if dtype != F32:
            ap = ap.bitcast(dtype)
            ap = ap[:, 0:int(np.prod(shape))]
        if len(shape) == 2:
            ap = ap.rearrange("p (a b) -> p a b", a=shape[0])
        elif len(shape) == 3:
            ap = ap.rearrange("p (a b c) -> p a b c", a=shape[0], b=shape[1])
        return Tile(ap, name)


class KB:
    def __init__(self, nc, st, debug):
        self.nc = nc
        self.S = Sched(nc, st)
        self.A = Arena(nc, st, 212000)
        self.psum = st.enter_context(nc.psum_tensor("psum", [128, 4096], F32))
        self.pb = [Buf("bank%d" % i) for i in range(8)]
        self.debug = debug

    def bank(self, i, n=1):
        return self.psum[:, i * 512:(i + n) * 512]

    def bank_bf(self, i):
        return self.psum[:, i * 512:(i + 1) * 512].bitcast(BF16)

    def dma(self, q, out, in_, R=(), W=(), sem=None):
        return self.S.op(q, lambda e: e.dma_start(out=out, in_=in_), reads=R, writes=W, dma=sem)

    def mm(self, out, lhsT, rhs, start, stop, R=(), W=()):
        return self.S.op("pe", lambda e: e.matmul(out, lhsT=lhsT, rhs=rhs, start=start, stop=stop,
                                                  skip_group_check=True),
                         reads=R, writes=W, skip_same=True)

    def tr(self, out, in_, R=(), W=()):
        ident = self.ident
        return self.S.op("pe", lambda e: e.transpose(out=out, in_=in_, identity=ident.ap),
                         reads=list(R) + [ident], writes=W, skip_same=True)

    def act(self, out, in_, func, R=(), W=(), **kw):
        return self.S.op("act", lambda e: e.activation(out=out, in_=in_, func=func, **kw), reads=R, writes=W)

    def v(self, fn, R=(), W=()):
        return self.S.op("dve", fn, reads=R, writes=W)

    def g(self, fn, R=(), W=()):
        return self.S.op("pool", fn, reads=R, writes=W)

    rstd_mode = "sqrt"

    def rstd_chain(self, ms_ap, t):
        self.v(lambda e: e.tensor_scalar(out=ms_ap, in0=ms_ap, scalar1=EPS, scalar2=None, op0=ALU.add), R=[t], W=[t])
        if self.rstd_mode == "ln":
            self.act(ms_ap, ms_ap, AF.Ln, R=[t], W=[t])
            self.act(ms_ap, ms_ap, AF.Exp, R=[t], W=[t], scale=-0.5)
        else:
            self.act(ms_ap, ms_ap, AF.Sqrt, R=[t], W=[t])
            self.v(lambda e: e.reciprocal(out=ms_ap, in_=ms_ap), R=[t], W=[t])

    def load_bcast(self, dram_row, n, name, sem):
        t = self.A.alloc([n], F32, name)
        self.dma("sp", t.ap, dram_row.partition_broadcast(128), W=[t], sem=sem)
        return t

    def load_w_groups(self, dram_w, rows, cols, name, sem, gcols):
        nch = rows // 128
        t = self.A.alloc([nch, cols], BF16, name)
        src = dram_w.rearrange("(c p) n -> p c n", p=128)
        bufs = []
        for g0 in range(0, cols, gcols):
            g1 = min(cols, g0 + gcols)
            b = Buf("%s_g%d" % (name, g0))
            bufs.append(b)
            self.dma("pool", t.ap[:, :, g0:g1], src[:, :, g0:g1], W=[b], sem=sem)
        return t, (lambda col: bufs[col // gcols])

    def load_w(self, dram_w, rows, cols, name, sem, defer=False):
        nch = rows // 128
        t = self.A.alloc([nch, cols], BF16, name)
        if defer:
            return t, (lambda: self._issue_w(t, dram_w, nch, sem))
        self._issue_w(t, dram_w, nch, sem)
        return t

    def _issue_w(self, t, dram_w, nch, sem):
        src = dram_w.rearrange("(c p) n -> p c n", p=128)
        step = max(1, nch // 4)
        for c0 in range(0, nch, step):
            c1 = min(nch, c0 + step)
            self.dma("pool", t.ap[:, c0:c1, :], src[:, c0:c1, :], W=[t], sem=sem)


def ffn_phase(k, src, dst, wg_d, wu_d, wd_d, gpre_d, gpost_d, tag):
    A, S = k.A, k.S
    mark = A.top
    GC = 640
    nchw = D // 128
    wg = A.alloc([nchw, DFF], BF16, "wg")
    wu = A.alloc([nchw, DFF], BF16, "wu")
    wgb, wub = [], []
    for g0 in range(0, DFF, GC):
        for (t_, d_, bl, sm) in ((wg, wg_d, wgb, "w0"), (wu, wu_d, wub, "w1")):
            b = Buf("wgrp")
            bl.append(b)
            g1 = min(DFF, g0 + GC)
            k.dma("pool", t_.ap[:, :, g0:g1], d_.rearrange("(c p) n -> p c n", p=128)[:, :, g0:g1],
                  W=[b], sem=sm)
    wd = k.load_w(wd_d, DFF, D, "wd", "w2")
    gpre = k.load_bcast(gpre_d, D, "gpre", "c0")
    gpost = k.load_bcast(gpost_d, D, "gpost", "c1")
    k.v(lambda e: e.tensor_scalar(out=gpost.ap, in0=gpost.ap, scalar1=0.5, scalar2=None, op0=ALU.mult),
        R=[gpost], W=[gpost])
    xs = [A.alloc([D], F32, "xs%d" % i) for i in range(2)]
    xr = [A.alloc([D], F32, "xr%d" % i) for i in range(2)]
    hn = [A.alloc([D], BF16, "hn%d" % i) for i in range(2)]
    hT = [A.alloc([8, 512], BF16, "hT%d" % i) for i in range(2)]
    AT = A.alloc([22, 512], BF16, "AT")
    sg = [A.alloc([512], BF16, "sg%d" % i) for i in range(2)]
    ytmp = A.alloc([D], F32, "ytmp")
    junk = A.alloc([D], BF16, "junk")
    ms = [A.alloc([4], F32, "ms%d" % i) for i in range(2)]
    ms2 = [A.alloc([1], F32, "ms2%d" % i) for i in range(2)]
    srct = src.rearrange("(n p) d -> n p d", p=128)
    dstt = dst.rearrange("(n p) d -> n p d", p=128)
    PB = k.pb
    cnt = {"x": 0, "r": 0}

    def pn_chain(c, tt):
        m = ms[c % 2]
        gt = 4 * c + tt
        i = tt % 2
        x = xs[i]
        k.dma("sp", x.ap, srct[gt], W=[x], sem="xs%d" % i)
        hh = hn[i]
        k.act(hh.ap, x.ap, AF.Square, R=[x], W=[hh, m], scale=1.0 / 32.0, accum_out=m.ap[:, tt:tt + 1])
        k.rstd_chain(m.ap[:, tt:tt + 1], m)
        k.v(lambda e: e.scalar_tensor_tensor(out=hh.ap, in0=x.ap, scalar=m.ap[:, tt:tt + 1], in1=gpre.ap,
                                             op0=ALU.mult, op1=ALU.mult), R=[x, m, gpre], W=[hh])

    def pn_tr(c, tt):
        h = hT[c % 2]
        hh = hn[tt % 2]
        pT = k.bank_bf(6)
        for dc in range(8):
            k.tr(pT[:, dc * 128:(dc + 1) * 128], hh.ap[:, dc * 128:(dc + 1) * 128], R=[hh], W=[PB[6]])
        k.act(h.ap[:, :, tt * 128:(tt + 1) * 128], pT.rearrange("p (a b) -> p a b", a=8), AF.Copy,
              R=[PB[6]], W=[h])

    def prenorm(c):
        for tt in range(4):
            pn_chain(c, tt)
            pn_tr(c, tt)

    def gateup(c):
        h = hT[c % 2]
        for f in range(22):
            gb, ub = f % 2, 2 + f % 2
            for dc in range(8):
                k.mm(k.bank(gb), wg.ap[:, dc, f * 128:(f + 1) * 128], h.ap[:, dc, :], dc == 0, dc == 7,
                     R=[wgb[f * 128 // GC], h], W=[PB[gb]])
            for dc in range(8):
                k.mm(k.bank(ub), wu.ap[:, dc, f * 128:(f + 1) * 128], h.ap[:, dc, :], dc == 0, dc == 7,
                     R=[wub[f * 128 // GC], h], W=[PB[ub]])
            s = sg[f % 2]
            k.act(s.ap, k.bank(gb), AF.Silu, R=[PB[gb]], W=[s])
            k.v(lambda e, s=s, ub=ub, f=f: e.tensor_tensor(out=AT.ap[:, f, :], in0=s.ap, in1=k.bank(ub), op=ALU.mult),
                R=[s, PB[ub]], W=[AT])
            if c + 1 < NCH:
                if f in (1, 6, 11, 16):
                    pn_chain(c + 1, (f - 1) // 5)
                if f in (4, 9, 14, 19):
                    pn_tr(c + 1, (f - 4) // 5)

    def down(c):
        for tt in range(4):
            gt = 4 * c + tt
            i = cnt["r"] % 2
            cnt["r"] += 1
            r = xr[i]
            k.dma("sp", r.ap, srct[gt], W=[r], sem="xr%d" % i)
            for half in range(2):
                for f in range(22):
                    k.mm(k.bank(4 + half), AT.ap[:, f, tt * 128:(tt + 1) * 128],
                         wd.ap[:, f, half * 512:(half + 1) * 512], f == 0, f == 21,
                         R=[AT, wd], W=[PB[4 + half]])
            m2 = ms2[i]
            k.v(lambda e: e.tensor_copy(out=ytmp.ap, in_=k.bank(4, 2)), R=[PB[4], PB[5]], W=[ytmp])
            k.act(junk.ap, ytmp.ap, AF.Square, R=[ytmp], W=[junk, m2], scale=1.0 / 32.0, accum_out=m2.ap)
            k.rstd_chain(m2.ap, m2)
            k.v(lambda e, m2=m2: e.scalar_tensor_tensor(out=ytmp.ap, in0=ytmp.ap, scalar=m2.ap, in1=gpost.ap,
                                                        op0=ALU.mult, op1=ALU.mult),
                R=[m2, gpost, ytmp], W=[ytmp])
            k.g(lambda e, r=r: e.tensor_tensor(out=r.ap, in0=r.ap, in1=ytmp.ap, op=ALU.add), R=[r, ytmp], W=[r])
            k.dma("pool", dstt[gt], r.ap, R=[r], sem="xo%d" % i)

    prenorm(0)
    for c in range(NCH):
        gateup(c)
        down(c)
    S.barrier()
    A.top = mark


def make_prenorm(k, gpre, nx=2):
    A = k.A
    xs = [A.alloc([D], F32, "pxs%d" % i) for i in range(nx)]
    hn = [A.alloc([D], BF16, "phn%d" % i) for i in range(2)]
    hT = [A.alloc([8, 512], BF16, "phT%d" % i) for i in range(2)]
    ms = [A.alloc([4], F32, "pms%d" % i) for i in range(2)]
    cnt = {"x": 0}
    PB = k.pb

    def norm_tile(x, m, tt, h, g, bank=6):
        i = cnt["x"] % 2
        cnt["x"] += 1
        hh = hn[i]
        k.act(hh.ap, x.ap if isinstance(x, Tile) else x[0], AF.Square, R=[x if isinstance(x, Tile) else x[1]],
              W=[hh, m], scale=1.0 / 32.0, accum_out=m.ap[:, tt:tt + 1])
        k.rstd_chain(m.ap[:, tt:tt + 1], m)
        xa = x.ap if isinstance(x, Tile) else x[0]
        xt = x if isinstance(x, Tile) else x[1]
        k.v(lambda e: e.scalar_tensor_tensor(out=hh.ap, in0=xa, scalar=m.ap[:, tt:tt + 1], in1=g.ap,
                                             op0=ALU.mult, op1=ALU.mult), R=[xt, m, g], W=[hh])
        pT = k.bank_bf(bank)
        for dc in range(8):
            k.tr(pT[:, dc * 128:(dc + 1) * 128], hh.ap[:, dc * 128:(dc + 1) * 128], R=[hh], W=[PB[bank]])
        k.act(h.ap[:, :, tt * 128:(tt + 1) * 128], pT.rearrange("p (a b) -> p a b", a=8), AF.Copy,
              R=[PB[bank]], W=[h])

    def prenorm(c, srct):
        h = hT[c % 2]
        m = ms[c % 2]
        for tt in range(4):
            gt = 4 * c + tt
            i = cnt["x"] % nx
            x = xs[i]
            k.dma("sp", x.ap, srct[gt], W=[x], sem="pxs%d" % i)
            norm_tile(x, m, tt, h, gpre)
        return h

    def chain(c, tt, srct):
        m = ms[c % 2]
        gt = 4 * c + tt
        x = xs[tt % nx]
        k.dma("sp", x.ap, srct[gt], W=[x], sem="pxs%d" % (tt % nx))
        hh = hn[tt % 2]
        k.act(hh.ap, x.ap, AF.Square, R=[x], W=[hh, m], scale=1.0 / 32.0, accum_out=m.ap[:, tt:tt + 1])
        k.rstd_chain(m.ap[:, tt:tt + 1], m)
        k.v(lambda e: e.scalar_tensor_tensor(out=hh.ap, in0=x.ap, scalar=m.ap[:, tt:tt + 1], in1=gpre.ap,
                                             op0=ALU.mult, op1=ALU.mult), R=[x, m, gpre], W=[hh])

    def tr(c, tt, bank=6):
        h = hT[c % 2]
        hh = hn[tt % 2]
        pT = k.bank_bf(bank)
        for dc in range(8):
            k.tr(pT[:, dc * 128:(dc + 1) * 128], hh.ap[:, dc * 128:(dc + 1) * 128], R=[hh], W=[PB[bank]])
        k.act(h.ap[:, :, tt * 128:(tt + 1) * 128], pT.rearrange("p (a b) -> p a b", a=8), AF.Copy,
              R=[PB[bank]], W=[h])
        return h

    prenorm.chain = chain
    prenorm.tr = tr
    prenorm.norm_tile = norm_tile
    prenorm.hT = hT
    prenorm.ms = ms
    return prenorm


FM = [(0, 4, 0.125), (512, 4, 1.0), (1536, 4, 0.125), (2048, 1, 1.0), (2176, 1, 1.0), (2304, 1, 1.0), (2560, 1, 1.0)]


def inproj_phase(k, x1, I, zT, vtok, gts, kv):
    A, S, PB = k.A, k.S, k.pb
    mark = A.top
    gpre = k.load_bcast(I["mix_pre_g"], D, "gpre", "c0")
    win, wgrp = k.load_w_groups(I["w_mix_in"], D, MIXIN, "win", "w0", 512)
    prenorm = make_prenorm(k, gpre, nx=4)
    stg = [A.alloc([512], BF16, "stg%d" % i) for i in range(3)]
    vst = [A.alloc([768], BF16, "vst%d" % i) for i in range(2)]
    gst = [A.alloc([24], F32, "gst%d" % i) for i in range(2)]
    srct = x1.rearrange("(n p) d -> n p d", p=128)
    vtokt = vtok.rearrange("(n p) d -> n p d", p=128)
    gtst = gts.rearrange("(n p) d -> n p d", p=128)
    KxT, Vx = kv
    gmem = k.load_bcast(I["mem_norm_g"], D, "gmem", "c0")
    wk, go_wk = k.load_w(I["xa_w_k"], D, D, "wk", "w3", defer=True)
    wv, go_wv = k.load_w(I["xa_w_v"], D, D, "wv", "w4", defer=True)
    mT = A.alloc([8, 256], BF16, "mT")
    msm = A.alloc([2], F32, "msm")
    memt = I["mem"].rearrange("(n p) d -> n p d", p=128)
    xm = [A.alloc([D], F32, "xm%d" % i) for i in range(2)]
    mhn = [A.alloc([D], BF16, "mhn%d" % i) for i in range(2)]
    for mt in range(2):
        k.dma("sp", xm[mt].ap, memt[mt], W=[xm[mt]], sem="c1")

    def kv_setup():
        for mt in range(2):
            k.act(mhn[mt].ap, xm[mt].ap, AF.Square, R=[xm[mt]], W=[mhn[mt], msm], scale=1.0 / 32.0,
                  accum_out=msm.ap[:, mt:mt + 1])
        k.rstd_chain(msm.ap, msm)
        for mt in range(2):
            k.v(lambda e, mt=mt: e.scalar_tensor_tensor(out=mhn[mt].ap, in0=xm[mt].ap, scalar=msm.ap[:, mt:mt + 1],
                                                        in1=gmem.ap, op0=ALU.mult, op1=ALU.mult),
                R=[xm[mt], msm, gmem], W=[mhn[mt]])
            pT = k.bank_bf(7)
            for dc in range(8):
                k.tr(pT[:, dc * 128:(dc + 1) * 128], mhn[mt].ap[:, dc * 128:(dc + 1) * 128], R=[mhn[mt]], W=[PB[7]])
            k.act(mT.ap[:, :, mt * 128:(mt + 1) * 128], pT.rearrange("p (a b) -> p a b", a=8), AF.Copy,
                  R=[PB[7]], W=[mT])
        for ft in range(8):
            bk = 4 + ft % 2
            for dc in range(8):
                k.mm(k.bank(bk)[:, 0:256], wk.ap[:, dc, ft * 128:(ft + 1) * 128], mT.ap[:, dc, :], dc == 0, dc == 7,
                     R=[wk, mT], W=[PB[bk]])
            k.act(KxT.ap[:, ft, :], k.bank(bk)[:, 0:256], AF.Copy, R=[PB[bk]], W=[KxT])
        for mt in range(2):
            for half in range(2):
                bk = 4 + half
                for dc in range(8):
                    k.mm(k.bank(bk), mT.ap[:, dc, mt * 128:(mt + 1) * 128], wv.ap[:, dc, half * 512:(half + 1) * 512],
                         dc == 0, dc == 7, R=[wv, mT], W=[PB[bk]])
                k.v(lambda e, mt=mt, half=half, bk=bk: e.tensor_copy(
                    out=Vx.ap[:, mt, half * 512:(half + 1) * 512], in_=k.bank(bk)), R=[PB[bk]], W=[Vx])

    hnext = prenorm(0, srct)
    for c in range(NCH):
        h = hnext
        i = 0
        for (z0, ntile, sc) in FM:
            for ft in range(ntile):
                bk = i % 2
                col = z0 + ft * 128
                for dc in range(8):
                    k.mm(k.bank(bk), win.ap[:, dc, col:col + 128], h.ap[:, dc, :], dc == 0, dc == 7,
                         R=[wgrp(col), h], W=[PB[bk]])
                s = stg[i % 3]
                if sc != 1.0:
                    k.v(lambda e, s=s, bk=bk, sc=sc: e.tensor_scalar(out=s.ap, in0=k.bank(bk), scalar1=sc, scalar2=None,
                                                                     op0=ALU.mult), R=[PB[bk]], W=[s])
                else:
                    k.act(s.ap, k.bank(bk), AF.Copy, R=[PB[bk]], W=[s])
                k.dma("pool" if sc != 1.0 else "act", zT[col:col + 128, c * 512:(c + 1) * 512], s.ap, R=[s],
                      sem="stg%d" % (i % 3))
                if c + 1 < NCH:
                    if i in (1, 5, 9, 13):
                        prenorm.chain(c + 1, (i - 1) // 4, srct)
                    if i in (3, 7, 11, 15):
                        hnext = prenorm.tr(c + 1, (i - 3) // 4)
                i += 1
        for tt in range(4):
            gt = 4 * c + tt
            hs = h.ap[:, :, tt * 128:(tt + 1) * 128]
            for dc in range(8):
                k.mm(k.bank(2), hs[:, dc, :], win.ap[:, dc, 1024:1536], dc == 0, dc == 7,
                     R=[wgrp(1024), h], W=[PB[2]])
            for (o0, z0, n) in ((0, 2432, 128), (128, 2688, 128), (256, 2816, 24)):
                for dc in range(8):
                    k.mm(k.bank(3)[:, o0:o0 + n], hs[:, dc, :], win.ap[:, dc, z0:z0 + n], dc == 0, dc == 7,
                         R=[wgrp(z0), h], W=[PB[3]])
            vs = vst[gt % 2]
            k.act(vs.ap[:, 0:512], k.bank(2), AF.Copy, R=[PB[2]], W=[vs])
            k.v(lambda e, vs=vs: e.tensor_copy(out=vs.ap[:, 512:768], in_=k.bank(3)[:, 0:256]), R=[PB[3]], W=[vs])
            k.dma("pool", vtokt[gt], vs.ap, R=[vs], sem="vst%d" % (gt % 2))
            gs = gst[gt % 2]
            k.v(lambda e, gs=gs: e.tensor_copy(out=gs.ap, in_=k.bank(3)[:, 256:280]), R=[PB[3]], W=[gs])
            k.dma("pool", gtst[gt], gs.ap, R=[gs], sem="gst%d" % (gt % 2))
        if c == 0:
            go_wk()
            go_wv()
        if c == 3:
            kv_setup()
    S.barrier()
    A.top = mark


class AttnPipe:
    def __init__(self, k, PT):
        self.k = k
        self.PT = PT
        self.n = 0
        self.pending = None
        self.obank_first = {}

    def _qk(self, st):
        k = self.k
        sb = st["sb"]
        lo, hi = st["lo"], st["hi"]
        out = k.bank(sb)[:, lo:hi]
        ex = st["extra"]
        k.mm(out, st["kT"], st["qT"], True, len(ex) == 0, R=st["Rqk"], W=[k.pb[sb]])
        for n, (lhsT, rhs, a, b, R) in enumerate(ex):
            k.mm(k.bank(sb)[:, a:b], lhsT, rhs, False, n == len(ex) - 1, R=R, W=[k.pb[sb]])

    def _rest(self, st):
        k = self.k
        sb = st["sb"]
        if "restfn" in st:
            st["restfn"](sb)
            return
        lo, hi = st["lo"], st["hi"]
        pt = st["pt"]
        k.act(pt.ap[0:st["kp"], lo:hi], k.bank(sb)[0:st["kp"], lo:hi], AF.Exp, R=[k.pb[sb]], W=[pt], scale=st["scale"])
        if st.get("mask") is not None:
            mt, eng = st["mask"]
            k.S.op(eng, lambda e: e.tensor_tensor(out=pt.ap[:, lo:hi], in0=pt.ap[:, lo:hi], in1=mt.ap[:, lo:hi],
                                                  op=ALU.mult), reads=[pt, mt], writes=[pt])
        for (oreg, ob, lhs_lo, lhs_hi, rhs, start, stop, R) in st["pv"]:
            k.mm(oreg, pt.ap[0:st["kp"], lhs_lo:lhs_hi], rhs, start, stop, R=[pt] + R, W=[k.pb[ob]])
        if st.get("evac") is not None:
            st["evac"]()

    def push(self, st):
        st["sb"] = self.n % 4
        st["pt"] = self.PT[self.n % len(self.PT)]
        self.n += 1
        if "qkfn" in st:
            st["qkfn"](st["sb"])
        else:
            self._qk(st)
        if self.pending is None:
            self.pending = []
        self.pending.append(st)
        if len(self.pending) > 3:
            self._rest(self.pending.pop(0))

    def flush(self):
        while self.pending:
            self._rest(self.pending.pop(0))


def attn_phase(k, I, zT, vtok, gts, osc):
    A, S, PB = k.A, k.S, k.pb
    mark = A.top
    def cload(name, shape, dtype, src, q="sp"):
        t = A.alloc(shape, dtype, name)
        k.dma(q, t.ap, src, W=[t], sem="c0")
        return t
    tric = cload("tric", [128], BF16, I["c_tric"])
    tril = cload("tril", [128], BF16, I["c_tril"])
    gates = cload("gates", [NT, 24], F32, gts.rearrange("(n p) j -> p n j", p=128))
    k.act(gates.ap, gates.ap, AF.Exp, R=[gates], W=[gates], scale=-1.0)
    k.v(lambda e: e.tensor_scalar(out=gates.ap, in0=gates.ap, scalar1=1.0, scalar2=None, op0=ALU.add), R=[gates], W=[gates])
    k.v(lambda e: e.reciprocal(out=gates.ap, in_=gates.ap), R=[gates], W=[gates])
    lq1 = k.load_bcast(I["da_lambda_q1"], 64, "lq1", "c1")
    lk1 = k.load_bcast(I["da_lambda_k1"], 64, "lk1", "c1")
    lq2 = k.load_bcast(I["da_lambda_q2"], 64, "lq2", "c1")
    lk2 = k.load_bcast(I["da_lambda_k2"], 64, "lk2", "c1")
    lsum = A.alloc([2], F32, "lsum")
    neglam = A.alloc([1], F32, "neglam")
    k.v(lambda e: e.tensor_tensor(out=lq1.ap, in0=lq1.ap, in1=lk1.ap, op=ALU.mult), R=[lq1, lk1], W=[lq1])
    k.v(lambda e: e.tensor_tensor(out=lq2.ap, in0=lq2.ap, in1=lk2.ap, op=ALU.mult), R=[lq2, lk2], W=[lq2])
    k.v(lambda e: e.reduce_sum(out=lsum.ap[:, 0:1], in_=lq1.ap, axis=AX.X), R=[lq1], W=[lsum])
    k.v(lambda e: e.reduce_sum(out=lsum.ap[:, 1:2], in_=lq2.ap, axis=AX.X), R=[lq2], W=[lsum])
    k.act(lsum.ap, lsum.ap, AF.Exp, R=[lsum], W=[lsum])
    lam_init = 0.8 - 0.6 * math.exp(-0.3 * 0)
    k.v(lambda e: e.tensor_tensor(out=neglam.ap, in0=lsum.ap[:, 1:2], in1=lsum.ap[:, 0:1], op=ALU.subtract),
        R=[lsum], W=[neglam])
    k.v(lambda e: e.tensor_scalar(out=neglam.ap, in0=neglam.ap, scalar1=-lam_init, scalar2=None, op0=ALU.add),
        R=[neglam], W=[neglam])
    gsub = k.load_bcast(I["da_subln_g"], 128, "gsub", "c1")
    k.v(lambda e: e.tensor_scalar(out=gsub.ap, in0=gsub.ap, scalar1=1.0 - lam_init, scalar2=None, op0=ALU.mult),
        R=[gsub], W=[gsub])

    QB = [A.alloc([T], BF16, "QB%d" % i) for i in range(2)]
    KBf = [A.alloc([T], BF16, "KB%d" % i) for i in range(2)]
    PT = [A.alloc([512], BF16, "PT%d" % i) for i in range(4)]
    pipe = AttnPipe(k, PT)
    rl = [A.alloc([4], F32, "rl%d" % i) for i in range(2)]
    rg = [A.alloc([4], F32, "rg%d" % i) for i in range(2)]
    t2 = [A.alloc([4, 128], F32, "t2%d" % i) for i in range(2)]
    ob = [A.alloc([4, 128], BF16, "ob%d" % i) for i in range(2)]
    msd = [A.alloc([4], F32, "msd%d" % i) for i in range(2)]
    osct = osc.rearrange("(n p) d -> p n d", p=128)
    state = {"cj": 0, "ev": 0}

    def oview(pair):
        return k.bank(4 + 2 * pair, 2).rearrange("p (j w) -> p j w", j=4)

    def load_q(buf, zrow, slot):
        k.dma("sp", buf.ap[0:64, :], zT[zrow:zrow + 64, :], W=[buf], sem="q%s" % buf.b.name)
        k.dma("sp", buf.ap[64:68, :], I["c_qaug"][slot], W=[buf], sem="q%s" % buf.b.name)

    def load_k(buf, zrow):
        k.dma("sp", buf.ap[0:64, :], zT[zrow:zrow + 64, :], W=[buf], sem="k%s" % buf.b.name)
        k.dma("sp", buf.ap[64:68, :], I["c_kaug"], W=[buf], sem="k%s" % buf.b.name)

    def load_v(buf, col, dv):
        src = vtok.rearrange("(n p) d -> p n d", p=128)
        for n0 in range(0, NT, 8):
            k.dma("sp", buf.ap[:, n0:n0 + 8, 0:dv], src[:, n0:n0 + 8, col:col + dv], W=[buf], sem="v%s" % buf.b.name)

    def run_job(qb, kT_of, v_of, dv1, tiles_of, extras_of, evac_of, scale=1.0, kp_of=None, hook=None):
        for c in range(NCH):
            pair = state["cj"] % 2
            state["cj"] += 1
            ov = oview(pair)
            tl = tiles_of(c)
            last_for = {}
            for n, (kt, j0, j1) in enumerate(tl):
                for j in range(j0, j1):
                    last_for[j] = n
            started = set()
            for n, (kt, j0, j1) in enumerate(tl):
                lo, hi = j0 * 128, j1 * 128
                kTap, kR = kT_of(kt)
                vap, vR = v_of(kt)
                kp = 128 if kp_of is None else kp_of(kt)
                pv = []
                for j in range(j0, j1):
                    obk = 4 + 2 * pair + j // 2
                    start = obk not in started
                    started.add(obk)
                    pv.append((ov[:, j, 0:dv1], obk, j * 128, (j + 1) * 128, vap, start, last_for[j] == n, vR))
                st = dict(kT=kTap, qT=qb.ap[0:68, c * 512 + lo:c * 512 + hi], lo=lo, hi=hi, Rqk=[qb] + kR,
                          extra=extras_of(c, kt, j0, j1), pv=pv, scale=scale, kp=kp,
                          evac=(evac_of(c, pair) if n == len(tl) - 1 else None))
                pipe.push(st)
            if hook is not None:
                hook(c)

    def causal_tiles(c):
        tl = [(kt, 0, 4) for kt in range(4 * c)]
        tl += [(4 * c + i, i, 4) for i in range(4)]
        return tl

    def causal_extras(c, kt, j0, j1):
        if kt >= 4 * c:
            i = kt - 4 * c
            return [(k.ident.ap, tric.ap, i * 128, (i + 1) * 128, [k.ident, tric])]
        return []

    def rl_of(ov, col, i, clamp=False):
        r = rl[i]
        if clamp:
            k.v(lambda e: e.tensor_scalar(out=r.ap.rearrange("p (a b) -> p a b", b=1), in0=ov[:, :, col:col + 1],
                                          scalar1=1e-30, scalar2=None, op0=ALU.max), R=[], W=[r])
            k.v(lambda e: e.reciprocal(out=r.ap, in_=r.ap), R=[r], W=[r])
        else:
            k.v(lambda e: e.reciprocal(out=r.ap.rearrange("p (a b) -> p a b", b=1), in_=ov[:, :, col:col + 1]),
                R=[], W=[r])
        return r

    cmask = cload("cmask", [2560], BF16, I["c_cmask"], q="act")
    expand = cload("expand", [4096], BF16, I["c_expand"], q="act")
    selA = cload("selA", [NT, 64], F32, I["c_selA"].rearrange("(n p) j -> p n j", p=128), q="act")
    selB = cload("selB", [NT, 64], F32, I["c_selB"].rearrange("(n p) j -> p n j", p=128), q="act")
    tricn = cload("tricn", [128], BF16, I["c_tricn"], q="act")
    w1 = [A.alloc([32, 128], BF16, "w1%d" % i) for i in range(2)]
    w2 = [A.alloc([64], BF16, "w2%d" % i) for i in range(2)]
    peT = [A.alloc([32], F32, "peT%d" % i) for i in range(2)]
    peTb = [A.alloc([32], BF16, "peTb%d" % i) for i in range(2)]
    cb = [A.alloc([1], F32, "cb%d" % i) for i in range(2)]
    for i, nm in enumerate(("k", "v")):
        k.dma("pool", w1[i].ap[0:64], I["cmp_%s_w1" % nm].rearrange("(pos d) h -> d pos h", d=64), W=[w1[i]], sem="w0")
        k.dma("pool", w2[i].ap, I["cmp_%s_w2" % nm], W=[w2[i]], sem="w0")
        pe_src = I["cmp_%s_pe" % nm].rearrange("pos d -> d pos")
        S.op("act", lambda e, i=i, pe_src=pe_src: e.dma_start(out=peT[i].ap[0:64], in_=pe_src,
                                                           allow_slow_non_contiguous=True),
             writes=[peT[i].b], dma="c1")
    def da_evac(h, m):
        def evac_of(c, pair):
            def ev():
                i = state["ev"] % 2
                state["ev"] += 1
                ov = oview(pair)
                OB = [PB[4 + 2 * pair], PB[5 + 2 * pair]]
                r = rl[i]
                k.v(lambda e: e.reciprocal(out=r.ap.rearrange("p (a b) -> p a b", b=1), in_=ov[:, :, 128:129]),
                    R=OB, W=[r])
                rb = r.ap.rearrange("p (a b) -> p a b", b=1).to_broadcast([128, 4, 128])
                dch = datmp.ap[:, 4 * c:4 * c + 4, :]
                if m == 0:
                    k.v(lambda e: e.tensor_tensor(out=dch, in0=ov[:, :, 0:128], in1=rb, op=ALU.mult),
                        R=OB + [r], W=[datmp])
                    return
                tt_ = t2[i]
                k.v(lambda e: e.tensor_tensor(out=tt_.ap, in0=ov[:, :, 0:128], in1=rb, op=ALU.mult),
                    R=OB + [r], W=[tt_])
                k.v(lambda e: e.scalar_tensor_tensor(out=dch, in0=tt_.ap, scalar=neglam.ap, in1=dch,
                                                     op0=ALU.mult, op1=ALU.add), R=[tt_, neglam, datmp], W=[datmp])
                k.v(lambda e: e.tensor_tensor(out=tt_.ap, in0=dch, in1=dch, op=ALU.mult), R=[datmp], W=[tt_])
                k.v(lambda e: e.reduce_sum(out=msall.ap[:, 4 * c:4 * c + 4], in_=tt_.ap, axis=AX.X), R=[tt_], W=[msall])
                if c == NCH - 1:
                    k.v(lambda e: e.tensor_scalar(out=msall.ap, in0=msall.ap, scalar1=1.0 / 128.0, scalar2=EPS,
                                                  op0=ALU.mult, op1=ALU.add), R=[msall], W=[msall])
                    k.act(msall.ap, msall.ap, AF.Ln, R=[msall], W=[msall])
                    k.act(msall.ap, msall.ap, AF.Exp, R=[msall], W=[msall], scale=-0.5)
                    for q4 in range(4):
                        sl = slice(8 * q4, 8 * q4 + 8)
                        mb = msall.ap[:, sl].rearrange("p (a b) -> p a b", b=1).to_broadcast([128, 8, 128])
                        gb = gsub.ap.rearrange("p (a b) -> p a b", a=1).broadcast_to([128, 8, 128])
                        k.v(lambda e, sl=sl, mb=mb: e.tensor_tensor(out=datmp.ap[:, sl, :], in0=datmp.ap[:, sl, :], in1=mb,
                                                                    op=ALU.mult), R=[datmp, msall], W=[datmp])
                        k.v(lambda e, sl=sl, gb=gb: e.tensor_tensor(out=oball.ap[:, sl, :], in0=datmp.ap[:, sl, :], in1=gb,
                                                                    op=ALU.mult), R=[datmp, gsub], W=[oball])
                    k.dma("pool", osct[:, :, h * 128:(h + 1) * 128], oball.ap, R=[oball], sem="oball")
            return ev
        return evac_of

    da_jobs = [(h, m) for h in range(4) for m in range(2)]

    def da_load(n):
        h, m = da_jobs[n]
        load_q(QB[n % 2], h * 128 + m * 64, h)
        load_k(KBf[n % 2], 512 + h * 128 + m * 64)
        if m == 0:
            load_v(VA[h % 2], h * 128, 128)

    if k.stage_attn & 1:
        mda = A.top
        VA = [A.alloc([NT, 129], BF16, "VA%d" % i) for i in range(2)]
        for t in VA:
            k.g(lambda e, t=t: e.memset(t.ap[:, :, 128:129], 1.0), W=[t])
        datmp = A.alloc([NT, 128], F32, "datmp")
        msall = A.alloc([NT], F32, "msall")
        oball = A.alloc([NT, 128], BF16, "oball")
        da_load(0)
        for n, (h, m) in enumerate(da_jobs):
            if n + 1 < len(da_jobs):
                da_load(n + 1)
            qb, kb, vb = QB[n % 2], KBf[n % 2], VA[h % 2]
            run_job(qb,
                    lambda kt, kb=kb: (kb.ap[0:68, kt * 128:(kt + 1) * 128], [kb]),
                    lambda kt, vb=vb: (vb.ap[:, kt, :], [vb]),
                    129, causal_tiles, causal_extras, da_evac(h, m))
        pipe.flush()
        S.barrier()
        A.top = mda

    if k.stage_attn & 2:
        VN = [A.alloc([NT, 65], BF16, "VN%d" % i) for i in range(2)]
        for t in VN:
            k.g(lambda e, t=t: e.memset(t.ap[:, :, 64:65], 1.0), W=[t])
        onsa = [A.alloc([NT, 64], F32, "onsa%d" % i) for i in range(4)]
        imp = A.alloc([NT, 64], F32, "imp")
        QS = [A.alloc([T], BF16, "QS%d" % i) for i in range(4)]
        Mtiles = [A.alloc([512], BF16, "Mt%d" % i) for i in range(3)]
        for i, nm in enumerate(("k", "v")):
            k.v(lambda e, i=i: e.tensor_copy(out=peTb[i].ap[0:64], in_=peT[i].ap[0:64]), R=[peT[i]], W=[peTb[i]])
            for pos in range(32):
                k.mm(k.bank(3)[:, i:i + 1], w1[i].ap[0:64, pos, :], peTb[i].ap[0:64, pos:pos + 1], pos == 0, pos == 31,
                     R=[w1[i], peTb[i]], W=[PB[3]])
            k.v(lambda e, i=i: e.tensor_copy(out=cb[i].ap, in_=k.bank(3)[:, i:i + 1]), R=[PB[3]], W=[cb[i]])
        cin = [QB[0], QB[1]]
        AcT = [A.alloc([256], BF16, "AcT%d" % i) for i in range(2)]
        KcT = A.alloc([256], BF16, "KcT")
        Vc = A.alloc([2, 129], BF16, "Vc")
        negT = A.alloc([T], BF16, "negT")
        score = A.alloc([NT, 64], F32, "score")
        sc2 = A.alloc([64], F32, "sc2")
        m8 = A.alloc([8], F32, "m8")
        thr = A.alloc([NT], F32, "thr")
        negm = A.alloc([NT, 64], BF16, "negm")
        k.g(lambda e: e.memset(Vc.ap, 0.0), W=[Vc])
        k.g(lambda e: e.memset(Vc.ap[:, :, 128:129], 1.0), W=[Vc])
        k.dma("sp", Vc.ap[:, :, 64:128], I["c_mcs"].rearrange("(n p) j -> p n j", p=128), W=[Vc], sem="c1")
        print("NSA arena top", A.top)
        k.g(lambda e: e.memset(KcT.ap, 0.0), W=[KcT])
        k.dma("sp", KcT.ap[64:68, :], I["c_kaugc"], W=[KcT], sem="c1")

        for g in range(2):
            for i, zr in enumerate((2048, 2176)):
                k.dma("sp", cin[i].ap[0:64, :], zT[zr + g * 64:zr + g * 64 + 64, :], W=[cin[i]], sem="cin%d" % i)
                for pos in range(32):
                    k.mm(k.bank(3)[:, 8:8 + 255], w1[i].ap[0:64, pos, :], cin[i].ap[0:64, pos:pos + 16 * 254 + 1:16],
                         pos == 0, pos == 31, R=[w1[i], cin[i]], W=[PB[3]])
                k.g(lambda e, i=i: e.memset(AcT[i].ap, 0.0), W=[AcT[i]])
                k.act(AcT[i].ap[:, 0:255], k.bank(3)[:, 8:8 + 255], AF.Silu, R=[PB[3], cb[i]], W=[AcT[i]], bias=cb[i].ap)
            k.mm(k.bank(3)[0:64, 0:255], w2[0].ap, AcT[0].ap[:, 0:255], True, True, R=[w2[0], AcT[0]], W=[PB[3]])
            k.v(lambda e: e.tensor_copy(out=KcT.ap[0:64, 0:255], in_=k.bank(3)[0:64, 0:255]), R=[PB[3]], W=[KcT])
            for ct in range(2):
                k.mm(k.bank(3)[:, 256 + ct * 64:256 + (ct + 1) * 64], AcT[1].ap[:, ct * 128:(ct + 1) * 128], w2[1].ap,
                     True, True, R=[w2[1], AcT[1]], W=[PB[3]])
            k.v(lambda e: e.tensor_copy(out=Vc.ap[:, :, 0:64],
                                        in_=k.bank(3)[:, 256:384].rearrange("p (a b) -> p a b", a=2)),
                R=[PB[3]], W=[Vc])

            def cmp_tiles(c):
                tl = [(0, 0, 4)]
                if c >= 4:
                    tl.append((1, 0, 4))
                return tl

            def cmp_extras(c, kt, j0, j1):
                if kt == 0 and c >= 5:
                    return []
                off = c * 512 if kt == 0 else (c - 4) * 512
                return [(k.ident.ap, cmask.ap[:, off:off + 512], 0, 512, [k.ident, cmask])]

            def cmp_evac(r, head):
                def evac_of(c, pair):
                    def ev():
                        i = state["ev"] % 2
                        state["ev"] += 1
                        ov = oview(pair)
                        OB = [PB[4 + 2 * pair], PB[5 + 2 * pair]]
                        r_ = rl[i]
                        r3 = r_.ap.rearrange("p (a b) -> p a b", b=1)
                        k.v(lambda e: e.tensor_scalar(out=r3, in0=ov[:, :, 128:129], scalar1=1e-30, scalar2=None,
                                                      op0=ALU.max), R=OB, W=[r_])
                        k.v(lambda e: e.reciprocal(out=r_.ap, in_=r_.ap), R=[r_], W=[r_])
                        g_ = rg[i]
                        k.v(lambda e: e.tensor_tensor(out=g_.ap.rearrange("p (a b) -> p a b", b=1), in0=r3,
                                                      in1=gates.ap[:, 4 * c:4 * c + 4, head * 3:head * 3 + 1], op=ALU.mult),
                            R=[r_, gates], W=[g_])
                        gb = g_.ap.rearrange("p (a b) -> p a b", b=1).to_broadcast([128, 4, 64])
                        k.v(lambda e: e.tensor_tensor(out=onsa[r].ap[:, 4 * c:4 * c + 4, :], in0=ov[:, :, 0:64], in1=gb,
                                                      op=ALU.mult), R=OB + [g_], W=[onsa[r]])
                        rb = r3.to_broadcast([128, 4, 64])
                        ich = imp.ap[:, 4 * c:4 * c + 4, :]
                        if r == 0:
                            k.v(lambda e: e.tensor_tensor(out=ich, in0=ov[:, :, 64:128], in1=rb, op=ALU.mult),
                                R=OB + [r_], W=[imp])
                        else:
                            tt_ = t2[i]
                            k.v(lambda e: e.tensor_tensor(out=tt_.ap[:, :, 0:64], in0=ov[:, :, 64:128], in1=rb,
                                                          op=ALU.mult), R=OB + [r_], W=[tt_])
                            k.g(lambda e: e.tensor_tensor(out=ich, in0=ich, in1=tt_.ap[:, :, 0:64], op=ALU.add),
                                R=[tt_, imp], W=[imp])
                    return ev
                return evac_of

            def acc_evac(r, head, branch, final):
                def evac_of(c, pair, ov=None, OB=None):
                    def ev(ov=ov, OB=OB):
                        i = state["ev"] % 2
                        state["ev"] += 1
                        if ov is None:
                            ov = oview(pair)
                            OB = [PB[4 + 2 * pair], PB[5 + 2 * pair]]
                        r_ = rl[i]
                        r3 = r_.ap.rearrange("p (a b) -> p a b", b=1)
                        k.v(lambda e: e.reciprocal(out=r3, in_=ov[:, :, 64:65]), R=OB, W=[r_])
                        g_ = rg[i]
                        k.v(lambda e: e.tensor_tensor(out=g_.ap.rearrange("p (a b) -> p a b", b=1), in0=r3,
                                                      in1=gates.ap[:, 4 * c:4 * c + 4, head * 3 + branch:head * 3 + branch + 1],
                                                      op=ALU.mult), R=[r_, gates], W=[g_])
                        gb = g_.ap.rearrange("p (a b) -> p a b", b=1).to_broadcast([128, 4, 64])
                        tt_ = t2[i]
                        k.v(lambda e: e.tensor_tensor(out=tt_.ap[:, :, 0:64], in0=ov[:, :, 0:64], in1=gb, op=ALU.mult),
                            R=OB + [g_], W=[tt_])
                        och = onsa[r].ap[:, 4 * c:4 * c + 4, :]
                        if not final:
                            k.g(lambda e: e.tensor_tensor(out=och, in0=och, in1=tt_.ap[:, :, 0:64], op=ALU.add),
                                R=[tt_, onsa[r]], W=[onsa[r]])
                        else:
                            o_ = ob[i]
                            k.v(lambda e: e.tensor_tensor(out=o_.ap[:, :, 0:64], in0=och, in1=tt_.ap[:, :, 0:64], op=ALU.add),
                                R=[tt_, onsa[r]], W=[o_])
                            k.dma("pool", osct[:, 4 * c:4 * c + 4, 512 + head * 64:512 + (head + 1) * 64], o_.ap[:, :, 0:64],
                                  R=[o_], sem="ob%d" % i)
                    return ev
                return evac_of

            for r in range(4):
                load_q(QS[r], 1536 + (g * 4 + r) * 64, 4 + g * 4 + r)
            kvs = KBf[0], VN[0]
            kvw = KBf[1], VN[1]
            load_k(kvs[0], 2304 + g * 64)
            load_v(kvs[1], 512 + g * 64, 64)
            load_k(kvw[0], 2560 + g * 64)
            load_v(kvw[1], 640 + g * 64, 64)

            for r in range(4):
                run_job(QS[r],
                        lambda kt: (KcT.ap[0:68, kt * 128:(kt + 1) * 128], [KcT]),
                        lambda kt: (Vc.ap[:, kt, :], [Vc]),
                        129, cmp_tiles, cmp_extras, cmp_evac(r, g * 4 + r))
            pipe.flush()

            k.v(lambda e: e.tensor_tensor(out=score.ap, in0=imp.ap, in1=selA.ap, op=ALU.mult), R=[imp, selA], W=[score])
            k.v(lambda e: e.tensor_tensor(out=score.ap, in0=score.ap, in1=selB.ap, op=ALU.add), R=[score, selB], W=[score])

            def sel_piece(n):
                k.v(lambda e: e.max(out=m8.ap, in_=score.ap[:, n, :]), R=[score], W=[m8])
                k.v(lambda e: e.match_replace(out=sc2.ap, in_to_replace=m8.ap, in_values=score.ap[:, n, :],
                                              imm_value=-3.0), R=[score, m8], W=[sc2])
                k.v(lambda e: e.max(out=m8.ap, in_=sc2.ap), R=[sc2], W=[m8])
                k.v(lambda e: e.tensor_copy(out=thr.ap[:, n:n + 1], in_=m8.ap[:, 7:8]), R=[m8], W=[thr])

            def sel_extras(c, kt, j0, j1):
                ex = [(expand.ap[0:64, kt * 128:(kt + 1) * 128], negT.ap[0:64, c * 512 + j0 * 128:c * 512 + j1 * 128],
                       j0 * 128, j1 * 128, [expand, negT])]
                return ex + causal_extras(c, kt, j0, j1)

            def win_tiles(c):
                tl = []
                for kt in range(max(0, 4 * c - 4), 4 * c + 4):
                    j0 = max(0, kt - 4 * c)
                    j1 = min(3, kt - 4 * c + 4) + 1
                    tl.append((kt, j0, j1))
                return tl

            def win_extras(c, kt, j0, j1):
                ex = []
                if kt >= 4 * c:
                    i = kt - 4 * c
                    ex.append((k.ident.ap, tric.ap, i * 128, (i + 1) * 128, [k.ident, tric]))
                if kt < 4 * c:
                    i = kt - 4 * c + 4
                    ex.append((k.ident.ap, tril.ap, i * 128, (i + 1) * 128, [k.ident, tril]))
                return ex

            seln = {"n": 0}

            def win_hook(c):
                sel_piece(seln["n"])
                seln["n"] += 1

            for r in range(4):
                kb, vb = kvw
                run_job(QS[r],
                        lambda kt, kb=kb: (kb.ap[0:68, kt * 128:(kt + 1) * 128], [kb]),
                        lambda kt, vb=vb: (vb.ap[:, kt, :], [vb]),
                        65, win_tiles, win_extras, acc_evac(r, g * 4 + r, 2, False), hook=win_hook)
            pipe.flush()
            tb = thr.ap.rearrange("p (a b) -> p a b", b=1).to_broadcast([128, NT, 64])
            k.v(lambda e: e.tensor_tensor(out=score.ap, in0=score.ap, in1=tb, op=ALU.is_ge), R=[score, thr], W=[score])
            k.v(lambda e: e.tensor_tensor(out=score.ap, in0=score.ap, in1=selA.ap, op=ALU.mult), R=[score, selA], W=[score])
            k.v(lambda e: e.tensor_copy(out=negm.ap, in_=score.ap), R=[score], W=[negm])
            for n0 in range(0, NT, 8):
                pT = k.bank_bf(3)
                for n in range(n0, n0 + 8):
                    k.tr(pT[0:64, (n - n0) * 128:(n - n0 + 1) * 128], negm.ap[:, n, :], R=[negm], W=[PB[3]])
                k.v(lambda e, n0=n0, pT=pT: e.tensor_copy(out=negT.ap[0:64, n0 * 128:(n0 + 8) * 128], in_=pT[0:64, :]),
                    R=[PB[3]], W=[negT])
            kb, vb = kvs
            mi = 0
            for c in range(NCH):
                tl = causal_tiles(c)
                started = set()
                for n, (kt, j0, j1) in enumerate(tl):
                    lo, hi = j0 * 128, j1 * 128
                    diag = kt >= 4 * c
                    Mt = Mtiles[mi % 3]
                    mi += 1
                    def mqk(sb, lo=lo, hi=hi, kt=kt, c=c, diag=diag):
                        k.mm(k.bank(sb)[:, lo:hi], expand.ap[0:64, kt * 128:(kt + 1) * 128],
                             negT.ap[0:64, c * 512 + lo:c * 512 + hi], True, not diag, R=[expand, negT], W=[PB[sb]])
                        if diag:
                            i = kt - 4 * c
                            k.mm(k.bank(sb)[:, i * 128:(i + 1) * 128], k.ident.ap, tricn.ap, False, True,
                                 R=[k.ident, tricn], W=[PB[sb]])

                    def mrest(sb, lo=lo, hi=hi, Mt=Mt):
                        k.v(lambda e: e.tensor_scalar(out=Mt.ap[:, lo:hi], in0=k.bank(sb)[:, lo:hi], scalar1=0.0,
                                                      scalar2=None, op0=ALU.max), R=[PB[sb]], W=[Mt])

                    pipe.push(dict(qkfn=mqk, restfn=mrest))
                    for r in range(4):
                        ovr = k.bank(4 + r)[:, 0:260].rearrange("p (j w) -> p j w", j=4)
                        pv = []
                        for j in range(j0, j1):
                            start = (4 + r) not in started
                            started.add(4 + r)
                            pv.append((ovr[:, j, 0:65], 4 + r, j * 128, (j + 1) * 128, vb.ap[:, kt, :], start,
                                       kt == 4 * c + j, [vb]))
                        last = n == len(tl) - 1
                        st = dict(kT=kb.ap[0:68, kt * 128:(kt + 1) * 128],
                                  qT=QS[r].ap[0:68, c * 512 + lo:c * 512 + hi], lo=lo, hi=hi, Rqk=[QS[r], kb],
                                  extra=[], pv=pv, scale=1.0, kp=128, mask=(Mt, "dve"),
                                  evac=(acc_evac(r, g * 4 + r, 1, True)(c, None, ovr, [PB[4 + r]]) if last else None))
                        pipe.push(st)
            pipe.flush()
    S.barrier()
    A.top = mark


def outxa_phase(k, I, x1, osc, x3, kv):
    A, S, PB = k.A, k.S, k.pb
    mark = A.top
    wout = k.load_w(I["w_mix_out"], D, D, "wout", "w0")
    wq = k.load_w(I["xa_w_q"], D, D, "wq", "w1")
    wo = k.load_w(I["xa_w_o"], D, D, "wo", "w2")
    gmixpost = k.load_bcast(I["mix_post_g"], D, "gmp", "c0")
    gxapre = k.load_bcast(I["xa_pre_g"], D, "gxp", "c0")
    gxapost = k.load_bcast(I["xa_post_g"], D, "gxo", "c0")
    KxT, Vx = kv
    ones = A.alloc([128], BF16, "ones")
    k.g(lambda e: e.memset(ones.ap, 1.0), W=[ones])
    hn = [A.alloc([D], BF16, "hn%d" % i) for i in range(2)]
    junk = A.alloc([D], BF16, "junk")
    TB = 2

    tfc = {"n": 0}

    def to_fm(src_ap, src_R, dst, col0, nhn, Wb=None):
        Wb = [dst] if Wb is None else Wb
        tb = (TB, 7)[tfc["n"] % 2]
        tfc["n"] += 1
        pT = k.bank_bf(tb)
        for dc in range(8):
            k.tr(pT[:, dc * 128:(dc + 1) * 128], src_ap[:, dc * 128:(dc + 1) * 128], R=src_R, W=[PB[tb]])
        k.act(dst.ap[:, :, col0:col0 + 128], pT.rearrange("p (a b) -> p a b", a=8), AF.Copy, R=[PB[tb]], W=Wb)

    ot = [A.alloc([D], BF16, "ot%d" % i) for i in range(4)]
    oT = A.alloc([8, 512], BF16, "oT")
    x2c = [[A.alloc([D], F32, "x2c%d_%d" % (i, j)) for j in range(4)] for i in range(3)]
    yb = [A.alloc([D], F32, "yb%d" % i) for i in range(4)]
    yb2 = [A.alloc([D], F32, "yb2%d" % i) for i in range(4)]
    msA = [A.alloc([4], F32, "msA%d" % i) for i in range(2)]
    msB = [A.alloc([4], F32, "msB%d" % i) for i in range(2)]
    msC = [A.alloc([4], F32, "msC%d" % i) for i in range(2)]
    h3T = A.alloc([8, 512], BF16, "h3T")
    qxT = [A.alloc([8, 512], BF16, "qxT%d" % i) for i in range(2)]
    PT = [A.alloc([512], BF16, "PTx%d" % i) for i in range(4)]
    oxT = oT
    rLb = [A.alloc([512], F32, "rLb%d" % i) for i in range(2)]
    osct = osc.rearrange("(n p) d -> n p d", p=128)
    x1tt = x1.rearrange("(n p) d -> n p d", p=128)
    x3t = x3.rearrange("(n p) d -> n p d", p=128)
    SB = [3, 4, 5]
    print("outxa arena top", A.top)
    cnt = {"s": 0, "e": 0}

    oTb = [Buf("oTb%d" % i) for i in range(4)]

    def proj_tile(srcT, tt, w, dst):
        for half in range(2):
            for fc in range(8):
                k.mm(k.bank(half), srcT.ap[:, fc, tt * 128:(tt + 1) * 128], w.ap[:, fc, half * 512:(half + 1) * 512],
                     fc == 0, fc == 7, R=[oTb[tt], w], W=[PB[half]])
            k.v(lambda e, half=half: e.tensor_copy(out=dst.ap[:, half * 512:(half + 1) * 512], in_=k.bank(half)),
                R=[PB[half]], W=[dst])

    def front1(c):
        X = x2c[c % 3]
        for tt in range(4):
            k.dma("sp", ot[tt].ap, osct[4 * c + tt], W=[ot[tt]], sem="ot%d" % tt)
        for tt in range(4):
            k.dma("sp", X[tt].ap, x1tt[4 * c + tt], W=[X[tt]], sem="x2c%d_%d" % (c % 3, tt))
        tf = lambda tt: to_fm(ot[tt].ap, [ot[tt]], oT, tt * 128, tt % 2, Wb=[oTb[tt]])
        pj = lambda tt: proj_tile(oT, tt, wout, yb[tt])
        tf(0); tf(1); pj(0); tf(2); pj(1); tf(3); pj(2); pj(3)

    def front2a(c):
        X = x2c[c % 3]
        mA, mB = msA[c % 2], msB[c % 2]
        for tt in range(4):
            k.act(junk.ap, yb[tt].ap, AF.Square, R=[yb[tt]], W=[junk, mA], scale=1.0 / 32.0, accum_out=mA.ap[:, tt:tt + 1])
        k.rstd_chain(mA.ap, mA)
        for tt in range(4):
            k.v(lambda e, tt=tt: e.scalar_tensor_tensor(out=yb[tt].ap, in0=yb[tt].ap, scalar=mA.ap[:, tt:tt + 1],
                                                        in1=gmixpost.ap, op0=ALU.mult, op1=ALU.mult),
                R=[yb[tt], mA, gmixpost], W=[yb[tt]])
            k.g(lambda e, tt=tt: e.tensor_tensor(out=X[tt].ap, in0=X[tt].ap, in1=yb[tt].ap, op=ALU.add),
                R=[yb[tt], X[tt]], W=[X[tt]])

    def front2b(c):
        X = x2c[c % 3]
        mA, mB = msA[c % 2], msB[c % 2]
        for tt in range(4):
            k.act(junk.ap, X[tt].ap, AF.Square, R=[X[tt]], W=[junk, mB], scale=1.0 / 32.0, accum_out=mB.ap[:, tt:tt + 1])
        k.rstd_chain(mB.ap, mB)

    def mixed(c, cb):
        X = x2c[c % 3]
        mB = msB[c % 2]

        def stt(tt):
            hh = hn[tt % 2]
            k.v(lambda e: e.scalar_tensor_tensor(out=hh.ap, in0=X[tt].ap, scalar=mB.ap[:, tt:tt + 1],
                                                 in1=gxapre.ap, op0=ALU.mult, op1=ALU.mult),
                R=[X[tt], mB, gxapre], W=[hh])

        tf = lambda tt: to_fm(hn[tt % 2].ap, [hn[tt % 2]], h3T, tt * 128, tt % 2)
        pj = lambda tt: proj_tile(oxT, tt, wo, yb2[tt])
        stt(0); stt(1); tf(0); pj(0); stt(2); tf(1); pj(1); stt(3); tf(2); pj(2); tf(3); pj(3)

    def front2c(c):
        q_ = qxT[c % 2]
        for ft in range(8):
            bk = SB[ft % 3]
            for dc in range(8):
                k.mm(k.bank(bk), wq.ap[:, dc, ft * 128:(ft + 1) * 128], h3T.ap[:, dc, :], dc == 0, dc == 7,
                     R=[wq, h3T], W=[PB[bk]])
            if ft % 2 == 0:
                k.act(q_.ap[:, ft, :], k.bank(bk), AF.Copy, R=[PB[bk]], W=[q_])
            else:
                k.v(lambda e, ft=ft, bk=bk: e.tensor_copy(out=q_.ap[:, ft, :], in_=k.bank(bk)), R=[PB[bk]], W=[q_])

    def back1(c, heads):
        q_ = qxT[c % 2]

        def qk(hh):
            for mt in range(2):
                bk = (3, 4)[mt]
                for j in range(2):
                    k.mm(k.bank(bk), KxT.ap[:, hh * 2 + j, mt * 128:(mt + 1) * 128], q_.ap[:, hh * 2 + j, :],
                         j == 0, j == 1, R=[KxT, q_], W=[PB[bk]])
                pt = PT[(2 * hh + mt) % 4]
                k.act(pt.ap, k.bank(bk), AF.Exp, R=[PB[bk]], W=[pt], scale=1.0 / 16.0)

        def lo(hh):
            pts = [PT[(2 * hh + mt) % 4] for mt in range(2)]
            for mt in range(2):
                k.mm(k.bank(7), ones.ap, pts[mt].ap, mt == 0, mt == 1, R=[ones, pts[mt]], W=[PB[7]])
            for dvc in range(2):
                for mt in range(2):
                    k.mm(k.bank(5 + dvc), Vx.ap[:, mt, hh * 256 + dvc * 128:hh * 256 + (dvc + 1) * 128], pts[mt].ap,
                         mt == 0, mt == 1, R=[Vx, pts[mt]], W=[PB[5 + dvc]])
            rL = rLb[hh % 2]
            k.act(rL.ap, k.bank(7), AF.Ln, R=[PB[7]], W=[rL])
            k.act(rL.ap, rL.ap, AF.Exp, R=[rL], W=[rL], scale=-1.0)
            for dvc in range(2):
                k.v(lambda e, dvc=dvc: e.tensor_tensor(out=oxT.ap[:, hh * 2 + dvc, :], in0=k.bank(5 + dvc), in1=rL.ap,
                                                       op=ALU.mult), R=[PB[5 + dvc], rL], W=oTb)

        qk(heads[0])
        for n, hh in enumerate(heads):
            if n + 1 < len(heads):
                qk(heads[n + 1])
            lo(hh)

    def back2(c):
        X = x2c[c % 3]
        mC = msC[c % 2]
        for tt in range(4):
            proj_tile(oxT, tt, wo, yb2[tt])

    def back2b(c):
        X = x2c[c % 3]
        mC = msC[c % 2]
        for tt in range(4):
            k.act(junk.ap, yb2[tt].ap, AF.Square, R=[yb2[tt]], W=[junk, mC], scale=1.0 / 32.0, accum_out=mC.ap[:, tt:tt + 1])
        k.rstd_chain(mC.ap, mC)
        for tt in range(4):
            gt = 4 * c + tt
            k.v(lambda e, tt=tt: e.scalar_tensor_tensor(out=yb2[tt].ap, in0=yb2[tt].ap, scalar=mC.ap[:, tt:tt + 1],
                                                        in1=gxapost.ap, op0=ALU.mult, op1=ALU.mult),
                R=[yb2[tt], mC, gxapost], W=[yb2[tt]])
            k.g(lambda e, tt=tt: e.tensor_tensor(out=yb2[tt].ap, in0=yb2[tt].ap, in1=X[tt].ap, op=ALU.add),
                R=[yb2[tt], X[tt]], W=[yb2[tt]])
            k.dma("pool", x3t[gt], yb2[tt].ap, R=[yb2[tt]], sem="x3o%d" % tt)

    front1(0)
    front2a(0)
    front2b(0)
    for tt in range(4):
        k.v(lambda e, tt=tt: e.scalar_tensor_tensor(out=hn[tt % 2].ap, in0=x2c[0][tt].ap, scalar=msB[0].ap[:, tt:tt + 1],
                                                    in1=gxapre.ap, op0=ALU.mult, op1=ALU.mult),
            R=[x2c[0][tt], msB[0], gxapre], W=[hn[tt % 2]])
        to_fm(hn[tt % 2].ap, [hn[tt % 2]], h3T, tt * 128, tt % 2)
    front2c(0)
    for c in range(NCH):
        nxt = c + 1 < NCH
        if nxt:
            front1(c + 1)
            front2a(c + 1)
        back1(c, (0, 1, 2, 3))
        if nxt:
            front2b(c + 1)
            mixed(c + 1, c)
            front2c(c + 1)
        else:
            back2(c)
        back2b(c)
    S.barrier()
    A.top = mark


def build(stage=99, debug=False, stage_attn=3):
    nc = bass.Bass("TRN2", target_bir_lowering=False)
    dt = lambda name, shape, dtype, kind: nc.dram_tensor(name, list(shape), dtype, kind=kind).ap()
    I = {}
    for name, shape in IN_SHAPES.items():
        I[name] = dt(name, shape, F32, "ExternalInput")
    for name, (shape, dtype) in CONST_SHAPES.items():
        I[name] = dt(name, shape, dtype, "ExternalInput")
    skind = "ExternalOutput" if debug else "Internal"
    x1 = dt("x1", [T, D], F32, skind)
    zT = dt("zT", [MIXIN, T], BF16, skind)
    vtok = dt("vtok", [T, 768], BF16, skind)
    gts = dt("gts", [T, 24], F32, skind)
    osc = dt("osc", [T, D], BF16, skind)
    x3 = dt("x3", [T, D], F32, skind)
    out = dt("out", [T, D], F32, "ExternalOutput")
    with ExitStack() as st:
        k = KB(nc, st, debug)
        k.stage_attn = stage_attn
        k.ident = k.A.alloc([128], BF16, "ident")
        k.dma("sp", k.ident.ap, I["c_ident"], W=[k.ident], sem="c9")
        ffn_phase(k, I["x"], x1, I["ffn1_w_gate"], I["ffn1_w_up"], I["ffn1_w_down"],
                  I["ffn1_pre_g"], I["ffn1_post_g"], "f1")
        base = k.A.top
        kv = (k.A.alloc([8, 256], BF16, "KxT"), k.A.alloc([2, D], BF16, "Vx"))
        if stage >= 2:
            inproj_phase(k, x1, I, zT, vtok, gts, kv)
        k.rstd_mode = "ln"
        if stage >= 3:
            attn_phase(k, I, zT, vtok, gts, osc)
        if stage >= 4:
            outxa_phase(k, I, x1, osc, x3, kv)
        k.rstd_mode = "sqrt"
        k.A.top = base
        if stage >= 5:
            ffn_phase(k, x3, out, I["ffn2_w_gate"], I["ffn2_w_up"], I["ffn2_w_down"],
                      I["ffn2_pre_g"], I["ffn2_post_g"], "f2")
        k.S.barrier()
        k.S.emit()
        print("ops", k.S.nops, "sems", k.S.nsem, "arena peak", k.A.peak)
    return nc


IN_SHAPES = {
    "x": (T, D), "mem": (256, D),
    "ffn1_pre_g": (1, D), "ffn1_post_g": (1, D),
    "ffn1_w_gate": (D, DFF), "ffn1_w_up": (D, DFF), "ffn1_w_down": (DFF, D),
    "mix_pre_g": (1, D), "mix_post_g": (1, D), "w_mix_in": (D, MIXIN),
    "da_lambda_q1": (1, 64), "da_lambda_k1": (1, 64), "da_lambda_q2": (1, 64), "da_lambda_k2": (1, 64),
    "da_subln_g": (1, 128),
    "cmp_k_pe": (32, 64), "cmp_k_w1": (2048, 128), "cmp_k_w2": (128, 64),
    "cmp_v_pe": (32, 64), "cmp_v_w1": (2048, 128), "cmp_v_w2": (128, 64),
    "w_mix_out": (D, D),
    "xa_pre_g": (1, D), "xa_post_g": (1, D), "mem_norm_g": (1, D),
    "xa_w_q": (D, D), "xa_w_k": (D, D), "xa_w_v": (D, D), "xa_w_o": (D, D),
    "ffn2_pre_g": (1, D), "ffn2_post_g": (1, D),
    "ffn2_w_gate": (D, DFF), "ffn2_w_up": (D, DFF), "ffn2_w_down": (DFF, D),
}
PER_CORE = ("x", "mem")

CONST_SHAPES = {
    "c_ident": ((128, 128), BF16),
    "c_tric": ((128, 128), BF16),
    "c_tril": ((128, 128), BF16),
    "c_tricn": ((128, 128), BF16),
    "c_cmask": ((128, 2560), BF16),
    "c_expand": ((128, 4096), BF16),
    "c_selA": ((T, 64), F32),
    "c_selB": ((T, 64), F32),
    "c_mcs": ((256, 64), BF16),
    "c_qaug": ((12, 4, T), BF16),
    "c_kaug": ((4, T), BF16),
    "c_kaugc": ((4, 256), BF16),
}


def make_consts():
    bf = ml_dtypes.bfloat16
    c = {}
    c["c_ident"] = np.eye(128, dtype=np.float32).astype(bf)
    kk = np.arange(128)[:, None]
    qq = np.arange(128)[None, :]
    c["c_tric"] = np.where(kk <= qq, 0.0, NEGBIG).astype(np.float32).astype(bf)
    c["c_tril"] = np.where(kk > qq, 0.0, NEGBIG).astype(np.float32).astype(bf)
    c["c_tricn"] = np.where(kk <= qq, 0.0, -1.0).astype(np.float32).astype(bf)
    tt = np.arange(2560)[None, :]
    c["c_cmask"] = np.where(tt - 16 * kk >= 31, 0.0, NEGBIG).astype(np.float32).astype(bf)
    ex = np.zeros((128, T), np.float32)
    ex[:64] = (np.arange(T)[None, :] // 64 == np.arange(64)[:, None])
    c["c_expand"] = ex.astype(bf)
    t = np.arange(T)
    cur = (t // 64)[:, None]
    blk = np.arange(64)[None, :]
    Am = (blk <= cur).astype(np.float32)
    forced = ((blk == 0) | (blk == cur) | (blk == cur - 1)).astype(np.float32)
    c["c_selA"] = Am
    c["c_selB"] = (1e4 * forced - (1.0 - Am)).astype(np.float32)
    cs = np.arange(255) * 16
    ss = np.arange(64) * 64
    ov = np.clip(np.minimum(cs[:, None] + 32, ss[None, :] + 64) - np.maximum(cs[:, None], ss[None, :]), 0, None)
    mcs = np.zeros((256, 64), np.float32)
    mcs[:255] = ov / 32.0
    c["c_mcs"] = mcs.astype(bf)
    a = (t // 64).astype(np.float32)
    b = (t % 64).astype(np.float32)
    slopes = list(2.0 ** (-8.0 * np.arange(1, 5) / 4)) + list(2.0 ** (-8.0 * np.arange(1, 9) / 8))
    qa = np.zeros((12, 4, T), np.float32)
    for s, sl in enumerate(slopes):
        qa[s, 0] = -sl * 64 * a
        qa[s, 1] = -sl * b
        qa[s, 2] = sl
        qa[s, 3] = sl
    c["c_qaug"] = qa.astype(bf)
    ka = np.stack([np.ones(T), np.ones(T), 64 * a, b]).astype(np.float32)
    c["c_kaug"] = ka.astype(bf)
    pc = np.arange(256) * 16 + 31
    kc = np.stack([np.ones(256), np.ones(256), 64.0 * (pc // 64), 1.0 * (pc % 64)]).astype(np.float32)
    kc[:, 255] = 0
    c["c_kaugc"] = kc.astype(bf)
    return c


_CACHE = {}


def kernel(**inputs):
    n = 8
    if "nc" not in _CACHE:
        _CACHE["nc"] = build()
    nc = _CACHE["nc"]
    consts = make_consts()
    in_maps = []
    for i in range(n):
        m = {}
        for name, shape in IN_SHAPES.items():
            a = np.asarray(inputs[name], dtype=np.float32)
            a = a[i] if name in PER_CORE else a[0]
            m[name] = np.ascontiguousarray(a.reshape(shape))
        m.update(consts)
        in_maps.append(m)
    res = run_bass_kernel_spmd(nc, in_maps, core_ids=list(range(n)))
    return np.stack([r["out"] for r in res.results], axis=0).astype(np.float32)
```

```python
import math
from contextlib import ExitStack

import numpy as np
import ml_dtypes

import concourse.bass as bass
import concourse.mybir as mybir
from concourse.bass_utils import run_bass_kernel_spmd

F32 = mybir.dt.float32
BF16 = mybir.dt.bfloat16
AF = mybir.ActivationFunctionType
ALU = mybir.AluOpType
AX = mybir.AxisListType

SEM_LIMIT = 30000
DT_SIZE = {F32: 4, BF16: 2}

T = 4096
D = 1024
DFF = 2816
NT = T // 128
NCH = T // 512
MIXIN = 2840
NEGBIG = -30000.0
EPS = 1e-6


class Buf:
    __slots__ = ("name", "w", "r")

    def __init__(self, name=""):
        self.name = name
        self.w = None
        self.r = {}


class Tile:
    __slots__ = ("ap", "b")

    def __init__(self, ap, name=""):
        self.ap = ap
        self.b = Buf(name)

    def __getitem__(self, k):
        return self.ap[k]


class Sched:
    ENG = ("pe", "act", "dve", "pool", "sp")

    def __init__(self, nc, stack):
        self.nc = nc
        self.stack = stack
        self.q = {e: [] for e in self.ENG}
        self.cnt = {}
        self.sems = {}
        self.seen = {e: {} for e in self.ENG}
        self.nsem = 0
        self.nops = 0

    def _sem(self, key):
        if key not in self.sems:
            self.sems[key] = self.stack.enter_context(self.nc.semaphore("s%d" % self.nsem))
            self.nsem += 1
        return self.sems[key]

    def _bump(self, base, inc):
        ep, v = self.cnt.get(base, (0, 0))
        if v + inc > SEM_LIMIT:
            ep, v = ep + 1, 0
        v += inc
        self.cnt[base] = (ep, v)
        key = (base, ep)
        self._sem(key)
        return key, v

    def op(self, eng, fn, reads=(), writes=(), dma=None, skip_same=False):
        reads = [t.b if isinstance(t, Tile) else t for t in reads]
        writes = [t.b if isinstance(t, Tile) else t for t in writes]
        deps = []
        for b in reads:
            if b.w is not None:
                deps.append(b.w)
        for b in writes:
            if b.w is not None:
                deps.append(b.w)
            deps.extend(b.r.items())
        if dma is not None:
            dma = (dma, eng)
            ep, v = self.cnt.get(dma, (0, 0))
            if v > 0:
                deps.append(((dma, ep), v))
        waits = {}
        seen = self.seen[eng]
        for key, v in deps:
            if skip_same and key[0] == eng:
                continue
            if seen.get(key, 0) >= v:
                continue
            if waits.get(key, 0) < v:
                waits[key] = v
        for key, v in waits.items():
            seen[key] = v
        if dma is None:
            key, v = self._bump(eng, 1)
            inc = 1
        else:
            key, v = self._bump(dma, 16)
            inc = 16
        self.q[eng].append((list(waits.items()), fn, key, inc))
        self.nops += 1
        ev = (key, v)
        for b in reads:
            if b.r.get(key, 0) < v:
                b.r[key] = v
        for b in writes:
            b.w = ev
            b.r = {}
        return ev

    def barrier(self):
        allv = [((base, ep), v) for base, (ep, v) in self.cnt.items() if v > 0]
        for e in self.ENG:
            waits = []
            for key, v in allv:
                if self.seen[e].get(key, 0) < v:
                    waits.append((key, v))
                    self.seen[e][key] = v
            if waits:
                self.q[e].append((waits, None, None, 0))

    def emit(self):
        nc = self.nc
        sems = self.sems
        q = self.q

        def replay(name, e):
            for waits, fn, key, inc in q[name]:
                for k, v in waits:
                    e.wait_ge(sems[k], v)
                if fn is not None:
                    fn(e).then_inc(sems[key], inc)

        with nc.Block() as block:
            @block.tensor
            def _(e):
                replay("pe", e)

            @block.scalar
            def _(e):
                replay("act", e)

            @block.vector
            def _(e):
                replay("dve", e)

            @block.gpsimd
            def _(e):
                replay("pool", e)

            @block.sync
            def _(e):
                replay("sp", e)


class Arena:
    def __init__(self, nc, stack, nbytes):
        self.t = stack.enter_context(nc.sbuf_tensor("arena", [128, nbytes // 4], F32))
        self.top = 0
        self.cap = nbytes
        self.peak = 0

    def alloc(self, shape, dtype, name=""):
        n = int(np.prod(shape)) * DT_SIZE[dtype]
        n4 = (n + 3) // 4
        off = self.top // 4
        self.top += n4 * 4
        self.peak = max(self.peak, self.top)
        assert self.top <= self.cap, ("SBUF arena overflow", name, self.top, self.cap)
        ap = self.t[:, off:off + n4]
        if dtype != F32:
            ap = ap.bitcast(dtype)
            ap = ap[:, 0:int(np.prod(shape))]
        if len(shape) == 2:
            ap = ap.rearrange("p (a b) -> p a b", a=shape[0])
        elif len(shape) == 3:
            ap = ap.rearrange("p (a b c) -> p a b c", a=shape[0], b=shape[1])
        return Tile(ap, name)


class KB:
    def __init__(self, nc, st, debug):
        self.nc = nc
        self.S = Sched(nc, st)
        self.A = Arena(nc, st, 212000)
        self.psum = st.enter_context(nc.psum_tensor("psum", [128, 4096], F32))
        self.pb = [Buf("bank%d" % i) for i in range(8)]
        self.debug = debug

    def bank(self, i, n=1):
        return self.psum[:, i * 512:(i + n) * 512]

    def bank_bf(self, i):
        return self.psum[:, i * 512:(i + 1) * 512].bitcast(BF16)

    def dma(self, q, out, in_, R=(), W=(), sem=None):
        return self.S.op(q, lambda e: e.dma_start(out=out, in_=in_), reads=R, writes=W, dma=sem)

    def mm(self, out, lhsT, rhs, start, stop, R=(), W=()):
        return self.S.op("pe", lambda e: e.matmul(out, lhsT=lhsT, rhs=rhs, start=start, stop=stop,
                                                  skip_group_check=True),
                         reads=R, writes=W, skip_same=True)

    def tr(self, out, in_, R=(), W=()):
        ident = self.ident
        return self.S.op("pe", lambda e: e.transpose(out=out, in_=in_, identity=ident.ap),
                         reads=list(R) + [ident], writes=W, skip_same=True)

    def act(self, out, in_, func, R=(), W=(), **kw):
        return self.S.op("act", lambda e: e.activation(out=out, in_=in_, func=func, **kw), reads=R, writes=W)

    def v(self, fn, R=(), W=()):
        return self.S.op("dve", fn, reads=R, writes=W)

    def g(self, fn, R=(), W=()):
        return self.S.op("pool", fn, reads=R, writes=W)

    rstd_mode = "sqrt"

    def rstd_chain(self, ms_ap, t):
        self.v(lambda e: e.tensor_scalar(out=ms_ap, in0=ms_ap, scalar1=EPS, scalar2=None, op0=ALU.add), R=[t], W=[t])
        if self.rstd_mode == "ln":
            self.act(ms_ap, ms_ap, AF.Ln, R=[t], W=[t])
            self.act(ms_ap, ms_ap, AF.Exp, R=[t], W=[t], scale=-0.5)
        else:
            self.act(ms_ap, ms_ap, AF.Sqrt, R=[t], W=[t])
            self.v(lambda e: e.reciprocal(out=ms_ap, in_=ms_ap), R=[t], W=[t])

    def load_bcast(self, dram_row, n, name, sem):
        t = self.A.alloc([n], F32, name)
        self.dma("sp", t.ap, dram_row.partition_broadcast(128), W=[t], sem=sem)
        return t

    def load_w_groups(self, dram_w, rows, cols, name, sem, gcols):
        nch = rows // 128
        t = self.A.alloc([nch, cols], BF16, name)
        src = dram_w.rearrange("(c p) n -> p c n", p=128)
        bufs = []
        for g0 in range(0, cols, gcols):
            g1 = min(cols, g0 + gcols)
            b = Buf("%s_g%d" % (name, g0))
            bufs.append(b)
            self.dma("pool", t.ap[:, :, g0:g1], src[:, :, g0:g1], W=[b], sem=sem)
        return t, (lambda col: bufs[col // gcols])

    def load_w(self, dram_w, rows, cols, name, sem, defer=False):
        nch = rows // 128
        t = self.A.alloc([nch, cols], BF16, name)
        if defer:
            return t, (lambda: self._issue_w(t, dram_w, nch, sem))
        self._issue_w(t, dram_w, nch, sem)
        return t

    def _issue_w(self, t, dram_w, nch, sem):
        src = dram_w.rearrange("(c p) n -> p c n", p=128)
        step = max(1, nch // 4)
        for c0 in range(0, nch, step):
            c1 = min(nch, c0 + step)
            self.dma("pool", t.ap[:, c0:c1, :], src[:, c0:c1, :], W=[t], sem=sem)


def ffn_phase(k, src, dst, wg_d, wu_d, wd_d, gpre_d, gpost_d, tag):
    A, S = k.A, k.S
    mark = A.top
    GC = 640
    nchw = D // 128
    wg = A.alloc([nchw, DFF], BF16, "wg")
    wu = A.alloc([nchw, DFF], BF16, "wu")
    wgb, wub = [], []
    for g0 in range(0, DFF, GC):
        for (t_, d_, bl, sm) in ((wg, wg_d, wgb, "w0"), (wu, wu_d, wub, "w1")):
            b = Buf("wgrp")
            bl.append(b)
            g1 = min(DFF, g0 + GC)
            k.dma("pool", t_.ap[:, :, g0:g1], d_.rearrange("(c p) n -> p c n", p=128)[:, :, g0:g1],
                  W=[b], sem=sm)
    wd = k.load_w(wd_d, DFF, D, "wd", "w2")
    gpre = k.load_bcast(gpre_d, D, "gpre", "c0")
    gpost = k.load_bcast(gpost_d, D, "gpost", "c1")
    k.v(lambda e: e.tensor_scalar(out=gpost.ap, in0=gpost.ap, scalar1=0.5, scalar2=None, op0=ALU.mult),
        R=[gpost], W=[gpost])
    xs = [A.alloc([D], F32, "xs%d" % i) for i in range(2)]
    xr = [A.alloc([D], F32, "xr%d" % i) for i in range(2)]
    hn = [A.alloc([D], BF16, "hn%d" % i) for i in range(2)]
    hT = [A.alloc([8, 512], BF16, "hT%d" % i) for i in range(2)]
    AT = A.alloc([22, 512], BF16, "AT")
    sg = [A.alloc([512], BF16, "sg%d" % i) for i in range(2)]
    ytmp = A.alloc([D], F32, "ytmp")
    junk = A.alloc([D], BF16, "junk")
    ms = [A.alloc([4], F32, "ms%d" % i) for i in range(2)]
    ms2 = [A.alloc([1], F32, "ms2%d" % i) for i in range(2)]
    srct = src.rearrange("(n p) d -> n p d", p=128)
    dstt = dst.rearrange("(n p) d -> n p d", p=128)
    PB = k.pb
    cnt = {"x": 0, "r": 0}

    def pn_chain(c, tt):
        m = ms[c % 2]
        gt = 4 * c + tt
        i = tt % 2
        x = xs[i]
        k.dma("sp", x.ap, srct[gt], W=[x], sem="xs%d" % i)
        hh = hn[i]
        k.act(hh.ap, x.ap, AF.Square, R=[x], W=[hh, m], scale=1.0 / 32.0, accum_out=m.ap[:, tt:tt + 1])
        k.rstd_chain(m.ap[:, tt:tt + 1], m)
        k.v(lambda e: e.scalar_tensor_tensor(out=hh.ap, in0=x.ap, scalar=m.ap[:, tt:tt + 1], in1=gpre.ap,
                                             op0=ALU.mult, op1=ALU.mult), R=[x, m, gpre], W=[hh])

    def pn_tr(c, tt):
        h = hT[c % 2]
        hh = hn[tt % 2]
        pT = k.bank_bf(6)
        for dc in range(8):
            k.tr(pT[:, dc * 128:(dc + 1) * 128], hh.ap[:, dc * 128:(dc + 1) * 128], R=[hh], W=[PB[6]])
        k.act(h.ap[:, :, tt * 128:(tt + 1) * 128], pT.rearrange("p (a b) -> p a b", a=8), AF.Copy,
              R=[PB[6]], W=[h])

    def prenorm(c):
        for tt in range(4):
            pn_chain(c, tt)
            pn_tr(c, tt)

    def gateup(c):
        h = hT[c % 2]
        for f in range(22):
            gb, ub = f % 2, 2 + f % 2
            for dc in range(8):
                k.mm(k.bank(gb), wg.ap[:, dc, f * 128:(f + 1) * 128], h.ap[:, dc, :], dc == 0, dc == 7,
                     R=[wgb[f * 128 // GC], h], W=[PB[gb]])
            for dc in range(8):
                k.mm(k.bank(ub), wu.ap[:, dc, f * 128:(f + 1) * 128], h.ap[:, dc, :], dc == 0, dc == 7,
                     R=[wub[f * 128 // GC], h], W=[PB[ub]])
            s = sg[f % 2]
            k.act(s.ap, k.bank(gb), AF.Silu, R=[PB[gb]], W=[s])
            k.v(lambda e, s=s, ub=ub, f=f: e.tensor_tensor(out=AT.ap[:, f, :], in0=s.ap, in1=k.bank(ub), op=ALU.mult),
                R=[s, PB[ub]], W=[AT])
            if c + 1 < NCH:
                if f in (1, 6, 11, 16):
                    pn_chain(c + 1, (f - 1) // 5)
                if f in (4, 9, 14, 19):
                    pn_tr(c + 1, (f - 4) // 5)

    def down(c):
        for tt in range(4):
            gt = 4 * c + tt
            i = cnt["r"] % 2
            cnt["r"] += 1
            r = xr[i]
            k.dma("sp", r.ap, srct[gt], W=[r], sem="xr%d" % i)
            yb0 = 4 + 2 * (tt % 2)
            for half in range(2):
                for f in range(22):
                    k.mm(k.bank(yb0 + half), AT.ap[:, f, tt * 128:(tt + 1) * 128],
                         wd.ap[:, f, half * 512:(half + 1) * 512], f == 0, f == 21,
                         R=[AT, wd], W=[PB[yb0 + half]])
            m2 = ms2[i]
            k.v(lambda e, yb0=yb0: e.tensor_copy(out=ytmp.ap, in_=k.bank(yb0, 2)), R=[PB[yb0], PB[yb0 + 1]], W=[ytmp])
            k.act(junk.ap, ytmp.ap, AF.Square, R=[ytmp], W=[junk, m2], scale=1.0 / 32.0, accum_out=m2.ap)
            k.rstd_chain(m2.ap, m2)
            k.v(lambda e, m2=m2: e.scalar_tensor_tensor(out=ytmp.ap, in0=ytmp.ap, scalar=m2.ap, in1=gpost.ap,
                                                        op0=ALU.mult, op1=ALU.mult),
                R=[m2, gpost, ytmp], W=[ytmp])
            k.g(lambda e, r=r: e.tensor_tensor(out=r.ap, in0=r.ap, in1=ytmp.ap, op=ALU.add), R=[r, ytmp], W=[r])
            k.dma("pool", dstt[gt], r.ap, R=[r], sem="xo%d" % i)

    prenorm(0)
    for c in range(NCH):
        gateup(c)
        down(c)
    S.barrier()
    A.top = mark


def make_prenorm(k, gpre, nx=2):
    A = k.A
    xs = [A.alloc([D], F32, "pxs%d" % i) for i in range(nx)]
    hn = [A.alloc([D], BF16, "phn%d" % i) for i in range(2)]
    hT = [A.alloc([8, 512], BF16, "phT%d" % i) for i in range(2)]
    ms = [A.alloc([4], F32, "pms%d" % i) for i in range(2)]
    cnt = {"x": 0}
    PB = k.pb

    def norm_tile(x, m, tt, h, g, bank=6):
        i = cnt["x"] % 2
        cnt["x"] += 1
        hh = hn[i]
        k.act(hh.ap, x.ap if isinstance(x, Tile) else x[0], AF.Square, R=[x if isinstance(x, Tile) else x[1]],
              W=[hh, m], scale=1.0 / 32.0, accum_out=m.ap[:, tt:tt + 1])
        k.rstd_chain(m.ap[:, tt:tt + 1], m)
        xa = x.ap if isinstance(x, Tile) else x[0]
        xt = x if isinstance(x, Tile) else x[1]
        k.v(lambda e: e.scalar_tensor_tensor(out=hh.ap, in0=xa, scalar=m.ap[:, tt:tt + 1], in1=g.ap,
                                             op0=ALU.mult, op1=ALU.mult), R=[xt, m, g], W=[hh])
        pT = k.bank_bf(bank)
        for dc in range(8):
            k.tr(pT[:, dc * 128:(dc + 1) * 128], hh.ap[:, dc * 128:(dc + 1) * 128], R=[hh], W=[PB[bank]])
        k.act(h.ap[:, :, tt * 128:(tt + 1) * 128], pT.rearrange("p (a b) -> p a b", a=8), AF.Copy,
              R=[PB[bank]], W=[h])

    def prenorm(c, srct):
        h = hT[c % 2]
        m = ms[c % 2]
        for tt in range(4):
            gt = 4 * c + tt
            i = cnt["x"] % nx
            x = xs[i]
            k.dma("sp", x.ap, srct[gt], W=[x], sem="pxs%d" % i)
            norm_tile(x, m, tt, h, gpre)
        return h

    def chain(c, tt, srct):
        m = ms[c % 2]
        gt = 4 * c + tt
        x = xs[tt % nx]
        k.dma("sp", x.ap, srct[gt], W=[x], sem="pxs%d" % (tt % nx))
        hh = hn[tt % 2]
        k.act(hh.ap, x.ap, AF.Square, R=[x], W=[hh, m], scale=1.0 / 32.0, accum_out=m.ap[:, tt:tt + 1])
        k.rstd_chain(m.ap[:, tt:tt + 1], m)
        k.v(lambda e: e.scalar_tensor_tensor(out=hh.ap, in0=x.ap, scalar=m.ap[:, tt:tt + 1], in1=gpre.ap,
                                             op0=ALU.mult, op1=ALU.mult), R=[x, m, gpre], W=[hh])

    def tr(c, tt, bank=6):
        h = hT[c % 2]
        hh = hn[tt % 2]
        pT = k.bank_bf(bank)
        for dc in range(8):
            k.tr(pT[:, dc * 128:(dc + 1) * 128], hh.ap[:, dc * 128:(dc + 1) * 128], R=[hh], W=[PB[bank]])
        k.act(h.ap[:, :, tt * 128:(tt + 1) * 128], pT.rearrange("p (a b) -> p a b", a=8), AF.Copy,
              R=[PB[bank]], W=[h])
        return h

    prenorm.chain = chain
    prenorm.tr = tr
    prenorm.norm_tile = norm_tile
    prenorm.hT = hT
    prenorm.ms = ms
    return prenorm


FM = [(0, 4, 0.125), (512, 4, 1.0), (1536, 4, 0.125), (2048, 1, 1.0), (2176, 1, 1.0), (2304, 1, 1.0), (2560, 1, 1.0)]


def inproj_phase(k, x1, I, zT, vtok, gts, kv):
    A, S, PB = k.A, k.S, k.pb
    mark = A.top
    gpre = k.load_bcast(I["mix_pre_g"], D, "gpre", "c0")
    win, wgrp = k.load_w_groups(I["w_mix_in"], D, MIXIN, "win", "w0", 512)
    prenorm = make_prenorm(k, gpre, nx=4)
    stg = [A.alloc([512], BF16, "stg%d" % i) for i in range(3)]
    vst = [A.alloc([768], BF16, "vst%d" % i) for i in range(2)]
    gst = [A.alloc([24], F32, "gst%d" % i) for i in range(2)]
    srct = x1.rearrange("(n p) d -> n p d", p=128)
    vtokt = vtok.rearrange("(n p) d -> n p d", p=128)
    gtst = gts.rearrange("(n p) d -> n p d", p=128)
    KxT, Vx = kv
    gmem = k.load_bcast(I["mem_norm_g"], D, "gmem", "c0")
    wk, go_wk = k.load_w(I["xa_w_k"], D, D, "wk", "w3", defer=True)
    wv, go_wv = k.load_w(I["xa_w_v"], D, D, "wv", "w4", defer=True)
    mT = A.alloc([8, 256], BF16, "mT")
    msm = A.alloc([2], F32, "msm")
    memt = I["mem"].rearrange("(n p) d -> n p d", p=128)
    xm = [A.alloc([D], F32, "xm%d" % i) for i in range(2)]
    mhn = [A.alloc([D], BF16, "mhn%d" % i) for i in range(2)]
    for mt in range(2):
        k.dma("sp", xm[mt].ap, memt[mt], W=[xm[mt]], sem="c1")

    def kv_setup():
        for mt in range(2):
            k.act(mhn[mt].ap, xm[mt].ap, AF.Square, R=[xm[mt]], W=[mhn[mt], msm], scale=1.0 / 32.0,
                  accum_out=msm.ap[:, mt:mt + 1])
        k.rstd_chain(msm.ap, msm)
        for mt in range(2):
            k.v(lambda e, mt=mt: e.scalar_tensor_tensor(out=mhn[mt].ap, in0=xm[mt].ap, scalar=msm.ap[:, mt:mt + 1],
                                                        in1=gmem.ap, op0=ALU.mult, op1=ALU.mult),
                R=[xm[mt], msm, gmem], W=[mhn[mt]])
            pT = k.bank_bf(7)
            for dc in range(8):
                k.tr(pT[:, dc * 128:(dc + 1) * 128], mhn[mt].ap[:, dc * 128:(dc + 1) * 128], R=[mhn[mt]], W=[PB[7]])
            k.act(mT.ap[:, :, mt * 128:(mt + 1) * 128], pT.rearrange("p (a b) -> p a b", a=8), AF.Copy,
                  R=[PB[7]], W=[mT])
        for ft in range(8):
            bk = 4 + ft % 2
            for dc in range(8):
                k.mm(k.bank(bk)[:, 0:256], wk.ap[:, dc, ft * 128:(ft + 1) * 128], mT.ap[:, dc, :], dc == 0, dc == 7,
                     R=[wk, mT], W=[PB[bk]])
            k.act(KxT.ap[:, ft, :], k.bank(bk)[:, 0:256], AF.Copy, R=[PB[bk]], W=[KxT])
        for mt in range(2):
            for half in range(2):
                bk = 4 + half
                for dc in range(8):
                    k.mm(k.bank(bk), mT.ap[:, dc, mt * 128:(mt + 1) * 128], wv.ap[:, dc, half * 512:(half + 1) * 512],
                         dc == 0, dc == 7, R=[wv, mT], W=[PB[bk]])
                k.v(lambda e, mt=mt, half=half, bk=bk: e.tensor_copy(
                    out=Vx.ap[:, mt, half * 512:(half + 1) * 512], in_=k.bank(bk)), R=[PB[bk]], W=[Vx])

    hnext = prenorm(0, srct)
    for c in range(NCH):
        h = hnext
        i = 0
        for (z0, ntile, sc) in FM:
            for ft in range(ntile):
                bk = i % 2
                col = z0 + ft * 128
                for dc in range(8):
                    k.mm(k.bank(bk), win.ap[:, dc, col:col + 128], h.ap[:, dc, :], dc == 0, dc == 7,
                         R=[wgrp(col), h], W=[PB[bk]])
                s = stg[i % 3]
                if sc != 1.0:
                    k.v(lambda e, s=s, bk=bk, sc=sc: e.tensor_scalar(out=s.ap, in0=k.bank(bk), scalar1=sc, scalar2=None,
                                                                     op0=ALU.mult), R=[PB[bk]], W=[s])
                else:
                    k.act(s.ap, k.bank(bk), AF.Copy, R=[PB[bk]], W=[s])
                k.dma("pool" if sc != 1.0 else "act", zT[col:col + 128, c * 512:(c + 1) * 512], s.ap, R=[s],
                      sem="stg%d" % (i % 3))
                if c + 1 < NCH:
                    if i in (1, 5, 9, 13):
                        prenorm.chain(c + 1, (i - 1) // 4, srct)
                    if i in (3, 7, 11, 15):
                        hnext = prenorm.tr(c + 1, (i - 3) // 4)
                i += 1
        for tt in range(4):
            gt = 4 * c + tt
            hs = h.ap[:, :, tt * 128:(tt + 1) * 128]
            for dc in range(8):
                k.mm(k.bank(2), hs[:, dc, :], win.ap[:, dc, 1024:1536], dc == 0, dc == 7,
                     R=[wgrp(1024), h], W=[PB[2]])
            for (o0, z0, n) in ((0, 2432, 128), (128, 2688, 128), (256, 2816, 24)):
                for dc in range(8):
                    k.mm(k.bank(3)[:, o0:o0 + n], hs[:, dc, :], win.ap[:, dc, z0:z0 + n], dc == 0, dc == 7,
                         R=[wgrp(z0), h], W=[PB[3]])
            vs = vst[gt % 2]
            k.act(vs.ap[:, 0:512], k.bank(2), AF.Copy, R=[PB[2]], W=[vs])
            k.v(lambda e, vs=vs: e.tensor_copy(out=vs.ap[:, 512:768], in_=k.bank(3)[:, 0:256]), R=[PB[3]], W=[vs])
            k.dma("pool", vtokt[gt], vs.ap, R=[vs], sem="vst%d" % (gt % 2))
            gs = gst[gt % 2]
            k.v(lambda e, gs=gs: e.tensor_copy(out=gs.ap, in_=k.bank(3)[:, 256:280]), R=[PB[3]], W=[gs])
            k.dma("pool", gtst[gt], gs.ap, R=[gs], sem="gst%d" % (gt % 2))
        if c == 0:
            go_wk()
            go_wv()
        if c == 3:
            kv_setup()
    S.barrier()
    A.top = mark


class AttnPipe:
    def __init__(self, k, PT):
        self.k = k
        self.PT = PT
        self.n = 0
        self.pending = None
        self.obank_first = {}

    def _qk(self, st):
        k = self.k
        sb = st["sb"]
        lo, hi = st["lo"], st["hi"]
        out = k.bank(sb)[:, lo:hi]
        ex = st["extra"]
        k.mm(out, st["kT"], st["qT"], True, len(ex) == 0, R=st["Rqk"], W=[k.pb[sb]])
        for n, (lhsT, rhs, a, b, R) in enumerate(ex):
            k.mm(k.bank(sb)[:, a:b], lhsT, rhs, False, n == len(ex) - 1, R=R, W=[k.pb[sb]])

    def _rest(self, st):
        k = self.k
        sb = st["sb"]
        if "restfn" in st:
            st["restfn"](sb)
            return
        lo, hi = st["lo"], st["hi"]
        pt = st["pt"]
        k.act(pt.ap[0:st["kp"], lo:hi], k.bank(sb)[0:st["kp"], lo:hi], AF.Exp, R=[k.pb[sb]], W=[pt], scale=st["scale"])
        if st.get("mask") is not None:
            mt, eng = st["mask"]
            k.S.op(eng, lambda e: e.tensor_tensor(out=pt.ap[:, lo:hi], in0=pt.ap[:, lo:hi], in1=mt.ap[:, lo:hi],
                                                  op=ALU.mult), reads=[pt, mt], writes=[pt])
        for (oreg, ob, lhs_lo, lhs_hi, rhs, start, stop, R) in st["pv"]:
            k.mm(oreg, pt.ap[0:st["kp"], lhs_lo:lhs_hi], rhs, start, stop, R=[pt] + R, W=[k.pb[ob]])
        if st.get("evac") is not None:
            st["evac"]()

    def push(self, st):
        st["sb"] = self.n % 4
        st["pt"] = self.PT[self.n % len(self.PT)]
        self.n += 1
        if "qkfn" in st:
            st["qkfn"](st["sb"])
        else:
            self._qk(st)
        if self.pending is None:
            self.pending = []
        self.pending.append(st)
        if len(self.pending) > 3:
            self._rest(self.pending.pop(0))

    def flush(self):
        while self.pending:
            self._rest(self.pending.pop(0))


def attn_phase(k, I, zT, vtok, gts, osc):
    A, S, PB = k.A, k.S, k.pb
    mark = A.top
    def cload(name, shape, dtype, src, q="sp"):
        t = A.alloc(shape, dtype, name)
        k.dma(q, t.ap, src, W=[t], sem="c0")
        return t
    tric = cload("tric", [128], BF16, I["c_tric"])
    tril = cload("tril", [128], BF16, I["c_tril"])
    gates = cload("gates", [NT, 24], F32, gts.rearrange("(n p) j -> p n j", p=128))
    k.act(gates.ap, gates.ap, AF.Exp, R=[gates], W=[gates], scale=-1.0)
    k.v(lambda e: e.tensor_scalar(out=gates.ap, in0=gates.ap, scalar1=1.0, scalar2=None, op0=ALU.add), R=[gates], W=[gates])
    k.v(lambda e: e.reciprocal(out=gates.ap, in_=gates.ap), R=[gates], W=[gates])
    lq1 = k.load_bcast(I["da_lambda_q1"], 64, "lq1", "c1")
    lk1 = k.load_bcast(I["da_lambda_k1"], 64, "lk1", "c1")
    lq2 = k.load_bcast(I["da_lambda_q2"], 64, "lq2", "c1")
    lk2 = k.load_bcast(I["da_lambda_k2"], 64, "lk2", "c1")
    lsum = A.alloc([2], F32, "lsum")
    neglam = A.alloc([1], F32, "neglam")
    k.v(lambda e: e.tensor_tensor(out=lq1.ap, in0=lq1.ap, in1=lk1.ap, op=ALU.mult), R=[lq1, lk1], W=[lq1])
    k.v(lambda e: e.tensor_tensor(out=lq2.ap, in0=lq2.ap, in1=lk2.ap, op=ALU.mult), R=[lq2, lk2], W=[lq2])
    k.v(lambda e: e.reduce_sum(out=lsum.ap[:, 0:1], in_=lq1.ap, axis=AX.X), R=[lq1], W=[lsum])
    k.v(lambda e: e.reduce_sum(out=lsum.ap[:, 1:2], in_=lq2.ap, axis=AX.X), R=[lq2], W=[lsum])
    k.act(lsum.ap, lsum.ap, AF.Exp, R=[lsum], W=[lsum])
    lam_init = 0.8 - 0.6 * math.exp(-0.3 * 0)
    k.v(lambda e: e.tensor_tensor(out=neglam.ap, in0=lsum.ap[:, 1:2], in1=lsum.ap[:, 0:1], op=ALU.subtract),
        R=[lsum], W=[neglam])
    k.v(lambda e: e.tensor_scalar(out=neglam.ap, in0=neglam.ap, scalar1=-lam_init, scalar2=None, op0=ALU.add),
        R=[neglam], W=[neglam])
    gsub = k.load_bcast(I["da_subln_g"], 128, "gsub", "c1")
    k.v(lambda e: e.tensor_scalar(out=gsub.ap, in0=gsub.ap, scalar1=1.0 - lam_init, scalar2=None, op0=ALU.mult),
        R=[gsub], W=[gsub])

    QB = [A.alloc([T], BF16, "QB%d" % i) for i in range(2)]
    KBf = [A.alloc([T], BF16, "KB%d" % i) for i in range(2)]
    PT = [A.alloc([512], BF16, "PT%d" % i) for i in range(4)]
    pipe = AttnPipe(k, PT)
    rl = [A.alloc([4], F32, "rl%d" % i) for i in range(2)]
    rg = [A.alloc([4], F32, "rg%d" % i) for i in range(2)]
    t2 = [A.alloc([4, 128], F32, "t2%d" % i) for i in range(2)]
    ob = [A.alloc([4, 128], BF16, "ob%d" % i) for i in range(2)]
    msd = [A.alloc([4], F32, "msd%d" % i) for i in range(2)]
    osct = osc.rearrange("(n p) d -> p n d", p=128)
    state = {"cj": 0, "ev": 0}

    def oview(pair):
        return k.bank(4 + 2 * pair, 2).rearrange("p (j w) -> p j w", j=4)

    def load_q(buf, zrow, slot):
        k.dma("sp", buf.ap[0:64, :], zT[zrow:zrow + 64, :], W=[buf], sem="q%s" % buf.b.name)
        k.dma("sp", buf.ap[64:68, :], I["c_qaug"][slot], W=[buf], sem="q%s" % buf.b.name)

    def load_k(buf, zrow):
        k.dma("sp", buf.ap[0:64, :], zT[zrow:zrow + 64, :], W=[buf], sem="k%s" % buf.b.name)
        k.dma("sp", buf.ap[64:68, :], I["c_kaug"], W=[buf], sem="k%s" % buf.b.name)

    def load_v(buf, col, dv):
        src = vtok.rearrange("(n p) d -> p n d", p=128)
        for n0 in range(0, NT, 8):
            k.dma("sp", buf.ap[:, n0:n0 + 8, 0:dv], src[:, n0:n0 + 8, col:col + dv], W=[buf], sem="v%s" % buf.b.name)

    def run_job(qb, kT_of, v_of, dv1, tiles_of, extras_of, evac_of, scale=1.0, kp_of=None, hook=None):
        for c in range(NCH):
            pair = state["cj"] % 2
            state["cj"] += 1
            ov = oview(pair)
            tl = tiles_of(c)
            last_for = {}
            for n, (kt, j0, j1) in enumerate(tl):
                for j in range(j0, j1):
                    last_for[j] = n
            started = set()
            for n, (kt, j0, j1) in enumerate(tl):
                lo, hi = j0 * 128, j1 * 128
                kTap, kR = kT_of(kt)
                vap, vR = v_of(kt)
                kp = 128 if kp_of is None else kp_of(kt)
                pv = []
                for j in range(j0, j1):
                    obk = 4 + 2 * pair + j // 2
                    start = obk not in started
                    started.add(obk)
                    pv.append((ov[:, j, 0:dv1], obk, j * 128, (j + 1) * 128, vap, start, last_for[j] == n, vR))
                st = dict(kT=kTap, qT=qb.ap[0:68, c * 512 + lo:c * 512 + hi], lo=lo, hi=hi, Rqk=[qb] + kR,
                          extra=extras_of(c, kt, j0, j1), pv=pv, scale=scale, kp=kp,
                          evac=(evac_of(c, pair) if n == len(tl) - 1 else None))
                pipe.push(st)
            if hook is not None:
                hook(c)

    def causal_tiles(c):
        tl = [(kt, 0, 4) for kt in range(4 * c)]
        tl += [(4 * c + i, i, 4) for i in range(4)]
        return tl

    def causal_extras(c, kt, j0, j1):
        if kt >= 4 * c:
            i = kt - 4 * c
            return [(k.ident.ap, tric.ap, i * 128, (i + 1) * 128, [k.ident, tric])]
        return []

    def rl_of(ov, col, i, clamp=False):
        r = rl[i]
        if clamp:
            k.v(lambda e: e.tensor_scalar(out=r.ap.rearrange("p (a b) -> p a b", b=1), in0=ov[:, :, col:col + 1],
                                          scalar1=1e-30, scalar2=None, op0=ALU.max), R=[], W=[r])
            k.v(lambda e: e.reciprocal(out=r.ap, in_=r.ap), R=[r], W=[r])
        else:
            k.v(lambda e: e.reciprocal(out=r.ap.rearrange("p (a b) -> p a b", b=1), in_=ov[:, :, col:col + 1]),
                R=[], W=[r])
        return r

    cmask = cload("cmask", [2560], BF16, I["c_cmask"], q="act")
    expand = cload("expand", [4096], BF16, I["c_expand"], q="act")
    selA = cload("selA", [NT, 64], F32, I["c_selA"].rearrange("(n p) j -> p n j", p=128), q="act")
    selB = cload("selB", [NT, 64], F32, I["c_selB"].rearrange("(n p) j -> p n j", p=128), q="act")
    tricn = cload("tricn", [128], BF16, I["c_tricn"], q="act")
    w1 = [A.alloc([32, 128], BF16, "w1%d" % i) for i in range(2)]
    w2 = [A.alloc([64], BF16, "w2%d" % i) for i in range(2)]
    peT = [A.alloc([32], F32, "peT%d" % i) for i in range(2)]
    peTb = [A.alloc([32], BF16, "peTb%d" % i) for i in range(2)]
    cb = [A.alloc([1], F32, "cb%d" % i) for i in range(2)]
    for i, nm in enumerate(("k", "v")):
        k.dma("pool", w1[i].ap[0:64], I["cmp_%s_w1" % nm].rearrange("(pos d) h -> d pos h", d=64), W=[w1[i]], sem="w0")
        k.dma("pool", w2[i].ap, I["cmp_%s_w2" % nm], W=[w2[i]], sem="w0")
        pe_src = I["cmp_%s_pe" % nm].rearrange("pos d -> d pos")
        S.op("act", lambda e, i=i, pe_src=pe_src: e.dma_start(out=peT[i].ap[0:64], in_=pe_src,
                                                           allow_slow_non_contiguous=True),
             writes=[peT[i].b], dma="c1")
    def da_evac(h, m):
        def evac_of(c, pair):
            def ev():
                i = state["ev"] % 2
                state["ev"] += 1
                ov = oview(pair)
                OB = [PB[4 + 2 * pair], PB[5 + 2 * pair]]
                r = rl[i]
                k.v(lambda e: e.reciprocal(out=r.ap.rearrange("p (a b) -> p a b", b=1), in_=ov[:, :, 128:129]),
                    R=OB, W=[r])
                rb = r.ap.rearrange("p (a b) -> p a b", b=1).to_broadcast([128, 4, 128])
                dch = datmp.ap[:, 4 * c:4 * c + 4, :]
                if m == 0:
                    k.v(lambda e: e.tensor_tensor(out=dch, in0=ov[:, :, 0:128], in1=rb, op=ALU.mult),
                        R=OB + [r], W=[datmp])
                    return
                tt_ = t2[i]
                k.v(lambda e: e.tensor_tensor(out=tt_.ap, in0=ov[:, :, 0:128], in1=rb, op=ALU.mult),
                    R=OB + [r], W=[tt_])
                k.v(lambda e: e.scalar_tensor_tensor(out=dch, in0=tt_.ap, scalar=neglam.ap, in1=dch,
                                                     op0=ALU.mult, op1=ALU.add), R=[tt_, neglam, datmp], W=[datmp])
                k.v(lambda e: e.tensor_tensor(out=tt_.ap, in0=dch, in1=dch, op=ALU.mult), R=[datmp], W=[tt_])
                k.v(lambda e: e.reduce_sum(out=msall.ap[:, 4 * c:4 * c + 4], in_=tt_.ap, axis=AX.X), R=[tt_], W=[msall])
                if c == NCH - 1:
                    k.v(lambda e: e.tensor_scalar(out=msall.ap, in0=msall.ap, scalar1=1.0 / 128.0, scalar2=EPS,
                                                  op0=ALU.mult, op1=ALU.add), R=[msall], W=[msall])
                    k.act(msall.ap, msall.ap, AF.Ln, R=[msall], W=[msall])
                    k.act(msall.ap, msall.ap, AF.Exp, R=[msall], W=[msall], scale=-0.5)
                    for q4 in range(4):
                        sl = slice(8 * q4, 8 * q4 + 8)
                        mb = msall.ap[:, sl].rearrange("p (a b) -> p a b", b=1).to_broadcast([128, 8, 128])
                        gb = gsub.ap.rearrange("p (a b) -> p a b", a=1).broadcast_to([128, 8, 128])
                        k.v(lambda e, sl=sl, mb=mb: e.tensor_tensor(out=datmp.ap[:, sl, :], in0=datmp.ap[:, sl, :], in1=mb,
                                                                    op=ALU.mult), R=[datmp, msall], W=[datmp])
                        k.v(lambda e, sl=sl, gb=gb: e.tensor_tensor(out=oball.ap[:, sl, :], in0=datmp.ap[:, sl, :], in1=gb,
                                                                    op=ALU.mult), R=[datmp, gsub], W=[oball])
                    k.dma("pool", osct[:, :, h * 128:(h + 1) * 128], oball.ap, R=[oball], sem="oball")
            return ev
        return evac_of

    da_jobs = [(h, m) for h in range(4) for m in range(2)]

    def da_load(n):
        h, m = da_jobs[n]
        load_q(QB[n % 2], h * 128 + m * 64, h)
        load_k(KBf[n % 2], 512 + h * 128 + m * 64)
        if m == 0:
            load_v(VA[h % 2], h * 128, 128)

    if k.stage_attn & 1:
        mda = A.top
        VA = [A.alloc([NT, 129], BF16, "VA%d" % i) for i in range(2)]
        for t in VA:
            k.g(lambda e, t=t: e.memset(t.ap[:, :, 128:129], 1.0), W=[t])
        datmp = A.alloc([NT, 128], F32, "datmp")
        msall = A.alloc([NT], F32, "msall")
        oball = A.alloc([NT, 128], BF16, "oball")
        da_load(0)
        for n, (h, m) in enumerate(da_jobs):
            if n + 1 < len(da_jobs):
                da_load(n + 1)
            qb, kb, vb = QB[n % 2], KBf[n % 2], VA[h % 2]
            run_job(qb,
                    lambda kt, kb=kb: (kb.ap[0:68, kt * 128:(kt + 1) * 128], [kb]),
                    lambda kt, vb=vb: (vb.ap[:, kt, :], [vb]),
                    129, causal_tiles, causal_extras, da_evac(h, m))
        pipe.flush()
        S.barrier()
        A.top = mda

    if k.stage_attn & 2:
        VN = [A.alloc([NT, 65], BF16, "VN%d" % i) for i in range(2)]
        for t in VN:
            k.g(lambda e, t=t: e.memset(t.ap[:, :, 64:65], 1.0), W=[t])
        onsa = [A.alloc([NT, 64], F32, "onsa%d" % i) for i in range(4)]
        imp = A.alloc([NT, 64], F32, "imp")
        QS = [A.alloc([T], BF16, "QS%d" % i) for i in range(4)]
        Mtiles = [A.alloc([512], BF16, "Mt%d" % i) for i in range(3)]
        for i, nm in enumerate(("k", "v")):
            k.v(lambda e, i=i: e.tensor_copy(out=peTb[i].ap[0:64], in_=peT[i].ap[0:64]), R=[peT[i]], W=[peTb[i]])
            for pos in range(32):
                k.mm(k.bank(3)[:, i:i + 1], w1[i].ap[0:64, pos, :], peTb[i].ap[0:64, pos:pos + 1], pos == 0, pos == 31,
                     R=[w1[i], peTb[i]], W=[PB[3]])
            k.v(lambda e, i=i: e.tensor_copy(out=cb[i].ap, in_=k.bank(3)[:, i:i + 1]), R=[PB[3]], W=[cb[i]])
        cin = [QB[0], QB[1]]
        AcT = [A.alloc([256], BF16, "AcT%d" % i) for i in range(2)]
        KcT = A.alloc([256], BF16, "KcT")
        Vc = A.alloc([2, 129], BF16, "Vc")
        negT = A.alloc([T], BF16, "negT")
        score = A.alloc([NT, 64], F32, "score")
        sc2 = A.alloc([64], F32, "sc2")
        m8 = A.alloc([8], F32, "m8")
        thr = A.alloc([NT], F32, "thr")
        negm = A.alloc([NT, 64], BF16, "negm")
        k.g(lambda e: e.memset(Vc.ap, 0.0), W=[Vc])
        k.g(lambda e: e.memset(Vc.ap[:, :, 128:129], 1.0), W=[Vc])
        k.dma("sp", Vc.ap[:, :, 64:128], I["c_mcs"].rearrange("(n p) j -> p n j", p=128), W=[Vc], sem="c1")
        print("NSA arena top", A.top)
        k.g(lambda e: e.memset(KcT.ap, 0.0), W=[KcT])
        k.dma("sp", KcT.ap[64:68, :], I["c_kaugc"], W=[KcT], sem="c1")

        for g in range(2):
            for i, zr in enumerate((2048, 2176)):
                k.dma("sp", cin[i].ap[0:64, :], zT[zr + g * 64:zr + g * 64 + 64, :], W=[cin[i]], sem="cin%d" % i)
                for pos in range(32):
                    k.mm(k.bank(3)[:, 8:8 + 255], w1[i].ap[0:64, pos, :], cin[i].ap[0:64, pos:pos + 16 * 254 + 1:16],
                         pos == 0, pos == 31, R=[w1[i], cin[i]], W=[PB[3]])
                k.g(lambda e, i=i: e.memset(AcT[i].ap, 0.0), W=[AcT[i]])
                k.act(AcT[i].ap[:, 0:255], k.bank(3)[:, 8:8 + 255], AF.Silu, R=[PB[3], cb[i]], W=[AcT[i]], bias=cb[i].ap)
            k.mm(k.bank(3)[0:64, 0:255], w2[0].ap, AcT[0].ap[:, 0:255], True, True, R=[w2[0], AcT[0]], W=[PB[3]])
            k.v(lambda e: e.tensor_copy(out=KcT.ap[0:64, 0:255], in_=k.bank(3)[0:64, 0:255]), R=[PB[3]], W=[KcT])
            for ct in range(2):
                k.mm(k.bank(3)[:, 256 + ct * 64:256 + (ct + 1) * 64], AcT[1].ap[:, ct * 128:(ct + 1) * 128], w2[1].ap,
                     True, True, R=[w2[1], AcT[1]], W=[PB[3]])
            k.v(lambda e: e.tensor_copy(out=Vc.ap[:, :, 0:64],
                                        in_=k.bank(3)[:, 256:384].rearrange("p (a b) -> p a b", a=2)),
                R=[PB[3]], W=[Vc])

            def cmp_tiles(c):
                tl = [(0, 0, 4)]
                if c >= 4:
                    tl.append((1, 0, 4))
                return tl

            def cmp_extras(c, kt, j0, j1):
                if kt == 0 and c >= 5:
                    return []
                off = c * 512 if kt == 0 else (c - 4) * 512
                return [(k.ident.ap, cmask.ap[:, off:off + 512], 0, 512, [k.ident, cmask])]

            def cmp_evac(r, head):
                def evac_of(c, pair):
                    def ev():
                        i = state["ev"] % 2
                        state["ev"] += 1
                        ov = oview(pair)
                        OB = [PB[4 + 2 * pair], PB[5 + 2 * pair]]
                        r_ = rl[i]
                        r3 = r_.ap.rearrange("p (a b) -> p a b", b=1)
                        k.v(lambda e: e.tensor_scalar(out=r3, in0=ov[:, :, 128:129], scalar1=1e-30, scalar2=None,
                                                      op0=ALU.max), R=OB, W=[r_])
                        k.v(lambda e: e.reciprocal(out=r_.ap, in_=r_.ap), R=[r_], W=[r_])
                        g_ = rg[i]
                        k.v(lambda e: e.tensor_tensor(out=g_.ap.rearrange("p (a b) -> p a b", b=1), in0=r3,
                                                      in1=gates.ap[:, 4 * c:4 * c + 4, head * 3:head * 3 + 1], op=ALU.mult),
                            R=[r_, gates], W=[g_])
                        gb = g_.ap.rearrange("p (a b) -> p a b", b=1).to_broadcast([128, 4, 64])
                        k.v(lambda e: e.tensor_tensor(out=onsa[r].ap[:, 4 * c:4 * c + 4, :], in0=ov[:, :, 0:64], in1=gb,
                                                      op=ALU.mult), R=OB + [g_], W=[onsa[r]])
                        rb = r3.to_broadcast([128, 4, 64])
                        ich = imp.ap[:, 4 * c:4 * c + 4, :]
                        if r == 0:
                            k.v(lambda e: e.tensor_tensor(out=ich, in0=ov[:, :, 64:128], in1=rb, op=ALU.mult),
                                R=OB + [r_], W=[imp])
                        else:
                            tt_ = t2[i]
                            k.v(lambda e: e.tensor_tensor(out=tt_.ap[:, :, 0:64], in0=ov[:, :, 64:128], in1=rb,
                                                          op=ALU.mult), R=OB + [r_], W=[tt_])
                            k.g(lambda e: e.tensor_tensor(out=ich, in0=ich, in1=tt_.ap[:, :, 0:64], op=ALU.add),
                                R=[tt_, imp], W=[imp])
                    return ev
                return evac_of

            def acc_evac(r, head, branch, final):
                def evac_of(c, pair, ov=None, OB=None):
                    def ev(ov=ov, OB=OB):
                        i = state["ev"] % 2
                        state["ev"] += 1
                        if ov is None:
                            ov = oview(pair)
                            OB = [PB[4 + 2 * pair], PB[5 + 2 * pair]]
                        r_ = rl[i]
                        r3 = r_.ap.rearrange("p (a b) -> p a b", b=1)
                        k.v(lambda e: e.reciprocal(out=r3, in_=ov[:, :, 64:65]), R=OB, W=[r_])
                        g_ = rg[i]
                        k.v(lambda e: e.tensor_tensor(out=g_.ap.rearrange("p (a b) -> p a b", b=1), in0=r3,
                                                      in1=gates.ap[:, 4 * c:4 * c + 4, head * 3 + branch:head * 3 + branch + 1],
                                                      op=ALU.mult), R=[r_, gates], W=[g_])
                        gb = g_.ap.rearrange("p (a b) -> p a b", b=1).to_broadcast([128, 4, 64])
                        tt_ = t2[i]
                        k.v(lambda e: e.tensor_tensor(out=tt_.ap[:, :, 0:64], in0=ov[:, :, 0:64], in1=gb, op=ALU.mult),
                            R=OB + [g_], W=[tt_])
                        och = onsa[r].ap[:, 4 * c:4 * c + 4, :]
                        if not final:
                            k.g(lambda e: e.tensor_tensor(out=och, in0=och, in1=tt_.ap[:, :, 0:64], op=ALU.add),
                                R=[tt_, onsa[r]], W=[onsa[r]])
                        else:
                            o_ = ob[i]
                            k.v(lambda e: e.tensor_tensor(out=o_.ap[:, :, 0:64], in0=och, in1=tt_.ap[:, :, 0:64], op=ALU.add),
                                R=[tt_, onsa[r]], W=[o_])
                            k.dma("pool", osct[:, 4 * c:4 * c + 4, 512 + head * 64:512 + (head + 1) * 64], o_.ap[:, :, 0:64],
                                  R=[o_], sem="ob%d" % i)
                    return ev
                return evac_of

            for r in range(4):
                load_q(QS[r], 1536 + (g * 4 + r) * 64, 4 + g * 4 + r)
            kvs = KBf[0], VN[0]
            kvw = KBf[1], VN[1]
            load_k(kvs[0], 2304 + g * 64)
            load_v(kvs[1], 512 + g * 64, 64)
            load_k(kvw[0], 2560 + g * 64)
            load_v(kvw[1], 640 + g * 64, 64)

            for r in range(4):
                run_job(QS[r],
                        lambda kt: (KcT.ap[0:68, kt * 128:(kt + 1) * 128], [KcT]),
                        lambda kt: (Vc.ap[:, kt, :], [Vc]),
                        129, cmp_tiles, cmp_extras, cmp_evac(r, g * 4 + r))
            pipe.flush()

            k.v(lambda e: e.tensor_tensor(out=score.ap, in0=imp.ap, in1=selA.ap, op=ALU.mult), R=[imp, selA], W=[score])
            k.v(lambda e: e.tensor_tensor(out=score.ap, in0=score.ap, in1=selB.ap, op=ALU.add), R=[score, selB], W=[score])

            def sel_piece(n):
                k.v(lambda e: e.max(out=m8.ap, in_=score.ap[:, n, :]), R=[score], W=[m8])
                k.v(lambda e: e.match_replace(out=sc2.ap, in_to_replace=m8.ap, in_values=score.ap[:, n, :],
                                              imm_value=-3.0), R=[score, m8], W=[sc2])
                k.v(lambda e: e.max(out=m8.ap, in_=sc2.ap), R=[sc2], W=[m8])
                k.v(lambda e: e.tensor_copy(out=thr.ap[:, n:n + 1], in_=m8.ap[:, 7:8]), R=[m8], W=[thr])

            def sel_extras(c, kt, j0, j1):
                ex = [(expand.ap[0:64, kt * 128:(kt + 1) * 128], negT.ap[0:64, c * 512 + j0 * 128:c * 512 + j1 * 128],
                       j0 * 128, j1 * 128, [expand, negT])]
                return ex + causal_extras(c, kt, j0, j1)

            def win_tiles(c):
                tl = []
                for kt in range(max(0, 4 * c - 4), 4 * c + 4):
                    j0 = max(0, kt - 4 * c)
                    j1 = min(3, kt - 4 * c + 4) + 1
                    tl.append((kt, j0, j1))
                return tl

            def win_extras(c, kt, j0, j1):
                ex = []
                if kt >= 4 * c:
                    i = kt - 4 * c
                    ex.append((k.ident.ap, tric.ap, i * 128, (i + 1) * 128, [k.ident, tric]))
                if kt < 4 * c:
                    i = kt - 4 * c + 4
                    ex.append((k.ident.ap, tril.ap, i * 128, (i + 1) * 128, [k.ident, tril]))
                return ex

            seln = {"n": 0}

            def win_hook(c):
                sel_piece(seln["n"])
                seln["n"] += 1

            for r in range(4):
                kb, vb = kvw
                run_job(QS[r],
                        lambda kt, kb=kb: (kb.ap[0:68, kt * 128:(kt + 1) * 128], [kb]),
                        lambda kt, vb=vb: (vb.ap[:, kt, :], [vb]),
                        65, win_tiles, win_extras, acc_evac(r, g * 4 + r, 2, False), hook=win_hook)
            pipe.flush()
            tb = thr.ap.rearrange("p (a b) -> p a b", b=1).to_broadcast([128, NT, 64])
            k.v(lambda e: e.tensor_tensor(out=score.ap, in0=score.ap, in1=tb, op=ALU.is_ge), R=[score, thr], W=[score])
            k.v(lambda e: e.tensor_tensor(out=score.ap, in0=score.ap, in1=selA.ap, op=ALU.mult), R=[score, selA], W=[score])
            k.v(lambda e: e.tensor_copy(out=negm.ap, in_=score.ap), R=[score], W=[negm])
            for n0 in range(0, NT, 8):
                pT = k.bank_bf(3)
                for n in range(n0, n0 + 8):
                    k.tr(pT[0:64, (n - n0) * 128:(n - n0 + 1) * 128], negm.ap[:, n, :], R=[negm], W=[PB[3]])
                k.v(lambda e, n0=n0, pT=pT: e.tensor_copy(out=negT.ap[0:64, n0 * 128:(n0 + 8) * 128], in_=pT[0:64, :]),
                    R=[PB[3]], W=[negT])
            kb, vb = kvs
            mi = 0
            for c in range(NCH):
                tl = causal_tiles(c)
                started = set()
                for n, (kt, j0, j1) in enumerate(tl):
                    lo, hi = j0 * 128, j1 * 128
                    diag = kt >= 4 * c
                    Mt = Mtiles[mi % 3]
                    mi += 1
                    def mqk(sb, lo=lo, hi=hi, kt=kt, c=c, diag=diag):
                        k.mm(k.bank(sb)[:, lo:hi], expand.ap[0:64, kt * 128:(kt + 1) * 128],
                             negT.ap[0:64, c * 512 + lo:c * 512 + hi], True, not diag, R=[expand, negT], W=[PB[sb]])
                        if diag:
                            i = kt - 4 * c
                            k.mm(k.bank(sb)[:, i * 128:(i + 1) * 128], k.ident.ap, tricn.ap, False, True,
                                 R=[k.ident, tricn], W=[PB[sb]])

                    def mrest(sb, lo=lo, hi=hi, Mt=Mt):
                        k.v(lambda e: e.tensor_scalar(out=Mt.ap[:, lo:hi], in0=k.bank(sb)[:, lo:hi], scalar1=0.0,
                                                      scalar2=None, op0=ALU.max), R=[PB[sb]], W=[Mt])

                    pipe.push(dict(qkfn=mqk, restfn=mrest))
                    for r in range(4):
                        ovr = k.bank(4 + r)[:, 0:260].rearrange("p (j w) -> p j w", j=4)
                        pv = []
                        for j in range(j0, j1):
                            start = (4 + r) not in started
                            started.add(4 + r)
                            pv.append((ovr[:, j, 0:65], 4 + r, j * 128, (j + 1) * 128, vb.ap[:, kt, :], start,
                                       kt == 4 * c + j, [vb]))
                        last = n == len(tl) - 1
                        st = dict(kT=kb.ap[0:68, kt * 128:(kt + 1) * 128],
                                  qT=QS[r].ap[0:68, c * 512 + lo:c * 512 + hi], lo=lo, hi=hi, Rqk=[QS[r], kb],
                                  extra=[], pv=pv, scale=1.0, kp=128, mask=(Mt, "dve"),
                                  evac=(acc_evac(r, g * 4 + r, 1, True)(c, None, ovr, [PB[4 + r]]) if last else None))
                        pipe.push(st)
            pipe.flush()
    S.barrier()
    A.top = mark


def outxa_phase(k, I, x1, osc, x3, kv):
    A, S, PB = k.A, k.S, k.pb
    mark = A.top
    wout = k.load_w(I["w_mix_out"], D, D, "wout", "w0")
    wq = k.load_w(I["xa_w_q"], D, D, "wq", "w1")
    wo = k.load_w(I["xa_w_o"], D, D, "wo", "w2")
    gmixpost = k.load_bcast(I["mix_post_g"], D, "gmp", "c0")
    gxapre = k.load_bcast(I["xa_pre_g"], D, "gxp", "c0")
    gxapost = k.load_bcast(I["xa_post_g"], D, "gxo", "c0")
    KxT, Vx = kv
    ones = A.alloc([128], BF16, "ones")
    k.g(lambda e: e.memset(ones.ap, 1.0), W=[ones])
    hn = [A.alloc([D], BF16, "hn%d" % i) for i in range(2)]
    junk = A.alloc([D], BF16, "junk")
    TB = 2

    tfc = {"n": 0}

    def to_fm(src_ap, src_R, dst, col0, nhn, Wb=None):
        Wb = [dst] if Wb is None else Wb
        tb = (TB, 7)[tfc["n"] % 2]
        tfc["n"] += 1
        pT = k.bank_bf(tb)
        for dc in range(8):
            k.tr(pT[:, dc * 128:(dc + 1) * 128], src_ap[:, dc * 128:(dc + 1) * 128], R=src_R, W=[PB[tb]])
        k.act(dst.ap[:, :, col0:col0 + 128], pT.rearrange("p (a b) -> p a b", a=8), AF.Copy, R=[PB[tb]], W=Wb)

    ot = [A.alloc([D], BF16, "ot%d" % i) for i in range(4)]
    oT = A.alloc([8, 512], BF16, "oT")
    x2c = [[A.alloc([D], F32, "x2c%d_%d" % (i, j)) for j in range(4)] for i in range(3)]
    yb = [A.alloc([D], F32, "yb%d" % i) for i in range(4)]
    yb2 = [A.alloc([D], F32, "yb2%d" % i) for i in range(4)]
    msA = [A.alloc([4], F32, "msA%d" % i) for i in range(2)]
    msB = [A.alloc([4], F32, "msB%d" % i) for i in range(2)]
    msC = [A.alloc([4], F32, "msC%d" % i) for i in range(2)]
    h3T = A.alloc([8, 512], BF16, "h3T")
    qxT = [A.alloc([8, 512], BF16, "qxT%d" % i) for i in range(2)]
    PT = [A.alloc([512], BF16, "PTx%d" % i) for i in range(4)]
    oxT = oT
    rLb = [A.alloc([512], F32, "rLb%d" % i) for i in range(2)]
    osct = osc.rearrange("(n p) d -> n p d", p=128)
    x1tt = x1.rearrange("(n p) d -> n p d", p=128)
    x3t = x3.rearrange("(n p) d -> n p d", p=128)
    SB = [3, 4, 5]
    print("outxa arena top", A.top)
    cnt = {"s": 0, "e": 0}

    oTb = [Buf("oTb%d" % i) for i in range(4)]

    def proj_tile(srcT, tt, w, dst):
        for half in range(2):
            for fc in range(8):
                k.mm(k.bank(half), srcT.ap[:, fc, tt * 128:(tt + 1) * 128], w.ap[:, fc, half * 512:(half + 1) * 512],
                     fc == 0, fc == 7, R=[oTb[tt], w], W=[PB[half]])
            k.v(lambda e, half=half: e.tensor_copy(out=dst.ap[:, half * 512:(half + 1) * 512], in_=k.bank(half)),
                R=[PB[half]], W=[dst])

    def front1(c):
        X = x2c[c % 3]
        for tt in range(4):
            k.dma("sp", ot[tt].ap, osct[4 * c + tt], W=[ot[tt]], sem="ot%d" % tt)
        for tt in range(4):
            k.dma("sp", X[tt].ap, x1tt[4 * c + tt], W=[X[tt]], sem="x2c%d_%d" % (c % 3, tt))
        tf = lambda tt: to_fm(ot[tt].ap, [ot[tt]], oT, tt * 128, tt % 2, Wb=[oTb[tt]])
        pj = lambda tt: proj_tile(oT, tt, wout, yb[tt])
        tf(0); tf(1); pj(0); tf(2); pj(1); tf(3); pj(2); pj(3)

    def front2a(c):
        X = x2c[c % 3]
        mA, mB = msA[c % 2], msB[c % 2]
        for tt in range(4):
            k.act(junk.ap, yb[tt].ap, AF.Square, R=[yb[tt]], W=[junk, mA], scale=1.0 / 32.0, accum_out=mA.ap[:, tt:tt + 1])
        k.rstd_chain(mA.ap, mA)
        for tt in range(4):
            k.v(lambda e, tt=tt: e.scalar_tensor_tensor(out=yb[tt].ap, in0=yb[tt].ap, scalar=mA.ap[:, tt:tt + 1],
                                                        in1=gmixpost.ap, op0=ALU.mult, op1=ALU.mult),
                R=[yb[tt], mA, gmixpost], W=[yb[tt]])
            k.g(lambda e, tt=tt: e.tensor_tensor(out=X[tt].ap, in0=X[tt].ap, in1=yb[tt].ap, op=ALU.add),
                R=[yb[tt], X[tt]], W=[X[tt]])

    def front2b(c):
        X = x2c[c % 3]
        mA, mB = msA[c % 2], msB[c % 2]
        for tt in range(4):
            k.act(junk.ap, X[tt].ap, AF.Square, R=[X[tt]], W=[junk, mB], scale=1.0 / 32.0, accum_out=mB.ap[:, tt:tt + 1])
        k.rstd_chain(mB.ap, mB)

    def mixed(c, cb):
        X = x2c[c % 3]
        mB = msB[c % 2]

        def stt(tt):
            hh = hn[tt % 2]
            k.v(lambda e: e.scalar_tensor_tensor(out=hh.ap, in0=X[tt].ap, scalar=mB.ap[:, tt:tt + 1],
                                                 in1=gxapre.ap, op0=ALU.mult, op1=ALU.mult),
                R=[X[tt], mB, gxapre], W=[hh])

        tf = lambda tt: to_fm(hn[tt % 2].ap, [hn[tt % 2]], h3T, tt * 128, tt % 2)
        pj = lambda tt: proj_tile(oxT, tt, wo, yb2[tt])
        stt(0); stt(1); tf(0); pj(0); stt(2); tf(1); pj(1); stt(3); tf(2); pj(2); tf(3); pj(3)

    def front2c(c):
        q_ = qxT[c % 2]
        for ft in range(8):
            bk = SB[ft % 3]
            for dc in range(8):
                k.mm(k.bank(bk), wq.ap[:, dc, ft * 128:(ft + 1) * 128], h3T.ap[:, dc, :], dc == 0, dc == 7,
                     R=[wq, h3T], W=[PB[bk]])
            if ft % 2 == 0:
                k.act(q_.ap[:, ft, :], k.bank(bk), AF.Copy, R=[PB[bk]], W=[q_])
            else:
                k.v(lambda e, ft=ft, bk=bk: e.tensor_copy(out=q_.ap[:, ft, :], in_=k.bank(bk)), R=[PB[bk]], W=[q_])

    def back1(c, heads):
        q_ = qxT[c % 2]

        def qk(hh):
            for mt in range(2):
                bk = (3, 4)[mt]
                for j in range(2):
                    k.mm(k.bank(bk), KxT.ap[:, hh * 2 + j, mt * 128:(mt + 1) * 128], q_.ap[:, hh * 2 + j, :],
                         j == 0, j == 1, R=[KxT, q_], W=[PB[bk]])
                pt = PT[(2 * hh + mt) % 4]
                k.act(pt.ap, k.bank(bk), AF.Exp, R=[PB[bk]], W=[pt], scale=1.0 / 16.0)

        def lo(hh):
            pts = [PT[(2 * hh + mt) % 4] for mt in range(2)]
            for mt in range(2):
                k.mm(k.bank(7), ones.ap, pts[mt].ap, mt == 0, mt == 1, R=[ones, pts[mt]], W=[PB[7]])
            for dvc in range(2):
                for mt in range(2):
                    k.mm(k.bank(5 + dvc), Vx.ap[:, mt, hh * 256 + dvc * 128:hh * 256 + (dvc + 1) * 128], pts[mt].ap,
                         mt == 0, mt == 1, R=[Vx, pts[mt]], W=[PB[5 + dvc]])
            rL = rLb[hh % 2]
            k.act(rL.ap, k.bank(7), AF.Ln, R=[PB[7]], W=[rL])
            k.act(rL.ap, rL.ap, AF.Exp, R=[rL], W=[rL], scale=-1.0)
            for dvc in range(2):
                k.v(lambda e, dvc=dvc: e.tensor_tensor(out=oxT.ap[:, hh * 2 + dvc, :], in0=k.bank(5 + dvc), in1=rL.ap,
                                                       op=ALU.mult), R=[PB[5 + dvc], rL], W=oTb)

        qk(heads[0])
        for n, hh in enumerate(heads):
            if n + 1 < len(heads):
                qk(heads[n + 1])
            lo(hh)

    def back2(c):
        X = x2c[c % 3]
        mC = msC[c % 2]
        for tt in range(4):
            proj_tile(oxT, tt, wo, yb2[tt])

    def back2b(c):
        X = x2c[c % 3]
        mC = msC[c % 2]
        for tt in range(4):
            k.act(junk.ap, yb2[tt].ap, AF.Square, R=[yb2[tt]], W=[junk, mC], scale=1.0 / 32.0, accum_out=mC.ap[:, tt:tt + 1])
        k.rstd_chain(mC.ap, mC)
        for tt in range(4):
            gt = 4 * c + tt
            k.v(lambda e, tt=tt: e.scalar_tensor_tensor(out=yb2[tt].ap, in0=yb2[tt].ap, scalar=mC.ap[:, tt:tt + 1],
                                                        in1=gxapost.ap, op0=ALU.mult, op1=ALU.mult),
                R=[yb2[tt], mC, gxapost], W=[yb2[tt]])
            k.g(lambda e, tt=tt: e.tensor_tensor(out=yb2[tt].ap, in0=yb2[tt].ap, in1=X[tt].ap, op=ALU.add),
                R=[yb2[tt], X[tt]], W=[yb2[tt]])
            k.dma("pool", x3t[gt], yb2[tt].ap, R=[yb2[tt]], sem="x3o%d" % tt)

    front1(0)
    front2a(0)
    front2b(0)
    for tt in range(4):
        k.v(lambda e, tt=tt: e.scalar_tensor_tensor(out=hn[tt % 2].ap, in0=x2c[0][tt].ap, scalar=msB[0].ap[:, tt:tt + 1],
                                                    in1=gxapre.ap, op0=ALU.mult, op1=ALU.mult),
            R=[x2c[0][tt], msB[0], gxapre], W=[hn[tt % 2]])
        to_fm(hn[tt % 2].ap, [hn[tt % 2]], h3T, tt * 128, tt % 2)
    front2c(0)
    for c in range(NCH):
        nxt = c + 1 < NCH
        if nxt:
            front1(c + 1)
            front2a(c + 1)
        back1(c, (0, 1, 2, 3))
        if nxt:
            front2b(c + 1)
            mixed(c + 1, c)
            front2c(c + 1)
        else:
            back2(c)
        back2b(c)
    S.barrier()
    A.top = mark


def build(stage=99, debug=False, stage_attn=3):
    nc = bass.Bass("TRN2", target_bir_lowering=False)
    dt = lambda name, shape, dtype, kind: nc.dram_tensor(name, list(shape), dtype, kind=kind).ap()
    I = {}
    for name, shape in IN_SHAPES.items():
        I[name] = dt(name, shape, F32, "ExternalInput")
    for name, (shape, dtype) in CONST_SHAPES.items():
        I[name] = dt(name, shape, dtype, "ExternalInput")
    skind = "ExternalOutput" if debug else "Internal"
    x1 = dt("x1", [T, D], F32, skind)
    zT = dt("zT", [MIXIN, T], BF16, skind)
    vtok = dt("vtok", [T, 768], BF16, skind)
    gts = dt("gts", [T, 24], F32, skind)
    osc = dt("osc", [T, D], BF16, skind)
    x3 = dt("x3", [T, D], F32, skind)
    out = dt("out", [T, D], F32, "ExternalOutput")
    with ExitStack() as st:
        k = KB(nc, st, debug)
        k.stage_attn = stage_attn
        k.ident = k.A.alloc([128], BF16, "ident")
        k.dma("sp", k.ident.ap, I["c_ident"], W=[k.ident], sem="c9")
        ffn_phase(k, I["x"], x1, I["ffn1_w_gate"], I["ffn1_w_up"], I["ffn1_w_down"],
                  I["ffn1_pre_g"], I["ffn1_post_g"], "f1")
        base = k.A.top
        kv = (k.A.alloc([8, 256], BF16, "KxT"), k.A.alloc([2, D], BF16, "Vx"))
        if stage >= 2:
            inproj_phase(k, x1, I, zT, vtok, gts, kv)
        k.rstd_mode = "ln"
        if stage >= 3:
            attn_phase(k, I, zT, vtok, gts, osc)
        if stage >= 4:
            outxa_phase(k, I, x1, osc, x3, kv)
        k.rstd_mode = "sqrt"
        k.A.top = base
        if stage >= 5:
            ffn_phase(k, x3, out, I["ffn2_w_gate"], I["ffn2_w_up"], I["ffn2_w_down"],
                      I["ffn2_pre_g"], I["ffn2_post_g"], "f2")
        k.S.barrier()
        k.S.emit()
        print("ops", k.S.nops, "sems", k.S.nsem, "arena peak", k.A.peak)
    return nc


IN_SHAPES = {
    "x": (T, D), "mem": (256, D),
    "ffn1_pre_g": (1, D), "ffn1_post_g": (1, D),
    "ffn1_w_gate": (D, DFF), "ffn1_w_up": (D, DFF), "ffn1_w_down": (DFF, D),
    "mix_pre_g": (1, D), "mix_post_g": (1, D), "w_mix_in": (D, MIXIN),
    "da_lambda_q1": (1, 64), "da_lambda_k1": (1, 64), "da_lambda_q2": (1, 64), "da_lambda_k2": (1, 64),
    "da_subln_g": (1, 128),
    "cmp_k_pe": (32, 64), "cmp_k_w1": (2048, 128), "cmp_k_w2": (128, 64),
    "cmp_v_pe": (32, 64), "cmp_v_w1": (2048, 128), "cmp_v_w2": (128, 64),
    "w_mix_out": (D, D),
    "xa_pre_g": (1, D), "xa_post_g": (1, D), "mem_norm_g": (1, D),
    "xa_w_q": (D, D), "xa_w_k": (D, D), "xa_w_v": (D, D), "xa_w_o": (D, D),
    "ffn2_pre_g": (1, D), "ffn2_post_g": (1, D),
    "ffn2_w_gate": (D, DFF), "ffn2_w_up": (D, DFF), "ffn2_w_down": (DFF, D),
}
PER_CORE = ("x", "mem")

CONST_SHAPES = {
    "c_ident": ((128, 128), BF16),
    "c_tric": ((128, 128), BF16),
    "c_tril": ((128, 128), BF16),
    "c_tricn": ((128, 128), BF16),
    "c_cmask": ((128, 2560), BF16),
    "c_expand": ((128, 4096), BF16),
    "c_selA": ((T, 64), F32),
    "c_selB": ((T, 64), F32),
    "c_mcs": ((256, 64), BF16),
    "c_qaug": ((12, 4, T), BF16),
    "c_kaug": ((4, T), BF16),
    "c_kaugc": ((4, 256), BF16),
}


def make_consts():
    bf = ml_dtypes.bfloat16
    c = {}
    c["c_ident"] = np.eye(128, dtype=np.float32).astype(bf)
    kk = np.arange(128)[:, None]
    qq = np.arange(128)[None, :]
    c["c_tric"] = np.where(kk <= qq, 0.0, NEGBIG).astype(np.float32).astype(bf)
    c["c_tril"] = np.where(kk > qq, 0.0, NEGBIG).astype(np.float32).astype(bf)
    c["c_tricn"] = np.where(kk <= qq, 0.0, -1.0).astype(np.float32).astype(bf)
    tt = np.arange(2560)[None, :]
    c["c_cmask"] = np.where(tt - 16 * kk >= 31, 0.0, NEGBIG).astype(np.float32).astype(bf)
    ex = np.zeros((128, T), np.float32)
    ex[:64] = (np.arange(T)[None, :] // 64 == np.arange(64)[:, None])
    c["c_expand"] = ex.astype(bf)
    t = np.arange(T)
    cur = (t // 64)[:, None]
    blk = np.arange(64)[None, :]
    Am = (blk <= cur).astype(np.float32)
    forced = ((blk == 0) | (blk == cur) | (blk == cur - 1)).astype(np.float32)
    c["c_selA"] = Am
    c["c_selB"] = (1e4 * forced - (1.0 - Am)).astype(np.float32)
    cs = np.arange(255) * 16
    ss = np.arange(64) * 64
    ov = np.clip(np.minimum(cs[:, None] + 32, ss[None, :] + 64) - np.maximum(cs[:, None], ss[None, :]), 0, None)
    mcs = np.zeros((256, 64), np.float32)
    mcs[:255] = ov / 32.0
    c["c_mcs"] = mcs.astype(bf)
    a = (t // 64).astype(np.float32)
    b = (t % 64).astype(np.float32)
    slopes = list(2.0 ** (-8.0 * np.arange(1, 5) / 4)) + list(2.0 ** (-8.0 * np.arange(1, 9) / 8))
    qa = np.zeros((12, 4, T), np.float32)
    for s, sl in enumerate(slopes):
        qa[s, 0] = -sl * 64 * a
        qa[s, 1] = -sl * b
        qa[s, 2] = sl
        qa[s, 3] = sl
    c["c_qaug"] = qa.astype(bf)
    ka = np.stack([np.ones(T), np.ones(T), 64 * a, b]).astype(np.float32)
    c["c_kaug"] = ka.astype(bf)
    pc = np.arange(256) * 16 + 31
    kc = np.stack([np.ones(256), np.ones(256), 64.0 * (pc // 64), 1.0 * (pc % 64)]).astype(np.float32)
    kc[:, 255] = 0
    c["c_kaugc"] = kc.astype(bf)
    return c


_CACHE = {}


def kernel(**inputs):
    n = 8
    if "nc" not in _CACHE:
        _CACHE["nc"] = build()
    nc = _CACHE["nc"]
    consts = make_consts()
    in_maps = []
    for i in range(n):
        m = {}
        for name, shape in IN_SHAPES.items():
            a = np.asarray(inputs[name], dtype=np.float32)
            a = a[i] if name in PER_CORE else a[0]
            m[name] = np.ascontiguousarray(a.reshape(shape))
        m.update(consts)
        in_maps.append(m)
    res = run_bass_kernel_spmd(nc, in_maps, core_ids=list(range(n)))
    return np.stack([r["out"] for r in res.results], axis=0).astype(np.float32)
```

```python
import math
from contextlib import ExitStack

import numpy as np
import ml_dtypes

import concourse.bass as bass
import concourse.mybir as mybir
from concourse.bass_utils import run_bass_kernel_spmd

F32 = mybir.dt.float32
BF16 = mybir.dt.bfloat16
AF = mybir.ActivationFunctionType
ALU = mybir.AluOpType
AX = mybir.AxisListType

SEM_LIMIT = 30000
DT_SIZE = {F32: 4, BF16: 2}

T = 4096
D = 1024
DFF = 2816
NT = T // 128
NCH = T // 512
MIXIN = 2840
NEGBIG = -30000.0
EPS = 1e-6


class Buf:
    __slots__ = ("name", "w", "r")

    def __init__(self, name=""):
        self.name = name
        self.w = None
        self.r = {}


class Tile:
    __slots__ = ("ap", "b")

    def __init__(self, ap, name=""):
        self.ap = ap
        self.b = Buf(name)

    def __getitem__(self, k):
        return self.ap[k]


class Sched:
    ENG = ("pe", "act", "dve", "pool", "sp")

    def __init__(self, nc, stack):
        self.nc = nc
        self.stack = stack
        self.q = {e: [] for e in self.ENG}
        self.cnt = {}
        self.sems = {}
        self.seen = {e: {} for e in self.ENG}
        self.nsem = 0
        self.nops = 0

    def _sem(self, key):
        if key not in self.sems:
            self.sems[key] = self.stack.enter_context(self.nc.semaphore("s%d" % self.nsem))
            self.nsem += 1
        return self.sems[key]

    def _bump(self, base, inc):
        ep, v = self.cnt.get(base, (0, 0))
        if v + inc > SEM_LIMIT:
            ep, v = ep + 1, 0
        v += inc
        self.cnt[base] = (ep, v)
        key = (base, ep)
        self._sem(key)
        return key, v

    def op(self, eng, fn, reads=(), writes=(), dma=None, skip_same=False):
        reads = [t.b if isinstance(t, Tile) else t for t in reads]
        writes = [t.b if isinstance(t, Tile) else t for t in writes]
        deps = []
        for b in reads:
            if b.w is not None:
                deps.append(b.w)
        for b in writes:
            if b.w is not None:
                deps.append(b.w)
            deps.extend(b.r.items())
        if dma is not None:
            dma = (dma, eng)
            ep, v = self.cnt.get(dma, (0, 0))
            if v > 0:
                deps.append(((dma, ep), v))
        waits = {}
        seen = self.seen[eng]
        for key, v in deps:
            if skip_same and key[0] == eng:
                continue
            if seen.get(key, 0) >= v:
                continue
            if waits.get(key, 0) < v:
                waits[key] = v
        for key, v in waits.items():
            seen[key] = v
        if dma is None:
            key, v = self._bump(eng, 1)
            inc = 1
        else:
            key, v = self._bump(dma, 16)
            inc = 16
        self.q[eng].append((list(waits.items()), fn, key, inc))
        self.nops += 1
        ev = (key, v)
        for b in reads:
            if b.r.get(key, 0) < v:
                b.r[key] = v
        for b in writes:
            b.w = ev
            b.r = {}
        return ev

    def barrier(self):
        allv = [((base, ep), v) for base, (ep, v) in self.cnt.items() if v > 0]
        for e in self.ENG:
            waits = []
            for key, v in allv:
                if self.seen[e].get(key, 0) < v:
                    waits.append((key, v))
                    self.seen[e][key] = v
            if waits:
                self.q[e].append((waits, None, None, 0))

    def emit(self):
        nc = self.nc
        sems = self.sems
        q = self.q

        def replay(name, e):
            for waits, fn, key, inc in q[name]:
                for k, v in waits:
                    e.wait_ge(sems[k], v)
                if fn is not None:
                    fn(e).then_inc(sems[key], inc)

        with nc.Block() as block:
            @block.tensor
            def _(e):
                replay("pe", e)

            @block.scalar
            def _(e):
                replay("act", e)

            @block.vector
            def _(e):
                replay("dve", e)

            @block.gpsimd
            def _(e):
                replay("pool", e)

            @block.sync
            def _(e):
                replay("sp", e)


class Arena:
    def __init__(self, nc, stack, nbytes):
        self.t = stack.enter_context(nc.sbuf_tensor("arena", [128, nbytes // 4], F32))
        self.top = 0
        self.cap = nbytes
        self.peak = 0

    def alloc(self, shape, dtype, name=""):
        n = int(np.prod(shape)) * DT_SIZE[dtype]
        n4 = (n + 3) // 4
        off = self.top // 4
        self.top += n4 * 4
        self.peak = max(self.peak, self.top)
        assert self.top <= self.cap, ("SBUF arena overflow", name, self.top, self.cap)
        ap = self.t[:, off:off + n4]
        if dtype != F32:
            ap = ap.bitcast(dtype)
            ap = ap[:, 0:int(np.prod(shape))]
        if len(shape) == 2:
            ap = ap.rearrange("p (a b) -> p a b", a=shape[0])
        elif len(shape) == 3:
            ap = ap.rearrange("p (a b c) -> p a b c", a=shape[0], b=shape[1])
        return Tile(ap, name)


class KB:
    def __init__(self, nc, st, debug):
        self.nc = nc
        self.S = Sched(nc, st)
        self.A = Arena(nc, st, 212000)
        self.psum = st.enter_context(nc.psum_tensor("psum", [128, 4096], F32))
        self.pb = [Buf("bank%d" % i) for i in range(8)]
        self.debug = debug

    def bank(self, i, n=1):
        return self.psum[:, i * 512:(i + n) * 512]

    def bank_bf(self, i):
        return self.psum[:, i * 512:(i + 1) * 512].bitcast(BF16)

    def dma(self, q, out, in_, R=(), W=(), sem=None):
        return self.S.op(q, lambda e: e.dma_start(out=out, in_=in_), reads=R, writes=W, dma=sem)

    def mm(self, out, lhsT, rhs, start, stop, R=(), W=()):
        return self.S.op("pe", lambda e: e.matmul(out, lhsT=lhsT, rhs=rhs, start=start, stop=stop,
                                                  skip_group_check=True),
                         reads=R, writes=W, skip_same=True)

    def tr(self, out, in_, R=(), W=()):
        ident = self.ident
        return self.S.op("pe", lambda e: e.transpose(out=out, in_=in_, identity=ident.ap),
                         reads=list(R) + [ident], writes=W, skip_same=True)

    def act(self, out, in_, func, R=(), W=(), **kw):
        return self.S.op("act", lambda e: e.activation(out=out, in_=in_, func=func, **kw), reads=R, writes=W)

    def v(self, fn, R=(), W=()):
        return self.S.op("dve", fn, reads=R, writes=W)

    def g(self, fn, R=(), W=()):
        return self.S.op("pool", fn, reads=R, writes=W)

    rstd_mode = "sqrt"

    def rstd_chain(self, ms_ap, t):
        self.v(lambda e: e.tensor_scalar(out=ms_ap, in0=ms_ap, scalar1=EPS, scalar2=None, op0=ALU.add), R=[t], W=[t])
        if self.rstd_mode == "ln":
            self.act(ms_ap, ms_ap, AF.Ln, R=[t], W=[t])
            self.act(ms_ap, ms_ap, AF.Exp, R=[t], W=[t], scale=-0.5)
        else:
            self.act(ms_ap, ms_ap, AF.Sqrt, R=[t], W=[t])
            self.v(lambda e: e.reciprocal(out=ms_ap, in_=ms_ap), R=[t], W=[t])

    def load_bcast(self, dram_row, n, name, sem):
        t = self.A.alloc([n], F32, name)
        self.dma("sp", t.ap, dram_row.partition_broadcast(128), W=[t], sem=sem)
        return t

    def load_w_groups(self, dram_w, rows, cols, name, sem, gcols):
        nch = rows // 128
        t = self.A.alloc([nch, cols], BF16, name)
        src = dram_w.rearrange("(c p) n -> p c n", p=128)
        bufs = []
        for g0 in range(0, cols, gcols):
            g1 = min(cols, g0 + gcols)
            b = Buf("%s_g%d" % (name, g0))
            bufs.append(b)
            self.dma("pool", t.ap[:, :, g0:g1], src[:, :, g0:g1], W=[b], sem=sem)
        return t, (lambda col: bufs[col // gcols])

    def load_w(self, dram_w, rows, cols, name, sem, defer=False):
        nch = rows // 128
        t = self.A.alloc([nch, cols], BF16, name)
        if defer:
            return t, (lambda: self._issue_w(t, dram_w, nch, sem))
        self._issue_w(t, dram_w, nch, sem)
        return t

    def _issue_w(self, t, dram_w, nch, sem):
        src = dram_w.rearrange("(c p) n -> p c n", p=128)
        step = max(1, nch // 4)
        for c0 in range(0, nch, step):
            c1 = min(nch, c0 + step)
            self.dma("pool", t.ap[:, c0:c1, :], src[:, c0:c1, :], W=[t], sem=sem)


def ffn_phase(k, src, dst, wg_d, wu_d, wd_d, gpre_d, gpost_d, tag):
    A, S = k.A, k.S
    mark = A.top
    GC = 640
    nchw = D // 128
    wg = A.alloc([nchw, DFF], BF16, "wg")
    wu = A.alloc([nchw, DFF], BF16, "wu")
    wgb, wub = [], []
    for g0 in range(0, DFF, GC):
        for (t_, d_, bl, sm) in ((wg, wg_d, wgb, "w0"), (wu, wu_d, wub, "w1")):
            b = Buf("wgrp")
            bl.append(b)
            g1 = min(DFF, g0 + GC)
            k.dma("pool", t_.ap[:, :, g0:g1], d_.rearrange("(c p) n -> p c n", p=128)[:, :, g0:g1],
                  W=[b], sem=sm)
    wd = k.load_w(wd_d, DFF, D, "wd", "w2")
    gpre = k.load_bcast(gpre_d, D, "gpre", "c0")
    gpost = k.load_bcast(gpost_d, D, "gpost", "c1")
    k.v(lambda e: e.tensor_scalar(out=gpost.ap, in0=gpost.ap, scalar1=0.5, scalar2=None, op0=ALU.mult),
        R=[gpost], W=[gpost])
    xs = [A.alloc([D], F32, "xs%d" % i) for i in range(2)]
    xr = [A.alloc([D], F32, "xr%d" % i) for i in range(2)]
    hn = [A.alloc([D], BF16, "hn%d" % i) for i in range(2)]
    hT = [A.alloc([8, 512], BF16, "hT%d" % i) for i in range(2)]
    AT = A.alloc([22, 512], BF16, "AT")
    sg = [A.alloc([512], BF16, "sg%d" % i) for i in range(2)]
    ytmp = A.alloc([D], F32, "ytmp")
    junk = A.alloc([D], BF16, "junk")
    ms = [A.alloc([4], F32, "ms%d" % i) for i in range(2)]
    ms2 = [A.alloc([1], F32, "ms2%d" % i) for i in range(2)]
    srct = src.rearrange("(n p) d -> n p d", p=128)
    dstt = dst.rearrange("(n p) d -> n p d", p=128)
    PB = k.pb
    cnt = {"x": 0, "r": 0}

    def pn_chain(c, tt):
        m = ms[c % 2]
        gt = 4 * c + tt
        i = tt % 2
        x = xs[i]
        k.dma("sp", x.ap, srct[gt], W=[x], sem="xs%d" % i)
        hh = hn[i]
        k.act(hh.ap, x.ap, AF.Square, R=[x], W=[hh, m], scale=1.0 / 32.0, accum_out=m.ap[:, tt:tt + 1])
        k.rstd_chain(m.ap[:, tt:tt + 1], m)
        k.v(lambda e: e.scalar_tensor_tensor(out=hh.ap, in0=x.ap, scalar=m.ap[:, tt:tt + 1], in1=gpre.ap,
                                             op0=ALU.mult, op1=ALU.mult), R=[x, m, gpre], W=[hh])

    def pn_tr(c, tt):
        h = hT[c % 2]
        hh = hn[tt % 2]
        pT = k.bank_bf(6)
        for dc in range(8):
            k.tr(pT[:, dc * 128:(dc + 1) * 128], hh.ap[:, dc * 128:(dc + 1) * 128], R=[hh], W=[PB[6]])
        k.act(h.ap[:, :, tt * 128:(tt + 1) * 128], pT.rearrange("p (a b) -> p a b", a=8), AF.Copy,
              R=[PB[6]], W=[h])

    def prenorm(c):
        for tt in range(4):
            pn_chain(c, tt)
            pn_tr(c, tt)

    def gateup(c):
        h = hT[c % 2]
        for f in range(22):
            gb, ub = f % 2, 2 + f % 2
            for dc in range(8):
                k.mm(k.bank(gb), wg.ap[:, dc, f * 128:(f + 1) * 128], h.ap[:, dc, :], dc == 0, dc == 7,
                     R=[wgb[f * 128 // GC], h], W=[PB[gb]])
            for dc in range(8):
                k.mm(k.bank(ub), wu.ap[:, dc, f * 128:(f + 1) * 128], h.ap[:, dc, :], dc == 0, dc == 7,
                     R=[wub[f * 128 // GC], h], W=[PB[ub]])
            s = sg[f % 2]
            k.act(s.ap, k.bank(gb), AF.Silu, R=[PB[gb]], W=[s])
            k.v(lambda e, s=s, ub=ub, f=f: e.tensor_tensor(out=AT.ap[:, f, :], in0=s.ap, in1=k.bank(ub), op=ALU.mult),
                R=[s, PB[ub]], W=[AT])
            if c + 1 < NCH:
                if f in (1, 6, 11, 16):
                    pn_chain(c + 1, (f - 1) // 5)
                if f in (4, 9, 14, 19):
                    pn_tr(c + 1, (f - 4) // 5)

    def down(c):
        for tt in range(4):
            gt = 4 * c + tt
            i = cnt["r"] % 2
            cnt["r"] += 1
            r = xr[i]
            k.dma("sp", r.ap, srct[gt], W=[r], sem="xr%d" % i)
            yb0 = 4 + 2 * (tt % 2)
            for half in range(2):
                for f in range(22):
                    k.mm(k.bank(yb0 + half), AT.ap[:, f, tt * 128:(tt + 1) * 128],
                         wd.ap[:, f, half * 512:(half + 1) * 512], f == 0, f == 21,
                         R=[AT, wd], W=[PB[yb0 + half]])
            m2 = ms2[i]
            k.v(lambda e, yb0=yb0: e.tensor_copy(out=ytmp.ap, in_=k.bank(yb0, 2)), R=[PB[yb0], PB[yb0 + 1]], W=[ytmp])
            k.act(junk.ap, ytmp.ap, AF.Square, R=[ytmp], W=[junk, m2], scale=1.0 / 32.0, accum_out=m2.ap)
            k.rstd_chain(m2.ap, m2)
            k.v(lambda e, m2=m2: e.scalar_tensor_tensor(out=ytmp.ap, in0=ytmp.ap, scalar=m2.ap, in1=gpost.ap,
                                                        op0=ALU.mult, op1=ALU.mult),
                R=[m2, gpost, ytmp], W=[ytmp])
            k.g(lambda e, r=r: e.tensor_tensor(out=r.ap, in0=r.ap, in1=ytmp.ap, op=ALU.add), R=[r, ytmp], W=[r])
            k.dma("pool", dstt[gt], r.ap, R=[r], sem="xo%d" % i)

    prenorm(0)
    for c in range(NCH):
        gateup(c)
        down(c)
    S.barrier()
    A.top = mark


def make_prenorm(k, gpre, nx=2):
    A = k.A
    xs = [A.alloc([D], F32, "pxs%d" % i) for i in range(nx)]
    hn = [A.alloc([D], BF16, "phn%d" % i) for i in range(2)]
    hT = [A.alloc([8, 512], BF16, "phT%d" % i) for i in range(2)]
    ms = [A.alloc([4], F32, "pms%d" % i) for i in range(2)]
    cnt = {"x": 0}
    PB = k.pb

    def norm_tile(x, m, tt, h, g, bank=6):
        i = cnt["x"] % 2
        cnt["x"] += 1
        hh = hn[i]
        k.act(hh.ap, x.ap if isinstance(x, Tile) else x[0], AF.Square, R=[x if isinstance(x, Tile) else x[1]],
              W=[hh, m], scale=1.0 / 32.0, accum_out=m.ap[:, tt:tt + 1])
        k.rstd_chain(m.ap[:, tt:tt + 1], m)
        xa = x.ap if isinstance(x, Tile) else x[0]
        xt = x if isinstance(x, Tile) else x[1]
        k.v(lambda e: e.scalar_tensor_tensor(out=hh.ap, in0=xa, scalar=m.ap[:, tt:tt + 1], in1=g.ap,
                                             op0=ALU.mult, op1=ALU.mult), R=[xt, m, g], W=[hh])
        pT = k.bank_bf(bank)
        for dc in range(8):
            k.tr(pT[:, dc * 128:(dc + 1) * 128], hh.ap[:, dc * 128:(dc + 1) * 128], R=[hh], W=[PB[bank]])
        k.act(h.ap[:, :, tt * 128:(tt + 1) * 128], pT.rearrange("p (a b) -> p a b", a=8), AF.Copy,
              R=[PB[bank]], W=[h])

    def prenorm(c, srct):
        h = hT[c % 2]
        m = ms[c % 2]
        for tt in range(4):
            gt = 4 * c + tt
            i = cnt["x"] % nx
            x = xs[i]
            k.dma("sp", x.ap, srct[gt], W=[x], sem="pxs%d" % i)
            norm_tile(x, m, tt, h, gpre)
        return h

    def chain(c, tt, srct):
        m = ms[c % 2]
        gt = 4 * c + tt
        x = xs[tt % nx]
        k.dma("sp", x.ap, srct[gt], W=[x], sem="pxs%d" % (tt % nx))
        hh = hn[tt % 2]
        k.act(hh.ap, x.ap, AF.Square, R=[x], W=[hh, m], scale=1.0 / 32.0, accum_out=m.ap[:, tt:tt + 1])
        k.rstd_chain(m.ap[:, tt:tt + 1], m)
        k.v(lambda e: e.scalar_tensor_tensor(out=hh.ap, in0=x.ap, scalar=m.ap[:, tt:tt + 1], in1=gpre.ap,
                                             op0=ALU.mult, op1=ALU.mult), R=[x, m, gpre], W=[hh])

    def tr(c, tt, bank=6):
        h = hT[c % 2]
        hh = hn[tt % 2]
        pT = k.bank_bf(bank)
        for dc in range(8):
            k.tr(pT[:, dc * 128:(dc + 1) * 128], hh.ap[:, dc * 128:(dc + 1) * 128], R=[hh], W=[PB[bank]])
        k.act(h.ap[:, :, tt * 128:(tt + 1) * 128], pT.rearrange("p (a b) -> p a b", a=8), AF.Copy,
              R=[PB[bank]], W=[h])
        return h

    prenorm.chain = chain
    prenorm.tr = tr
    prenorm.norm_tile = norm_tile
    prenorm.hT = hT
    prenorm.ms = ms
    return prenorm


FM = [(0, 4, 0.125), (512, 4, 1.0), (1536, 4, 0.125), (2048, 1, 1.0), (2176, 1, 1.0), (2304, 1, 1.0), (2560, 1, 1.0)]


def inproj_phase(k, x1, I, zT, vtok, gts, kv):
    A, S, PB = k.A, k.S, k.pb
    mark = A.top
    gpre = k.load_bcast(I["mix_pre_g"], D, "gpre", "c0")
    win, wgrp = k.load_w_groups(I["w_mix_in"], D, MIXIN, "win", "w0", 512)
    prenorm = make_prenorm(k, gpre, nx=4)
    stg = [A.alloc([512], BF16, "stg%d" % i) for i in range(3)]
    vst = [A.alloc([768], BF16, "vst%d" % i) for i in range(2)]
    gst = [A.alloc([24], F32, "gst%d" % i) for i in range(2)]
    srct = x1.rearrange("(n p) d -> n p d", p=128)
    vtokt = vtok.rearrange("(n p) d -> n p d", p=128)
    gtst = gts.rearrange("(n p) d -> n p d", p=128)
    KxT, Vx = kv
    gmem = k.load_bcast(I["mem_norm_g"], D, "gmem", "c0")
    wk, go_wk = k.load_w(I["xa_w_k"], D, D, "wk", "w3", defer=True)
    wv, go_wv = k.load_w(I["xa_w_v"], D, D, "wv", "w4", defer=True)
    mT = A.alloc([8, 256], BF16, "mT")
    msm = A.alloc([2], F32, "msm")
    memt = I["mem"].rearrange("(n p) d -> n p d", p=128)
    xm = [A.alloc([D], F32, "xm%d" % i) for i in range(2)]
    mhn = [A.alloc([D], BF16, "mhn%d" % i) for i in range(2)]
    for mt in range(2):
        k.dma("sp", xm[mt].ap, memt[mt], W=[xm[mt]], sem="c1")

    def kv_setup():
        for mt in range(2):
            k.act(mhn[mt].ap, xm[mt].ap, AF.Square, R=[xm[mt]], W=[mhn[mt], msm], scale=1.0 / 32.0,
                  accum_out=msm.ap[:, mt:mt + 1])
        k.rstd_chain(msm.ap, msm)
        for mt in range(2):
            k.v(lambda e, mt=mt: e.scalar_tensor_tensor(out=mhn[mt].ap, in0=xm[mt].ap, scalar=msm.ap[:, mt:mt + 1],
                                                        in1=gmem.ap, op0=ALU.mult, op1=ALU.mult),
                R=[xm[mt], msm, gmem], W=[mhn[mt]])
            pT = k.bank_bf(7)
            for dc in range(8):
                k.tr(pT[:, dc * 128:(dc + 1) * 128], mhn[mt].ap[:, dc * 128:(dc + 1) * 128], R=[mhn[mt]], W=[PB[7]])
            k.act(mT.ap[:, :, mt * 128:(mt + 1) * 128], pT.rearrange("p (a b) -> p a b", a=8), AF.Copy,
                  R=[PB[7]], W=[mT])
        for ft in range(8):
            bk = 4 + ft % 2
            for dc in range(8):
                k.mm(k.bank(bk)[:, 0:256], wk.ap[:, dc, ft * 128:(ft + 1) * 128], mT.ap[:, dc, :], dc == 0, dc == 7,
                     R=[wk, mT], W=[PB[bk]])
            k.act(KxT.ap[:, ft, :], k.bank(bk)[:, 0:256], AF.Copy, R=[PB[bk]], W=[KxT])
        for mt in range(2):
            for half in range(2):
                bk = 4 + half
                for dc in range(8):
                    k.mm(k.bank(bk), mT.ap[:, dc, mt * 128:(mt + 1) * 128], wv.ap[:, dc, half * 512:(half + 1) * 512],
                         dc == 0, dc == 7, R=[wv, mT], W=[PB[bk]])
                k.v(lambda e, mt=mt, half=half, bk=bk: e.tensor_copy(
                    out=Vx.ap[:, mt, half * 512:(half + 1) * 512], in_=k.bank(bk)), R=[PB[bk]], W=[Vx])

    hnext = prenorm(0, srct)
    for c in range(NCH):
        h = hnext
        i = 0
        for (z0, ntile, sc) in FM:
            for ft in range(ntile):
                bk = i % 2
                col = z0 + ft * 128
                for dc in range(8):
                    k.mm(k.bank(bk), win.ap[:, dc, col:col + 128], h.ap[:, dc, :], dc == 0, dc == 7,
                         R=[wgrp(col), h], W=[PB[bk]])
                s = stg[i % 3]
                if sc != 1.0:
                    k.v(lambda e, s=s, bk=bk, sc=sc: e.tensor_scalar(out=s.ap, in0=k.bank(bk), scalar1=sc, scalar2=None,
                                                                     op0=ALU.mult), R=[PB[bk]], W=[s])
                else:
                    k.act(s.ap, k.bank(bk), AF.Copy, R=[PB[bk]], W=[s])
                k.dma("pool" if sc != 1.0 else "act", zT[col:col + 128, c * 512:(c + 1) * 512], s.ap, R=[s],
                      sem="stg%d" % (i % 3))
                if c + 1 < NCH:
                    if i in (1, 5, 9, 13):
                        prenorm.chain(c + 1, (i - 1) // 4, srct)
                    if i in (3, 7, 11, 15):
                        hnext = prenorm.tr(c + 1, (i - 3) // 4)
                i += 1
        for tt in range(4):
            gt = 4 * c + tt
            hs = h.ap[:, :, tt * 128:(tt + 1) * 128]
            ba = 2 + 2 * (tt % 2)
            bb = ba + 1
            for dc in range(8):
                k.mm(k.bank(ba), hs[:, dc, :], win.ap[:, dc, 1024:1536], dc == 0, dc == 7,
                     R=[wgrp(1024), h], W=[PB[ba]])
            for (o0, z0, n) in ((0, 2432, 128), (128, 2688, 128), (256, 2816, 24)):
                for dc in range(8):
                    k.mm(k.bank(bb)[:, o0:o0 + n], hs[:, dc, :], win.ap[:, dc, z0:z0 + n], dc == 0, dc == 7,
                         R=[wgrp(z0), h], W=[PB[bb]])
            vs = vst[gt % 2]
            k.act(vs.ap[:, 0:512], k.bank(ba), AF.Copy, R=[PB[ba]], W=[vs])
            k.v(lambda e, vs=vs, bb=bb: e.tensor_copy(out=vs.ap[:, 512:768], in_=k.bank(bb)[:, 0:256]), R=[PB[bb]], W=[vs])
            k.dma("pool", vtokt[gt], vs.ap, R=[vs], sem="vst%d" % (gt % 2))
            gs = gst[gt % 2]
            k.v(lambda e, gs=gs, bb=bb: e.tensor_copy(out=gs.ap, in_=k.bank(bb)[:, 256:280]), R=[PB[bb]], W=[gs])
            k.dma("pool", gtst[gt], gs.ap, R=[gs], sem="gst%d" % (gt % 2))
        if c == 0:
            go_wk()
            go_wv()
        if c == 3:
            kv_setup()
    S.barrier()
    A.top = mark


class AttnPipe:
    def __init__(self, k, PT):
        self.k = k
        self.PT = PT
        self.n = 0
        self.pending = None
        self.obank_first = {}

    def _qk(self, st):
        k = self.k
        sb = st["sb"]
        lo, hi = st["lo"], st["hi"]
        out = k.bank(sb)[:, lo:hi]
        ex = st["extra"]
        k.mm(out, st["kT"], st["qT"], True, len(ex) == 0, R=st["Rqk"], W=[k.pb[sb]])
        for n, (lhsT, rhs, a, b, R) in enumerate(ex):
            k.mm(k.bank(sb)[:, a:b], lhsT, rhs, False, n == len(ex) - 1, R=R, W=[k.pb[sb]])

    def _rest(self, st):
        k = self.k
        sb = st["sb"]
        if "restfn" in st:
            st["restfn"](sb)
            return
        lo, hi = st["lo"], st["hi"]
        pt = st["pt"]
        k.act(pt.ap[0:st["kp"], lo:hi], k.bank(sb)[0:st["kp"], lo:hi], AF.Exp, R=[k.pb[sb]], W=[pt], scale=st["scale"])
        if st.get("mask") is not None:
            mt, eng = st["mask"]
            k.S.op(eng, lambda e: e.tensor_tensor(out=pt.ap[:, lo:hi], in0=pt.ap[:, lo:hi], in1=mt.ap[:, lo:hi],
                                                  op=ALU.mult), reads=[pt, mt], writes=[pt])
        for (oreg, ob, lhs_lo, lhs_hi, rhs, start, stop, R) in st["pv"]:
            k.mm(oreg, pt.ap[0:st["kp"], lhs_lo:lhs_hi], rhs, start, stop, R=[pt] + R, W=[k.pb[ob]])
        if st.get("evac") is not None:
            st["evac"]()

    def push(self, st):
        st["sb"] = self.n % 4
        st["pt"] = self.PT[self.n % len(self.PT)]
        self.n += 1
        if "qkfn" in st:
            st["qkfn"](st["sb"])
        else:
            self._qk(st)
        if self.pending is None:
            self.pending = []
        self.pending.append(st)
        if len(self.pending) > 3:
            self._rest(self.pending.pop(0))

    def flush(self):
        while self.pending:
            self._rest(self.pending.pop(0))


def attn_phase(k, I, zT, vtok, gts, osc):
    A, S, PB = k.A, k.S, k.pb
    mark = A.top
    def cload(name, shape, dtype, src, q="sp"):
        t = A.alloc(shape, dtype, name)
        k.dma(q, t.ap, src, W=[t], sem="c0")
        return t
    tric = cload("tric", [128], BF16, I["c_tric"])
    tril = cload("tril", [128], BF16, I["c_tril"])
    gates = cload("gates", [NT, 24], F32, gts.rearrange("(n p) j -> p n j", p=128))
    k.act(gates.ap, gates.ap, AF.Exp, R=[gates], W=[gates], scale=-1.0)
    k.v(lambda e: e.tensor_scalar(out=gates.ap, in0=gates.ap, scalar1=1.0, scalar2=None, op0=ALU.add), R=[gates], W=[gates])
    k.v(lambda e: e.reciprocal(out=gates.ap, in_=gates.ap), R=[gates], W=[gates])
    lq1 = k.load_bcast(I["da_lambda_q1"], 64, "lq1", "c1")
    lk1 = k.load_bcast(I["da_lambda_k1"], 64, "lk1", "c1")
    lq2 = k.load_bcast(I["da_lambda_q2"], 64, "lq2", "c1")
    lk2 = k.load_bcast(I["da_lambda_k2"], 64, "lk2", "c1")
    lsum = A.alloc([2], F32, "lsum")
    neglam = A.alloc([1], F32, "neglam")
    k.v(lambda e: e.tensor_tensor(out=lq1.ap, in0=lq1.ap, in1=lk1.ap, op=ALU.mult), R=[lq1, lk1], W=[lq1])
    k.v(lambda e: e.tensor_tensor(out=lq2.ap, in0=lq2.ap, in1=lk2.ap, op=ALU.mult), R=[lq2, lk2], W=[lq2])
    k.v(lambda e: e.reduce_sum(out=lsum.ap[:, 0:1], in_=lq1.ap, axis=AX.X), R=[lq1], W=[lsum])
    k.v(lambda e: e.reduce_sum(out=lsum.ap[:, 1:2], in_=lq2.ap, axis=AX.X), R=[lq2], W=[lsum])
    k.act(lsum.ap, lsum.ap, AF.Exp, R=[lsum], W=[lsum])
    lam_init = 0.8 - 0.6 * math.exp(-0.3 * 0)
    k.v(lambda e: e.tensor_tensor(out=neglam.ap, in0=lsum.ap[:, 1:2], in1=lsum.ap[:, 0:1], op=ALU.subtract),
        R=[lsum], W=[neglam])
    k.v(lambda e: e.tensor_scalar(out=neglam.ap, in0=neglam.ap, scalar1=-lam_init, scalar2=None, op0=ALU.add),
        R=[neglam], W=[neglam])
    gsub = k.load_bcast(I["da_subln_g"], 128, "gsub", "c1")
    k.v(lambda e: e.tensor_scalar(out=gsub.ap, in0=gsub.ap, scalar1=1.0 - lam_init, scalar2=None, op0=ALU.mult),
        R=[gsub], W=[gsub])

    QB = [A.alloc([T], BF16, "QB%d" % i) for i in range(2)]
    KBf = [A.alloc([T], BF16, "KB%d" % i) for i in range(2)]
    PT = [A.alloc([512], BF16, "PT%d" % i) for i in range(4)]
    pipe = AttnPipe(k, PT)
    rl = [A.alloc([4], F32, "rl%d" % i) for i in range(2)]
    rg = [A.alloc([4], F32, "rg%d" % i) for i in range(2)]
    t2 = [A.alloc([4, 128], F32, "t2%d" % i) for i in range(2)]
    ob = [A.alloc([4, 128], BF16, "ob%d" % i) for i in range(2)]
    msd = [A.alloc([4], F32, "msd%d" % i) for i in range(2)]
    osct = osc.rearrange("(n p) d -> p n d", p=128)
    state = {"cj": 0, "ev": 0}

    def oview(pair):
        return k.bank(4 + 2 * pair, 2).rearrange("p (j w) -> p j w", j=4)

    def load_q(buf, zrow, slot):
        k.dma("sp", buf.ap[0:64, :], zT[zrow:zrow + 64, :], W=[buf], sem="q%s" % buf.b.name)
        k.dma("sp", buf.ap[64:68, :], I["c_qaug"][slot], W=[buf], sem="q%s" % buf.b.name)

    def load_k(buf, zrow):
        k.dma("sp", buf.ap[0:64, :], zT[zrow:zrow + 64, :], W=[buf], sem="k%s" % buf.b.name)
        k.dma("sp", buf.ap[64:68, :], I["c_kaug"], W=[buf], sem="k%s" % buf.b.name)

    def load_v(buf, col, dv):
        src = vtok.rearrange("(n p) d -> p n d", p=128)
        for n0 in range(0, NT, 8):
            k.dma("sp", buf.ap[:, n0:n0 + 8, 0:dv], src[:, n0:n0 + 8, col:col + dv], W=[buf], sem="v%s" % buf.b.name)

    def run_job(qb, kT_of, v_of, dv1, tiles_of, extras_of, evac_of, scale=1.0, kp_of=None, hook=None):
        for c in range(NCH):
            pair = state["cj"] % 2
            state["cj"] += 1
            ov = oview(pair)
            tl = tiles_of(c)
            last_for = {}
            for n, (kt, j0, j1) in enumerate(tl):
                for j in range(j0, j1):
                    last_for[j] = n
            started = set()
            for n, (kt, j0, j1) in enumerate(tl):
                lo, hi = j0 * 128, j1 * 128
                kTap, kR = kT_of(kt)
                vap, vR = v_of(kt)
                kp = 128 if kp_of is None else kp_of(kt)
                pv = []
                for j in range(j0, j1):
                    obk = 4 + 2 * pair + j // 2
                    start = obk not in started
                    started.add(obk)
                    pv.append((ov[:, j, 0:dv1], obk, j * 128, (j + 1) * 128, vap, start, last_for[j] == n, vR))
                st = dict(kT=kTap, qT=qb.ap[0:68, c * 512 + lo:c * 512 + hi], lo=lo, hi=hi, Rqk=[qb] + kR,
                          extra=extras_of(c, kt, j0, j1), pv=pv, scale=scale, kp=kp,
                          evac=(evac_of(c, pair) if n == len(tl) - 1 else None))
                pipe.push(st)
            if hook is not None:
                hook(c)

    def causal_tiles(c):
        tl = [(kt, 0, 4) for kt in range(4 * c)]
        tl += [(4 * c + i, i, 4) for i in range(4)]
        return tl

    def causal_extras(c, kt, j0, j1):
        if kt >= 4 * c:
            i = kt - 4 * c
            return [(k.ident.ap, tric.ap, i * 128, (i + 1) * 128, [k.ident, tric])]
        return []

    def rl_of(ov, col, i, clamp=False):
        r = rl[i]
        if clamp:
            k.v(lambda e: e.tensor_scalar(out=r.ap.rearrange("p (a b) -> p a b", b=1), in0=ov[:, :, col:col + 1],
                                          scalar1=1e-30, scalar2=None, op0=ALU.max), R=[], W=[r])
            k.v(lambda e: e.reciprocal(out=r.ap, in_=r.ap), R=[r], W=[r])
        else:
            k.v(lambda e: e.reciprocal(out=r.ap.rearrange("p (a b) -> p a b", b=1), in_=ov[:, :, col:col + 1]),
                R=[], W=[r])
        return r

    cmask = cload("cmask", [2560], BF16, I["c_cmask"], q="act")
    expand = cload("expand", [4096], BF16, I["c_expand"], q="act")
    selA = cload("selA", [NT, 64], F32, I["c_selA"].rearrange("(n p) j -> p n j", p=128), q="act")
    selB = cload("selB", [NT, 64], F32, I["c_selB"].rearrange("(n p) j -> p n j", p=128), q="act")
    tricn = cload("tricn", [128], BF16, I["c_tricn"], q="act")
    w1 = [A.alloc([32, 128], BF16, "w1%d" % i) for i in range(2)]
    w2 = [A.alloc([64], BF16, "w2%d" % i) for i in range(2)]
    peT = [A.alloc([32], F32, "peT%d" % i) for i in range(2)]
    peTb = [A.alloc([32], BF16, "peTb%d" % i) for i in range(2)]
    cb = [A.alloc([1], F32, "cb%d" % i) for i in range(2)]
    for i, nm in enumerate(("k", "v")):
        k.dma("pool", w1[i].ap[0:64], I["cmp_%s_w1" % nm].rearrange("(pos d) h -> d pos h", d=64), W=[w1[i]], sem="w0")
        k.dma("pool", w2[i].ap, I["cmp_%s_w2" % nm], W=[w2[i]], sem="w0")
        pe_src = I["cmp_%s_pe" % nm].rearrange("pos d -> d pos")
        S.op("act", lambda e, i=i, pe_src=pe_src: e.dma_start(out=peT[i].ap[0:64], in_=pe_src,
                                                           allow_slow_non_contiguous=True),
             writes=[peT[i].b], dma="c1")
    def da_evac(h, m):
        def evac_of(c, pair):
            def ev():
                i = state["ev"] % 2
                state["ev"] += 1
                ov = oview(pair)
                OB = [PB[4 + 2 * pair], PB[5 + 2 * pair]]
                r = rl[i]
                k.v(lambda e: e.reciprocal(out=r.ap.rearrange("p (a b) -> p a b", b=1), in_=ov[:, :, 128:129]),
                    R=OB, W=[r])
                rb = r.ap.rearrange("p (a b) -> p a b", b=1).to_broadcast([128, 4, 128])
                dch = datmp.ap[:, 4 * c:4 * c + 4, :]
                if m == 0:
                    k.v(lambda e: e.tensor_tensor(out=dch, in0=ov[:, :, 0:128], in1=rb, op=ALU.mult),
                        R=OB + [r], W=[datmp])
                    return
                tt_ = t2[i]
                k.v(lambda e: e.tensor_tensor(out=tt_.ap, in0=ov[:, :, 0:128], in1=rb, op=ALU.mult),
                    R=OB + [r], W=[tt_])
                k.v(lambda e: e.scalar_tensor_tensor(out=dch, in0=tt_.ap, scalar=neglam.ap, in1=dch,
                                                     op0=ALU.mult, op1=ALU.add), R=[tt_, neglam, datmp], W=[datmp])
                k.v(lambda e: e.tensor_tensor(out=tt_.ap, in0=dch, in1=dch, op=ALU.mult), R=[datmp], W=[tt_])
                k.v(lambda e: e.reduce_sum(out=msall.ap[:, 4 * c:4 * c + 4], in_=tt_.ap, axis=AX.X), R=[tt_], W=[msall])
                if c == NCH - 1:
                    k.v(lambda e: e.tensor_scalar(out=msall.ap, in0=msall.ap, scalar1=1.0 / 128.0, scalar2=EPS,
                                                  op0=ALU.mult, op1=ALU.add), R=[msall], W=[msall])
                    k.act(msall.ap, msall.ap, AF.Ln, R=[msall], W=[msall])
                    k.act(msall.ap, msall.ap, AF.Exp, R=[msall], W=[msall], scale=-0.5)
                    for q4 in range(4):
                        sl = slice(8 * q4, 8 * q4 + 8)
                        mb = msall.ap[:, sl].rearrange("p (a b) -> p a b", b=1).to_broadcast([128, 8, 128])
                        gb = gsub.ap.rearrange("p (a b) -> p a b", a=1).broadcast_to([128, 8, 128])
                        k.v(lambda e, sl=sl, mb=mb: e.tensor_tensor(out=datmp.ap[:, sl, :], in0=datmp.ap[:, sl, :], in1=mb,
                                                                    op=ALU.mult), R=[datmp, msall], W=[datmp])
                        k.v(lambda e, sl=sl, gb=gb: e.tensor_tensor(out=oball.ap[:, sl, :], in0=datmp.ap[:, sl, :], in1=gb,
                                                                    op=ALU.mult), R=[datmp, gsub], W=[oball])
                    k.dma("pool", osct[:, :, h * 128:(h + 1) * 128], oball.ap, R=[oball], sem="oball")
            return ev
        return evac_of

    da_jobs = [(h, m) for h in range(4) for m in range(2)]

    def da_load(n):
        h, m = da_jobs[n]
        load_q(QB[n % 2], h * 128 + m * 64, h)
        load_k(KBf[n % 2], 512 + h * 128 + m * 64)
        if m == 0:
            load_v(VA[h % 2], h * 128, 128)

    if k.stage_attn & 1:
        mda = A.top
        VA = [A.alloc([NT, 129], BF16, "VA%d" % i) for i in range(2)]
        for t in VA:
            k.g(lambda e, t=t: e.memset(t.ap[:, :, 128:129], 1.0), W=[t])
        datmp = A.alloc([NT, 128], F32, "datmp")
        msall = A.alloc([NT], F32, "msall")
        oball = A.alloc([NT, 128], BF16, "oball")
        da_load(0)
        for n, (h, m) in enumerate(da_jobs):
            if n + 1 < len(da_jobs):
                da_load(n + 1)
            qb, kb, vb = QB[n % 2], KBf[n % 2], VA[h % 2]
            run_job(qb,
                    lambda kt, kb=kb: (kb.ap[0:68, kt * 128:(kt + 1) * 128], [kb]),
                    lambda kt, vb=vb: (vb.ap[:, kt, :], [vb]),
                    129, causal_tiles, causal_extras, da_evac(h, m))
        pipe.flush()
        S.barrier()
        A.top = mda

    if k.stage_attn & 2:
        VN = [A.alloc([NT, 65], BF16, "VN%d" % i) for i in range(2)]
        for t in VN:
            k.g(lambda e, t=t: e.memset(t.ap[:, :, 64:65], 1.0), W=[t])
        onsa = [A.alloc([NT, 64], F32, "onsa%d" % i) for i in range(4)]
        imp = A.alloc([NT, 64], F32, "imp")
        QS = [A.alloc([T], BF16, "QS%d" % i) for i in range(4)]
        Mtiles = [A.alloc([512], BF16, "Mt%d" % i) for i in range(3)]
        for i, nm in enumerate(("k", "v")):
            k.v(lambda e, i=i: e.tensor_copy(out=peTb[i].ap[0:64], in_=peT[i].ap[0:64]), R=[peT[i]], W=[peTb[i]])
            for pos in range(32):
                k.mm(k.bank(3)[:, i:i + 1], w1[i].ap[0:64, pos, :], peTb[i].ap[0:64, pos:pos + 1], pos == 0, pos == 31,
                     R=[w1[i], peTb[i]], W=[PB[3]])
            k.v(lambda e, i=i: e.tensor_copy(out=cb[i].ap, in_=k.bank(3)[:, i:i + 1]), R=[PB[3]], W=[cb[i]])
        cin = [QB[0], QB[1]]
        AcT = [A.alloc([256], BF16, "AcT%d" % i) for i in range(2)]
        KcT = A.alloc([256], BF16, "KcT")
        Vc = A.alloc([2, 129], BF16, "Vc")
        negT = A.alloc([T], BF16, "negT")
        score = A.alloc([NT, 64], F32, "score")
        sc2 = A.alloc([64], F32, "sc2")
        m8 = A.alloc([8], F32, "m8")
        thr = A.alloc([NT], F32, "thr")
        negm = A.alloc([NT, 64], BF16, "negm")
        k.g(lambda e: e.memset(Vc.ap, 0.0), W=[Vc])
        k.g(lambda e: e.memset(Vc.ap[:, :, 128:129], 1.0), W=[Vc])
        k.dma("sp", Vc.ap[:, :, 64:128], I["c_mcs"].rearrange("(n p) j -> p n j", p=128), W=[Vc], sem="c1")
        print("NSA arena top", A.top)
        k.g(lambda e: e.memset(KcT.ap, 0.0), W=[KcT])
        k.dma("sp", KcT.ap[64:68, :], I["c_kaugc"], W=[KcT], sem="c1")

        for g in range(2):
            for i, zr in enumerate((2048, 2176)):
                k.dma("sp", cin[i].ap[0:64, :], zT[zr + g * 64:zr + g * 64 + 64, :], W=[cin[i]], sem="cin%d" % i)
                for pos in range(32):
                    k.mm(k.bank(3)[:, 8:8 + 255], w1[i].ap[0:64, pos, :], cin[i].ap[0:64, pos:pos + 16 * 254 + 1:16],
                         pos == 0, pos == 31, R=[w1[i], cin[i]], W=[PB[3]])
                k.g(lambda e, i=i: e.memset(AcT[i].ap, 0.0), W=[AcT[i]])
                k.act(AcT[i].ap[:, 0:255], k.bank(3)[:, 8:8 + 255], AF.Silu, R=[PB[3], cb[i]], W=[AcT[i]], bias=cb[i].ap)
            k.mm(k.bank(3)[0:64, 0:255], w2[0].ap, AcT[0].ap[:, 0:255], True, True, R=[w2[0], AcT[0]], W=[PB[3]])
            k.v(lambda e: e.tensor_copy(out=KcT.ap[0:64, 0:255], in_=k.bank(3)[0:64, 0:255]), R=[PB[3]], W=[KcT])
            for ct in range(2):
                k.mm(k.bank(3)[:, 256 + ct * 64:256 + (ct + 1) * 64], AcT[1].ap[:, ct * 128:(ct + 1) * 128], w2[1].ap,
                     True, True, R=[w2[1], AcT[1]], W=[PB[3]])
            k.v(lambda e: e.tensor_copy(out=Vc.ap[:, :, 0:64],
                                        in_=k.bank(3)[:, 256:384].rearrange("p (a b) -> p a b", a=2)),
                R=[PB[3]], W=[Vc])

            def cmp_tiles(c):
                tl = [(0, 0, 4)]
                if c >= 4:
                    tl.append((1, 0, 4))
                return tl

            def cmp_extras(c, kt, j0, j1):
                if kt == 0 and c >= 5:
                    return []
                off = c * 512 if kt == 0 else (c - 4) * 512
                return [(k.ident.ap, cmask.ap[:, off:off + 512], 0, 512, [k.ident, cmask])]

            def cmp_evac(r, head):
                def evac_of(c, pair):
                    def ev():
                        i = state["ev"] % 2
                        state["ev"] += 1
                        ov = oview(pair)
                        OB = [PB[4 + 2 * pair], PB[5 + 2 * pair]]
                        r_ = rl[i]
                        r3 = r_.ap.rearrange("p (a b) -> p a b", b=1)
                        k.v(lambda e: e.tensor_scalar(out=r3, in0=ov[:, :, 128:129], scalar1=1e-30, scalar2=None,
                                                      op0=ALU.max), R=OB, W=[r_])
                        k.v(lambda e: e.reciprocal(out=r_.ap, in_=r_.ap), R=[r_], W=[r_])
                        g_ = rg[i]
                        k.v(lambda e: e.tensor_tensor(out=g_.ap.rearrange("p (a b) -> p a b", b=1), in0=r3,
                                                      in1=gates.ap[:, 4 * c:4 * c + 4, head * 3:head * 3 + 1], op=ALU.mult),
                            R=[r_, gates], W=[g_])
                        gb = g_.ap.rearrange("p (a b) -> p a b", b=1).to_broadcast([128, 4, 64])
                        k.v(lambda e: e.tensor_tensor(out=onsa[r].ap[:, 4 * c:4 * c + 4, :], in0=ov[:, :, 0:64], in1=gb,
                                                      op=ALU.mult), R=OB + [g_], W=[onsa[r]])
                        rb = r3.to_broadcast([128, 4, 64])
                        ich = imp.ap[:, 4 * c:4 * c + 4, :]
                        if r == 0:
                            k.v(lambda e: e.tensor_tensor(out=ich, in0=ov[:, :, 64:128], in1=rb, op=ALU.mult),
                                R=OB + [r_], W=[imp])
                        else:
                            tt_ = t2[i]
                            k.v(lambda e: e.tensor_tensor(out=tt_.ap[:, :, 0:64], in0=ov[:, :, 64:128], in1=rb,
                                                          op=ALU.mult), R=OB + [r_], W=[tt_])
                            k.g(lambda e: e.tensor_tensor(out=ich, in0=ich, in1=tt_.ap[:, :, 0:64], op=ALU.add),
                                R=[tt_, imp], W=[imp])
                    return ev
                return evac_of

            def acc_evac(r, head, branch, final):
                def evac_of(c, pair, ov=None, OB=None):
                    def ev(ov=ov, OB=OB):
                        i = state["ev"] % 2
                        state["ev"] += 1
                        if ov is None:
                            ov = oview(pair)
                            OB = [PB[4 + 2 * pair], PB[5 + 2 * pair]]
                        r_ = rl[i]
                        r3 = r_.ap.rearrange("p (a b) -> p a b", b=1)
                        k.v(lambda e: e.reciprocal(out=r3, in_=ov[:, :, 64:65]), R=OB, W=[r_])
                        g_ = rg[i]
                        k.v(lambda e: e.tensor_tensor(out=g_.ap.rearrange("p (a b) -> p a b", b=1), in0=r3,
                                                      in1=gates.ap[:, 4 * c:4 * c + 4, head * 3 + branch:head * 3 + branch + 1],
                                                      op=ALU.mult), R=[r_, gates], W=[g_])
                        gb = g_.ap.rearrange("p (a b) -> p a b", b=1).to_broadcast([128, 4, 64])
                        tt_ = t2[i]
                        k.v(lambda e: e.tensor_tensor(out=tt_.ap[:, :, 0:64], in0=ov[:, :, 0:64], in1=gb, op=ALU.mult),
                            R=OB + [g_], W=[tt_])
                        och = onsa[r].ap[:, 4 * c:4 * c + 4, :]
                        if not final:
                            k.g(lambda e: e.tensor_tensor(out=och, in0=och, in1=tt_.ap[:, :, 0:64], op=ALU.add),
                                R=[tt_, onsa[r]], W=[onsa[r]])
                        else:
                            o_ = ob[i]
                            k.v(lambda e: e.tensor_tensor(out=o_.ap[:, :, 0:64], in0=och, in1=tt_.ap[:, :, 0:64], op=ALU.add),
                                R=[tt_, onsa[r]], W=[o_])
                            k.dma("pool", osct[:, 4 * c:4 * c + 4, 512 + head * 64:512 + (head + 1) * 64], o_.ap[:, :, 0:64],
                                  R=[o_], sem="ob%d" % i)
                    return ev
                return evac_of

            for r in range(4):
                load_q(QS[r], 1536 + (g * 4 + r) * 64, 4 + g * 4 + r)
            kvs = KBf[0], VN[0]
            kvw = KBf[1], VN[1]
            load_k(kvs[0], 2304 + g * 64)
            load_v(kvs[1], 512 + g * 64, 64)
            load_k(kvw[0], 2560 + g * 64)
            load_v(kvw[1], 640 + g * 64, 64)

            for r in range(4):
                run_job(QS[r],
                        lambda kt: (KcT.ap[0:68, kt * 128:(kt + 1) * 128], [KcT]),
                        lambda kt: (Vc.ap[:, kt, :], [Vc]),
                        129, cmp_tiles, cmp_extras, cmp_evac(r, g * 4 + r))
            pipe.flush()

            k.v(lambda e: e.tensor_tensor(out=score.ap, in0=imp.ap, in1=selA.ap, op=ALU.mult), R=[imp, selA], W=[score])
            k.v(lambda e: e.tensor_tensor(out=score.ap, in0=score.ap, in1=selB.ap, op=ALU.add), R=[score, selB], W=[score])

            def sel_piece(n):
                k.v(lambda e: e.max(out=m8.ap, in_=score.ap[:, n, :]), R=[score], W=[m8])
                k.v(lambda e: e.match_replace(out=sc2.ap, in_to_replace=m8.ap, in_values=score.ap[:, n, :],
                                              imm_value=-3.0), R=[score, m8], W=[sc2])
                k.v(lambda e: e.max(out=m8.ap, in_=sc2.ap), R=[sc2], W=[m8])
                k.v(lambda e: e.tensor_copy(out=thr.ap[:, n:n + 1], in_=m8.ap[:, 7:8]), R=[m8], W=[thr])

            def sel_extras(c, kt, j0, j1):
                ex = [(expand.ap[0:64, kt * 128:(kt + 1) * 128], negT.ap[0:64, c * 512 + j0 * 128:c * 512 + j1 * 128],
                       j0 * 128, j1 * 128, [expand, negT])]
                return ex + causal_extras(c, kt, j0, j1)

            def win_tiles(c):
                tl = []
                for kt in range(max(0, 4 * c - 4), 4 * c + 4):
                    j0 = max(0, kt - 4 * c)
                    j1 = min(3, kt - 4 * c + 4) + 1
                    tl.append((kt, j0, j1))
                return tl

            def win_extras(c, kt, j0, j1):
                ex = []
                if kt >= 4 * c:
                    i = kt - 4 * c
                    ex.append((k.ident.ap, tric.ap, i * 128, (i + 1) * 128, [k.ident, tric]))
                if kt < 4 * c:
                    i = kt - 4 * c + 4
                    ex.append((k.ident.ap, tril.ap, i * 128, (i + 1) * 128, [k.ident, tril]))
                return ex

            seln = {"n": 0}

            def win_hook(c):
                sel_piece(seln["n"])
                seln["n"] += 1

            for r in range(4):
                kb, vb = kvw
                run_job(QS[r],
                        lambda kt, kb=kb: (kb.ap[0:68, kt * 128:(kt + 1) * 128], [kb]),
                        lambda kt, vb=vb: (vb.ap[:, kt, :], [vb]),
                        65, win_tiles, win_extras, acc_evac(r, g * 4 + r, 2, False), hook=win_hook)
            pipe.flush()
            tb = thr.ap.rearrange("p (a b) -> p a b", b=1).to_broadcast([128, NT, 64])
            k.v(lambda e: e.tensor_tensor(out=score.ap, in0=score.ap, in1=tb, op=ALU.is_ge), R=[score, thr], W=[score])
            k.v(lambda e: e.tensor_tensor(out=score.ap, in0=score.ap, in1=selA.ap, op=ALU.mult), R=[score, selA], W=[score])
            k.v(lambda e: e.tensor_copy(out=negm.ap, in_=score.ap), R=[score], W=[negm])
            for n0 in range(0, NT, 8):
                pT = k.bank_bf(3)
                for n in range(n0, n0 + 8):
                    k.tr(pT[0:64, (n - n0) * 128:(n - n0 + 1) * 128], negm.ap[:, n, :], R=[negm], W=[PB[3]])
                k.v(lambda e, n0=n0, pT=pT: e.tensor_copy(out=negT.ap[0:64, n0 * 128:(n0 + 8) * 128], in_=pT[0:64, :]),
                    R=[PB[3]], W=[negT])
            kb, vb = kvs
            mi = 0
            for c in range(NCH):
                tl = causal_tiles(c)
                started = set()
                for n, (kt, j0, j1) in enumerate(tl):
                    lo, hi = j0 * 128, j1 * 128
                    diag = kt >= 4 * c
                    Mt = Mtiles[mi % 3]
                    mi += 1
                    def mqk(sb, lo=lo, hi=hi, kt=kt, c=c, diag=diag):
                        k.mm(k.bank(sb)[:, lo:hi], expand.ap[0:64, kt * 128:(kt + 1) * 128],
                             negT.ap[0:64, c * 512 + lo:c * 512 + hi], True, not diag, R=[expand, negT], W=[PB[sb]])
                        if diag:
                            i = kt - 4 * c
                            k.mm(k.bank(sb)[:, i * 128:(i + 1) * 128], k.ident.ap, tricn.ap, False, True,
                                 R=[k.ident, tricn], W=[PB[sb]])

                    def mrest(sb, lo=lo, hi=hi, Mt=Mt):
                        k.v(lambda e: e.tensor_scalar(out=Mt.ap[:, lo:hi], in0=k.bank(sb)[:, lo:hi], scalar1=0.0,
                                                      scalar2=None, op0=ALU.max), R=[PB[sb]], W=[Mt])

                    pipe.push(dict(qkfn=mqk, restfn=mrest))
                    for r in range(4):
                        ovr = k.bank(4 + r)[:, 0:260].rearrange("p (j w) -> p j w", j=4)
                        pv = []
                        for j in range(j0, j1):
                            start = (4 + r) not in started
                            started.add(4 + r)
                            pv.append((ovr[:, j, 0:65], 4 + r, j * 128, (j + 1) * 128, vb.ap[:, kt, :], start,
                                       kt == 4 * c + j, [vb]))
                        last = n == len(tl) - 1
                        st = dict(kT=kb.ap[0:68, kt * 128:(kt + 1) * 128],
                                  qT=QS[r].ap[0:68, c * 512 + lo:c * 512 + hi], lo=lo, hi=hi, Rqk=[QS[r], kb],
                                  extra=[], pv=pv, scale=1.0, kp=128, mask=(Mt, "dve"),
                                  evac=(acc_evac(r, g * 4 + r, 1, True)(c, None, ovr, [PB[4 + r]]) if last else None))
                        pipe.push(st)
            pipe.flush()
    S.barrier()
    A.top = mark


def outxa_phase(k, I, x1, osc, x3, kv):
    A, S, PB = k.A, k.S, k.pb
    mark = A.top
    wout = k.load_w(I["w_mix_out"], D, D, "wout", "w0")
    wq = k.load_w(I["xa_w_q"], D, D, "wq", "w1")
    wo = k.load_w(I["xa_w_o"], D, D, "wo", "w2")
    gmixpost = k.load_bcast(I["mix_post_g"], D, "gmp", "c0")
    gxapre = k.load_bcast(I["xa_pre_g"], D, "gxp", "c0")
    gxapost = k.load_bcast(I["xa_post_g"], D, "gxo", "c0")
    KxT, Vx = kv
    ones = A.alloc([128], BF16, "ones")
    k.g(lambda e: e.memset(ones.ap, 1.0), W=[ones])
    hn = [A.alloc([D], BF16, "hn%d" % i) for i in range(2)]
    junk = A.alloc([D], BF16, "junk")
    TB = 2

    tfc = {"n": 0}

    def to_fm(src_ap, src_R, dst, col0, nhn, Wb=None):
        Wb = [dst] if Wb is None else Wb
        tb = (TB, 7)[tfc["n"] % 2]
        tfc["n"] += 1
        pT = k.bank_bf(tb)
        for dc in range(8):
            k.tr(pT[:, dc * 128:(dc + 1) * 128], src_ap[:, dc * 128:(dc + 1) * 128], R=src_R, W=[PB[tb]])
        k.act(dst.ap[:, :, col0:col0 + 128], pT.rearrange("p (a b) -> p a b", a=8), AF.Copy, R=[PB[tb]], W=Wb)

    ot = [A.alloc([D], BF16, "ot%d" % i) for i in range(4)]
    oT = A.alloc([8, 512], BF16, "oT")
    x2c = [[A.alloc([D], F32, "x2c%d_%d" % (i, j)) for j in range(4)] for i in range(3)]
    yb = [A.alloc([D], F32, "yb%d" % i) for i in range(4)]
    yb2 = [A.alloc([D], F32, "yb2%d" % i) for i in range(4)]
    msA = [A.alloc([4], F32, "msA%d" % i) for i in range(2)]
    msB = [A.alloc([4], F32, "msB%d" % i) for i in range(2)]
    msC = [A.alloc([4], F32, "msC%d" % i) for i in range(2)]
    h3T = A.alloc([8, 512], BF16, "h3T")
    qxT = [A.alloc([8, 512], BF16, "qxT%d" % i) for i in range(2)]
    PT = [A.alloc([512], BF16, "PTx%d" % i) for i in range(4)]
    oxT = oT
    rLb = [A.alloc([512], F32, "rLb%d" % i) for i in range(2)]
    osct = osc.rearrange("(n p) d -> n p d", p=128)
    x1tt = x1.rearrange("(n p) d -> n p d", p=128)
    x3t = x3.rearrange("(n p) d -> n p d", p=128)
    SB = [3, 4, 5]
    print("outxa arena top", A.top)
    cnt = {"s": 0, "e": 0}

    oTb = [Buf("oTb%d" % i) for i in range(4)]

    def proj_tile(srcT, tt, w, dst):
        for half in range(2):
            for fc in range(8):
                k.mm(k.bank(half), srcT.ap[:, fc, tt * 128:(tt + 1) * 128], w.ap[:, fc, half * 512:(half + 1) * 512],
                     fc == 0, fc == 7, R=[oTb[tt], w], W=[PB[half]])
            k.v(lambda e, half=half: e.tensor_copy(out=dst.ap[:, half * 512:(half + 1) * 512], in_=k.bank(half)),
                R=[PB[half]], W=[dst])

    def front1(c):
        X = x2c[c % 3]
        for tt in range(4):
            k.dma("sp", ot[tt].ap, osct[4 * c + tt], W=[ot[tt]], sem="ot%d" % tt)
        for tt in range(4):
            k.dma("sp", X[tt].ap, x1tt[4 * c + tt], W=[X[tt]], sem="x2c%d_%d" % (c % 3, tt))
        tf = lambda tt: to_fm(ot[tt].ap, [ot[tt]], oT, tt * 128, tt % 2, Wb=[oTb[tt]])
        pj = lambda tt: proj_tile(oT, tt, wout, yb[tt])
        tf(0); tf(1); pj(0); tf(2); pj(1); tf(3); pj(2); pj(3)

    def front2a(c):
        X = x2c[c % 3]
        mA, mB = msA[c % 2], msB[c % 2]
        for tt in range(4):
            k.act(junk.ap, yb[tt].ap, AF.Square, R=[yb[tt]], W=[junk, mA], scale=1.0 / 32.0, accum_out=mA.ap[:, tt:tt + 1])
        k.rstd_chain(mA.ap, mA)
        for tt in range(4):
            k.v(lambda e, tt=tt: e.scalar_tensor_tensor(out=yb[tt].ap, in0=yb[tt].ap, scalar=mA.ap[:, tt:tt + 1],
                                                        in1=gmixpost.ap, op0=ALU.mult, op1=ALU.mult),
                R=[yb[tt], mA, gmixpost], W=[yb[tt]])
            k.g(lambda e, tt=tt: e.tensor_tensor(out=X[tt].ap, in0=X[tt].ap, in1=yb[tt].ap, op=ALU.add),
                R=[yb[tt], X[tt]], W=[X[tt]])

    def front2b(c):
        X = x2c[c % 3]
        mA, mB = msA[c % 2], msB[c % 2]
        for tt in range(4):
            k.act(junk.ap, X[tt].ap, AF.Square, R=[X[tt]], W=[junk, mB], scale=1.0 / 32.0, accum_out=mB.ap[:, tt:tt + 1])
        k.rstd_chain(mB.ap, mB)

    def mixed(c, cb):
        X = x2c[c % 3]
        mB = msB[c % 2]

        def stt(tt):
            hh = hn[tt % 2]
            k.v(lambda e: e.scalar_tensor_tensor(out=hh.ap, in0=X[tt].ap, scalar=mB.ap[:, tt:tt + 1],
                                                 in1=gxapre.ap, op0=ALU.mult, op1=ALU.mult),
                R=[X[tt], mB, gxapre], W=[hh])

        tf = lambda tt: to_fm(hn[tt % 2].ap, [hn[tt % 2]], h3T, tt * 128, tt % 2)
        pj = lambda tt: proj_tile(oxT, tt, wo, yb2[tt])
        stt(0); stt(1); tf(0); pj(0); stt(2); tf(1); pj(1); stt(3); tf(2); pj(2); tf(3); pj(3)

    def front2c(c):
        q_ = qxT[c % 2]
        for ft in range(8):
            bk = SB[ft % 3]
            for dc in range(8):
                k.mm(k.bank(bk), wq.ap[:, dc, ft * 128:(ft + 1) * 128], h3T.ap[:, dc, :], dc == 0, dc == 7,
                     R=[wq, h3T], W=[PB[bk]])
            if ft % 2 == 0:
                k.act(q_.ap[:, ft, :], k.bank(bk), AF.Copy, R=[PB[bk]], W=[q_])
            else:
                k.v(lambda e, ft=ft, bk=bk: e.tensor_copy(out=q_.ap[:, ft, :], in_=k.bank(bk)), R=[PB[bk]], W=[q_])

    def back1(c, heads):
        q_ = qxT[c % 2]

        def qk(hh):
            for mt in range(2):
                bk = (3, 4)[mt]
                for j in range(2):
                    k.mm(k.bank(bk), KxT.ap[:, hh * 2 + j, mt * 128:(mt + 1) * 128], q_.ap[:, hh * 2 + j, :],
                         j == 0, j == 1, R=[KxT, q_], W=[PB[bk]])
                pt = PT[(2 * hh + mt) % 4]
                k.act(pt.ap, k.bank(bk), AF.Exp, R=[PB[bk]], W=[pt], scale=1.0 / 16.0)

        def lo(hh):
            pts = [PT[(2 * hh + mt) % 4] for mt in range(2)]
            for mt in range(2):
                k.mm(k.bank(7), ones.ap, pts[mt].ap, mt == 0, mt == 1, R=[ones, pts[mt]], W=[PB[7]])
            for dvc in range(2):
                for mt in range(2):
                    k.mm(k.bank(5 + dvc), Vx.ap[:, mt, hh * 256 + dvc * 128:hh * 256 + (dvc + 1) * 128], pts[mt].ap,
                         mt == 0, mt == 1, R=[Vx, pts[mt]], W=[PB[5 + dvc]])
            rL = rLb[hh % 2]
            k.act(rL.ap, k.bank(7), AF.Ln, R=[PB[7]], W=[rL])
            k.act(rL.ap, rL.ap, AF.Exp, R=[rL], W=[rL], scale=-1.0)
            for dvc in range(2):
                k.v(lambda e, dvc=dvc: e.tensor_tensor(out=oxT.ap[:, hh * 2 + dvc, :], in0=k.bank(5 + dvc), in1=rL.ap,
                                                       op=ALU.mult), R=[PB[5 + dvc], rL], W=oTb)

        qk(heads[0])
        for n, hh in enumerate(heads):
            if n + 1 < len(heads):
                qk(heads[n + 1])
            lo(hh)

    def back2(c):
        X = x2c[c % 3]
        mC = msC[c % 2]
        for tt in range(4):
            proj_tile(oxT, tt, wo, yb2[tt])

    def back2b(c):
        X = x2c[c % 3]
        mC = msC[c % 2]
        for tt in range(4):
            k.act(junk.ap, yb2[tt].ap, AF.Square, R=[yb2[tt]], W=[junk, mC], scale=1.0 / 32.0, accum_out=mC.ap[:, tt:tt + 1])
        k.rstd_chain(mC.ap, mC)
        for tt in range(4):
            gt = 4 * c + tt
            k.v(lambda e, tt=tt: e.scalar_tensor_tensor(out=yb2[tt].ap, in0=yb2[tt].ap, scalar=mC.ap[:, tt:tt + 1],
                                                        in1=gxapost.ap, op0=ALU.mult, op1=ALU.mult),
                R=[yb2[tt], mC, gxapost], W=[yb2[tt]])
            k.g(lambda e, tt=tt: e.tensor_tensor(out=yb2[tt].ap, in0=yb2[tt].ap, in1=X[tt].ap, op=ALU.add),
                R=[yb2[tt], X[tt]], W=[yb2[tt]])
            k.dma("pool", x3t[gt], yb2[tt].ap, R=[yb2[tt]], sem="x3o%d" % tt)

    front1(0)
    front2a(0)
    front2b(0)
    for tt in range(4):
        k.v(lambda e, tt=tt: e.scalar_tensor_tensor(out=hn[tt % 2].ap, in0=x2c[0][tt].ap, scalar=msB[0].ap[:, tt:tt + 1],
                                                    in1=gxapre.ap, op0=ALU.mult, op1=ALU.mult),
            R=[x2c[0][tt], msB[0], gxapre], W=[hn[tt % 2]])
        to_fm(hn[tt % 2].ap, [hn[tt % 2]], h3T, tt * 128, tt % 2)
    front2c(0)
    for c in range(NCH):
        nxt = c + 1 < NCH
        if nxt:
            front1(c + 1)
            front2a(c + 1)
        back1(c, (0, 1, 2, 3))
        if nxt:
            front2b(c + 1)
            mixed(c + 1, c)
            front2c(c + 1)
        else:
            back2(c)
        back2b(c)
    S.barrier()
    A.top = mark


def build(stage=99, debug=False, stage_attn=3):
    nc = bass.Bass("TRN2", target_bir_lowering=False)
    dt = lambda name, shape, dtype, kind: nc.dram_tensor(name, list(shape), dtype, kind=kind).ap()
    I = {}
    for name, shape in IN_SHAPES.items():
        I[name] = dt(name, shape, F32, "ExternalInput")
    for name, (shape, dtype) in CONST_SHAPES.items():
        I[name] = dt(name, shape, dtype, "ExternalInput")
    skind = "ExternalOutput" if debug else "Internal"
    x1 = dt("x1", [T, D], F32, skind)
    zT = dt("zT", [MIXIN, T], BF16, skind)
    vtok = dt("vtok", [T, 768], BF16, skind)
    gts = dt("gts", [T, 24], F32, skind)
    osc = dt("osc", [T, D], BF16, skind)
    x3 = dt("x3", [T, D], F32, skind)
    out = dt("out", [T, D], F32, "ExternalOutput")
    with ExitStack() as st:
        k = KB(nc, st, debug)
        k.stage_attn = stage_attn
        k.ident = k.A.alloc([128], BF16, "ident")
        k.dma("sp", k.ident.ap, I["c_ident"], W=[k.ident], sem="c9")
        ffn_phase(k, I["x"], x1, I["ffn1_w_gate"], I["ffn1_w_up"], I["ffn1_w_down"],
                  I["ffn1_pre_g"], I["ffn1_post_g"], "f1")
        base = k.A.top
        kv = (k.A.alloc([8, 256], BF16, "KxT"), k.A.alloc([2, D], BF16, "Vx"))
        if stage >= 2:
            inproj_phase(k, x1, I, zT, vtok, gts, kv)
        k.rstd_mode = "ln"
        if stage >= 3:
            attn_phase(k, I, zT, vtok, gts, osc)
        if stage >= 4:
            outxa_phase(k, I, x1, osc, x3, kv)
        k.rstd_mode = "sqrt"
        k.A.top = base
        if stage >= 5:
            ffn_phase(k, x3, out, I["ffn2_w_gate"], I["ffn2_w_up"], I["ffn2_w_down"],
                      I["ffn2_pre_g"], I["ffn2_post_g"], "f2")
        k.S.barrier()
        k.S.emit()
        print("ops", k.S.nops, "sems", k.S.nsem, "arena peak", k.A.peak)
    return nc


IN_SHAPES = {
    "x": (T, D), "mem": (256, D),
    "ffn1_pre_g": (1, D), "ffn1_post_g": (1, D),
    "ffn1_w_gate": (D, DFF), "ffn1_w_up": (D, DFF), "ffn1_w_down": (DFF, D),
    "mix_pre_g": (1, D), "mix_post_g": (1, D), "w_mix_in": (D, MIXIN),
    "da_lambda_q1": (1, 64), "da_lambda_k1": (1, 64), "da_lambda_q2": (1, 64), "da_lambda_k2": (1, 64),
    "da_subln_g": (1, 128),
    "cmp_k_pe": (32, 64), "cmp_k_w1": (2048, 128), "cmp_k_w2": (128, 64),
    "cmp_v_pe": (32, 64), "cmp_v_w1": (2048, 128), "cmp_v_w2": (128, 64),
    "w_mix_out": (D, D),
    "xa_pre_g": (1, D), "xa_post_g": (1, D), "mem_norm_g": (1, D),
    "xa_w_q": (D, D), "xa_w_k": (D, D), "xa_w_v": (D, D), "xa_w_o": (D, D),
    "ffn2_pre_g": (1, D), "ffn2_post_g": (1, D),
    "ffn2_w_gate": (D, DFF), "ffn2_w_up": (D, DFF), "ffn2_w_down": (DFF, D),
}
PER_CORE = ("x", "mem")

CONST_SHAPES = {
    "c_ident": ((128, 128), BF16),
    "c_tric": ((128, 128), BF16),
    "c_tril": ((128, 128), BF16),
    "c_tricn": ((128, 128), BF16),
    "c_cmask": ((128, 2560), BF16),
    "c_expand": ((128, 4096), BF16),
    "c_selA": ((T, 64), F32),
    "c_selB": ((T, 64), F32),
    "c_mcs": ((256, 64), BF16),
    "c_qaug": ((12, 4, T), BF16),
    "c_kaug": ((4, T), BF16),
    "c_kaugc": ((4, 256), BF16),
}


def make_consts():
    bf = ml_dtypes.bfloat16
    c = {}
    c["c_ident"] = np.eye(128, dtype=np.float32).astype(bf)
    kk = np.arange(128)[:, None]
    qq = np.arange(128)[None, :]
    c["c_tric"] = np.where(kk <= qq, 0.0, NEGBIG).astype(np.float32).astype(bf)
    c["c_tril"] = np.where(kk > qq, 0.0, NEGBIG).astype(np.float32).astype(bf)
    c["c_tricn"] = np.where(kk <= qq, 0.0, -1.0).astype(np.float32).astype(bf)
    tt = np.arange(2560)[None, :]
    c["c_cmask"] = np.where(tt - 16 * kk >= 31, 0.0, NEGBIG).astype(np.float32).astype(bf)
    ex = np.zeros((128, T), np.float32)
    ex[:64] = (np.arange(T)[None, :] // 64 == np.arange(64)[:, None])
    c["c_expand"] = ex.astype(bf)
    t = np.arange(T)
    cur = (t // 64)[:, None]
    blk = np.arange(64)[None, :]
    Am = (blk <= cur).astype(np.float32)
    forced = ((blk == 0) | (blk == cur) | (blk == cur - 1)).astype(np.float32)
    c["c_selA"] = Am
    c["c_selB"] = (1e4 * forced - (1.0 - Am)).astype(np.float32)
    cs = np.arange(255) * 16
    ss = np.arange(64) * 64
    ov = np.clip(np.minimum(cs[:, None] + 32, ss[None, :] + 64) - np.maximum(cs[:, None], ss[None, :]), 0, None)
    mcs = np.zeros((256, 64), np.float32)
    mcs[:255] = ov / 32.0
    c["c_mcs"] = mcs.astype(bf)
    a = (t // 64).astype(np.float32)
    b = (t % 64).astype(np.float32)
    slopes = list(2.0 ** (-8.0 * np.arange(1, 5) / 4)) + list(2.0 ** (-8.0 * np.arange(1, 9) / 8))
    qa = np.zeros((12, 4, T), np.float32)
    for s, sl in enumerate(slopes):
        qa[s, 0] = -sl * 64 * a
        qa[s, 1] = -sl * b
        qa[s, 2] = sl
        qa[s, 3] = sl
    c["c_qaug"] = qa.astype(bf)
    ka = np.stack([np.ones(T), np.ones(T), 64 * a, b]).astype(np.float32)
    c["c_kaug"] = ka.astype(bf)
    pc = np.arange(256) * 16 + 31
    kc = np.stack([np.ones(256), np.ones(256), 64.0 * (pc // 64), 1.0 * (pc % 64)]).astype(np.float32)
    kc[:, 255] = 0
    c["c_kaugc"] = kc.astype(bf)
    return c


_CACHE = {}


def kernel(**inputs):
    n = 8
    if "nc" not in _CACHE:
        _CACHE["nc"] = build()
    nc = _CACHE["nc"]
    consts = make_consts()
    in_maps = []
    for i in range(n):
        m = {}
        for name, shape in IN_SHAPES.items():
            a = np.asarray(inputs[name], dtype=np.float32)
            a = a[i] if name in PER_CORE else a[0]
            m[name] = np.ascontiguousarray(a.reshape(shape))
        m.update(consts)
        in_maps.append(m)
    res = run_bass_kernel_spmd(nc, in_maps, core_ids=list(range(n)))
    return np.stack([r["out"] for r in res.results], axis=0).astype(np.float32)
```

```python
import math
from contextlib import ExitStack

import numpy as np
import ml_dtypes

import concourse.bass as bass
import concourse.mybir as mybir
from concourse.bass_utils import run_bass_kernel_spmd

F32 = mybir.dt.float32
BF16 = mybir.dt.bfloat16
AF = mybir.ActivationFunctionType
ALU = mybir.AluOpType
AX = mybir.AxisListType

SEM_LIMIT = 30000
DT_SIZE = {F32: 4, BF16: 2}

T = 4096
D = 1024
DFF = 2816
NT = T // 128
NCH = T // 512
MIXIN = 2840
NEGBIG = -30000.0
EPS = 1e-6


class Buf:
    __slots__ = ("name", "w", "r")

    def __init__(self, name=""):
        self.name = name
        self.w = None
        self.r = {}


class Tile:
    __slots__ = ("ap", "b")

    def __init__(self, ap, name=""):
        self.ap = ap
        self.b = Buf(name)

    def __getitem__(self, k):
        return self.ap[k]


class Sched:
    ENG = ("pe", "act", "dve", "pool", "sp")

    def __init__(self, nc, stack):
        self.nc = nc
        self.stack = stack
        self.q = {e: [] for e in self.ENG}
        self.cnt = {}
        self.sems = {}
        self.seen = {e: {} for e in self.ENG}
        self.nsem = 0
        self.nops = 0

    def _sem(self, key):
        if key not in self.sems:
            self.sems[key] = self.stack.enter_context(self.nc.semaphore("s%d" % self.nsem))
            self.nsem += 1
        return self.sems[key]

    def _bump(self, base, inc):
        ep, v = self.cnt.get(base, (0, 0))
        if v + inc > SEM_LIMIT:
            ep, v = ep + 1, 0
        v += inc
        self.cnt[base] = (ep, v)
        key = (base, ep)
        self._sem(key)
        return key, v

    def op(self, eng, fn, reads=(), writes=(), dma=None, skip_same=False):
        reads = [t.b if isinstance(t, Tile) else t for t in reads]
        writes = [t.b if isinstance(t, Tile) else t for t in writes]
        deps = []
        for b in reads:
            if b.w is not None:
                deps.append(b.w)
        for b in writes:
            if b.w is not None:
                deps.append(b.w)
            deps.extend(b.r.items())
        if dma is not None:
            dma = (dma, eng)
            ep, v = self.cnt.get(dma, (0, 0))
            if v > 0:
                deps.append(((dma, ep), v))
        waits = {}
        seen = self.seen[eng]
        for key, v in deps:
            if skip_same and key[0] == eng:
                continue
            if seen.get(key, 0) >= v:
                continue
            if waits.get(key, 0) < v:
                waits[key] = v
        for key, v in waits.items():
            seen[key] = v
        if dma is None:
            key, v = self._bump(eng, 1)
            inc = 1
        else:
            key, v = self._bump(dma, 16)
            inc = 16
        self.q[eng].append((list(waits.items()), fn, key, inc))
        self.nops += 1
        ev = (key, v)
        for b in reads:
            if b.r.get(key, 0) < v:
                b.r[key] = v
        for b in writes:
            b.w = ev
            b.r = {}
        return ev

    def barrier(self):
        allv = [((base, ep), v) for base, (ep, v) in self.cnt.items() if v > 0]
        for e in self.ENG:
            waits = []
            for key, v in allv:
                if self.seen[e].get(key, 0) < v:
                    waits.append((key, v))
                    self.seen[e][key] = v
            if waits:
                self.q[e].append((waits, None, None, 0))

    def emit(self):
        nc = self.nc
        sems = self.sems
        q = self.q

        def replay(name, e):
            for waits, fn, key, inc in q[name]:
                for k, v in waits:
                    e.wait_ge(sems[k], v)
                if fn is not None:
                    fn(e).then_inc(sems[key], inc)

        with nc.Block() as block:
            @block.tensor
            def _(e):
                replay("pe", e)

            @block.scalar
            def _(e):
                replay("act", e)

            @block.vector
            def _(e):
                replay("dve", e)

            @block.gpsimd
            def _(e):
                replay("pool", e)

            @block.sync
            def _(e):
                replay("sp", e)


class Arena:
    def __init__(self, nc, stack, nbytes):
        self.t = stack.enter_context(nc.sbuf_tensor("arena", [128, nbytes // 4], F32))
        self.top = 0
        self.cap = nbytes
        self.peak = 0

    def alloc(self, shape, dtype, name=""):
        n = int(np.prod(shape)) * DT_SIZE[dtype]
        n4 = (n + 3) // 4
        off = self.top // 4
        self.top += n4 * 4
        self.peak = max(self.peak, self.top)
        assert self.top <= self.cap, ("SBUF arena overflow", name, self.top, self.cap)
        ap = self.t[:, off:off + n4]
        if dtype != F32:
            ap = ap.bitcast(dtype)
            ap = ap[:, 0:int(np.prod(shape))]
        if len(shape) == 2:
            ap = ap.rearrange("p (a b) -> p a b", a=shape[0])
        elif len(shape) == 3:
            ap = ap.rearrange("p (a b c) -> p a b c", a=shape[0], b=shape[1])
        return Tile(ap, name)


class KB:
    def __init__(self, nc, st, debug):
        self.nc = nc
        self.S = Sched(nc, st)
        self.A = Arena(nc, st, 212000)
        self.psum = st.enter_context(nc.psum_tensor("psum", [128, 4096], F32))
        self.pb = [Buf("bank%d" % i) for i in range(8)]
        self.debug = debug

    def bank(self, i, n=1):
        return self.psum[:, i * 512:(i + n) * 512]

    def bank_bf(self, i):
        return self.psum[:, i * 512:(i + 1) * 512].bitcast(BF16)

    def dma(self, q, out, in_, R=(), W=(), sem=None):
        return self.S.op(q, lambda e: e.dma_start(out=out, in_=in_), reads=R, writes=W, dma=sem)

    def mm(self, out, lhsT, rhs, start, stop, R=(), W=()):
        return self.S.op("pe", lambda e: e.matmul(out, lhsT=lhsT, rhs=rhs, start=start, stop=stop,
                                                  skip_group_check=True),
                         reads=R, writes=W, skip_same=True)

    def tr(self, out, in_, R=(), W=()):
        ident = self.ident
        return self.S.op("pe", lambda e: e.transpose(out=out, in_=in_, identity=ident.ap),
                         reads=list(R) + [ident], writes=W, skip_same=True)

    def act(self, out, in_, func, R=(), W=(), **kw):
        return self.S.op("act", lambda e: e.activation(out=out, in_=in_, func=func, **kw), reads=R, writes=W)

    def v(self, fn, R=(), W=()):
        return self.S.op("dve", fn, reads=R, writes=W)

    def g(self, fn, R=(), W=()):
        return self.S.op("pool", fn, reads=R, writes=W)

    rstd_mode = "sqrt"

    def rstd_chain(self, ms_ap, t):
        self.v(lambda e: e.tensor_scalar(out=ms_ap, in0=ms_ap, scalar1=EPS, scalar2=None, op0=ALU.add), R=[t], W=[t])
        if self.rstd_mode == "ln":
            self.act(ms_ap, ms_ap, AF.Ln, R=[t], W=[t])
            self.act(ms_ap, ms_ap, AF.Exp, R=[t], W=[t], scale=-0.5)
        else:
            self.act(ms_ap, ms_ap, AF.Sqrt, R=[t], W=[t])
            self.v(lambda e: e.reciprocal(out=ms_ap, in_=ms_ap), R=[t], W=[t])

    def load_bcast(self, dram_row, n, name, sem):
        t = self.A.alloc([n], F32, name)
        self.dma("sp", t.ap, dram_row.partition_broadcast(128), W=[t], sem=sem)
        return t

    def load_w_groups(self, dram_w, rows, cols, name, sem, gcols):
        nch = rows // 128
        t = self.A.alloc([nch, cols], BF16, name)
        src = dram_w.rearrange("(c p) n -> p c n", p=128)
        bufs = []
        for g0 in range(0, cols, gcols):
            g1 = min(cols, g0 + gcols)
            b = Buf("%s_g%d" % (name, g0))
            bufs.append(b)
            self.dma("pool", t.ap[:, :, g0:g1], src[:, :, g0:g1], W=[b], sem=sem)
        return t, (lambda col: bufs[col // gcols])

    def load_w(self, dram_w, rows, cols, name, sem, defer=False):
        nch = rows // 128
        t = self.A.alloc([nch, cols], BF16, name)
        if defer:
            return t, (lambda: self._issue_w(t, dram_w, nch, sem))
        self._issue_w(t, dram_w, nch, sem)
        return t

    def _issue_w(self, t, dram_w, nch, sem):
        src = dram_w.rearrange("(c p) n -> p c n", p=128)
        step = max(1, nch // 4)
        for c0 in range(0, nch, step):
            c1 = min(nch, c0 + step)
            self.dma("pool", t.ap[:, c0:c1, :], src[:, c0:c1, :], W=[t], sem=sem)


def ffn_phase(k, src, dst, wg_d, wu_d, wd_d, gpre_d, gpost_d, tag):
    A, S = k.A, k.S
    mark = A.top
    GC = 640
    nchw = D // 128
    wg = A.alloc([nchw, DFF], BF16, "wg")
    wu = A.alloc([nchw, DFF], BF16, "wu")
    wgb, wub = [], []
    wg0b, wu0b = Buf("wg0"), Buf("wu0")
    for (t_, d_, b0, sm) in ((wg, wg_d, wg0b, "w0"), (wu, wu_d, wu0b, "w1")):
        k.dma("pool", t_.ap[:, :, 0:128], d_.rearrange("(c p) n -> p c n", p=128)[:, :, 0:128], W=[b0], sem=sm)
    for g0 in range(0, DFF, GC):
        for (t_, d_, bl, sm) in ((wg, wg_d, wgb, "w0"), (wu, wu_d, wub, "w1")):
            b = Buf("wgrp")
            bl.append(b)
            g1 = min(DFF, g0 + GC)
            ga = 128 if g0 == 0 else g0
            k.dma("pool", t_.ap[:, :, ga:g1], d_.rearrange("(c p) n -> p c n", p=128)[:, :, ga:g1],
                  W=[b], sem=sm)
    wd = k.load_w(wd_d, DFF, D, "wd", "w2")
    gpre = k.load_bcast(gpre_d, D, "gpre", "c0")
    gpost = k.load_bcast(gpost_d, D, "gpost", "c1")
    k.v(lambda e: e.tensor_scalar(out=gpost.ap, in0=gpost.ap, scalar1=0.5, scalar2=None, op0=ALU.mult),
        R=[gpost], W=[gpost])
    xs = [A.alloc([D], F32, "xs%d" % i) for i in range(2)]
    xr = [A.alloc([D], F32, "xr%d" % i) for i in range(2)]
    hn = [A.alloc([D], BF16, "hn%d" % i) for i in range(2)]
    hT = [A.alloc([8, 512], BF16, "hT%d" % i) for i in range(2)]
    AT = A.alloc([22, 512], BF16, "AT")
    sg = [A.alloc([512], BF16, "sg%d" % i) for i in range(2)]
    ytmp = A.alloc([D], F32, "ytmp")
    junk = A.alloc([D], BF16, "junk")
    ms = [A.alloc([4], F32, "ms%d" % i) for i in range(2)]
    ms2 = [A.alloc([1], F32, "ms2%d" % i) for i in range(2)]
    srct = src.rearrange("(n p) d -> n p d", p=128)
    dstt = dst.rearrange("(n p) d -> n p d", p=128)
    PB = k.pb
    cnt = {"x": 0, "r": 0}

    def pn_chain(c, tt):
        m = ms[c % 2]
        gt = 4 * c + tt
        i = tt % 2
        x = xs[i]
        k.dma("sp", x.ap, srct[gt], W=[x], sem="xs%d" % i)
        hh = hn[i]
        k.act(hh.ap, x.ap, AF.Square, R=[x], W=[hh, m], scale=1.0 / 32.0, accum_out=m.ap[:, tt:tt + 1])
        k.rstd_chain(m.ap[:, tt:tt + 1], m)
        k.v(lambda e: e.scalar_tensor_tensor(out=hh.ap, in0=x.ap, scalar=m.ap[:, tt:tt + 1], in1=gpre.ap,
                                             op0=ALU.mult, op1=ALU.mult), R=[x, m, gpre], W=[hh])

    def pn_tr(c, tt):
        h = hT[c % 2]
        hh = hn[tt % 2]
        pT = k.bank_bf(6)
        for dc in range(8):
            k.tr(pT[:, dc * 128:(dc + 1) * 128], hh.ap[:, dc * 128:(dc + 1) * 128], R=[hh], W=[PB[6]])
        k.act(h.ap[:, :, tt * 128:(tt + 1) * 128], pT.rearrange("p (a b) -> p a b", a=8), AF.Copy,
              R=[PB[6]], W=[h])

    def prenorm(c):
        for tt in range(4):
            pn_chain(c, tt)
            pn_tr(c, tt)

    def gateup(c):
        h = hT[c % 2]
        for f in range(22):
            gb, ub = f % 2, 2 + f % 2
            for dc in range(8):
                k.mm(k.bank(gb), wg.ap[:, dc, f * 128:(f + 1) * 128], h.ap[:, dc, :], dc == 0, dc == 7,
                     R=[wg0b if f == 0 else wgb[f * 128 // GC], h], W=[PB[gb]])
            for dc in range(8):
                k.mm(k.bank(ub), wu.ap[:, dc, f * 128:(f + 1) * 128], h.ap[:, dc, :], dc == 0, dc == 7,
                     R=[wu0b if f == 0 else wub[f * 128 // GC], h], W=[PB[ub]])
            s = sg[f % 2]
            k.act(s.ap, k.bank(gb), AF.Silu, R=[PB[gb]], W=[s])
            k.v(lambda e, s=s, ub=ub, f=f: e.tensor_tensor(out=AT.ap[:, f, :], in0=s.ap, in1=k.bank(ub), op=ALU.mult),
                R=[s, PB[ub]], W=[AT])
            if c + 1 < NCH:
                if f in (1, 6, 11, 16):
                    pn_chain(c + 1, (f - 1) // 5)
                if f in (4, 9, 14, 19):
                    pn_tr(c + 1, (f - 4) // 5)

    def down(c):
        for tt in range(4):
            gt = 4 * c + tt
            i = cnt["r"] % 2
            cnt["r"] += 1
            r = xr[i]
            k.dma("sp", r.ap, srct[gt], W=[r], sem="xr%d" % i)
            yb0 = 4 + 2 * (tt % 2)
            for half in range(2):
                for f in range(22):
                    k.mm(k.bank(yb0 + half), AT.ap[:, f, tt * 128:(tt + 1) * 128],
                         wd.ap[:, f, half * 512:(half + 1) * 512], f == 0, f == 21,
                         R=[AT, wd], W=[PB[yb0 + half]])
            m2 = ms2[i]
            k.v(lambda e, yb0=yb0: e.tensor_copy(out=ytmp.ap, in_=k.bank(yb0, 2)), R=[PB[yb0], PB[yb0 + 1]], W=[ytmp])
            k.act(junk.ap, ytmp.ap, AF.Square, R=[ytmp], W=[junk, m2], scale=1.0 / 32.0, accum_out=m2.ap)
            k.rstd_chain(m2.ap, m2)
            k.v(lambda e, m2=m2: e.scalar_tensor_tensor(out=ytmp.ap, in0=ytmp.ap, scalar=m2.ap, in1=gpost.ap,
                                                        op0=ALU.mult, op1=ALU.mult),
                R=[m2, gpost, ytmp], W=[ytmp])
            k.g(lambda e, r=r: e.tensor_tensor(out=r.ap, in0=r.ap, in1=ytmp.ap, op=ALU.add), R=[r, ytmp], W=[r])
            k.dma("pool", dstt[gt], r.ap, R=[r], sem="xo%d" % i)

    prenorm(0)
    for c in range(NCH):
        gateup(c)
        down(c)
    S.barrier()
    A.top = mark


def make_prenorm(k, gpre, nx=2):
    A = k.A
    xs = [A.alloc([D], F32, "pxs%d" % i) for i in range(nx)]
    hn = [A.alloc([D], BF16, "phn%d" % i) for i in range(2)]
    hT = [A.alloc([8, 512], BF16, "phT%d" % i) for i in range(2)]
    ms = [A.alloc([4], F32, "pms%d" % i) for i in range(2)]
    cnt = {"x": 0}
    PB = k.pb

    def norm_tile(x, m, tt, h, g, bank=6):
        i = cnt["x"] % 2
        cnt["x"] += 1
        hh = hn[i]
        k.act(hh.ap, x.ap if isinstance(x, Tile) else x[0], AF.Square, R=[x if isinstance(x, Tile) else x[1]],
              W=[hh, m], scale=1.0 / 32.0, accum_out=m.ap[:, tt:tt + 1])
        k.rstd_chain(m.ap[:, tt:tt + 1], m)
        xa = x.ap if isinstance(x, Tile) else x[0]
        xt = x if isinstance(x, Tile) else x[1]
        k.v(lambda e: e.scalar_tensor_tensor(out=hh.ap, in0=xa, scalar=m.ap[:, tt:tt + 1], in1=g.ap,
                                             op0=ALU.mult, op1=ALU.mult), R=[xt, m, g], W=[hh])
        pT = k.bank_bf(bank)
        for dc in range(8):
            k.tr(pT[:, dc * 128:(dc + 1) * 128], hh.ap[:, dc * 128:(dc + 1) * 128], R=[hh], W=[PB[bank]])
        k.act(h.ap[:, :, tt * 128:(tt + 1) * 128], pT.rearrange("p (a b) -> p a b", a=8), AF.Copy,
              R=[PB[bank]], W=[h])

    def prenorm(c, srct):
        h = hT[c % 2]
        m = ms[c % 2]
        for tt in range(4):
            gt = 4 * c + tt
            i = cnt["x"] % nx
            x = xs[i]
            k.dma("sp", x.ap, srct[gt], W=[x], sem="pxs%d" % i)
            norm_tile(x, m, tt, h, gpre)
        return h

    def chain(c, tt, srct):
        m = ms[c % 2]
        gt = 4 * c + tt
        x = xs[tt % nx]
        k.dma("sp", x.ap, srct[gt], W=[x], sem="pxs%d" % (tt % nx))
        hh = hn[tt % 2]
        k.act(hh.ap, x.ap, AF.Square, R=[x], W=[hh, m], scale=1.0 / 32.0, accum_out=m.ap[:, tt:tt + 1])
        k.rstd_chain(m.ap[:, tt:tt + 1], m)
        k.v(lambda e: e.scalar_tensor_tensor(out=hh.ap, in0=x.ap, scalar=m.ap[:, tt:tt + 1], in1=gpre.ap,
                                             op0=ALU.mult, op1=ALU.mult), R=[x, m, gpre], W=[hh])

    def tr(c, tt, bank=6):
        h = hT[c % 2]
        hh = hn[tt % 2]
        pT = k.bank_bf(bank)
        for dc in range(8):
            k.tr(pT[:, dc * 128:(dc + 1) * 128], hh.ap[:, dc * 128:(dc + 1) * 128], R=[hh], W=[PB[bank]])
        k.act(h.ap[:, :, tt * 128:(tt + 1) * 128], pT.rearrange("p (a b) -> p a b", a=8), AF.Copy,
              R=[PB[bank]], W=[h])
        return h

    prenorm.chain = chain
    prenorm.tr = tr
    prenorm.norm_tile = norm_tile
    prenorm.hT = hT
    prenorm.ms = ms
    return prenorm


FM = [(0, 4, 0.125), (512, 4, 1.0), (1536, 4, 0.125), (2048, 1, 1.0), (2176, 1, 1.0), (2304, 1, 1.0), (2560, 1, 1.0)]


def inproj_phase(k, x1, I, zT, vtok, gts, kv):
    A, S, PB = k.A, k.S, k.pb
    mark = A.top
    gpre = k.load_bcast(I["mix_pre_g"], D, "gpre", "c0")
    win, wgrp = k.load_w_groups(I["w_mix_in"], D, MIXIN, "win", "w0", 512)
    prenorm = make_prenorm(k, gpre, nx=4)
    stg = [A.alloc([512], BF16, "stg%d" % i) for i in range(3)]
    vst = [A.alloc([768], BF16, "vst%d" % i) for i in range(2)]
    gst = [A.alloc([24], F32, "gst%d" % i) for i in range(2)]
    srct = x1.rearrange("(n p) d -> n p d", p=128)
    vtokt = vtok.rearrange("(n p) d -> n p d", p=128)
    gtst = gts.rearrange("(n p) d -> n p d", p=128)
    KxT, Vx = kv
    gmem = k.load_bcast(I["mem_norm_g"], D, "gmem", "c0")
    wk = A.alloc([D // 128, D], BF16, "wk")
    wv = A.alloc([D // 128, D], BF16, "wv")
    kvp = {"n": 0}

    def kv_piece():
        n = kvp["n"]
        if n >= 16:
            return
        kvp["n"] += 1
        t_, d_, sm = (wk, I["xa_w_k"], "w3") if n < 8 else (wv, I["xa_w_v"], "w4")
        c0 = n % 8
        k.dma("pool", t_.ap[:, c0:c0 + 1, :], d_.rearrange("(c p) n -> p c n", p=128)[:, c0:c0 + 1, :],
              W=[t_], sem=sm)
    mT = A.alloc([8, 256], BF16, "mT")
    msm = A.alloc([2], F32, "msm")
    memt = I["mem"].rearrange("(n p) d -> n p d", p=128)
    xm = [A.alloc([D], F32, "xm%d" % i) for i in range(2)]
    mhn = [A.alloc([D], BF16, "mhn%d" % i) for i in range(2)]
    for mt in range(2):
        k.dma("sp", xm[mt].ap, memt[mt], W=[xm[mt]], sem="c1")

    def kv_setup():
        for mt in range(2):
            k.act(mhn[mt].ap, xm[mt].ap, AF.Square, R=[xm[mt]], W=[mhn[mt], msm], scale=1.0 / 32.0,
                  accum_out=msm.ap[:, mt:mt + 1])
        k.rstd_chain(msm.ap, msm)
        for mt in range(2):
            k.v(lambda e, mt=mt: e.scalar_tensor_tensor(out=mhn[mt].ap, in0=xm[mt].ap, scalar=msm.ap[:, mt:mt + 1],
                                                        in1=gmem.ap, op0=ALU.mult, op1=ALU.mult),
                R=[xm[mt], msm, gmem], W=[mhn[mt]])
            pT = k.bank_bf(7)
            for dc in range(8):
                k.tr(pT[:, dc * 128:(dc + 1) * 128], mhn[mt].ap[:, dc * 128:(dc + 1) * 128], R=[mhn[mt]], W=[PB[7]])
            k.act(mT.ap[:, :, mt * 128:(mt + 1) * 128], pT.rearrange("p (a b) -> p a b", a=8), AF.Copy,
                  R=[PB[7]], W=[mT])
        for ft in range(8):
            bk = 4 + ft % 2
            for dc in range(8):
                k.mm(k.bank(bk)[:, 0:256], wk.ap[:, dc, ft * 128:(ft + 1) * 128], mT.ap[:, dc, :], dc == 0, dc == 7,
                     R=[wk, mT], W=[PB[bk]])
            k.act(KxT.ap[:, ft, :], k.bank(bk)[:, 0:256], AF.Copy, R=[PB[bk]], W=[KxT])
        for mt in range(2):
            for half in range(2):
                bk = 4 + half
                for dc in range(8):
                    k.mm(k.bank(bk), mT.ap[:, dc, mt * 128:(mt + 1) * 128], wv.ap[:, dc, half * 512:(half + 1) * 512],
                         dc == 0, dc == 7, R=[wv, mT], W=[PB[bk]])
                k.v(lambda e, mt=mt, half=half, bk=bk: e.tensor_copy(
                    out=Vx.ap[:, mt, half * 512:(half + 1) * 512], in_=k.bank(bk)), R=[PB[bk]], W=[Vx])

    hnext = prenorm(0, srct)
    for c in range(NCH):
        h = hnext
        i = 0
        for (z0, ntile, sc) in FM:
            for ft in range(ntile):
                bk = i % 2
                col = z0 + ft * 128
                for dc in range(8):
                    k.mm(k.bank(bk), win.ap[:, dc, col:col + 128], h.ap[:, dc, :], dc == 0, dc == 7,
                         R=[wgrp(col), h], W=[PB[bk]])
                s = stg[i % 3]
                if sc != 1.0:
                    k.v(lambda e, s=s, bk=bk, sc=sc: e.tensor_scalar(out=s.ap, in0=k.bank(bk), scalar1=sc, scalar2=None,
                                                                     op0=ALU.mult), R=[PB[bk]], W=[s])
                else:
                    k.act(s.ap, k.bank(bk), AF.Copy, R=[PB[bk]], W=[s])
                k.dma("pool" if sc != 1.0 else "act", zT[col:col + 128, c * 512:(c + 1) * 512], s.ap, R=[s],
                      sem="stg%d" % (i % 3))
                if c < 2 and i % 2 == 1:
                    kv_piece()
                if c + 1 < NCH:
                    if i in (1, 5, 9, 13):
                        prenorm.chain(c + 1, (i - 1) // 4, srct)
                    if i in (3, 7, 11, 15):
                        hnext = prenorm.tr(c + 1, (i - 3) // 4)
                i += 1
        for tt in range(4):
            gt = 4 * c + tt
            hs = h.ap[:, :, tt * 128:(tt + 1) * 128]
            for dc in range(8):
                k.mm(k.bank(2), hs[:, dc, :], win.ap[:, dc, 1024:1536], dc == 0, dc == 7,
                     R=[wgrp(1024), h], W=[PB[2]])
            for (o0, z0, n) in ((0, 2432, 128), (128, 2688, 128), (256, 2816, 24)):
                for dc in range(8):
                    k.mm(k.bank(3)[:, o0:o0 + n], hs[:, dc, :], win.ap[:, dc, z0:z0 + n], dc == 0, dc == 7,
                         R=[wgrp(z0), h], W=[PB[3]])
            vs = vst[gt % 2]
            k.act(vs.ap[:, 0:512], k.bank(2), AF.Copy, R=[PB[2]], W=[vs])
            k.v(lambda e, vs=vs: e.tensor_copy(out=vs.ap[:, 512:768], in_=k.bank(3)[:, 0:256]), R=[PB[3]], W=[vs])
            k.dma("pool", vtokt[gt], vs.ap, R=[vs], sem="vst%d" % (gt % 2))
            gs = gst[gt % 2]
            k.v(lambda e, gs=gs: e.tensor_copy(out=gs.ap, in_=k.bank(3)[:, 256:280]), R=[PB[3]], W=[gs])
            k.dma("pool", gtst[gt], gs.ap, R=[gs], sem="gst%d" % (gt % 2))
        if c == 3:
            while kvp["n"] < 16:
                kv_piece()
            kv_setup()
    S.barrier()
    A.top = mark


class AttnPipe:
    def __init__(self, k, PT):
        self.k = k
        self.PT = PT
        self.n = 0
        self.pending = None
        self.obank_first = {}

    def _qk(self, st):
        k = self.k
        sb = st["sb"]
        lo, hi = st["lo"], st["hi"]
        out = k.bank(sb)[:, lo:hi]
        ex = st["extra"]
        k.mm(out, st["kT"], st["qT"], True, len(ex) == 0, R=st["Rqk"], W=[k.pb[sb]])
        for n, (lhsT, rhs, a, b, R) in enumerate(ex):
            k.mm(k.bank(sb)[:, a:b], lhsT, rhs, False, n == len(ex) - 1, R=R, W=[k.pb[sb]])

    def _rest(self, st):
        k = self.k
        sb = st["sb"]
        if "restfn" in st:
            st["restfn"](sb)
            return
        lo, hi = st["lo"], st["hi"]
        pt = st["pt"]
        k.act(pt.ap[0:st["kp"], lo:hi], k.bank(sb)[0:st["kp"], lo:hi], AF.Exp, R=[k.pb[sb]], W=[pt], scale=st["scale"])
        if st.get("mask") is not None:
            mt, eng = st["mask"]
            k.S.op(eng, lambda e: e.tensor_tensor(out=pt.ap[:, lo:hi], in0=pt.ap[:, lo:hi], in1=mt.ap[:, lo:hi],
                                                  op=ALU.mult), reads=[pt, mt], writes=[pt])
        for (oreg, ob, lhs_lo, lhs_hi, rhs, start, stop, R) in st["pv"]:
            k.mm(oreg, pt.ap[0:st["kp"], lhs_lo:lhs_hi], rhs, start, stop, R=[pt] + R, W=[k.pb[ob]])
        if st.get("evac") is not None:
            st["evac"]()

    def push(self, st):
        st["sb"] = self.n % 4
        st["pt"] = self.PT[self.n % len(self.PT)]
        self.n += 1
        if "qkfn" in st:
            st["qkfn"](st["sb"])
        else:
            self._qk(st)
        if self.pending is None:
            self.pending = []
        self.pending.append(st)
        if len(self.pending) > 3:
            self._rest(self.pending.pop(0))

    def flush(self):
        while self.pending:
            self._rest(self.pending.pop(0))


def attn_phase(k, I, zT, vtok, gts, osc):
    A, S, PB = k.A, k.S, k.pb
    mark = A.top
    deferred = []

    def cload(name, shape, dtype, src, q="sp", defer=False):
        t = A.alloc(shape, dtype, name)
        if defer:
            deferred.append(lambda: k.dma(q, t.ap, src, W=[t], sem="c0"))
        else:
            k.dma(q, t.ap, src, W=[t], sem="c0")
        return t
    tric = cload("tric", [128], BF16, I["c_tric"])
    tril = cload("tril", [128], BF16, I["c_tril"])
    gates = cload("gates", [NT, 24], F32, gts.rearrange("(n p) j -> p n j", p=128))
    k.act(gates.ap, gates.ap, AF.Exp, R=[gates], W=[gates], scale=-1.0)
    k.v(lambda e: e.tensor_scalar(out=gates.ap, in0=gates.ap, scalar1=1.0, scalar2=None, op0=ALU.add), R=[gates], W=[gates])
    k.v(lambda e: e.reciprocal(out=gates.ap, in_=gates.ap), R=[gates], W=[gates])
    lq1 = k.load_bcast(I["da_lambda_q1"], 64, "lq1", "c1")
    lk1 = k.load_bcast(I["da_lambda_k1"], 64, "lk1", "c1")
    lq2 = k.load_bcast(I["da_lambda_q2"], 64, "lq2", "c1")
    lk2 = k.load_bcast(I["da_lambda_k2"], 64, "lk2", "c1")
    lsum = A.alloc([2], F32, "lsum")
    neglam = A.alloc([1], F32, "neglam")
    k.v(lambda e: e.tensor_tensor(out=lq1.ap, in0=lq1.ap, in1=lk1.ap, op=ALU.mult), R=[lq1, lk1], W=[lq1])
    k.v(lambda e: e.tensor_tensor(out=lq2.ap, in0=lq2.ap, in1=lk2.ap, op=ALU.mult), R=[lq2, lk2], W=[lq2])
    k.v(lambda e: e.reduce_sum(out=lsum.ap[:, 0:1], in_=lq1.ap, axis=AX.X), R=[lq1], W=[lsum])
    k.v(lambda e: e.reduce_sum(out=lsum.ap[:, 1:2], in_=lq2.ap, axis=AX.X), R=[lq2], W=[lsum])
    k.act(lsum.ap, lsum.ap, AF.Exp, R=[lsum], W=[lsum])
    lam_init = 0.8 - 0.6 * math.exp(-0.3 * 0)
    k.v(lambda e: e.tensor_tensor(out=neglam.ap, in0=lsum.ap[:, 1:2], in1=lsum.ap[:, 0:1], op=ALU.subtract),
        R=[lsum], W=[neglam])
    k.v(lambda e: e.tensor_scalar(out=neglam.ap, in0=neglam.ap, scalar1=-lam_init, scalar2=None, op0=ALU.add),
        R=[neglam], W=[neglam])
    gsub = k.load_bcast(I["da_subln_g"], 128, "gsub", "c1")
    k.v(lambda e: e.tensor_scalar(out=gsub.ap, in0=gsub.ap, scalar1=1.0 - lam_init, scalar2=None, op0=ALU.mult),
        R=[gsub], W=[gsub])

    QB = [A.alloc([T], BF16, "QB%d" % i) for i in range(2)]
    KBf = [A.alloc([T], BF16, "KB%d" % i) for i in range(2)]
    PT = [A.alloc([512], BF16, "PT%d" % i) for i in range(4)]
    pipe = AttnPipe(k, PT)
    rl = [A.alloc([4], F32, "rl%d" % i) for i in range(2)]
    rg = [A.alloc([4], F32, "rg%d" % i) for i in range(2)]
    t2 = [A.alloc([4, 128], F32, "t2%d" % i) for i in range(2)]
    ob = [A.alloc([4, 128], BF16, "ob%d" % i) for i in range(2)]
    msd = [A.alloc([4], F32, "msd%d" % i) for i in range(2)]
    osct = osc.rearrange("(n p) d -> p n d", p=128)
    state = {"cj": 0, "ev": 0}

    def oview(pair):
        return k.bank(4 + 2 * pair, 2).rearrange("p (j w) -> p j w", j=4)

    def load_q(buf, zrow, slot):
        k.dma("sp", buf.ap[0:64, :], zT[zrow:zrow + 64, :], W=[buf], sem="q%s" % buf.b.name)
        k.dma("sp", buf.ap[64:68, :], I["c_qaug"][slot], W=[buf], sem="q%s" % buf.b.name)

    def load_k(buf, zrow):
        k.dma("sp", buf.ap[0:64, :], zT[zrow:zrow + 64, :], W=[buf], sem="k%s" % buf.b.name)
        k.dma("sp", buf.ap[64:68, :], I["c_kaug"], W=[buf], sem="k%s" % buf.b.name)

    def load_v(buf, col, dv):
        src = vtok.rearrange("(n p) d -> p n d", p=128)
        for n0 in range(0, NT, 8):
            k.dma("sp", buf.ap[:, n0:n0 + 8, 0:dv], src[:, n0:n0 + 8, col:col + dv], W=[buf], sem="v%s" % buf.b.name)

    def run_job(qb, kT_of, v_of, dv1, tiles_of, extras_of, evac_of, scale=1.0, kp_of=None, hook=None):
        for c in range(NCH):
            pair = state["cj"] % 2
            state["cj"] += 1
            ov = oview(pair)
            tl = tiles_of(c)
            last_for = {}
            for n, (kt, j0, j1) in enumerate(tl):
                for j in range(j0, j1):
                    last_for[j] = n
            started = set()
            for n, (kt, j0, j1) in enumerate(tl):
                lo, hi = j0 * 128, j1 * 128
                kTap, kR = kT_of(kt)
                vap, vR = v_of(kt)
                kp = 128 if kp_of is None else kp_of(kt)
                pv = []
                for j in range(j0, j1):
                    obk = 4 + 2 * pair + j // 2
                    start = obk not in started
                    started.add(obk)
                    pv.append((ov[:, j, 0:dv1], obk, j * 128, (j + 1) * 128, vap, start, last_for[j] == n, vR))
                st = dict(kT=kTap, qT=qb.ap[0:68, c * 512 + lo:c * 512 + hi], lo=lo, hi=hi, Rqk=[qb] + kR,
                          extra=extras_of(c, kt, j0, j1), pv=pv, scale=scale, kp=kp,
                          evac=(evac_of(c, pair) if n == len(tl) - 1 else None))
                pipe.push(st)
            if hook is not None:
                hook(c)

    def causal_tiles(c):
        tl = [(kt, 0, 4) for kt in range(4 * c)]
        tl += [(4 * c + i, i, 4) for i in range(4)]
        return tl

    def causal_extras(c, kt, j0, j1):
        if kt >= 4 * c:
            i = kt - 4 * c
            return [(k.ident.ap, tric.ap, i * 128, (i + 1) * 128, [k.ident, tric])]
        return []

    def rl_of(ov, col, i, clamp=False):
        r = rl[i]
        if clamp:
            k.v(lambda e: e.tensor_scalar(out=r.ap.rearrange("p (a b) -> p a b", b=1), in0=ov[:, :, col:col + 1],
                                          scalar1=1e-30, scalar2=None, op0=ALU.max), R=[], W=[r])
            k.v(lambda e: e.reciprocal(out=r.ap, in_=r.ap), R=[r], W=[r])
        else:
            k.v(lambda e: e.reciprocal(out=r.ap.rearrange("p (a b) -> p a b", b=1), in_=ov[:, :, col:col + 1]),
                R=[], W=[r])
        return r

    cmask = cload("cmask", [2560], BF16, I["c_cmask"], q="act", defer=True)
    expand = cload("expand", [4096], BF16, I["c_expand"], q="act", defer=True)
    selA = cload("selA", [NT, 64], F32, I["c_selA"].rearrange("(n p) j -> p n j", p=128), q="act", defer=True)
    selB = cload("selB", [NT, 64], F32, I["c_selB"].rearrange("(n p) j -> p n j", p=128), q="act", defer=True)
    tricn = cload("tricn", [128], BF16, I["c_tricn"], q="act", defer=True)
    w1 = [A.alloc([32, 128], BF16, "w1%d" % i) for i in range(2)]
    w2 = [A.alloc([64], BF16, "w2%d" % i) for i in range(2)]
    peT = [A.alloc([32], F32, "peT%d" % i) for i in range(2)]
    peTb = [A.alloc([32], BF16, "peTb%d" % i) for i in range(2)]
    cb = [A.alloc([1], F32, "cb%d" % i) for i in range(2)]
    for i, nm in enumerate(("k", "v")):
        pe_src = I["cmp_%s_pe" % nm].rearrange("pos d -> d pos")

        def _ld(i=i, nm=nm, pe_src=pe_src):
            k.dma("pool", w1[i].ap[0:64], I["cmp_%s_w1" % nm].rearrange("(pos d) h -> d pos h", d=64), W=[w1[i]], sem="w0")
            k.dma("pool", w2[i].ap, I["cmp_%s_w2" % nm], W=[w2[i]], sem="w0")
            S.op("act", lambda e: e.dma_start(out=peT[i].ap[0:64], in_=pe_src, allow_slow_non_contiguous=True),
                 writes=[peT[i].b], dma="c1")
        deferred.append(_ld)
    def da_evac(h, m):
        def evac_of(c, pair):
            def ev():
                i = state["ev"] % 2
                state["ev"] += 1
                ov = oview(pair)
                OB = [PB[4 + 2 * pair], PB[5 + 2 * pair]]
                r = rl[i]
                k.v(lambda e: e.reciprocal(out=r.ap.rearrange("p (a b) -> p a b", b=1), in_=ov[:, :, 128:129]),
                    R=OB, W=[r])
                rb = r.ap.rearrange("p (a b) -> p a b", b=1).to_broadcast([128, 4, 128])
                dch = datmp.ap[:, 4 * c:4 * c + 4, :]
                if m == 0:
                    k.v(lambda e: e.tensor_tensor(out=dch, in0=ov[:, :, 0:128], in1=rb, op=ALU.mult),
                        R=OB + [r], W=[datmp])
                    return
                tt_ = t2[i]
                k.v(lambda e: e.tensor_tensor(out=tt_.ap, in0=ov[:, :, 0:128], in1=rb, op=ALU.mult),
                    R=OB + [r], W=[tt_])
                k.v(lambda e: e.scalar_tensor_tensor(out=dch, in0=tt_.ap, scalar=neglam.ap, in1=dch,
                                                     op0=ALU.mult, op1=ALU.add), R=[tt_, neglam, datmp], W=[datmp])
                k.v(lambda e: e.tensor_tensor(out=tt_.ap, in0=dch, in1=dch, op=ALU.mult), R=[datmp], W=[tt_])
                k.v(lambda e: e.reduce_sum(out=msall.ap[:, 4 * c:4 * c + 4], in_=tt_.ap, axis=AX.X), R=[tt_], W=[msall])
                if c == NCH - 1:
                    k.v(lambda e: e.tensor_scalar(out=msall.ap, in0=msall.ap, scalar1=1.0 / 128.0, scalar2=EPS,
                                                  op0=ALU.mult, op1=ALU.add), R=[msall], W=[msall])
                    k.act(msall.ap, msall.ap, AF.Ln, R=[msall], W=[msall])
                    k.act(msall.ap, msall.ap, AF.Exp, R=[msall], W=[msall], scale=-0.5)
                    for q4 in range(4):
                        sl = slice(8 * q4, 8 * q4 + 8)
                        mb = msall.ap[:, sl].rearrange("p (a b) -> p a b", b=1).to_broadcast([128, 8, 128])
                        gb = gsub.ap.rearrange("p (a b) -> p a b", a=1).broadcast_to([128, 8, 128])
                        k.v(lambda e, sl=sl, mb=mb: e.tensor_tensor(out=datmp.ap[:, sl, :], in0=datmp.ap[:, sl, :], in1=mb,
                                                                    op=ALU.mult), R=[datmp, msall], W=[datmp])
                        k.v(lambda e, sl=sl, gb=gb: e.tensor_tensor(out=oball.ap[:, sl, :], in0=datmp.ap[:, sl, :], in1=gb,
                                                                    op=ALU.mult), R=[datmp, gsub], W=[oball])
                    k.dma("pool", osct[:, :, h * 128:(h + 1) * 128], oball.ap, R=[oball], sem="oball")
            return ev
        return evac_of

    da_jobs = [(h, m) for h in range(4) for m in range(2)]

    def da_load(n):
        h, m = da_jobs[n]
        load_q(QB[n % 2], h * 128 + m * 64, h)
        load_k(KBf[n % 2], 512 + h * 128 + m * 64)
        if m == 0:
            load_v(VA[h % 2], h * 128, 128)

    if k.stage_attn & 1:
        mda = A.top
        VA = [A.alloc([NT, 129], BF16, "VA%d" % i) for i in range(2)]
        for t in VA:
            k.g(lambda e, t=t: e.memset(t.ap[:, :, 128:129], 1.0), W=[t])
        datmp = A.alloc([NT, 128], F32, "datmp")
        msall = A.alloc([NT], F32, "msall")
        oball = A.alloc([NT, 128], BF16, "oball")
        da_load(0)
        for n, (h, m) in enumerate(da_jobs):
            if n + 1 < len(da_jobs):
                da_load(n + 1)
            if n == 1:
                while deferred:
                    deferred.pop(0)()
            qb, kb, vb = QB[n % 2], KBf[n % 2], VA[h % 2]
            run_job(qb,
                    lambda kt, kb=kb: (kb.ap[0:68, kt * 128:(kt + 1) * 128], [kb]),
                    lambda kt, vb=vb: (vb.ap[:, kt, :], [vb]),
                    129, causal_tiles, causal_extras, da_evac(h, m))
        pipe.flush()
        S.barrier()
        A.top = mda

    if k.stage_attn & 2:
        while deferred:
            deferred.pop(0)()
        VN = [A.alloc([NT, 65], BF16, "VN%d" % i) for i in range(2)]
        for t in VN:
            k.g(lambda e, t=t: e.memset(t.ap[:, :, 64:65], 1.0), W=[t])
        onsa = [A.alloc([NT, 64], F32, "onsa%d" % i) for i in range(4)]
        imp = A.alloc([NT, 64], F32, "imp")
        QS = [A.alloc([T], BF16, "QS%d" % i) for i in range(4)]
        Mtiles = [A.alloc([512], BF16, "Mt%d" % i) for i in range(3)]
        for i, nm in enumerate(("k", "v")):
            k.v(lambda e, i=i: e.tensor_copy(out=peTb[i].ap[0:64], in_=peT[i].ap[0:64]), R=[peT[i]], W=[peTb[i]])
            for pos in range(32):
                k.mm(k.bank(3)[:, i:i + 1], w1[i].ap[0:64, pos, :], peTb[i].ap[0:64, pos:pos + 1], pos == 0, pos == 31,
                     R=[w1[i], peTb[i]], W=[PB[3]])
            k.v(lambda e, i=i: e.tensor_copy(out=cb[i].ap, in_=k.bank(3)[:, i:i + 1]), R=[PB[3]], W=[cb[i]])
        cin = [QB[0], QB[1]]
        AcT = [A.alloc([256], BF16, "AcT%d" % i) for i in range(2)]
        KcT = A.alloc([256], BF16, "KcT")
        Vc = A.alloc([2, 129], BF16, "Vc")
        negT = A.alloc([T], BF16, "negT")
        score = A.alloc([NT, 64], F32, "score")
        sc2 = A.alloc([64], F32, "sc2")
        m8 = A.alloc([8], F32, "m8")
        thr = A.alloc([NT], F32, "thr")
        negm = A.alloc([NT, 64], BF16, "negm")
        k.g(lambda e: e.memset(Vc.ap, 0.0), W=[Vc])
        k.g(lambda e: e.memset(Vc.ap[:, :, 128:129], 1.0), W=[Vc])
        k.dma("sp", Vc.ap[:, :, 64:128], I["c_mcs"].rearrange("(n p) j -> p n j", p=128), W=[Vc], sem="c1")
        print("NSA arena top", A.top)
        k.g(lambda e: e.memset(KcT.ap, 0.0), W=[KcT])
        k.dma("sp", KcT.ap[64:68, :], I["c_kaugc"], W=[KcT], sem="c1")

        for g in range(2):
            for i, zr in enumerate((2048, 2176)):
                k.dma("sp", cin[i].ap[0:64, :], zT[zr + g * 64:zr + g * 64 + 64, :], W=[cin[i]], sem="cin%d" % i)
                for pos in range(32):
                    k.mm(k.bank(3)[:, 8:8 + 255], w1[i].ap[0:64, pos, :], cin[i].ap[0:64, pos:pos + 16 * 254 + 1:16],
                         pos == 0, pos == 31, R=[w1[i], cin[i]], W=[PB[3]])
                k.g(lambda e, i=i: e.memset(AcT[i].ap, 0.0), W=[AcT[i]])
                k.act(AcT[i].ap[:, 0:255], k.bank(3)[:, 8:8 + 255], AF.Silu, R=[PB[3], cb[i]], W=[AcT[i]], bias=cb[i].ap)
            k.mm(k.bank(3)[0:64, 0:255], w2[0].ap, AcT[0].ap[:, 0:255], True, True, R=[w2[0], AcT[0]], W=[PB[3]])
            k.v(lambda e: e.tensor_copy(out=KcT.ap[0:64, 0:255], in_=k.bank(3)[0:64, 0:255]), R=[PB[3]], W=[KcT])
            for ct in range(2):
                k.mm(k.bank(3)[:, 256 + ct * 64:256 + (ct + 1) * 64], AcT[1].ap[:, ct * 128:(ct + 1) * 128], w2[1].ap,
                     True, True, R=[w2[1], AcT[1]], W=[PB[3]])
            k.v(lambda e: e.tensor_copy(out=Vc.ap[:, :, 0:64],
                                        in_=k.bank(3)[:, 256:384].rearrange("p (a b) -> p a b", a=2)),
                R=[PB[3]], W=[Vc])

            def cmp_tiles(c):
                tl = [(0, 0, 4)]
                if c >= 4:
                    tl.append((1, 0, 4))
                return tl

            def cmp_extras(c, kt, j0, j1):
                if kt == 0 and c >= 5:
                    return []
                off = c * 512 if kt == 0 else (c - 4) * 512
                return [(k.ident.ap, cmask.ap[:, off:off + 512], 0, 512, [k.ident, cmask])]

            def cmp_evac(r, head):
                def evac_of(c, pair):
                    def ev():
                        i = state["ev"] % 2
                        state["ev"] += 1
                        ov = oview(pair)
                        OB = [PB[4 + 2 * pair], PB[5 + 2 * pair]]
                        r_ = rl[i]
                        r3 = r_.ap.rearrange("p (a b) -> p a b", b=1)
                        k.v(lambda e: e.tensor_scalar(out=r3, in0=ov[:, :, 128:129], scalar1=1e-30, scalar2=None,
                                                      op0=ALU.max), R=OB, W=[r_])
                        k.v(lambda e: e.reciprocal(out=r_.ap, in_=r_.ap), R=[r_], W=[r_])
                        g_ = rg[i]
                        k.v(lambda e: e.tensor_tensor(out=g_.ap.rearrange("p (a b) -> p a b", b=1), in0=r3,
                                                      in1=gates.ap[:, 4 * c:4 * c + 4, head * 3:head * 3 + 1], op=ALU.mult),
                            R=[r_, gates], W=[g_])
                        gb = g_.ap.rearrange("p (a b) -> p a b", b=1).to_broadcast([128, 4, 64])
                        k.v(lambda e: e.tensor_tensor(out=onsa[r].ap[:, 4 * c:4 * c + 4, :], in0=ov[:, :, 0:64], in1=gb,
                                                      op=ALU.mult), R=OB + [g_], W=[onsa[r]])
                        rb = r3.to_broadcast([128, 4, 64])
                        ich = imp.ap[:, 4 * c:4 * c + 4, :]
                        if r == 0:
                            k.v(lambda e: e.tensor_tensor(out=ich, in0=ov[:, :, 64:128], in1=rb, op=ALU.mult),
                                R=OB + [r_], W=[imp])
                        else:
                            tt_ = t2[i]
                            k.v(lambda e: e.tensor_tensor(out=tt_.ap[:, :, 0:64], in0=ov[:, :, 64:128], in1=rb,
                                                          op=ALU.mult), R=OB + [r_], W=[tt_])
                            k.g(lambda e: e.tensor_tensor(out=ich, in0=ich, in1=tt_.ap[:, :, 0:64], op=ALU.add),
                                R=[tt_, imp], W=[imp])
                    return ev
                return evac_of

            def acc_evac(r, head, branch, final):
                def evac_of(c, pair, ov=None, OB=None):
                    def ev(ov=ov, OB=OB):
                        i = state["ev"] % 2
                        state["ev"] += 1
                        if ov is None:
                            ov = oview(pair)
                            OB = [PB[4 + 2 * pair], PB[5 + 2 * pair]]
                        r_ = rl[i]
                        r3 = r_.ap.rearrange("p (a b) -> p a b", b=1)
                        k.v(lambda e: e.reciprocal(out=r3, in_=ov[:, :, 64:65]), R=OB, W=[r_])
                        g_ = rg[i]
                        k.v(lambda e: e.tensor_tensor(out=g_.ap.rearrange("p (a b) -> p a b", b=1), in0=r3,
                                                      in1=gates.ap[:, 4 * c:4 * c + 4, head * 3 + branch:head * 3 + branch + 1],
                                                      op=ALU.mult), R=[r_, gates], W=[g_])
                        gb = g_.ap.rearrange("p (a b) -> p a b", b=1).to_broadcast([128, 4, 64])
                        tt_ = t2[i]
                        k.v(lambda e: e.tensor_tensor(out=tt_.ap[:, :, 0:64], in0=ov[:, :, 0:64], in1=gb, op=ALU.mult),
                            R=OB + [g_], W=[tt_])
                        och = onsa[r].ap[:, 4 * c:4 * c + 4, :]
                        if not final:
                            k.g(lambda e: e.tensor_tensor(out=och, in0=och, in1=tt_.ap[:, :, 0:64], op=ALU.add),
                                R=[tt_, onsa[r]], W=[onsa[r]])
                        else:
                            o_ = ob[i]
                            k.v(lambda e: e.tensor_tensor(out=o_.ap[:, :, 0:64], in0=och, in1=tt_.ap[:, :, 0:64], op=ALU.add),
                                R=[tt_, onsa[r]], W=[o_])
                            k.dma("pool", osct[:, 4 * c:4 * c + 4, 512 + head * 64:512 + (head + 1) * 64], o_.ap[:, :, 0:64],
                                  R=[o_], sem="ob%d" % i)
                    return ev
                return evac_of

            for r in range(4):
                load_q(QS[r], 1536 + (g * 4 + r) * 64, 4 + g * 4 + r)
            kvs = KBf[0], VN[0]
            kvw = KBf[1], VN[1]
            load_k(kvs[0], 2304 + g * 64)
            load_v(kvs[1], 512 + g * 64, 64)
            load_k(kvw[0], 2560 + g * 64)
            load_v(kvw[1], 640 + g * 64, 64)

            for r in range(4):
                run_job(QS[r],
                        lambda kt: (KcT.ap[0:68, kt * 128:(kt + 1) * 128], [KcT]),
                        lambda kt: (Vc.ap[:, kt, :], [Vc]),
                        129, cmp_tiles, cmp_extras, cmp_evac(r, g * 4 + r))
            pipe.flush()

            k.v(lambda e: e.tensor_tensor(out=score.ap, in0=imp.ap, in1=selA.ap, op=ALU.mult), R=[imp, selA], W=[score])
            k.v(lambda e: e.tensor_tensor(out=score.ap, in0=score.ap, in1=selB.ap, op=ALU.add), R=[score, selB], W=[score])

            def sel_piece(n):
                k.v(lambda e: e.max(out=m8.ap, in_=score.ap[:, n, :]), R=[score], W=[m8])
                k.v(lambda e: e.match_replace(out=sc2.ap, in_to_replace=m8.ap, in_values=score.ap[:, n, :],
                                              imm_value=-3.0), R=[score, m8], W=[sc2])
                k.v(lambda e: e.max(out=m8.ap, in_=sc2.ap), R=[sc2], W=[m8])
                k.v(lambda e: e.tensor_copy(out=thr.ap[:, n:n + 1], in_=m8.ap[:, 7:8]), R=[m8], W=[thr])

            def sel_extras(c, kt, j0, j1):
                ex = [(expand.ap[0:64, kt * 128:(kt + 1) * 128], negT.ap[0:64, c * 512 + j0 * 128:c * 512 + j1 * 128],
                       j0 * 128, j1 * 128, [expand, negT])]
                return ex + causal_extras(c, kt, j0, j1)

            def win_tiles(c):
                tl = []
                for kt in range(max(0, 4 * c - 4), 4 * c + 4):
                    j0 = max(0, kt - 4 * c)
                    j1 = min(3, kt - 4 * c + 4) + 1
                    tl.append((kt, j0, j1))
                return tl

            def win_extras(c, kt, j0, j1):
                ex = []
                if kt >= 4 * c:
                    i = kt - 4 * c
                    ex.append((k.ident.ap, tric.ap, i * 128, (i + 1) * 128, [k.ident, tric]))
                if kt < 4 * c:
                    i = kt - 4 * c + 4
                    ex.append((k.ident.ap, tril.ap, i * 128, (i + 1) * 128, [k.ident, tril]))
                return ex

            seln = {"n": 0}

            def win_hook(c):
                sel_piece(seln["n"])
                seln["n"] += 1

            for r in range(4):
                kb, vb = kvw
                run_job(QS[r],
                        lambda kt, kb=kb: (kb.ap[0:68, kt * 128:(kt + 1) * 128], [kb]),
                        lambda kt, vb=vb: (vb.ap[:, kt, :], [vb]),
                        65, win_tiles, win_extras, acc_evac(r, g * 4 + r, 2, False), hook=win_hook)
            pipe.flush()
            tb = thr.ap.rearrange("p (a b) -> p a b", b=1).to_broadcast([128, NT, 64])
            k.v(lambda e: e.tensor_tensor(out=score.ap, in0=score.ap, in1=tb, op=ALU.is_ge), R=[score, thr], W=[score])
            k.v(lambda e: e.tensor_tensor(out=score.ap, in0=score.ap, in1=selA.ap, op=ALU.mult), R=[score, selA], W=[score])
            k.v(lambda e: e.tensor_copy(out=negm.ap, in_=score.ap), R=[score], W=[negm])
            for n0 in range(0, NT, 8):
                pT = k.bank_bf(3)
                for n in range(n0, n0 + 8):
                    k.tr(pT[0:64, (n - n0) * 128:(n - n0 + 1) * 128], negm.ap[:, n, :], R=[negm], W=[PB[3]])
                k.v(lambda e, n0=n0, pT=pT: e.tensor_copy(out=negT.ap[0:64, n0 * 128:(n0 + 8) * 128], in_=pT[0:64, :]),
                    R=[PB[3]], W=[negT])
            kb, vb = kvs
            mi = 0
            for c in range(NCH):
                tl = causal_tiles(c)
                started = set()
                for n, (kt, j0, j1) in enumerate(tl):
                    lo, hi = j0 * 128, j1 * 128
                    diag = kt >= 4 * c
                    Mt = Mtiles[mi % 3]
                    mi += 1
                    def mqk(sb, lo=lo, hi=hi, kt=kt, c=c, diag=diag):
                        k.mm(k.bank(sb)[:, lo:hi], expand.ap[0:64, kt * 128:(kt + 1) * 128],
                             negT.ap[0:64, c * 512 + lo:c * 512 + hi], True, not diag, R=[expand, negT], W=[PB[sb]])
                        if diag:
                            i = kt - 4 * c
                            k.mm(k.bank(sb)[:, i * 128:(i + 1) * 128], k.ident.ap, tricn.ap, False, True,
                                 R=[k.ident, tricn], W=[PB[sb]])

                    def mrest(sb, lo=lo, hi=hi, Mt=Mt):
                        k.v(lambda e: e.tensor_scalar(out=Mt.ap[:, lo:hi], in0=k.bank(sb)[:, lo:hi], scalar1=0.0,
                                                      scalar2=None, op0=ALU.max), R=[PB[sb]], W=[Mt])

                    pipe.push(dict(qkfn=mqk, restfn=mrest))
                    for r in range(4):
                        ovr = k.bank(4 + r)[:, 0:260].rearrange("p (j w) -> p j w", j=4)
                        pv = []
                        for j in range(j0, j1):
                            start = (4 + r) not in started
                            started.add(4 + r)
                            pv.append((ovr[:, j, 0:65], 4 + r, j * 128, (j + 1) * 128, vb.ap[:, kt, :], start,
                                       kt == 4 * c + j, [vb]))
                        last = n == len(tl) - 1
                        st = dict(kT=kb.ap[0:68, kt * 128:(kt + 1) * 128],
                                  qT=QS[r].ap[0:68, c * 512 + lo:c * 512 + hi], lo=lo, hi=hi, Rqk=[QS[r], kb],
                                  extra=[], pv=pv, scale=1.0, kp=128, mask=(Mt, "dve"),
                                  evac=(acc_evac(r, g * 4 + r, 1, True)(c, None, ovr, [PB[4 + r]]) if last else None))
                        pipe.push(st)
            pipe.flush()
    S.barrier()
    A.top = mark


def outxa_phase(k, I, x1, osc, x3, kv):
    A, S, PB = k.A, k.S, k.pb
    mark = A.top
    wout = k.load_w(I["w_mix_out"], D, D, "wout", "w0")
    wq = k.load_w(I["xa_w_q"], D, D, "wq", "w1")
    wo = k.load_w(I["xa_w_o"], D, D, "wo", "w2")
    gmixpost = k.load_bcast(I["mix_post_g"], D, "gmp", "c0")
    gxapre = k.load_bcast(I["xa_pre_g"], D, "gxp", "c0")
    gxapost = k.load_bcast(I["xa_post_g"], D, "gxo", "c0")
    KxT, Vx = kv
    ones = A.alloc([128], BF16, "ones")
    k.g(lambda e: e.memset(ones.ap, 1.0), W=[ones])
    hn = [A.alloc([D], BF16, "hn%d" % i) for i in range(2)]
    junk = A.alloc([D], BF16, "junk")
    TB = 2

    tfc = {"n": 0}

    def to_fm(src_ap, src_R, dst, col0, nhn, Wb=None):
        Wb = [dst] if Wb is None else Wb
        tb = (TB, 7)[tfc["n"] % 2]
        tfc["n"] += 1
        pT = k.bank_bf(tb)
        for dc in range(8):
            k.tr(pT[:, dc * 128:(dc + 1) * 128], src_ap[:, dc * 128:(dc + 1) * 128], R=src_R, W=[PB[tb]])
        k.act(dst.ap[:, :, col0:col0 + 128], pT.rearrange("p (a b) -> p a b", a=8), AF.Copy, R=[PB[tb]], W=Wb)

    ot = [A.alloc([D], BF16, "ot%d" % i) for i in range(4)]
    oT = A.alloc([8, 512], BF16, "oT")
    x2c = [[A.alloc([D], F32, "x2c%d_%d" % (i, j)) for j in range(4)] for i in range(3)]
    yb = [A.alloc([D], F32, "yb%d" % i) for i in range(4)]
    yb2 = [A.alloc([D], F32, "yb2%d" % i) for i in range(4)]
    msA = [A.alloc([4], F32, "msA%d" % i) for i in range(2)]
    msB = [A.alloc([4], F32, "msB%d" % i) for i in range(2)]
    msC = [A.alloc([4], F32, "msC%d" % i) for i in range(2)]
    h3T = A.alloc([8, 512], BF16, "h3T")
    qxT = [A.alloc([8, 512], BF16, "qxT%d" % i) for i in range(2)]
    PT = [A.alloc([512], BF16, "PTx%d" % i) for i in range(4)]
    oxT = oT
    rLb = [A.alloc([512], F32, "rLb%d" % i) for i in range(2)]
    osct = osc.rearrange("(n p) d -> n p d", p=128)
    x1tt = x1.rearrange("(n p) d -> n p d", p=128)
    x3t = x3.rearrange("(n p) d -> n p d", p=128)
    SB = [3, 4, 5]
    print("outxa arena top", A.top)
    cnt = {"s": 0, "e": 0}

    oTb = [Buf("oTb%d" % i) for i in range(4)]

    def proj_tile(srcT, tt, w, dst):
        for half in range(2):
            for fc in range(8):
                k.mm(k.bank(half), srcT.ap[:, fc, tt * 128:(tt + 1) * 128], w.ap[:, fc, half * 512:(half + 1) * 512],
                     fc == 0, fc == 7, R=[oTb[tt], w], W=[PB[half]])
            k.v(lambda e, half=half: e.tensor_copy(out=dst.ap[:, half * 512:(half + 1) * 512], in_=k.bank(half)),
                R=[PB[half]], W=[dst])

    def front1(c):
        X = x2c[c % 3]
        for tt in range(4):
            k.dma("sp", ot[tt].ap, osct[4 * c + tt], W=[ot[tt]], sem="ot%d" % tt)
        for tt in range(4):
            k.dma("sp", X[tt].ap, x1tt[4 * c + tt], W=[X[tt]], sem="x2c%d_%d" % (c % 3, tt))
        tf = lambda tt: to_fm(ot[tt].ap, [ot[tt]], oT, tt * 128, tt % 2, Wb=[oTb[tt]])
        pj = lambda tt: proj_tile(oT, tt, wout, yb[tt])
        tf(0); tf(1); pj(0); tf(2); pj(1); tf(3); pj(2); pj(3)

    def front2a(c):
        X = x2c[c % 3]
        mA, mB = msA[c % 2], msB[c % 2]
        for tt in range(4):
            k.act(junk.ap, yb[tt].ap, AF.Square, R=[yb[tt]], W=[junk, mA], scale=1.0 / 32.0, accum_out=mA.ap[:, tt:tt + 1])
        k.rstd_chain(mA.ap, mA)
        for tt in range(4):
            k.v(lambda e, tt=tt: e.scalar_tensor_tensor(out=yb[tt].ap, in0=yb[tt].ap, scalar=mA.ap[:, tt:tt + 1],
                                                        in1=gmixpost.ap, op0=ALU.mult, op1=ALU.mult),
                R=[yb[tt], mA, gmixpost], W=[yb[tt]])
            k.g(lambda e, tt=tt: e.tensor_tensor(out=X[tt].ap, in0=X[tt].ap, in1=yb[tt].ap, op=ALU.add),
                R=[yb[tt], X[tt]], W=[X[tt]])

    def front2b(c):
        X = x2c[c % 3]
        mA, mB = msA[c % 2], msB[c % 2]
        for tt in range(4):
            k.act(junk.ap, X[tt].ap, AF.Square, R=[X[tt]], W=[junk, mB], scale=1.0 / 32.0, accum_out=mB.ap[:, tt:tt + 1])
        k.rstd_chain(mB.ap, mB)

    def mixed(c, cb):
        X = x2c[c % 3]
        mB = msB[c % 2]

        def stt(tt):
            hh = hn[tt % 2]
            k.v(lambda e: e.scalar_tensor_tensor(out=hh.ap, in0=X[tt].ap, scalar=mB.ap[:, tt:tt + 1],
                                                 in1=gxapre.ap, op0=ALU.mult, op1=ALU.mult),
                R=[X[tt], mB, gxapre], W=[hh])

        tf = lambda tt: to_fm(hn[tt % 2].ap, [hn[tt % 2]], h3T, tt * 128, tt % 2)
        pj = lambda tt: proj_tile(oxT, tt, wo, yb2[tt])
        stt(0); stt(1); tf(0); pj(0); stt(2); tf(1); pj(1); stt(3); tf(2); pj(2); tf(3); pj(3)

    def front2c(c):
        q_ = qxT[c % 2]
        for ft in range(8):
            bk = SB[ft % 3]
            for dc in range(8):
                k.mm(k.bank(bk), wq.ap[:, dc, ft * 128:(ft + 1) * 128], h3T.ap[:, dc, :], dc == 0, dc == 7,
                     R=[wq, h3T], W=[PB[bk]])
            if ft % 2 == 0:
                k.act(q_.ap[:, ft, :], k.bank(bk), AF.Copy, R=[PB[bk]], W=[q_])
            else:
                k.v(lambda e, ft=ft, bk=bk: e.tensor_copy(out=q_.ap[:, ft, :], in_=k.bank(bk)), R=[PB[bk]], W=[q_])

    def back1(c, heads):
        q_ = qxT[c % 2]

        def qk(hh):
            for mt in range(2):
                bk = (3, 4)[mt]
                for j in range(2):
                    k.mm(k.bank(bk), KxT.ap[:, hh * 2 + j, mt * 128:(mt + 1) * 128], q_.ap[:, hh * 2 + j, :],
                         j == 0, j == 1, R=[KxT, q_], W=[PB[bk]])
                pt = PT[(2 * hh + mt) % 4]
                k.act(pt.ap, k.bank(bk), AF.Exp, R=[PB[bk]], W=[pt], scale=1.0 / 16.0)

        def lo(hh):
            pts = [PT[(2 * hh + mt) % 4] for mt in range(2)]
            for mt in range(2):
                k.mm(k.bank(7), ones.ap, pts[mt].ap, mt == 0, mt == 1, R=[ones, pts[mt]], W=[PB[7]])
            for dvc in range(2):
                for mt in range(2):
                    k.mm(k.bank(5 + dvc), Vx.ap[:, mt, hh * 256 + dvc * 128:hh * 256 + (dvc + 1) * 128], pts[mt].ap,
                         mt == 0, mt == 1, R=[Vx, pts[mt]], W=[PB[5 + dvc]])
            rL = rLb[hh % 2]
            k.act(rL.ap, k.bank(7), AF.Ln, R=[PB[7]], W=[rL])
            k.act(rL.ap, rL.ap, AF.Exp, R=[rL], W=[rL], scale=-1.0)
            for dvc in range(2):
                k.v(lambda e, dvc=dvc: e.tensor_tensor(out=oxT.ap[:, hh * 2 + dvc, :], in0=k.bank(5 + dvc), in1=rL.ap,
                                                       op=ALU.mult), R=[PB[5 + dvc], rL], W=oTb)

        qk(heads[0])
        for n, hh in enumerate(heads):
            if n + 1 < len(heads):
                qk(heads[n + 1])
            lo(hh)

    def back2(c):
        X = x2c[c % 3]
        mC = msC[c % 2]
        for tt in range(4):
            proj_tile(oxT, tt, wo, yb2[tt])

    def back2b(c):
        X = x2c[c % 3]
        mC = msC[c % 2]
        for tt in range(4):
            k.act(junk.ap, yb2[tt].ap, AF.Square, R=[yb2[tt]], W=[junk, mC], scale=1.0 / 32.0, accum_out=mC.ap[:, tt:tt + 1])
        k.rstd_chain(mC.ap, mC)
        for tt in range(4):
            gt = 4 * c + tt
            k.v(lambda e, tt=tt: e.scalar_tensor_tensor(out=yb2[tt].ap, in0=yb2[tt].ap, scalar=mC.ap[:, tt:tt + 1],
                                                        in1=gxapost.ap, op0=ALU.mult, op1=ALU.mult),
                R=[yb2[tt], mC, gxapost], W=[yb2[tt]])
            k.g(lambda e, tt=tt: e.tensor_tensor(out=yb2[tt].ap, in0=yb2[tt].ap, in1=X[tt].ap, op=ALU.add),
                R=[yb2[tt], X[tt]], W=[yb2[tt]])
            k.dma("pool", x3t[gt], yb2[tt].ap, R=[yb2[tt]], sem="x3o%d" % tt)

    front1(0)
    front2a(0)
    front2b(0)
    for tt in range(4):
        k.v(lambda e, tt=tt: e.scalar_tensor_tensor(out=hn[tt % 2].ap, in0=x2c[0][tt].ap, scalar=msB[0].ap[:, tt:tt + 1],
                                                    in1=gxapre.ap, op0=ALU.mult, op1=ALU.mult),
            R=[x2c[0][tt], msB[0], gxapre], W=[hn[tt % 2]])
        to_fm(hn[tt % 2].ap, [hn[tt % 2]], h3T, tt * 128, tt % 2)
    front2c(0)
    for c in range(NCH):
        nxt = c + 1 < NCH
        if nxt:
            front1(c + 1)
            front2a(c + 1)
        back1(c, (0, 1, 2, 3))
        if nxt:
            front2b(c + 1)
            mixed(c + 1, c)
            front2c(c + 1)
        else:
            back2(c)
        back2b(c)
    S.barrier()
    A.top = mark


def build(stage=99, debug=False, stage_attn=3):
    nc = bass.Bass("TRN2", target_bir_lowering=False)
    dt = lambda name, shape, dtype, kind: nc.dram_tensor(name, list(shape), dtype, kind=kind).ap()
    I = {}
    for name, shape in IN_SHAPES.items():
        I[name] = dt(name, shape, F32, "ExternalInput")
    for name, (shape, dtype) in CONST_SHAPES.items():
        I[name] = dt(name, shape, dtype, "ExternalInput")
    skind = "ExternalOutput" if debug else "Internal"
    x1 = dt("x1", [T, D], F32, skind)
    zT = dt("zT", [MIXIN, T], BF16, skind)
    vtok = dt("vtok", [T, 768], BF16, skind)
    gts = dt("gts", [T, 24], F32, skind)
    osc = dt("osc", [T, D], BF16, skind)
    x3 = dt("x3", [T, D], F32, skind)
    out = dt("out", [T, D], F32, "ExternalOutput")
    with ExitStack() as st:
        k = KB(nc, st, debug)
        k.stage_attn = stage_attn
        k.ident = k.A.alloc([128], BF16, "ident")
        k.dma("sp", k.ident.ap, I["c_ident"], W=[k.ident], sem="c9")
        ffn_phase(k, I["x"], x1, I["ffn1_w_gate"], I["ffn1_w_up"], I["ffn1_w_down"],
                  I["ffn1_pre_g"], I["ffn1_post_g"], "f1")
        base = k.A.top
        kv = (k.A.alloc([8, 256], BF16, "KxT"), k.A.alloc([2, D], BF16, "Vx"))
        if stage >= 2:
            inproj_phase(k, x1, I, zT, vtok, gts, kv)
        k.rstd_mode = "ln"
        if stage >= 3:
            attn_phase(k, I, zT, vtok, gts, osc)
        if stage >= 4:
            outxa_phase(k, I, x1, osc, x3, kv)
        k.rstd_mode = "sqrt"
        k.A.top = base
        if stage >= 5:
            ffn_phase(k, x3, out, I["ffn2_w_gate"], I["ffn2_w_up"], I["ffn2_w_down"],
                      I["ffn2_pre_g"], I["ffn2_post_g"], "f2")
        k.S.barrier()
        k.S.emit()
        print("ops", k.S.nops, "sems", k.S.nsem, "arena peak", k.A.peak)
    return nc


IN_SHAPES = {
    "x": (T, D), "mem": (256, D),
    "ffn1_pre_g": (1, D), "ffn1_post_g": (1, D),
    "ffn1_w_gate": (D, DFF), "ffn1_w_up": (D, DFF), "ffn1_w_down": (DFF, D),
    "mix_pre_g": (1, D), "mix_post_g": (1, D), "w_mix_in": (D, MIXIN),
    "da_lambda_q1": (1, 64), "da_lambda_k1": (1, 64), "da_lambda_q2": (1, 64), "da_lambda_k2": (1, 64),
    "da_subln_g": (1, 128),
    "cmp_k_pe": (32, 64), "cmp_k_w1": (2048, 128), "cmp_k_w2": (128, 64),
    "cmp_v_pe": (32, 64), "cmp_v_w1": (2048, 128), "cmp_v_w2": (128, 64),
    "w_mix_out": (D, D),
    "xa_pre_g": (1, D), "xa_post_g": (1, D), "mem_norm_g": (1, D),
    "xa_w_q": (D, D), "xa_w_k": (D, D), "xa_w_v": (D, D), "xa_w_o": (D, D),
    "ffn2_pre_g": (1, D), "ffn2_post_g": (1, D),
    "ffn2_w_gate": (D, DFF), "ffn2_w_up": (D, DFF), "ffn2_w_down": (DFF, D),
}
PER_CORE = ("x", "mem")

CONST_SHAPES = {
    "c_ident": ((128, 128), BF16),
    "c_tric": ((128, 128), BF16),
    "c_tril": ((128, 128), BF16),
    "c_tricn": ((128, 128), BF16),
    "c_cmask": ((128, 2560), BF16),
    "c_expand": ((128, 4096), BF16),
    "c_selA": ((T, 64), F32),
    "c_selB": ((T, 64), F32),
    "c_mcs": ((256, 64), BF16),
    "c_qaug": ((12, 4, T), BF16),
    "c_kaug": ((4, T), BF16),
    "c_kaugc": ((4, 256), BF16),
}


def make_consts():
    bf = ml_dtypes.bfloat16
    c = {}
    c["c_ident"] = np.eye(128, dtype=np.float32).astype(bf)
    kk = np.arange(128)[:, None]
    qq = np.arange(128)[None, :]
    c["c_tric"] = np.where(kk <= qq, 0.0, NEGBIG).astype(np.float32).astype(bf)
    c["c_tril"] = np.where(kk > qq, 0.0, NEGBIG).astype(np.float32).astype(bf)
    c["c_tricn"] = np.where(kk <= qq, 0.0, -1.0).astype(np.float32).astype(bf)
    tt = np.arange(2560)[None, :]
    c["c_cmask"] = np.where(tt - 16 * kk >= 31, 0.0, NEGBIG).astype(np.float32).astype(bf)
    ex = np.zeros((128, T), np.float32)
    ex[:64] = (np.arange(T)[None, :] // 64 == np.arange(64)[:, None])
    c["c_expand"] = ex.astype(bf)
    t = np.arange(T)
    cur = (t // 64)[:, None]
    blk = np.arange(64)[None, :]
    Am = (blk <= cur).astype(np.float32)
    forced = ((blk == 0) | (blk == cur) | (blk == cur - 1)).astype(np.float32)
    c["c_selA"] = Am
    c["c_selB"] = (1e4 * forced - (1.0 - Am)).astype(np.float32)
    cs = np.arange(255) * 16
    ss = np.arange(64) * 64
    ov = np.clip(np.minimum(cs[:, None] + 32, ss[None, :] + 64) - np.maximum(cs[:, None], ss[None, :]), 0, None)
    mcs = np.zeros((256, 64), np.float32)
    mcs[:255] = ov / 32.0
    c["c_mcs"] = mcs.astype(bf)
    a = (t // 64).astype(np.float32)
    b = (t % 64).astype(np.float32)
    slopes = list(2.0 ** (-8.0 * np.arange(1, 5) / 4)) + list(2.0 ** (-8.0 * np.arange(1, 9) / 8))
    qa = np.zeros((12, 4, T), np.float32)
    for s, sl in enumerate(slopes):
        qa[s, 0] = -sl * 64 * a
        qa[s, 1] = -sl * b
        qa[s, 2] = sl
        qa[s, 3] = sl
    c["c_qaug"] = qa.astype(bf)
    ka = np.stack([np.ones(T), np.ones(T), 64 * a, b]).astype(np.float32)
    c["c_kaug"] = ka.astype(bf)
    pc = np.arange(256) * 16 + 31
    kc = np.stack([np.ones(256), np.ones(256), 64.0 * (pc // 64), 1.0 * (pc % 64)]).astype(np.float32)
    kc[:, 255] = 0
    c["c_kaugc"] = kc.astype(bf)
    return c


_CACHE = {}


def kernel(**inputs):
    n = 8
    if "nc" not in _CACHE:
        _CACHE["nc"] = build()
    nc = _CACHE["nc"]
    consts = make_consts()
    in_maps = []
    for i in range(n):
        m = {}
        for name, shape in IN_SHAPES.items():
            a = np.asarray(inputs[name], dtype=np.float32)
            a = a[i] if name in PER_CORE else a[0]
            m[name] = np.ascontiguousarray(a.reshape(shape))
        m.update(consts)
        in_maps.append(m)
    res = run_bass_kernel_spmd(nc, in_maps, core_ids=list(range(n)))
    return np.stack([r["out"] for r in res.results], axis=0).astype(np.float32)
```

```python
import math
from contextlib import ExitStack

import numpy as np
import ml_dtypes

import concourse.bass as bass
import concourse.mybir as mybir
from concourse.bass_utils import run_bass_kernel_spmd

F32 = mybir.dt.float32
BF16 = mybir.dt.bfloat16
AF = mybir.ActivationFunctionType
ALU = mybir.AluOpType
AX = mybir.AxisListType

SEM_LIMIT = 30000
DT_SIZE = {F32: 4, BF16: 2}

T = 4096
D = 1024
DFF = 2816
NT = T // 128
NCH = T // 512
MIXIN = 2840
NEGBIG = -30000.0
EPS = 1e-6


class Buf:
    __slots__ = ("name", "w", "r")

    def __init__(self, name=""):
        self.name = name
        self.w = None
        self.r = {}


class Tile:
    __slots__ = ("ap", "b", "b2")

    def __init__(self, ap, name=""):
        self.ap = ap
        self.b = Buf(name)
        self.b2 = None

    def __getitem__(self, k):
        return self.ap[k]


class Sched:
    ENG = ("pe", "act", "dve", "pool", "sp")

    def __init__(self, nc, stack):
        self.nc = nc
        self.stack = stack
        self.q = {e: [] for e in self.ENG}
        self.cnt = {}
        self.sems = {}
        self.seen = {e: {} for e in self.ENG}
        self.nsem = 0
        self.nops = 0

    def _sem(self, key):
        if key not in self.sems:
            self.sems[key] = self.stack.enter_context(self.nc.semaphore("s%d" % self.nsem))
            self.nsem += 1
        return self.sems[key]

    def _bump(self, base, inc):
        ep, v = self.cnt.get(base, (0, 0))
        if v + inc > SEM_LIMIT:
            ep, v = ep + 1, 0
        v += inc
        self.cnt[base] = (ep, v)
        key = (base, ep)
        self._sem(key)
        return key, v

    def op(self, eng, fn, reads=(), writes=(), dma=None, skip_same=False):
        rr = []
        for t in reads:
            if isinstance(t, Tile):
                rr.append(t.b)
                if t.b2 is not None:
                    rr.append(t.b2)
            else:
                rr.append(t)
        reads = rr
        writes = [t.b if isinstance(t, Tile) else t for t in writes]
        deps = []
        for b in reads:
            if b.w is not None:
                deps.append(b.w)
        for b in writes:
            if b.w is not None:
                deps.append(b.w)
            deps.extend(b.r.items())
        if dma is not None:
            dma = (dma, eng)
            ep, v = self.cnt.get(dma, (0, 0))
            if v > 0:
                deps.append(((dma, ep), v))
        waits = {}
        seen = self.seen[eng]
        for key, v in deps:
            if skip_same and key[0] == eng:
                continue
            if seen.get(key, 0) >= v:
                continue
            if waits.get(key, 0) < v:
                waits[key] = v
        for key, v in waits.items():
            seen[key] = v
        if dma is None:
            key, v = self._bump(eng, 1)
            inc = 1
        else:
            key, v = self._bump(dma, 16)
            inc = 16
        self.q[eng].append((list(waits.items()), fn, key, inc))
        self.nops += 1
        ev = (key, v)
        for b in reads:
            if b.r.get(key, 0) < v:
                b.r[key] = v
        for b in writes:
            b.w = ev
            b.r = {}
        return ev

    def barrier(self):
        allv = [((base, ep), v) for base, (ep, v) in self.cnt.items() if v > 0]
        for e in self.ENG:
            waits = []
            for key, v in allv:
                if self.seen[e].get(key, 0) < v:
                    waits.append((key, v))
                    self.seen[e][key] = v
            if waits:
                self.q[e].append((waits, None, None, 0))

    def emit(self):
        nc = self.nc
        sems = self.sems
        q = self.q

        def replay(name, e):
            for waits, fn, key, inc in q[name]:
                for k, v in waits:
                    e.wait_ge(sems[k], v)
                if fn is not None:
                    fn(e).then_inc(sems[key], inc)

        with nc.Block() as block:
            @block.tensor
            def _(e):
                replay("pe", e)

            @block.scalar
            def _(e):
                replay("act", e)

            @block.vector
            def _(e):
                replay("dve", e)

            @block.gpsimd
            def _(e):
                replay("pool", e)

            @block.sync
            def _(e):
                replay("sp", e)


class Arena:
    def __init__(self, nc, stack, nbytes):
        self.t = stack.enter_context(nc.sbuf_tensor("arena", [128, nbytes // 4], F32))
        self.top = 0
        self.cap = nbytes
        self.peak = 0

    def alloc(self, shape, dtype, name=""):
        n = int(np.prod(shape)) * DT_SIZE[dtype]
        n4 = (n + 3) // 4
        off = self.top // 4
        self.top += n4 * 4
        self.peak = max(self.peak, self.top)
        assert self.top <= self.cap, ("SBUF arena overflow", name, self.top, self.cap)
        ap = self.t[:, off:off + n4]
        if dtype != F32:
            ap = ap.bitcast(dtype)
            ap = ap[:, 0:int(np.prod(shape))]
        if len(shape) == 2:
            ap = ap.rearrange("p (a b) -> p a b", a=shape[0])
        elif len(shape) == 3:
            ap = ap.rearrange("p (a b c) -> p a b c", a=shape[0], b=shape[1])
        return Tile(ap, name)


class KB:
    def __init__(self, nc, st, debug):
        self.nc = nc
        self.S = Sched(nc, st)
        self.A = Arena(nc, st, 212000)
        self.psum = st.enter_context(nc.psum_tensor("psum", [128, 4096], F32))
        self.pb = [Buf("bank%d" % i) for i in range(8)]
        self.debug = debug

    def bank(self, i, n=1):
        return self.psum[:, i * 512:(i + n) * 512]

    def bank_bf(self, i):
        return self.psum[:, i * 512:(i + 1) * 512].bitcast(BF16)

    def dma(self, q, out, in_, R=(), W=(), sem=None):
        return self.S.op(q, lambda e: e.dma_start(out=out, in_=in_), reads=R, writes=W, dma=sem)

    def mm(self, out, lhsT, rhs, start, stop, R=(), W=()):
        return self.S.op("pe", lambda e: e.matmul(out, lhsT=lhsT, rhs=rhs, start=start, stop=stop,
                                                  skip_group_check=True),
                         reads=R, writes=W, skip_same=True)

    def tr(self, out, in_, R=(), W=()):
        ident = self.ident
        return self.S.op("pe", lambda e: e.transpose(out=out, in_=in_, identity=ident.ap),
                         reads=list(R) + [ident], writes=W, skip_same=True)

    def act(self, out, in_, func, R=(), W=(), **kw):
        return self.S.op("act", lambda e: e.activation(out=out, in_=in_, func=func, **kw), reads=R, writes=W)

    def v(self, fn, R=(), W=()):
        return self.S.op("dve", fn, reads=R, writes=W)

    def g(self, fn, R=(), W=()):
        return self.S.op("pool", fn, reads=R, writes=W)

    rstd_mode = "sqrt"

    def rstd_chain(self, ms_ap, t):
        self.v(lambda e: e.tensor_scalar(out=ms_ap, in0=ms_ap, scalar1=EPS, scalar2=None, op0=ALU.add), R=[t], W=[t])
        if self.rstd_mode == "ln":
            self.act(ms_ap, ms_ap, AF.Ln, R=[t], W=[t])
            self.act(ms_ap, ms_ap, AF.Exp, R=[t], W=[t], scale=-0.5)
        else:
            self.act(ms_ap, ms_ap, AF.Sqrt, R=[t], W=[t])
            self.v(lambda e: e.reciprocal(out=ms_ap, in_=ms_ap), R=[t], W=[t])

    def load_bcast(self, dram_row, n, name, sem):
        t = self.A.alloc([n], F32, name)
        self.dma("sp", t.ap, dram_row.partition_broadcast(128), W=[t], sem=sem)
        return t

    def load_w_groups(self, dram_w, rows, cols, name, sem, gcols):
        nch = rows // 128
        t = self.A.alloc([nch, cols], BF16, name)
        src = dram_w.rearrange("(c p) n -> p c n", p=128)
        bufs = []
        for g0 in range(0, cols, gcols):
            g1 = min(cols, g0 + gcols)
            b = Buf("%s_g%d" % (name, g0))
            bufs.append(b)
            self.dma("pool", t.ap[:, :, g0:g1], src[:, :, g0:g1], W=[b], sem=sem)
        return t, (lambda col: bufs[col // gcols])

    def load_w(self, dram_w, rows, cols, name, sem, defer=False):
        nch = rows // 128
        t = self.A.alloc([nch, cols], BF16, name)
        if defer:
            return t, (lambda: self._issue_w(t, dram_w, nch, sem))
        self._issue_w(t, dram_w, nch, sem)
        return t

    def _issue_w(self, t, dram_w, nch, sem):
        src = dram_w.rearrange("(c p) n -> p c n", p=128)
        step = max(1, nch // 4)
        for c0 in range(0, nch, step):
            c1 = min(nch, c0 + step)
            self.dma("pool", t.ap[:, c0:c1, :], src[:, c0:c1, :], W=[t], sem=sem)


def ffn_phase(k, src, dst, wg_d, wu_d, wd_d, gpre_d, gpost_d, tag):
    A, S = k.A, k.S
    mark = A.top
    GC = 640
    nchw = D // 128
    wg = A.alloc([nchw, DFF], BF16, "wg")
    wu = A.alloc([nchw, DFF], BF16, "wu")
    wgb, wub = [], []
    wg0b, wu0b = Buf("wg0"), Buf("wu0")
    for (t_, d_, b0, sm) in ((wg, wg_d, wg0b, "w0"), (wu, wu_d, wu0b, "w1")):
        k.dma("pool", t_.ap[:, :, 0:128], d_.rearrange("(c p) n -> p c n", p=128)[:, :, 0:128], W=[b0], sem=sm)
    for g0 in range(0, DFF, GC):
        for (t_, d_, bl, sm) in ((wg, wg_d, wgb, "w0"), (wu, wu_d, wub, "w1")):
            b = Buf("wgrp")
            bl.append(b)
            g1 = min(DFF, g0 + GC)
            ga = 128 if g0 == 0 else g0
            k.dma("pool", t_.ap[:, :, ga:g1], d_.rearrange("(c p) n -> p c n", p=128)[:, :, ga:g1],
                  W=[b], sem=sm)
    wd = k.load_w(wd_d, DFF, D, "wd", "w2")
    gpre = k.load_bcast(gpre_d, D, "gpre", "c0")
    gpost = k.load_bcast(gpost_d, D, "gpost", "c1")
    k.v(lambda e: e.tensor_scalar(out=gpost.ap, in0=gpost.ap, scalar1=0.5, scalar2=None, op0=ALU.mult),
        R=[gpost], W=[gpost])
    xs = [A.alloc([D], F32, "xs%d" % i) for i in range(2)]
    xr = [A.alloc([D], F32, "xr%d" % i) for i in range(2)]
    hn = [A.alloc([D], BF16, "hn%d" % i) for i in range(2)]
    hT = [A.alloc([8, 512], BF16, "hT%d" % i) for i in range(2)]
    AT = A.alloc([22, 512], BF16, "AT")
    sg = [A.alloc([512], BF16, "sg%d" % i) for i in range(2)]
    ytmp = A.alloc([D], F32, "ytmp")
    junk = A.alloc([D], BF16, "junk")
    ms = [A.alloc([4], F32, "ms%d" % i) for i in range(2)]
    ms2 = [A.alloc([1], F32, "ms2%d" % i) for i in range(2)]
    srct = src.rearrange("(n p) d -> n p d", p=128)
    dstt = dst.rearrange("(n p) d -> n p d", p=128)
    PB = k.pb
    cnt = {"x": 0, "r": 0}

    def pn_chain(c, tt):
        m = ms[c % 2]
        gt = 4 * c + tt
        i = tt % 2
        x = xs[i]
        k.dma("sp", x.ap, srct[gt], W=[x], sem="xs%d" % i)
        hh = hn[i]
        k.act(hh.ap, x.ap, AF.Square, R=[x], W=[hh, m], scale=1.0 / 32.0, accum_out=m.ap[:, tt:tt + 1])
        k.rstd_chain(m.ap[:, tt:tt + 1], m)
        k.v(lambda e: e.scalar_tensor_tensor(out=hh.ap, in0=x.ap, scalar=m.ap[:, tt:tt + 1], in1=gpre.ap,
                                             op0=ALU.mult, op1=ALU.mult), R=[x, m, gpre], W=[hh])

    def pn_tr(c, tt):
        h = hT[c % 2]
        hh = hn[tt % 2]
        pT = k.bank_bf(6)
        for dc in range(8):
            k.tr(pT[:, dc * 128:(dc + 1) * 128], hh.ap[:, dc * 128:(dc + 1) * 128], R=[hh], W=[PB[6]])
        k.act(h.ap[:, :, tt * 128:(tt + 1) * 128], pT.rearrange("p (a b) -> p a b", a=8), AF.Copy,
              R=[PB[6]], W=[h])

    def prenorm(c):
        for tt in range(4):
            pn_chain(c, tt)
            pn_tr(c, tt)

    def gateup(c):
        h = hT[c % 2]
        for f in range(22):
            gb, ub = f % 2, 2 + f % 2
            for dc in range(8):
                k.mm(k.bank(gb), wg.ap[:, dc, f * 128:(f + 1) * 128], h.ap[:, dc, :], dc == 0, dc == 7,
                     R=[wg0b if f == 0 else wgb[f * 128 // GC], h], W=[PB[gb]])
            for dc in range(8):
                k.mm(k.bank(ub), wu.ap[:, dc, f * 128:(f + 1) * 128], h.ap[:, dc, :], dc == 0, dc == 7,
                     R=[wu0b if f == 0 else wub[f * 128 // GC], h], W=[PB[ub]])
            s = sg[f % 2]
            k.act(s.ap, k.bank(gb), AF.Silu, R=[PB[gb]], W=[s])
            k.v(lambda e, s=s, ub=ub, f=f: e.tensor_tensor(out=AT.ap[:, f, :], in0=s.ap, in1=k.bank(ub), op=ALU.mult),
                R=[s, PB[ub]], W=[AT])
            if c + 1 < NCH:
                if f in (1, 6, 11, 16):
                    pn_chain(c + 1, (f - 1) // 5)
                if f in (4, 9, 14, 19):
                    pn_tr(c + 1, (f - 4) // 5)

    def down(c):
        for tt in range(4):
            gt = 4 * c + tt
            i = cnt["r"] % 2
            cnt["r"] += 1
            r = xr[i]
            k.dma("sp", r.ap, srct[gt], W=[r], sem="xr%d" % i)
            yb0 = 4 + 2 * (tt % 2)
            for half in range(2):
                for f in range(22):
                    k.mm(k.bank(yb0 + half), AT.ap[:, f, tt * 128:(tt + 1) * 128],
                         wd.ap[:, f, half * 512:(half + 1) * 512], f == 0, f == 21,
                         R=[AT, wd], W=[PB[yb0 + half]])
            m2 = ms2[i]
            k.v(lambda e, yb0=yb0: e.tensor_copy(out=ytmp.ap, in_=k.bank(yb0, 2)), R=[PB[yb0], PB[yb0 + 1]], W=[ytmp])
            k.act(junk.ap, ytmp.ap, AF.Square, R=[ytmp], W=[junk, m2], scale=1.0 / 32.0, accum_out=m2.ap)
            k.rstd_chain(m2.ap, m2)
            k.v(lambda e, m2=m2: e.scalar_tensor_tensor(out=ytmp.ap, in0=ytmp.ap, scalar=m2.ap, in1=gpost.ap,
                                                        op0=ALU.mult, op1=ALU.mult),
                R=[m2, gpost, ytmp], W=[ytmp])
            k.g(lambda e, r=r: e.tensor_tensor(out=r.ap, in0=r.ap, in1=ytmp.ap, op=ALU.add), R=[r, ytmp], W=[r])
            k.dma("pool", dstt[gt], r.ap, R=[r], sem="xo%d" % i)

    prenorm(0)
    for c in range(NCH):
        gateup(c)
        down(c)
    S.barrier()
    A.top = mark


def make_prenorm(k, gpre, nx=2):
    A = k.A
    xs = [A.alloc([D], F32, "pxs%d" % i) for i in range(nx)]
    hn = [A.alloc([D], BF16, "phn%d" % i) for i in range(2)]
    hT = [A.alloc([8, 512], BF16, "phT%d" % i) for i in range(2)]
    ms = [A.alloc([4], F32, "pms%d" % i) for i in range(2)]
    cnt = {"x": 0}
    PB = k.pb

    def norm_tile(x, m, tt, h, g, bank=6):
        i = cnt["x"] % 2
        cnt["x"] += 1
        hh = hn[i]
        k.act(hh.ap, x.ap if isinstance(x, Tile) else x[0], AF.Square, R=[x if isinstance(x, Tile) else x[1]],
              W=[hh, m], scale=1.0 / 32.0, accum_out=m.ap[:, tt:tt + 1])
        k.rstd_chain(m.ap[:, tt:tt + 1], m)
        xa = x.ap if isinstance(x, Tile) else x[0]
        xt = x if isinstance(x, Tile) else x[1]
        k.v(lambda e: e.scalar_tensor_tensor(out=hh.ap, in0=xa, scalar=m.ap[:, tt:tt + 1], in1=g.ap,
                                             op0=ALU.mult, op1=ALU.mult), R=[xt, m, g], W=[hh])
        pT = k.bank_bf(bank)
        for dc in range(8):
            k.tr(pT[:, dc * 128:(dc + 1) * 128], hh.ap[:, dc * 128:(dc + 1) * 128], R=[hh], W=[PB[bank]])
        k.act(h.ap[:, :, tt * 128:(tt + 1) * 128], pT.rearrange("p (a b) -> p a b", a=8), AF.Copy,
              R=[PB[bank]], W=[h])

    def prenorm(c, srct):
        h = hT[c % 2]
        m = ms[c % 2]
        for tt in range(4):
            gt = 4 * c + tt
            i = cnt["x"] % nx
            x = xs[i]
            k.dma("sp", x.ap, srct[gt], W=[x], sem="pxs%d" % i)
            norm_tile(x, m, tt, h, gpre)
        return h

    def chain(c, tt, srct):
        m = ms[c % 2]
        gt = 4 * c + tt
        x = xs[tt % nx]
        k.dma("sp", x.ap, srct[gt], W=[x], sem="pxs%d" % (tt % nx))
        hh = hn[tt % 2]
        k.act(hh.ap, x.ap, AF.Square, R=[x], W=[hh, m], scale=1.0 / 32.0, accum_out=m.ap[:, tt:tt + 1])
        k.rstd_chain(m.ap[:, tt:tt + 1], m)
        k.v(lambda e: e.scalar_tensor_tensor(out=hh.ap, in0=x.ap, scalar=m.ap[:, tt:tt + 1], in1=gpre.ap,
                                             op0=ALU.mult, op1=ALU.mult), R=[x, m, gpre], W=[hh])

    def tr(c, tt, bank=6):
        h = hT[c % 2]
        hh = hn[tt % 2]
        pT = k.bank_bf(bank)
        for dc in range(8):
            k.tr(pT[:, dc * 128:(dc + 1) * 128], hh.ap[:, dc * 128:(dc + 1) * 128], R=[hh], W=[PB[bank]])
        k.act(h.ap[:, :, tt * 128:(tt + 1) * 128], pT.rearrange("p (a b) -> p a b", a=8), AF.Copy,
              R=[PB[bank]], W=[h])
        return h

    prenorm.chain = chain
    prenorm.tr = tr
    prenorm.norm_tile = norm_tile
    prenorm.hT = hT
    prenorm.ms = ms
    return prenorm


FM = [(0, 4, 0.125), (512, 4, 1.0), (1536, 4, 0.125), (2048, 1, 1.0), (2176, 1, 1.0), (2304, 1, 1.0), (2560, 1, 1.0)]


def inproj_phase(k, x1, I, zT, vtok, gts, kv):
    A, S, PB = k.A, k.S, k.pb
    mark = A.top
    gpre = k.load_bcast(I["mix_pre_g"], D, "gpre", "c0")
    win, wgrp = k.load_w_groups(I["w_mix_in"], D, MIXIN, "win", "w0", 512)
    prenorm = make_prenorm(k, gpre, nx=4)
    stg = [A.alloc([512], BF16, "stg%d" % i) for i in range(3)]
    vst = [A.alloc([768], BF16, "vst%d" % i) for i in range(2)]
    gst = [A.alloc([24], F32, "gst%d" % i) for i in range(2)]
    srct = x1.rearrange("(n p) d -> n p d", p=128)
    vtokt = vtok.rearrange("(n p) d -> n p d", p=128)
    gtst = gts.rearrange("(n p) d -> n p d", p=128)
    KxT, Vx = kv
    gmem = k.load_bcast(I["mem_norm_g"], D, "gmem", "c0")
    wk = A.alloc([D // 128, D], BF16, "wk")
    wv = A.alloc([D // 128, D], BF16, "wv")
    kvp = {"n": 0}

    def kv_piece():
        n = kvp["n"]
        if n >= 16:
            return
        kvp["n"] += 1
        t_, d_, sm = (wk, I["xa_w_k"], "w3") if n < 8 else (wv, I["xa_w_v"], "w4")
        c0 = n % 8
        k.dma("pool", t_.ap[:, c0:c0 + 1, :], d_.rearrange("(c p) n -> p c n", p=128)[:, c0:c0 + 1, :],
              W=[t_], sem=sm)
    mT = A.alloc([8, 256], BF16, "mT")
    msm = A.alloc([2], F32, "msm")
    memt = I["mem"].rearrange("(n p) d -> n p d", p=128)
    xm = [A.alloc([D], F32, "xm%d" % i) for i in range(2)]
    mhn = [A.alloc([D], BF16, "mhn%d" % i) for i in range(2)]
    for mt in range(2):
        k.dma("sp", xm[mt].ap, memt[mt], W=[xm[mt]], sem="c1")

    def kv_setup():
        for mt in range(2):
            k.act(mhn[mt].ap, xm[mt].ap, AF.Square, R=[xm[mt]], W=[mhn[mt], msm], scale=1.0 / 32.0,
                  accum_out=msm.ap[:, mt:mt + 1])
        k.rstd_chain(msm.ap, msm)
        for mt in range(2):
            k.v(lambda e, mt=mt: e.scalar_tensor_tensor(out=mhn[mt].ap, in0=xm[mt].ap, scalar=msm.ap[:, mt:mt + 1],
                                                        in1=gmem.ap, op0=ALU.mult, op1=ALU.mult),
                R=[xm[mt], msm, gmem], W=[mhn[mt]])
            pT = k.bank_bf(7)
            for dc in range(8):
                k.tr(pT[:, dc * 128:(dc + 1) * 128], mhn[mt].ap[:, dc * 128:(dc + 1) * 128], R=[mhn[mt]], W=[PB[7]])
            k.act(mT.ap[:, :, mt * 128:(mt + 1) * 128], pT.rearrange("p (a b) -> p a b", a=8), AF.Copy,
                  R=[PB[7]], W=[mT])
        for ft in range(8):
            bk = 4 + ft % 2
            for dc in range(8):
                k.mm(k.bank(bk)[:, 0:256], wk.ap[:, dc, ft * 128:(ft + 1) * 128], mT.ap[:, dc, :], dc == 0, dc == 7,
                     R=[wk, mT], W=[PB[bk]])
            k.act(KxT.ap[:, ft, :], k.bank(bk)[:, 0:256], AF.Copy, R=[PB[bk]], W=[KxT])
        for mt in range(2):
            for half in range(2):
                bk = 4 + half
                for dc in range(8):
                    k.mm(k.bank(bk), mT.ap[:, dc, mt * 128:(mt + 1) * 128], wv.ap[:, dc, half * 512:(half + 1) * 512],
                         dc == 0, dc == 7, R=[wv, mT], W=[PB[bk]])
                k.v(lambda e, mt=mt, half=half, bk=bk: e.tensor_copy(
                    out=Vx.ap[:, mt, half * 512:(half + 1) * 512], in_=k.bank(bk)), R=[PB[bk]], W=[Vx])

    hnext = prenorm(0, srct)
    for c in range(NCH):
        h = hnext
        i = 0
        for (z0, ntile, sc) in FM:
            for ft in range(ntile):
                bk = i % 2
                col = z0 + ft * 128
                for dc in range(8):
                    k.mm(k.bank(bk), win.ap[:, dc, col:col + 128], h.ap[:, dc, :], dc == 0, dc == 7,
                         R=[wgrp(col), h], W=[PB[bk]])
                s = stg[i % 3]
                if sc != 1.0:
                    k.v(lambda e, s=s, bk=bk, sc=sc: e.tensor_scalar(out=s.ap, in0=k.bank(bk), scalar1=sc, scalar2=None,
                                                                     op0=ALU.mult), R=[PB[bk]], W=[s])
                else:
                    k.act(s.ap, k.bank(bk), AF.Copy, R=[PB[bk]], W=[s])
                k.dma("pool" if sc != 1.0 else "act", zT[col:col + 128, c * 512:(c + 1) * 512], s.ap, R=[s],
                      sem="stg%d" % (i % 3))
                if c < 2 and i % 2 == 1:
                    kv_piece()
                if c + 1 < NCH:
                    if i in (1, 5, 9, 13):
                        prenorm.chain(c + 1, (i - 1) // 4, srct)
                    if i in (3, 7, 11, 15):
                        hnext = prenorm.tr(c + 1, (i - 3) // 4)
                i += 1
        for tt in range(4):
            gt = 4 * c + tt
            hs = h.ap[:, :, tt * 128:(tt + 1) * 128]
            for dc in range(8):
                k.mm(k.bank(2), hs[:, dc, :], win.ap[:, dc, 1024:1536], dc == 0, dc == 7,
                     R=[wgrp(1024), h], W=[PB[2]])
            for (o0, z0, n) in ((0, 2432, 128), (128, 2688, 128), (256, 2816, 24)):
                for dc in range(8):
                    k.mm(k.bank(3)[:, o0:o0 + n], hs[:, dc, :], win.ap[:, dc, z0:z0 + n], dc == 0, dc == 7,
                         R=[wgrp(z0), h], W=[PB[3]])
            vs = vst[gt % 2]
            k.act(vs.ap[:, 0:512], k.bank(2), AF.Copy, R=[PB[2]], W=[vs])
            k.v(lambda e, vs=vs: e.tensor_copy(out=vs.ap[:, 512:768], in_=k.bank(3)[:, 0:256]), R=[PB[3]], W=[vs])
            k.dma("pool", vtokt[gt], vs.ap, R=[vs], sem="vst%d" % (gt % 2))
            gs = gst[gt % 2]
            k.v(lambda e, gs=gs: e.tensor_copy(out=gs.ap, in_=k.bank(3)[:, 256:280]), R=[PB[3]], W=[gs])
            k.dma("pool", gtst[gt], gs.ap, R=[gs], sem="gst%d" % (gt % 2))
        if c == 3:
            while kvp["n"] < 16:
                kv_piece()
            kv_setup()
    S.barrier()
    A.top = mark


class AttnPipe:
    def __init__(self, k, PT):
        self.k = k
        self.PT = PT
        self.n = 0
        self.pending = None
        self.obank_first = {}

    def _qk(self, st):
        k = self.k
        sb = st["sb"]
        lo, hi = st["lo"], st["hi"]
        out = k.bank(sb)[:, lo:hi]
        ex = st["extra"]
        k.mm(out, st["kT"], st["qT"], True, len(ex) == 0, R=st["Rqk"], W=[k.pb[sb]])
        for n, (lhsT, rhs, a, b, R) in enumerate(ex):
            k.mm(k.bank(sb)[:, a:b], lhsT, rhs, False, n == len(ex) - 1, R=R, W=[k.pb[sb]])

    def _rest(self, st):
        k = self.k
        sb = st["sb"]
        if "restfn" in st:
            st["restfn"](sb)
            return
        lo, hi = st["lo"], st["hi"]
        pt = st["pt"]
        k.act(pt.ap[0:st["kp"], lo:hi], k.bank(sb)[0:st["kp"], lo:hi], AF.Exp, R=[k.pb[sb]], W=[pt], scale=st["scale"])
        if st.get("mask") is not None:
            mt, eng = st["mask"]
            k.S.op(eng, lambda e: e.tensor_tensor(out=pt.ap[:, lo:hi], in0=pt.ap[:, lo:hi], in1=mt.ap[:, lo:hi],
                                                  op=ALU.mult), reads=[pt, mt], writes=[pt])
        for (oreg, ob, lhs_lo, lhs_hi, rhs, start, stop, R) in st["pv"]:
            k.mm(oreg, pt.ap[0:st["kp"], lhs_lo:lhs_hi], rhs, start, stop, R=[pt] + R, W=[k.pb[ob]])
        if st.get("evac") is not None:
            st["evac"]()

    def push(self, st):
        st["sb"] = self.n % 4
        st["pt"] = self.PT[self.n % len(self.PT)]
        self.n += 1
        if "qkfn" in st:
            st["qkfn"](st["sb"])
        else:
            self._qk(st)
        if self.pending is None:
            self.pending = []
        self.pending.append(st)
        if len(self.pending) > 3:
            self._rest(self.pending.pop(0))

    def flush(self):
        while self.pending:
            self._rest(self.pending.pop(0))


def attn_phase(k, I, zT, vtok, gts, osc):
    A, S, PB = k.A, k.S, k.pb
    mark = A.top
    deferred = []

    def cload(name, shape, dtype, src, q="sp", defer=False):
        t = A.alloc(shape, dtype, name)
        if defer:
            deferred.append(lambda: k.dma(q, t.ap, src, W=[t], sem="cc_" + name))
        else:
            k.dma(q, t.ap, src, W=[t], sem="cc_" + name)
        return t
    tric = cload("tric", [128], BF16, I["c_tric"])
    tril = cload("tril", [128], BF16, I["c_tril"])
    gates = cload("gates", [NT, 24], F32, gts.rearrange("(n p) j -> p n j", p=128))
    k.act(gates.ap, gates.ap, AF.Exp, R=[gates], W=[gates], scale=-1.0)
    k.v(lambda e: e.tensor_scalar(out=gates.ap, in0=gates.ap, scalar1=1.0, scalar2=None, op0=ALU.add), R=[gates], W=[gates])
    k.v(lambda e: e.reciprocal(out=gates.ap, in_=gates.ap), R=[gates], W=[gates])
    lq1 = k.load_bcast(I["da_lambda_q1"], 64, "lq1", "cb_lq1")
    lk1 = k.load_bcast(I["da_lambda_k1"], 64, "lk1", "cb_lk1")
    lq2 = k.load_bcast(I["da_lambda_q2"], 64, "lq2", "cb_lq2")
    lk2 = k.load_bcast(I["da_lambda_k2"], 64, "lk2", "cb_lk2")
    lsum = A.alloc([2], F32, "lsum")
    neglam = A.alloc([1], F32, "neglam")
    k.v(lambda e: e.tensor_tensor(out=lq1.ap, in0=lq1.ap, in1=lk1.ap, op=ALU.mult), R=[lq1, lk1], W=[lq1])
    k.v(lambda e: e.tensor_tensor(out=lq2.ap, in0=lq2.ap, in1=lk2.ap, op=ALU.mult), R=[lq2, lk2], W=[lq2])
    k.v(lambda e: e.reduce_sum(out=lsum.ap[:, 0:1], in_=lq1.ap, axis=AX.X), R=[lq1], W=[lsum])
    k.v(lambda e: e.reduce_sum(out=lsum.ap[:, 1:2], in_=lq2.ap, axis=AX.X), R=[lq2], W=[lsum])
    k.act(lsum.ap, lsum.ap, AF.Exp, R=[lsum], W=[lsum])
    lam_init = 0.8 - 0.6 * math.exp(-0.3 * 0)
    k.v(lambda e: e.tensor_tensor(out=neglam.ap, in0=lsum.ap[:, 1:2], in1=lsum.ap[:, 0:1], op=ALU.subtract),
        R=[lsum], W=[neglam])
    k.v(lambda e: e.tensor_scalar(out=neglam.ap, in0=neglam.ap, scalar1=-lam_init, scalar2=None, op0=ALU.add),
        R=[neglam], W=[neglam])
    gsub = k.load_bcast(I["da_subln_g"], 128, "gsub", "cb_gsub")
    k.v(lambda e: e.tensor_scalar(out=gsub.ap, in0=gsub.ap, scalar1=1.0 - lam_init, scalar2=None, op0=ALU.mult),
        R=[gsub], W=[gsub])

    QB = [A.alloc([T], BF16, "QB%d" % i) for i in range(2)]
    KBf = [A.alloc([T], BF16, "KB%d" % i) for i in range(2)]
    PT = [A.alloc([512], BF16, "PT%d" % i) for i in range(4)]
    pipe = AttnPipe(k, PT)
    rl = [A.alloc([4], F32, "rl%d" % i) for i in range(2)]
    rg = [A.alloc([4], F32, "rg%d" % i) for i in range(2)]
    t2 = [A.alloc([4, 128], F32, "t2%d" % i) for i in range(2)]
    ob = [A.alloc([4, 128], BF16, "ob%d" % i) for i in range(2)]
    msd = [A.alloc([4], F32, "msd%d" % i) for i in range(2)]
    osct = osc.rearrange("(n p) d -> p n d", p=128)
    state = {"cj": 0, "ev": 0}

    def oview(pair):
        return k.bank(4 + 2 * pair, 2).rearrange("p (j w) -> p j w", j=4)

    def load_q(buf, zrow, slot):
        if buf.b2 is None:
            buf.b2 = Buf(buf.b.name + "_aug")
        k.dma("sp", buf.ap[0:64, :], zT[zrow:zrow + 64, :], W=[buf], sem="q%s" % buf.b.name)
        k.dma("sp", buf.ap[64:68, :], I["c_qaug"][slot], W=[buf.b2], sem="qa%s" % buf.b.name)

    def load_k(buf, zrow):
        if buf.b2 is None:
            buf.b2 = Buf(buf.b.name + "_aug")
        k.dma("sp", buf.ap[0:64, :], zT[zrow:zrow + 64, :], W=[buf], sem="k%s" % buf.b.name)
        k.dma("sp", buf.ap[64:68, :], I["c_kaug"], W=[buf.b2], sem="ka%s" % buf.b.name)

    def load_v(buf, col, dv):
        src = vtok.rearrange("(n p) d -> p n d", p=128)
        for n0 in range(0, NT, 8):
            k.dma("sp", buf.ap[:, n0:n0 + 8, 0:dv], src[:, n0:n0 + 8, col:col + dv], W=[buf], sem="v%s" % buf.b.name)

    def run_job(qb, kT_of, v_of, dv1, tiles_of, extras_of, evac_of, scale=1.0, kp_of=None, hook=None):
        for c in range(NCH):
            pair = state["cj"] % 2
            state["cj"] += 1
            ov = oview(pair)
            tl = tiles_of(c)
            last_for = {}
            for n, (kt, j0, j1) in enumerate(tl):
                for j in range(j0, j1):
                    last_for[j] = n
            started = set()
            for n, (kt, j0, j1) in enumerate(tl):
                lo, hi = j0 * 128, j1 * 128
                kTap, kR = kT_of(kt)
                vap, vR = v_of(kt)
                kp = 128 if kp_of is None else kp_of(kt)
                pv = []
                for j in range(j0, j1):
                    obk = 4 + 2 * pair + j // 2
                    start = obk not in started
                    started.add(obk)
                    pv.append((ov[:, j, 0:dv1], obk, j * 128, (j + 1) * 128, vap, start, last_for[j] == n, vR))
                st = dict(kT=kTap, qT=qb.ap[0:68, c * 512 + lo:c * 512 + hi], lo=lo, hi=hi, Rqk=[qb] + kR,
                          extra=extras_of(c, kt, j0, j1), pv=pv, scale=scale, kp=kp,
                          evac=(evac_of(c, pair) if n == len(tl) - 1 else None))
                pipe.push(st)
            if hook is not None:
                hook(c)

    def causal_tiles(c):
        tl = [(kt, 0, 4) for kt in range(4 * c)]
        tl += [(4 * c + i, i, 4) for i in range(4)]
        return tl

    def causal_extras(c, kt, j0, j1):
        if kt >= 4 * c:
            i = kt - 4 * c
            return [(k.ident.ap, tric.ap, i * 128, (i + 1) * 128, [k.ident, tric])]
        return []

    def rl_of(ov, col, i, clamp=False):
        r = rl[i]
        if clamp:
            k.v(lambda e: e.tensor_scalar(out=r.ap.rearrange("p (a b) -> p a b", b=1), in0=ov[:, :, col:col + 1],
                                          scalar1=1e-30, scalar2=None, op0=ALU.max), R=[], W=[r])
            k.v(lambda e: e.reciprocal(out=r.ap, in_=r.ap), R=[r], W=[r])
        else:
            k.v(lambda e: e.reciprocal(out=r.ap.rearrange("p (a b) -> p a b", b=1), in_=ov[:, :, col:col + 1]),
                R=[], W=[r])
        return r

    cmask = cload("cmask", [2560], BF16, I["c_cmask"], q="act", defer=True)
    expand = cload("expand", [4096], BF16, I["c_expand"], q="act", defer=True)
    selA = cload("selA", [NT, 64], F32, I["c_selA"].rearrange("(n p) j -> p n j", p=128), q="act", defer=True)
    selB = cload("selB", [NT, 64], F32, I["c_selB"].rearrange("(n p) j -> p n j", p=128), q="act", defer=True)
    tricn = cload("tricn", [128], BF16, I["c_tricn"], q="act", defer=True)
    w1 = [A.alloc([32, 128], BF16, "w1%d" % i) for i in range(2)]
    w2 = [A.alloc([64], BF16, "w2%d" % i) for i in range(2)]
    peT = [A.alloc([32], F32, "peT%d" % i) for i in range(2)]
    peTb = [A.alloc([32], BF16, "peTb%d" % i) for i in range(2)]
    cb = [A.alloc([1], F32, "cb%d" % i) for i in range(2)]
    for i, nm in enumerate(("k", "v")):
        pe_src = I["cmp_%s_pe" % nm].rearrange("pos d -> d pos")

        def _ld(i=i, nm=nm, pe_src=pe_src):
            k.dma("pool", w1[i].ap[0:64], I["cmp_%s_w1" % nm].rearrange("(pos d) h -> d pos h", d=64), W=[w1[i]], sem="w0")
            k.dma("pool", w2[i].ap, I["cmp_%s_w2" % nm], W=[w2[i]], sem="w0")
            S.op("act", lambda e: e.dma_start(out=peT[i].ap[0:64], in_=pe_src, allow_slow_non_contiguous=True),
                 writes=[peT[i].b], dma="c1")
        deferred.append(_ld)
    def da_evac(h, m):
        def evac_of(c, pair):
            def ev():
                i = state["ev"] % 2
                state["ev"] += 1
                ov = oview(pair)
                OB = [PB[4 + 2 * pair], PB[5 + 2 * pair]]
                r = rl[i]
                k.v(lambda e: e.reciprocal(out=r.ap.rearrange("p (a b) -> p a b", b=1), in_=ov[:, :, 128:129]),
                    R=OB, W=[r])
                rb = r.ap.rearrange("p (a b) -> p a b", b=1).to_broadcast([128, 4, 128])
                dch = datmp.ap[:, 4 * c:4 * c + 4, :]
                if m == 0:
                    k.v(lambda e: e.tensor_tensor(out=dch, in0=ov[:, :, 0:128], in1=rb, op=ALU.mult),
                        R=OB + [r], W=[datmp])
                    return
                tt_ = t2[i]
                k.v(lambda e: e.tensor_tensor(out=tt_.ap, in0=ov[:, :, 0:128], in1=rb, op=ALU.mult),
                    R=OB + [r], W=[tt_])
                k.v(lambda e: e.scalar_tensor_tensor(out=dch, in0=tt_.ap, scalar=neglam.ap, in1=dch,
                                                     op0=ALU.mult, op1=ALU.add), R=[tt_, neglam, datmp], W=[datmp])
                k.v(lambda e: e.tensor_tensor(out=tt_.ap, in0=dch, in1=dch, op=ALU.mult), R=[datmp], W=[tt_])
                k.v(lambda e: e.reduce_sum(out=msall.ap[:, 4 * c:4 * c + 4], in_=tt_.ap, axis=AX.X), R=[tt_], W=[msall])
                if c == NCH - 1:
                    k.v(lambda e: e.tensor_scalar(out=msall.ap, in0=msall.ap, scalar1=1.0 / 128.0, scalar2=EPS,
                                                  op0=ALU.mult, op1=ALU.add), R=[msall], W=[msall])
                    k.act(msall.ap, msall.ap, AF.Ln, R=[msall], W=[msall])
                    k.act(msall.ap, msall.ap, AF.Exp, R=[msall], W=[msall], scale=-0.5)
                    for q4 in range(4):
                        sl = slice(8 * q4, 8 * q4 + 8)
                        mb = msall.ap[:, sl].rearrange("p (a b) -> p a b", b=1).to_broadcast([128, 8, 128])
                        gb = gsub.ap.rearrange("p (a b) -> p a b", a=1).broadcast_to([128, 8, 128])
                        k.v(lambda e, sl=sl, mb=mb: e.tensor_tensor(out=datmp.ap[:, sl, :], in0=datmp.ap[:, sl, :], in1=mb,
                                                                    op=ALU.mult), R=[datmp, msall], W=[datmp])
                        k.v(lambda e, sl=sl, gb=gb: e.tensor_tensor(out=oball.ap[:, sl, :], in0=datmp.ap[:, sl, :], in1=gb,
                                                                    op=ALU.mult), R=[datmp, gsub], W=[oball])
                    k.dma("pool", osct[:, :, h * 128:(h + 1) * 128], oball.ap, R=[oball], sem="oball")
            return ev
        return evac_of

    da_jobs = [(h, m) for h in range(4) for m in range(2)]

    def da_load(n):
        h, m = da_jobs[n]
        load_q(QB[n % 2], h * 128 + m * 64, h)
        load_k(KBf[n % 2], 512 + h * 128 + m * 64)
        if m == 0:
            load_v(VA[h % 2], h * 128, 128)

    if k.stage_attn & 1:
        mda = A.top
        VA = [A.alloc([NT, 129], BF16, "VA%d" % i) for i in range(2)]
        for t in VA:
            k.g(lambda e, t=t: e.memset(t.ap[:, :, 128:129], 1.0), W=[t])
        datmp = A.alloc([NT, 128], F32, "datmp")
        msall = A.alloc([NT], F32, "msall")
        oball = A.alloc([NT, 128], BF16, "oball")
        da_load(0)
        for n, (h, m) in enumerate(da_jobs):
            if n + 1 < len(da_jobs):
                da_load(n + 1)
            if n == 1:
                while deferred:
                    deferred.pop(0)()
            qb, kb, vb = QB[n % 2], KBf[n % 2], VA[h % 2]
            run_job(qb,
                    lambda kt, kb=kb: (kb.ap[0:68, kt * 128:(kt + 1) * 128], [kb]),
                    lambda kt, vb=vb: (vb.ap[:, kt, :], [vb]),
                    129, causal_tiles, causal_extras, da_evac(h, m))
        pipe.flush()
        S.barrier()
        A.top = mda

    if k.stage_attn & 2:
        while deferred:
            deferred.pop(0)()
        VN = [A.alloc([NT, 65], BF16, "VN%d" % i) for i in range(2)]
        for t in VN:
            k.g(lambda e, t=t: e.memset(t.ap[:, :, 64:65], 1.0), W=[t])
        onsa = [A.alloc([NT, 64], F32, "onsa%d" % i) for i in range(4)]
        imp = A.alloc([NT, 64], F32, "imp")
        QS = [A.alloc([T], BF16, "QS%d" % i) for i in range(4)]
        Mtiles = [A.alloc([512], BF16, "Mt%d" % i) for i in range(3)]
        for i, nm in enumerate(("k", "v")):
            k.v(lambda e, i=i: e.tensor_copy(out=peTb[i].ap[0:64], in_=peT[i].ap[0:64]), R=[peT[i]], W=[peTb[i]])
            for pos in range(32):
                k.mm(k.bank(3)[:, i:i + 1], w1[i].ap[0:64, pos, :], peTb[i].ap[0:64, pos:pos + 1], pos == 0, pos == 31,
                     R=[w1[i], peTb[i]], W=[PB[3]])
            k.v(lambda e, i=i: e.tensor_copy(out=cb[i].ap, in_=k.bank(3)[:, i:i + 1]), R=[PB[3]], W=[cb[i]])
        cin = [QB[0], QB[1]]
        AcT = [A.alloc([256], BF16, "AcT%d" % i) for i in range(2)]
        KcT = A.alloc([256], BF16, "KcT")
        Vc = A.alloc([2, 129], BF16, "Vc")
        negT = A.alloc([T], BF16, "negT")
        score = A.alloc([NT, 64], F32, "score")
        sc2 = A.alloc([64], F32, "sc2")
        m8 = A.alloc([8], F32, "m8")
        thr = A.alloc([NT], F32, "thr")
        negm = A.alloc([NT, 64], BF16, "negm")
        k.g(lambda e: e.memset(Vc.ap, 0.0), W=[Vc])
        k.g(lambda e: e.memset(Vc.ap[:, :, 128:129], 1.0), W=[Vc])
        k.dma("sp", Vc.ap[:, :, 64:128], I["c_mcs"].rearrange("(n p) j -> p n j", p=128), W=[Vc], sem="c1")
        print("NSA arena top", A.top)
        k.g(lambda e: e.memset(KcT.ap, 0.0), W=[KcT])
        k.dma("sp", KcT.ap[64:68, :], I["c_kaugc"], W=[KcT], sem="c1")

        for g in range(2):
            for i, zr in enumerate((2048, 2176)):
                k.dma("sp", cin[i].ap[0:64, :], zT[zr + g * 64:zr + g * 64 + 64, :], W=[cin[i]], sem="cin%d" % i)
                for pos in range(32):
                    k.mm(k.bank(3)[:, 8:8 + 255], w1[i].ap[0:64, pos, :], cin[i].ap[0:64, pos:pos + 16 * 254 + 1:16],
                         pos == 0, pos == 31, R=[w1[i], cin[i]], W=[PB[3]])
                k.g(lambda e, i=i: e.memset(AcT[i].ap, 0.0), W=[AcT[i]])
                k.act(AcT[i].ap[:, 0:255], k.bank(3)[:, 8:8 + 255], AF.Silu, R=[PB[3], cb[i]], W=[AcT[i]], bias=cb[i].ap)
            k.mm(k.bank(3)[0:64, 0:255], w2[0].ap, AcT[0].ap[:, 0:255], True, True, R=[w2[0], AcT[0]], W=[PB[3]])
            k.v(lambda e: e.tensor_copy(out=KcT.ap[0:64, 0:255], in_=k.bank(3)[0:64, 0:255]), R=[PB[3]], W=[KcT])
            for ct in range(2):
                k.mm(k.bank(3)[:, 256 + ct * 64:256 + (ct + 1) * 64], AcT[1].ap[:, ct * 128:(ct + 1) * 128], w2[1].ap,
                     True, True, R=[w2[1], AcT[1]], W=[PB[3]])
            k.v(lambda e: e.tensor_copy(out=Vc.ap[:, :, 0:64],
                                        in_=k.bank(3)[:, 256:384].rearrange("p (a b) -> p a b", a=2)),
                R=[PB[3]], W=[Vc])

            def cmp_tiles(c):
                tl = [(0, 0, 4)]
                if c >= 4:
                    tl.append((1, 0, 4))
                return tl

            def cmp_extras(c, kt, j0, j1):
                if kt == 0 and c >= 5:
                    return []
                off = c * 512 if kt == 0 else (c - 4) * 512
                return [(k.ident.ap, cmask.ap[:, off:off + 512], 0, 512, [k.ident, cmask])]

            def cmp_evac(r, head):
                def evac_of(c, pair):
                    def ev():
                        i = state["ev"] % 2
                        state["ev"] += 1
                        ov = oview(pair)
                        OB = [PB[4 + 2 * pair], PB[5 + 2 * pair]]
                        r_ = rl[i]
                        r3 = r_.ap.rearrange("p (a b) -> p a b", b=1)
                        k.v(lambda e: e.tensor_scalar(out=r3, in0=ov[:, :, 128:129], scalar1=1e-30, scalar2=None,
                                                      op0=ALU.max), R=OB, W=[r_])
                        k.v(lambda e: e.reciprocal(out=r_.ap, in_=r_.ap), R=[r_], W=[r_])
                        g_ = rg[i]
                        k.v(lambda e: e.tensor_tensor(out=g_.ap.rearrange("p (a b) -> p a b", b=1), in0=r3,
                                                      in1=gates.ap[:, 4 * c:4 * c + 4, head * 3:head * 3 + 1], op=ALU.mult),
                            R=[r_, gates], W=[g_])
                        gb = g_.ap.rearrange("p (a b) -> p a b", b=1).to_broadcast([128, 4, 64])
                        k.v(lambda e: e.tensor_tensor(out=onsa[r].ap[:, 4 * c:4 * c + 4, :], in0=ov[:, :, 0:64], in1=gb,
                                                      op=ALU.mult), R=OB + [g_], W=[onsa[r]])
                        rb = r3.to_broadcast([128, 4, 64])
                        ich = imp.ap[:, 4 * c:4 * c + 4, :]
                        if r == 0:
                            k.v(lambda e: e.tensor_tensor(out=ich, in0=ov[:, :, 64:128], in1=rb, op=ALU.mult),
                                R=OB + [r_], W=[imp])
                        else:
                            tt_ = t2[i]
                            k.v(lambda e: e.tensor_tensor(out=tt_.ap[:, :, 0:64], in0=ov[:, :, 64:128], in1=rb,
                                                          op=ALU.mult), R=OB + [r_], W=[tt_])
                            k.g(lambda e: e.tensor_tensor(out=ich, in0=ich, in1=tt_.ap[:, :, 0:64], op=ALU.add),
                                R=[tt_, imp], W=[imp])
                    return ev
                return evac_of

            def acc_evac(r, head, branch, final):
                def evac_of(c, pair, ov=None, OB=None):
                    def ev(ov=ov, OB=OB):
                        i = state["ev"] % 2
                        state["ev"] += 1
                        if ov is None:
                            ov = oview(pair)
                            OB = [PB[4 + 2 * pair], PB[5 + 2 * pair]]
                        r_ = rl[i]
                        r3 = r_.ap.rearrange("p (a b) -> p a b", b=1)
                        k.v(lambda e: e.reciprocal(out=r3, in_=ov[:, :, 64:65]), R=OB, W=[r_])
                        g_ = rg[i]
                        k.v(lambda e: e.tensor_tensor(out=g_.ap.rearrange("p (a b) -> p a b", b=1), in0=r3,
                                                      in1=gates.ap[:, 4 * c:4 * c + 4, head * 3 + branch:head * 3 + branch + 1],
                                                      op=ALU.mult), R=[r_, gates], W=[g_])
                        gb = g_.ap.rearrange("p (a b) -> p a b", b=1).to_broadcast([128, 4, 64])
                        tt_ = t2[i]
                        k.v(lambda e: e.tensor_tensor(out=tt_.ap[:, :, 0:64], in0=ov[:, :, 0:64], in1=gb, op=ALU.mult),
                            R=OB + [g_], W=[tt_])
                        och = onsa[r].ap[:, 4 * c:4 * c + 4, :]
                        if not final:
                            k.g(lambda e: e.tensor_tensor(out=och, in0=och, in1=tt_.ap[:, :, 0:64], op=ALU.add),
                                R=[tt_, onsa[r]], W=[onsa[r]])
                        else:
                            o_ = ob[i]
                            k.v(lambda e: e.tensor_tensor(out=o_.ap[:, :, 0:64], in0=och, in1=tt_.ap[:, :, 0:64], op=ALU.add),
                                R=[tt_, onsa[r]], W=[o_])
                            k.dma("pool", osct[:, 4 * c:4 * c + 4, 512 + head * 64:512 + (head + 1) * 64], o_.ap[:, :, 0:64],
                                  R=[o_], sem="ob%d" % i)
                    return ev
                return evac_of

            for r in range(4):
                load_q(QS[r], 1536 + (g * 4 + r) * 64, 4 + g * 4 + r)
            kvs = KBf[0], VN[0]
            kvw = KBf[1], VN[1]
            load_k(kvs[0], 2304 + g * 64)
            load_v(kvs[1], 512 + g * 64, 64)
            load_k(kvw[0], 2560 + g * 64)
            load_v(kvw[1], 640 + g * 64, 64)

            for r in range(4):
                run_job(QS[r],
                        lambda kt: (KcT.ap[0:68, kt * 128:(kt + 1) * 128], [KcT]),
                        lambda kt: (Vc.ap[:, kt, :], [Vc]),
                        129, cmp_tiles, cmp_extras, cmp_evac(r, g * 4 + r))
            pipe.flush()

            k.v(lambda e: e.tensor_tensor(out=score.ap, in0=imp.ap, in1=selA.ap, op=ALU.mult), R=[imp, selA], W=[score])
            k.v(lambda e: e.tensor_tensor(out=score.ap, in0=score.ap, in1=selB.ap, op=ALU.add), R=[score, selB], W=[score])

            def sel_piece(n):
                k.v(lambda e: e.max(out=m8.ap, in_=score.ap[:, n, :]), R=[score], W=[m8])
                k.v(lambda e: e.match_replace(out=sc2.ap, in_to_replace=m8.ap, in_values=score.ap[:, n, :],
                                              imm_value=-3.0), R=[score, m8], W=[sc2])
                k.v(lambda e: e.max(out=m8.ap, in_=sc2.ap), R=[sc2], W=[m8])
                k.v(lambda e: e.tensor_copy(out=thr.ap[:, n:n + 1], in_=m8.ap[:, 7:8]), R=[m8], W=[thr])

            def sel_extras(c, kt, j0, j1):
                ex = [(expand.ap[0:64, kt * 128:(kt + 1) * 128], negT.ap[0:64, c * 512 + j0 * 128:c * 512 + j1 * 128],
                       j0 * 128, j1 * 128, [expand, negT])]
                return ex + causal_extras(c, kt, j0, j1)

            def win_tiles(c):
                tl = []
                for kt in range(max(0, 4 * c - 4), 4 * c + 4):
                    j0 = max(0, kt - 4 * c)
                    j1 = min(3, kt - 4 * c + 4) + 1
                    tl.append((kt, j0, j1))
                return tl

            def win_extras(c, kt, j0, j1):
                ex = []
                if kt >= 4 * c:
                    i = kt - 4 * c
                    ex.append((k.ident.ap, tric.ap, i * 128, (i + 1) * 128, [k.ident, tric]))
                if kt < 4 * c:
                    i = kt - 4 * c + 4
                    ex.append((k.ident.ap, tril.ap, i * 128, (i + 1) * 128, [k.ident, tril]))
                return ex

            seln = {"n": 0}

            def win_hook(c):
                sel_piece(seln["n"])
                seln["n"] += 1

            for r in range(4):
                kb, vb = kvw
                run_job(QS[r],
                        lambda kt, kb=kb: (kb.ap[0:68, kt * 128:(kt + 1) * 128], [kb]),
                        lambda kt, vb=vb: (vb.ap[:, kt, :], [vb]),
                        65, win_tiles, win_extras, acc_evac(r, g * 4 + r, 2, False), hook=win_hook)
            pipe.flush()
            tb = thr.ap.rearrange("p (a b) -> p a b", b=1).to_broadcast([128, NT, 64])
            k.v(lambda e: e.tensor_tensor(out=score.ap, in0=score.ap, in1=tb, op=ALU.is_ge), R=[score, thr], W=[score])
            k.v(lambda e: e.tensor_tensor(out=score.ap, in0=score.ap, in1=selA.ap, op=ALU.mult), R=[score, selA], W=[score])
            k.v(lambda e: e.tensor_copy(out=negm.ap, in_=score.ap), R=[score], W=[negm])
            for n0 in range(0, NT, 8):
                pT = k.bank_bf(3)
                for n in range(n0, n0 + 8):
                    k.tr(pT[0:64, (n - n0) * 128:(n - n0 + 1) * 128], negm.ap[:, n, :], R=[negm], W=[PB[3]])
                k.v(lambda e, n0=n0, pT=pT: e.tensor_copy(out=negT.ap[0:64, n0 * 128:(n0 + 8) * 128], in_=pT[0:64, :]),
                    R=[PB[3]], W=[negT])
            kb, vb = kvs
            mi = 0
            for c in range(NCH):
                tl = causal_tiles(c)
                started = set()
                for n, (kt, j0, j1) in enumerate(tl):
                    lo, hi = j0 * 128, j1 * 128
                    diag = kt >= 4 * c
                    Mt = Mtiles[mi % 3]
                    mi += 1
                    def mqk(sb, lo=lo, hi=hi, kt=kt, c=c, diag=diag):
                        k.mm(k.bank(sb)[:, lo:hi], expand.ap[0:64, kt * 128:(kt + 1) * 128],
                             negT.ap[0:64, c * 512 + lo:c * 512 + hi], True, not diag, R=[expand, negT], W=[PB[sb]])
                        if diag:
                            i = kt - 4 * c
                            k.mm(k.bank(sb)[:, i * 128:(i + 1) * 128], k.ident.ap, tricn.ap, False, True,
                                 R=[k.ident, tricn], W=[PB[sb]])

                    def mrest(sb, lo=lo, hi=hi, Mt=Mt):
                        k.v(lambda e: e.tensor_scalar(out=Mt.ap[:, lo:hi], in0=k.bank(sb)[:, lo:hi], scalar1=0.0,
                                                      scalar2=None, op0=ALU.max), R=[PB[sb]], W=[Mt])

                    pipe.push(dict(qkfn=mqk, restfn=mrest))
                    for r in range(4):
                        ovr = k.bank(4 + r)[:, 0:260].rearrange("p (j w) -> p j w", j=4)
                        pv = []
                        for j in range(j0, j1):
                            start = (4 + r) not in started
                            started.add(4 + r)
                            pv.append((ovr[:, j, 0:65], 4 + r, j * 128, (j + 1) * 128, vb.ap[:, kt, :], start,
                                       kt == 4 * c + j, [vb]))
                        last = n == len(tl) - 1
                        st = dict(kT=kb.ap[0:68, kt * 128:(kt + 1) * 128],
                                  qT=QS[r].ap[0:68, c * 512 + lo:c * 512 + hi], lo=lo, hi=hi, Rqk=[QS[r], kb],
                                  extra=[], pv=pv, scale=1.0, kp=128, mask=(Mt, "dve"),
                                  evac=(acc_evac(r, g * 4 + r, 1, True)(c, None, ovr, [PB[4 + r]]) if last else None))
                        pipe.push(st)
            pipe.flush()
    S.barrier()
    A.top = mark


def outxa_phase(k, I, x1, osc, x3, kv):
    A, S, PB = k.A, k.S, k.pb
    mark = A.top
    wout = k.load_w(I["w_mix_out"], D, D, "wout", "w0")
    wq = k.load_w(I["xa_w_q"], D, D, "wq", "w1")
    wo = k.load_w(I["xa_w_o"], D, D, "wo", "w2")
    gmixpost = k.load_bcast(I["mix_post_g"], D, "gmp", "c0")
    gxapre = k.load_bcast(I["xa_pre_g"], D, "gxp", "c1")
    gxapost = k.load_bcast(I["xa_post_g"], D, "gxo", "c2")
    KxT, Vx = kv
    ones = A.alloc([128], BF16, "ones")
    k.g(lambda e: e.memset(ones.ap, 1.0), W=[ones])
    hn = [A.alloc([D], BF16, "hn%d" % i) for i in range(2)]
    junk = A.alloc([D], BF16, "junk")
    TB = 2

    tfc = {"n": 0}

    def to_fm(src_ap, src_R, dst, col0, nhn, Wb=None):
        Wb = [dst] if Wb is None else Wb
        tb = (TB, 7)[tfc["n"] % 2]
        tfc["n"] += 1
        pT = k.bank_bf(tb)
        for dc in range(8):
            k.tr(pT[:, dc * 128:(dc + 1) * 128], src_ap[:, dc * 128:(dc + 1) * 128], R=src_R, W=[PB[tb]])
        k.act(dst.ap[:, :, col0:col0 + 128], pT.rearrange("p (a b) -> p a b", a=8), AF.Copy, R=[PB[tb]], W=Wb)

    ot = [A.alloc([D], BF16, "ot%d" % i) for i in range(4)]
    oT = A.alloc([8, 512], BF16, "oT")
    x2c = [[A.alloc([D], F32, "x2c%d_%d" % (i, j)) for j in range(4)] for i in range(3)]
    yb = [A.alloc([D], F32, "yb%d" % i) for i in range(4)]
    yb2 = [A.alloc([D], F32, "yb2%d" % i) for i in range(4)]
    msA = [A.alloc([4], F32, "msA%d" % i) for i in range(2)]
    msB = [A.alloc([4], F32, "msB%d" % i) for i in range(2)]
    msC = [A.alloc([4], F32, "msC%d" % i) for i in range(2)]
    h3T = A.alloc([8, 512], BF16, "h3T")
    qxT = [A.alloc([8, 512], BF16, "qxT%d" % i) for i in range(2)]
    PT = [A.alloc([512], BF16, "PTx%d" % i) for i in range(4)]
    oxT = oT
    rLb = [A.alloc([512], F32, "rLb%d" % i) for i in range(2)]
    osct = osc.rearrange("(n p) d -> n p d", p=128)
    x1tt = x1.rearrange("(n p) d -> n p d", p=128)
    x3t = x3.rearrange("(n p) d -> n p d", p=128)
    SB = [3, 4, 5]
    print("outxa arena top", A.top)
    cnt = {"s": 0, "e": 0}

    oTb = [Buf("oTb%d" % i) for i in range(4)]

    def proj_tile(srcT, tt, w, dst):
        for half in range(2):
            for fc in range(8):
                k.mm(k.bank(half), srcT.ap[:, fc, tt * 128:(tt + 1) * 128], w.ap[:, fc, half * 512:(half + 1) * 512],
                     fc == 0, fc == 7, R=[oTb[tt], w], W=[PB[half]])
            k.v(lambda e, half=half: e.tensor_copy(out=dst.ap[:, half * 512:(half + 1) * 512], in_=k.bank(half)),
                R=[PB[half]], W=[dst])

    def front1(c):
        X = x2c[c % 3]
        for tt in range(4):
            k.dma("sp", ot[tt].ap, osct[4 * c + tt], W=[ot[tt]], sem="ot%d" % tt)
        for tt in range(4):
            k.dma("sp", X[tt].ap, x1tt[4 * c + tt], W=[X[tt]], sem="x2c%d_%d" % (c % 3, tt))
        tf = lambda tt: to_fm(ot[tt].ap, [ot[tt]], oT, tt * 128, tt % 2, Wb=[oTb[tt]])
        pj = lambda tt: proj_tile(oT, tt, wout, yb[tt])
        tf(0); tf(1); pj(0); tf(2); pj(1); tf(3); pj(2); pj(3)

    def front2a(c):
        X = x2c[c % 3]
        mA, mB = msA[c % 2], msB[c % 2]
        for tt in range(4):
            k.act(junk.ap, yb[tt].ap, AF.Square, R=[yb[tt]], W=[junk, mA], scale=1.0 / 32.0, accum_out=mA.ap[:, tt:tt + 1])
        k.rstd_chain(mA.ap, mA)
        for tt in range(4):
            k.v(lambda e, tt=tt: e.scalar_tensor_tensor(out=yb[tt].ap, in0=yb[tt].ap, scalar=mA.ap[:, tt:tt + 1],
                                                        in1=gmixpost.ap, op0=ALU.mult, op1=ALU.mult),
                R=[yb[tt], mA, gmixpost], W=[yb[tt]])
            k.g(lambda e, tt=tt: e.tensor_tensor(out=X[tt].ap, in0=X[tt].ap, in1=yb[tt].ap, op=ALU.add),
                R=[yb[tt], X[tt]], W=[X[tt]])

    def front2b(c):
        X = x2c[c % 3]
        mA, mB = msA[c % 2], msB[c % 2]
        for tt in range(4):
            k.act(junk.ap, X[tt].ap, AF.Square, R=[X[tt]], W=[junk, mB], scale=1.0 / 32.0, accum_out=mB.ap[:, tt:tt + 1])
        k.rstd_chain(mB.ap, mB)

    def mixed(c, cb):
        X = x2c[c % 3]
        mB = msB[c % 2]

        def stt(tt):
            hh = hn[tt % 2]
            k.v(lambda e: e.scalar_tensor_tensor(out=hh.ap, in0=X[tt].ap, scalar=mB.ap[:, tt:tt + 1],
                                                 in1=gxapre.ap, op0=ALU.mult, op1=ALU.mult),
                R=[X[tt], mB, gxapre], W=[hh])

        tf = lambda tt: to_fm(hn[tt % 2].ap, [hn[tt % 2]], h3T, tt * 128, tt % 2)
        pj = lambda tt: proj_tile(oxT, tt, wo, yb2[tt])
        stt(0); stt(1); tf(0); pj(0); stt(2); tf(1); pj(1); stt(3); tf(2); pj(2); tf(3); pj(3)

    def front2c(c):
        q_ = qxT[c % 2]
        for ft in range(8):
            bk = SB[ft % 3]
            for dc in range(8):
                k.mm(k.bank(bk), wq.ap[:, dc, ft * 128:(ft + 1) * 128], h3T.ap[:, dc, :], dc == 0, dc == 7,
                     R=[wq, h3T], W=[PB[bk]])
            if ft % 2 == 0:
                k.act(q_.ap[:, ft, :], k.bank(bk), AF.Copy, R=[PB[bk]], W=[q_])
            else:
                k.v(lambda e, ft=ft, bk=bk: e.tensor_copy(out=q_.ap[:, ft, :], in_=k.bank(bk)), R=[PB[bk]], W=[q_])

    def back1(c, heads):
        q_ = qxT[c % 2]

        def qk(hh):
            for mt in range(2):
                bk = (3, 4)[mt]
                for j in range(2):
                    k.mm(k.bank(bk), KxT.ap[:, hh * 2 + j, mt * 128:(mt + 1) * 128], q_.ap[:, hh * 2 + j, :],
                         j == 0, j == 1, R=[KxT, q_], W=[PB[bk]])
                pt = PT[(2 * hh + mt) % 4]
                k.act(pt.ap, k.bank(bk), AF.Exp, R=[PB[bk]], W=[pt], scale=1.0 / 16.0)

        def lo(hh):
            pts = [PT[(2 * hh + mt) % 4] for mt in range(2)]
            for mt in range(2):
                k.mm(k.bank(7), ones.ap, pts[mt].ap, mt == 0, mt == 1, R=[ones, pts[mt]], W=[PB[7]])
            for dvc in range(2):
                for mt in range(2):
                    k.mm(k.bank(5 + dvc), Vx.ap[:, mt, hh * 256 + dvc * 128:hh * 256 + (dvc + 1) * 128], pts[mt].ap,
                         mt == 0, mt == 1, R=[Vx, pts[mt]], W=[PB[5 + dvc]])
            rL = rLb[hh % 2]
            k.act(rL.ap, k.bank(7), AF.Ln, R=[PB[7]], W=[rL])
            k.act(rL.ap, rL.ap, AF.Exp, R=[rL], W=[rL], scale=-1.0)
            for dvc in range(2):
                k.v(lambda e, dvc=dvc: e.tensor_tensor(out=oxT.ap[:, hh * 2 + dvc, :], in0=k.bank(5 + dvc), in1=rL.ap,
                                                       op=ALU.mult), R=[PB[5 + dvc], rL], W=oTb)

        qk(heads[0])
        for n, hh in enumerate(heads):
            if n + 1 < len(heads):
                qk(heads[n + 1])
            lo(hh)

    def back2(c):
        X = x2c[c % 3]
        mC = msC[c % 2]
        for tt in range(4):
            proj_tile(oxT, tt, wo, yb2[tt])

    def back2b(c):
        X = x2c[c % 3]
        mC = msC[c % 2]
        for tt in range(4):
            k.act(junk.ap, yb2[tt].ap, AF.Square, R=[yb2[tt]], W=[junk, mC], scale=1.0 / 32.0, accum_out=mC.ap[:, tt:tt + 1])
        k.rstd_chain(mC.ap, mC)
        for tt in range(4):
            gt = 4 * c + tt
            k.v(lambda e, tt=tt: e.scalar_tensor_tensor(out=yb2[tt].ap, in0=yb2[tt].ap, scalar=mC.ap[:, tt:tt + 1],
                                                        in1=gxapost.ap, op0=ALU.mult, op1=ALU.mult),
                R=[yb2[tt], mC, gxapost], W=[yb2[tt]])
            k.g(lambda e, tt=tt: e.tensor_tensor(out=yb2[tt].ap, in0=yb2[tt].ap, in1=X[tt].ap, op=ALU.add),
                R=[yb2[tt], X[tt]], W=[yb2[tt]])
            k.dma("pool", x3t[gt], yb2[tt].ap, R=[yb2[tt]], sem="x3o%d" % tt)

    front1(0)
    front2a(0)
    front2b(0)
    for tt in range(4):
        k.v(lambda e, tt=tt: e.scalar_tensor_tensor(out=hn[tt % 2].ap, in0=x2c[0][tt].ap, scalar=msB[0].ap[:, tt:tt + 1],
                                                    in1=gxapre.ap, op0=ALU.mult, op1=ALU.mult),
            R=[x2c[0][tt], msB[0], gxapre], W=[hn[tt % 2]])
        to_fm(hn[tt % 2].ap, [hn[tt % 2]], h3T, tt * 128, tt % 2)
    front2c(0)
    for c in range(NCH):
        nxt = c + 1 < NCH
        if nxt:
            front1(c + 1)
            front2a(c + 1)
        back1(c, (0, 1, 2, 3))
        if nxt:
            front2b(c + 1)
            mixed(c + 1, c)
            front2c(c + 1)
        else:
            back2(c)
        back2b(c)
    S.barrier()
    A.top = mark


def build(stage=99, debug=False, stage_attn=3):
    nc = bass.Bass("TRN2", target_bir_lowering=False)
    dt = lambda name, shape, dtype, kind: nc.dram_tensor(name, list(shape), dtype, kind=kind).ap()
    I = {}
    for name, shape in IN_SHAPES.items():
        I[name] = dt(name, shape, F32, "ExternalInput")
    for name, (shape, dtype) in CONST_SHAPES.items():
        I[name] = dt(name, shape, dtype, "ExternalInput")
    skind = "ExternalOutput" if debug else "Internal"
    x1 = dt("x1", [T, D], F32, skind)
    zT = dt("zT", [MIXIN, T], BF16, skind)
    vtok = dt("vtok", [T, 768], BF16, skind)
    gts = dt("gts", [T, 24], F32, skind)
    osc = dt("osc", [T, D], BF16, skind)
    x3 = dt("x3", [T, D], F32, skind)
    out = dt("out", [T, D], F32, "ExternalOutput")
    with ExitStack() as st:
        k = KB(nc, st, debug)
        k.stage_attn = stage_attn
        k.ident = k.A.alloc([128], BF16, "ident")
        k.dma("sp", k.ident.ap, I["c_ident"], W=[k.ident], sem="c9")
        ffn_phase(k, I["x"], x1, I["ffn1_w_gate"], I["ffn1_w_up"], I["ffn1_w_down"],
                  I["ffn1_pre_g"], I["ffn1_post_g"], "f1")
        base = k.A.top
        kv = (k.A.alloc([8, 256], BF16, "KxT"), k.A.alloc([2, D], BF16, "Vx"))
        if stage >= 2:
            inproj_phase(k, x1, I, zT, vtok, gts, kv)
        k.rstd_mode = "ln"
        if stage >= 3:
            attn_phase(k, I, zT, vtok, gts, osc)
        if stage >= 4:
            outxa_phase(k, I, x1, osc, x3, kv)
        k.rstd_mode = "sqrt"
        k.A.top = base
        if stage >= 5:
            ffn_phase(k, x3, out, I["ffn2_w_gate"], I["ffn2_w_up"], I["ffn2_w_down"],
                      I["ffn2_pre_g"], I["ffn2_post_g"], "f2")
        k.S.barrier()
        k.S.emit()
        print("ops", k.S.nops, "sems", k.S.nsem, "arena peak", k.A.peak)
    return nc


IN_SHAPES = {
    "x": (T, D), "mem": (256, D),
    "ffn1_pre_g": (1, D), "ffn1_post_g": (1, D),
    "ffn1_w_gate": (D, DFF), "ffn1_w_up": (D, DFF), "ffn1_w_down": (DFF, D),
    "mix_pre_g": (1, D), "mix_post_g": (1, D), "w_mix_in": (D, MIXIN),
    "da_lambda_q1": (1, 64), "da_lambda_k1": (1, 64), "da_lambda_q2": (1, 64), "da_lambda_k2": (1, 64),
    "da_subln_g": (1, 128),
    "cmp_k_pe": (32, 64), "cmp_k_w1": (2048, 128), "cmp_k_w2": (128, 64),
    "cmp_v_pe": (32, 64), "cmp_v_w1": (2048, 128), "cmp_v_w2": (128, 64),
    "w_mix_out": (D, D),
    "xa_pre_g": (1, D), "xa_post_g": (1, D), "mem_norm_g": (1, D),
    "xa_w_q": (D, D), "xa_w_k": (D, D), "xa_w_v": (D, D), "xa_w_o": (D, D),
    "ffn2_pre_g": (1, D), "ffn2_post_g": (1, D),
    "ffn2_w_gate": (D, DFF), "ffn2_w_up": (D, DFF), "ffn2_w_down": (DFF, D),
}
PER_CORE = ("x", "mem")

CONST_SHAPES = {
    "c_ident": ((128, 128), BF16),
    "c_tric": ((128, 128), BF16),
    "c_tril": ((128, 128), BF16),
    "c_tricn": ((128, 128), BF16),
    "c_cmask": ((128, 2560), BF16),
    "c_expand": ((128, 4096), BF16),
    "c_selA": ((T, 64), F32),
    "c_selB": ((T, 64), F32),
    "c_mcs": ((256, 64), BF16),
    "c_qaug": ((12, 4, T), BF16),
    "c_kaug": ((4, T), BF16),
    "c_kaugc": ((4, 256), BF16),
}


def make_consts():
    bf = ml_dtypes.bfloat16
    c = {}
    c["c_ident"] = np.eye(128, dtype=np.float32).astype(bf)
    kk = np.arange(128)[:, None]
    qq = np.arange(128)[None, :]
    c["c_tric"] = np.where(kk <= qq, 0.0, NEGBIG).astype(np.float32).astype(bf)
    c["c_tril"] = np.where(kk > qq, 0.0, NEGBIG).astype(np.float32).astype(bf)
    c["c_tricn"] = np.where(kk <= qq, 0.0, -1.0).astype(np.float32).astype(bf)
    tt = np.arange(2560)[None, :]
    c["c_cmask"] = np.where(tt - 16 * kk >= 31, 0.0, NEGBIG).astype(np.float32).astype(bf)
    ex = np.zeros((128, T), np.float32)
    ex[:64] = (np.arange(T)[None, :] // 64 == np.arange(64)[:, None])
    c["c_expand"] = ex.astype(bf)
    t = np.arange(T)
    cur = (t // 64)[:, None]
    blk = np.arange(64)[None, :]
    Am = (blk <= cur).astype(np.float32)
    forced = ((blk == 0) | (blk == cur) | (blk == cur - 1)).astype(np.float32)
    c["c_selA"] = Am
    c["c_selB"] = (1e4 * forced - (1.0 - Am)).astype(np.float32)
    cs = np.arange(255) * 16
    ss = np.arange(64) * 64
    ov = np.clip(np.minimum(cs[:, None] + 32, ss[None, :] + 64) - np.maximum(cs[:, None], ss[None, :]), 0, None)
    mcs = np.zeros((256, 64), np.float32)
    mcs[:255] = ov / 32.0
    c["c_mcs"] = mcs.astype(bf)
    a = (t // 64).astype(np.float32)
    b = (t % 64).astype(np.float32)
    slopes = list(2.0 ** (-8.0 * np.arange(1, 5) / 4)) + list(2.0 ** (-8.0 * np.arange(1, 9) / 8))
    qa = np.zeros((12, 4, T), np.float32)
    for s, sl in enumerate(slopes):
        qa[s, 0] = -sl * 64 * a
        qa[s, 1] = -sl * b
        qa[s, 2] = sl
        qa[s, 3] = sl
    c["c_qaug"] = qa.astype(bf)
    ka = np.stack([np.ones(T), np.ones(T), 64 * a, b]).astype(np.float32)
    c["c_kaug"] = ka.astype(bf)
    pc = np.arange(256) * 16 + 31
    kc = np.stack([np.ones(256), np.ones(256), 64.0 * (pc // 64), 1.0 * (pc % 64)]).astype(np.float32)
    kc[:, 255] = 0
    c["c_kaugc"] = kc.astype(bf)
    return c


_CACHE = {}


def kernel(**inputs):
    n = 8
    if "nc" not in _CACHE:
        _CACHE["nc"] = build()
    nc = _CACHE["nc"]
    consts = make_consts()
    in_maps = []
    for i in range(n):
        m = {}
        for name, shape in IN_SHAPES.items():
            a = np.asarray(inputs[name], dtype=np.float32)
            a = a[i] if name in PER_CORE else a[0]
            m[name] = np.ascontiguousarray(a.reshape(shape))
        m.update(consts)
        in_maps.append(m)
    res = run_bass_kernel_spmd(nc, in_maps, core_ids=list(range(n)))
    return np.stack([r["out"] for r in res.results], axis=0).astype(np.float32)
```
